# Optimizing a Trainium2 kernel written in Bass

```python
import jax, jax.numpy as jnp
from jax import lax
import numpy as np

D_MODEL = 1024
BATCH = 16
SEQ = 2048
DEPTH = 4

N_MIXERS = 2
HEAD_DIM = 64
N_HEADS = D_MODEL // HEAD_DIM
N_KV = 4
GQA = N_HEADS // N_KV
ROT_DIM = HEAD_DIM // 4
ROPE_THETA = 500000.0
NORM_EPS = 1e-6
Q_W = N_HEADS * HEAD_DIM
KV_W = N_KV * HEAD_DIM

CMP_LEN = 32
CMP_STRIDE = 16
CMP_HID = 4 * HEAD_DIM
SEL_BLOCK = 64
SEL_COUNT = 8
WINDOW = 256
NSA_Q_BLOCK = 32
NSA_IN = 2 * Q_W + 6 * KV_W + 3 * N_HEADS

MOBA_BLOCK = 256
MOBA_TOPK = 3
MOBA_Q_BLOCK = 8
MOBA_IN = 2 * Q_W + 2 * KV_W

N_A = (DEPTH + 1) // 2
N_B = DEPTH // 2

kernel_name = "nsa_moba_hybrid_adaln_trunk"


def rms_norm(x, g):
    xf = x.astype(jnp.float32)
    y = xf * lax.rsqrt(jnp.mean(xf * xf, axis=-1, keepdims=True) + NORM_EPS)
    return (y * g.astype(jnp.float32)).astype(x.dtype)


def rope_tables(pos):
    inv_freq = ROPE_THETA ** (-jnp.arange(0, ROT_DIM, 2, dtype=jnp.float32) / ROT_DIM)
    ang = pos.astype(jnp.float32)[..., None] * inv_freq
    return jnp.cos(ang), jnp.sin(ang)


def partial_rope(x, cos, sin):
    half = ROT_DIM // 2
    c = cos[:, :, None, :].astype(x.dtype)
    s = sin[:, :, None, :].astype(x.dtype)
    x1, x2, rest = x[..., :half], x[..., half:ROT_DIM], x[..., ROT_DIM:]
    return jnp.concatenate([x1 * c - x2 * s, x2 * c + x1 * s, rest], axis=-1)


def masked_softmax(s, mask):
    s = jnp.where(mask, s.astype(jnp.float32), -jnp.inf)
    m = jnp.max(s, axis=-1, keepdims=True)
    e = jnp.exp(s - jnp.where(jnp.isfinite(m), m, 0.0))
    return e / jnp.maximum(jnp.sum(e, axis=-1, keepdims=True), 1e-30)


def split_cols(a, widths):
    return jnp.split(a, np.cumsum(widths)[:-1].tolist(), axis=-1)


def cmp_to_sel_weights(n_cmp, n_sel):
    cs = np.arange(n_cmp)[:, None] * CMP_STRIDE
    ss = np.arange(n_sel)[None, :] * SEL_BLOCK
    shared = np.clip(np.minimum(cs + CMP_LEN, ss + SEL_BLOCK) - np.maximum(cs, ss), 0, None)
    return jnp.asarray(shared / CMP_LEN, dtype=jnp.float32)


def nsa_mixer(h, cos, sin, positions, w_in, w_out, q_g, k_g, cmp_pe, cmp_w1, cmp_b1, cmp_w2, gate_b):
    B, S, _ = h.shape
    q, kc, vc, ks, vs, kw, vw, gl, z = split_cols(h @ w_in, [Q_W] + [KV_W] * 6 + [3 * N_HEADS, Q_W])
    q = partial_rope(rms_norm(q.reshape(B, S, N_HEADS, HEAD_DIM), q_g), cos, sin)
    q = q.reshape(B, S, N_KV, GQA, HEAD_DIM)
    gates = jax.nn.sigmoid(gl + gate_b).reshape(B, S, N_KV, GQA, 3)
    scale = HEAD_DIM ** -0.5

    n_cmp = (S - CMP_LEN) // CMP_STRIDE + 1
    blk_idx = np.arange(n_cmp)[:, None] * CMP_STRIDE + np.arange(CMP_LEN)[None, :]
    cmp_end = blk_idx[:, -1]

    def compress(t, i):
        blocks = t.reshape(B, S, N_KV, HEAD_DIM)[:, blk_idx] + cmp_pe[i][None, None, :, None, :]
        flat = blocks.transpose(0, 1, 3, 2, 4).reshape(B, n_cmp, N_KV, CMP_LEN * HEAD_DIM)
        return jax.nn.gelu(flat @ cmp_w1[i] + cmp_b1[i]) @ cmp_w2[i]

    cos_c, sin_c = rope_tables(positions[:, cmp_end])
    k_cmp = partial_rope(rms_norm(compress(kc, 0), k_g[0]), cos_c, sin_c)
    v_cmp = compress(vc, 1)
    cmp_end_j = jnp.asarray(cmp_end, dtype=jnp.int32)

    n_sel = S // SEL_BLOCK
    k_sel = min(SEL_COUNT, n_sel)
    sel_w = cmp_to_sel_weights(n_cmp, n_sel)
    ks = partial_rope(rms_norm(ks.reshape(B, S, N_KV, HEAD_DIM), k_g[1]), cos, sin)
    ks_blk = ks.reshape(B, n_sel, SEL_BLOCK, N_KV, HEAD_DIM).transpose(0, 3, 1, 2, 4)
    vs_blk = vs.reshape(B, n_sel, SEL_BLOCK, N_KV, HEAD_DIM).transpose(0, 3, 1, 2, 4)

    pad = ((0, 0), (WINDOW, 0), (0, 0), (0, 0))
    kw_pad = jnp.pad(partial_rope(rms_norm(kw.reshape(B, S, N_KV, HEAD_DIM), k_g[2]), cos, sin), pad)
    vw_pad = jnp.pad(vw.reshape(B, S, N_KV, HEAD_DIM), pad)

    b_i = jnp.arange(B)[:, None, None, None]
    g_i = jnp.arange(N_KV)[None, None, :, None]
    sel_blocks = jnp.arange(n_sel)
    C = NSA_Q_BLOCK

    def q_block(ci):
        t0 = ci * C
        t = t0 + jnp.arange(C)
        qc = lax.dynamic_slice_in_dim(q, t0, C, axis=1)
        gc = lax.dynamic_slice_in_dim(gates, t0, C, axis=1)
        m_cmp = (cmp_end_j[None, :] <= t[:, None])[None, :, None, None, :]
        p_cmp = masked_softmax(jnp.einsum('bcgrd,bngd->bcgrn', qc, k_cmp) * scale, m_cmp)
        o_cmp = jnp.einsum('bcgrn,bngd->bcgrd', p_cmp.astype(v_cmp.dtype), v_cmp)
        imp = jnp.einsum('bcgrn,nj->bcgj', p_cmp, sel_w)
        cur = (t // SEL_BLOCK)[:, None]
        valid = sel_blocks[None, :] <= cur
        forced = (sel_blocks[None, :] == 0) | (sel_blocks[None, :] == cur) | (sel_blocks[None, :] == cur - 1)
        score = jnp.where(forced[None, :, None, :], jnp.inf,
                          jnp.where(valid[None, :, None, :], imp, -jnp.inf))
        _, sel = lax.top_k(score, k_sel)
        kg_sel = ks_blk[b_i, g_i, sel].reshape(B, C, N_KV, k_sel * SEL_BLOCK, HEAD_DIM)
        vg_sel = vs_blk[b_i, g_i, sel].reshape(B, C, N_KV, k_sel * SEL_BLOCK, HEAD_DIM)
        kpos = (sel[..., None] * SEL_BLOCK + jnp.arange(SEL_BLOCK)).reshape(B, C, N_KV, 1, k_sel * SEL_BLOCK)
        m_slc = kpos <= t[None, :, None, None, None]
        p_slc = masked_softmax(jnp.einsum('bcgrd,bcgnd->bcgrn', qc, kg_sel) * scale, m_slc)
        o_slc = jnp.einsum('bcgrn,bcgnd->bcgrd', p_slc.astype(vg_sel.dtype), vg_sel)
        kwc = lax.dynamic_slice_in_dim(kw_pad, t0, C + WINDOW, axis=1)
        vwc = lax.dynamic_slice_in_dim(vw_pad, t0, C + WINDOW, axis=1)
        kp = t0 - WINDOW + jnp.arange(C + WINDOW)
        dist = t[:, None] - kp[None, :]
        m_win = ((dist >= 0) & (dist < WINDOW) & (kp[None, :] >= 0))[None, :, None, None, :]
        p_win = masked_softmax(jnp.einsum('bcgrd,bkgd->bcgrk', qc, kwc) * scale, m_win)
        o_win = jnp.einsum('bcgrk,bkgd->bcgrd', p_win.astype(vwc.dtype), vwc)
        return gc[..., 0:1] * o_cmp + gc[..., 1:2] * o_slc + gc[..., 2:3] * o_win

    o = lax.map(q_block, jnp.arange(S // C))
    o = jnp.moveaxis(o, 0, 1).reshape(B, S, Q_W)
    return (o * jax.nn.silu(z)) @ w_out


def moba_mixer(h, cos, sin, w_in, w_out, q_g, k_g):
    B, S, _ = h.shape
    q, k, v, z = split_cols(h @ w_in, [Q_W, KV_W, KV_W, Q_W])
    q = partial_rope(rms_norm(q.reshape(B, S, N_HEADS, HEAD_DIM), q_g), cos, sin)
    q = q.reshape(B, S, N_KV, GQA, HEAD_DIM)
    k = partial_rope(rms_norm(k.reshape(B, S, N_KV, HEAD_DIM), k_g), cos, sin)
    v = v.reshape(B, S, N_KV, HEAD_DIM)
    scale = HEAD_DIM ** -0.5

    n_blk = -(-S // MOBA_BLOCK)
    pad = ((0, 0), (0, n_blk * MOBA_BLOCK - S), (0, 0), (0, 0))
    k_pad = jnp.pad(k, pad)
    v_pad = jnp.pad(v, pad)
    k_blk = k_pad.reshape(B, n_blk, MOBA_BLOCK, N_KV, HEAD_DIM)
    k_mean = jnp.mean(k_blk.astype(jnp.float32), axis=2).astype(k.dtype)
    k_blk_t = k_blk.transpose(0, 3, 1, 2, 4)
    v_blk_t = v_pad.reshape(B, n_blk, MOBA_BLOCK, N_KV, HEAD_DIM).transpose(0, 3, 1, 2, 4)
    k_top = min(MOBA_TOPK, n_blk - 1)
    b_i = jnp.arange(B)[:, None, None, None, None]
    g_i = jnp.arange(N_KV)[None, None, :, None, None]
    blk = jnp.arange(n_blk)
    C = MOBA_Q_BLOCK

    def q_block(ci):
        t0 = ci * C
        t = t0 + jnp.arange(C)
        qc = lax.dynamic_slice_in_dim(q, t0, C, axis=1)
        own = t0 // MOBA_BLOCK
        bstart = own * MOBA_BLOCK
        k_own = lax.dynamic_slice_in_dim(k_pad, bstart, MOBA_BLOCK, axis=1)
        v_own = lax.dynamic_slice_in_dim(v_pad, bstart, MOBA_BLOCK, axis=1)
        kpos = bstart + jnp.arange(MOBA_BLOCK)
        m_own = jnp.broadcast_to((kpos[None, :] <= t[:, None])[None, :, None, None, :],
                                 (B, C, N_KV, GQA, MOBA_BLOCK))
        s_own = jnp.einsum('bcgrd,bkgd->bcgrk', qc, k_own) * scale
        if k_top > 0:
            s_gate = jnp.einsum('bcgrd,bjgd->bcgrj', qc, k_mean).astype(jnp.float32)
            s_gate = jnp.where(blk < own, s_gate, -jnp.inf)
            _, sel = lax.top_k(s_gate, k_top)
            kg = k_blk_t[b_i, g_i, sel]
            vg = v_blk_t[b_i, g_i, sel].reshape(B, C, N_KV, GQA, k_top * MOBA_BLOCK, HEAD_DIM)
            s_past = jnp.einsum('bcgrd,bcgrkld->bcgrkl', qc, kg).reshape(B, C, N_KV, GQA, k_top * MOBA_BLOCK) * scale
            m_past = jnp.broadcast_to((sel < own)[..., None],
                                      (B, C, N_KV, GQA, k_top, MOBA_BLOCK)).reshape(B, C, N_KV, GQA, k_top * MOBA_BLOCK)
            p = masked_softmax(jnp.concatenate([s_past, s_own], axis=-1),
                               jnp.concatenate([m_past, m_own], axis=-1)).astype(v.dtype)
            n_past = k_top * MOBA_BLOCK
            return (jnp.einsum('bcgrn,bcgrnd->bcgrd', p[..., :n_past], vg)
                    + jnp.einsum('bcgrk,bkgd->bcgrd', p[..., n_past:], v_own))
        p = masked_softmax(s_own, m_own).astype(v.dtype)
        return jnp.einsum('bcgrk,bkgd->bcgrd', p, v_own)

    o = lax.map(q_block, jnp.arange(S // C))
    o = jnp.moveaxis(o, 0, 1).reshape(B, S, Q_W)
    return (o * jax.nn.silu(z)) @ w_out


def setup_inputs(seed: int = 0) -> dict:
    key = jax.random.key(seed)
    ks = jax.random.split(key, 20)
    D = D_MODEL

    def nrm(k, shape, s):
        return jax.random.normal(k, shape, jnp.float32) * s

    x = nrm(ks[0], (BATCH, SEQ, D), 1.0)
    c = nrm(ks[1], (BATCH, D), 1.0)
    offset = jax.random.randint(ks[2], (BATCH, 1), 0, 4096, dtype=jnp.int32)
    positions = offset + jnp.arange(SEQ, dtype=jnp.int32)[None, :]
    return {
        "x": x,
        "c": c,
        "positions": positions,
        "norm_g": 1.0 + nrm(ks[3], (DEPTH, D), 0.02),
        "ada_w": nrm(ks[4], (DEPTH, D, 3 * D), 0.5 * D ** -0.5),
        "ada_b": nrm(ks[5], (DEPTH, 3 * D), 0.02),
        "nsa_w_in": nrm(ks[6], (N_A, D, NSA_IN), D ** -0.5),
        "nsa_w_out": nrm(ks[7], (N_A, Q_W, D), Q_W ** -0.5),
        "nsa_q_norm": 1.0 + nrm(ks[8], (N_A, HEAD_DIM), 0.02),
        "nsa_k_norm": 1.0 + nrm(ks[9], (N_A, 3, HEAD_DIM), 0.02),
        "nsa_cmp_pe": nrm(ks[10], (N_A, 2, CMP_LEN, HEAD_DIM), 0.1),
        "nsa_cmp_w1": nrm(ks[11], (N_A, 2, CMP_LEN * HEAD_DIM, CMP_HID), (CMP_LEN * HEAD_DIM) ** -0.5),
        "nsa_cmp_b1": nrm(ks[12], (N_A, 2, CMP_HID), 0.02),
        "nsa_cmp_w2": nrm(ks[13], (N_A, 2, CMP_HID, HEAD_DIM), CMP_HID ** -0.5),
        "nsa_gate_b": nrm(ks[14], (N_A, 3 * N_HEADS), 0.1),
        "moba_w_in": nrm(ks[15], (N_B, D, MOBA_IN), D ** -0.5),
        "moba_w_out": nrm(ks[16], (N_B, Q_W, D), Q_W ** -0.5),
        "moba_q_norm": 1.0 + nrm(ks[17], (N_B, HEAD_DIM), 0.02),
        "moba_k_norm": 1.0 + nrm(ks[18], (N_B, HEAD_DIM), 0.02),
    }


def reference(x, c, positions, norm_g, ada_w, ada_b, nsa_w_in, nsa_w_out, nsa_q_norm, nsa_k_norm,
              nsa_cmp_pe, nsa_cmp_w1, nsa_cmp_b1, nsa_cmp_w2, nsa_gate_b,
              moba_w_in, moba_w_out, moba_q_norm, moba_k_norm):
    cos, sin = rope_tables(positions)
    cond = jax.nn.silu(c)
    for i in range(DEPTH):
        shift, scale, gate = jnp.split(cond @ ada_w[i] + ada_b[i], 3, axis=-1)
        h = rms_norm(x, norm_g[i]) * (1 + scale[:, None, :]) + shift[:, None, :]
        j = i // N_MIXERS
        if i % N_MIXERS == 0:
            y = nsa_mixer(h, cos, sin, positions, nsa_w_in[j], nsa_w_out[j], nsa_q_norm[j], nsa_k_norm[j],
                          nsa_cmp_pe[j], nsa_cmp_w1[j], nsa_cmp_b1[j], nsa_cmp_w2[j], nsa_gate_b[j])
        else:
            y = moba_mixer(h, cos, sin, moba_w_in[j], moba_w_out[j], moba_q_norm[j], moba_k_norm[j])
        x = x + gate[:, None, :] * y
    return x
```

```python
import numpy as np
from contextlib import ExitStack
import concourse.bass as bass
import concourse.mybir as mybir
from concourse.bass_utils import run_bass_kernel_spmd

F32 = mybir.dt.float32
BF16 = mybir.dt.bfloat16
I32 = mybir.dt.int32
AF = mybir.ActivationFunctionType
ALU = mybir.AluOpType
AX = mybir.AxisListType

S = 2048
D = 1024
NT = 16
HD = 64
NH = 16
NG = 4
NEG = -30000.0
EPS = 1e-6
N_CORES = 8
NSA_IN = 3632
MOBA_IN = 2560
NDMA = 24
FRESH_POOL_SEMS = [False]
POOL_DESC_LIMIT = 4608


class Sched:
    def __init__(self, nc, es):
        self.nc = nc
        self.es = es
        self.nfresh = 0
        self.capture = None
        self.pool_fifo = []
        self.pool_desc = 0
        self.eng = {"pe": nc.tensor, "act": nc.scalar, "dve": nc.vector, "pool": nc.gpsimd, "sp": nc.sync}
        self.semh = {}
        for k in self.eng:
            self.semh[("e", k)] = es.enter_context(nc.semaphore("sem_" + k))
        for i in range(NDMA):
            self.semh[("d", i)] = es.enter_context(nc.semaphore("dsem%d" % i))
        self.cnt = {k: 0 for k in self.eng}
        self.dcnt = [0] * NDMA
        self.drr = 0
        self.seen = {k: {} for k in self.eng}
        self.lastw = {}
        self.readers = {}

    def _deps(self, eng, reads, writes, strict=False):
        deps = {}

        def add(t, same_ok):
            sk, v = t
            if same_ok and not strict and sk == ("e", eng) and eng == "pe":
                return
            if deps.get(sk, 0) < v:
                deps[sk] = v

        for k in reads:
            if k in self.lastw:
                add(self.lastw[k], False)
        for k in writes:
            if k in self.lastw:
                add(self.lastw[k], True)
            for sk, v in self.readers.get(k, {}).items():
                add((sk, v), True)
        return deps

    def _wait(self, eng, deps):
        for sk, v in deps.items():
            if eng == "pe" and sk == ("e", "pe"):
                continue
            if self.seen[eng].get(sk, 0) < v:
                self.eng[eng].wait_ge(self.semh[sk], v)
                self.seen[eng][sk] = v

    def _record(self, tok, reads, writes):
        for k in writes:
            self.lastw[k] = tok
            self.readers[k] = {}
        for k in reads:
            d = self.readers.setdefault(k, {})
            if d.get(tok[0], 0) < tok[1]:
                d[tok[0]] = tok[1]

    def op(self, eng, fn, reads=(), writes=()):
        if self.capture is not None:
            self.capture.append(("op", eng, fn, tuple(reads), tuple(writes), None))
            return
        self._wait(eng, self._deps(eng, reads, writes))
        inst = fn(self.eng[eng])
        self.cnt[eng] += 1
        inst.then_inc(self.semh[("e", eng)], 1)
        self._record((("e", eng), self.cnt[eng]), reads, writes)

    def replay(self, item):
        kind, a, b, reads, writes, kw = item
        if kind == "op":
            self.op(a, b, reads, writes)
        else:
            self.dma(a, b[0], b[1], reads, writes, **kw)

    def dma(self, queue, out, in_, reads=(), writes=(), **kw):
        if self.capture is not None:
            self.capture.append(("dma", queue, (out, in_), tuple(reads), tuple(writes), dict(kw)))
            return
        if FRESH_POOL_SEMS[0] and queue == "pool":
            kw.pop("ndesc", None)
            idx = NDMA + self.nfresh
            self.nfresh += 1
            self.semh[("d", idx)] = self.es.enter_context(self.nc.semaphore("fsem%d" % idx))
            self.dcnt.append(0)
            self._wait(queue, self._deps(queue, reads, writes, strict=True))
            inst = self.eng[queue].dma_start(out=out, in_=in_, **kw)
            self.dcnt[idx] += 16
            inst.then_inc(self.semh[("d", idx)], 16)
            self._record((("d", idx), self.dcnt[idx]), reads, writes)
            return
        ndesc = kw.pop("ndesc", 1024)
        idx = self.drr
        self.drr = (idx + 1) % NDMA
        if queue == "pool":
            while self.pool_fifo and self.pool_desc + ndesc > POOL_DESC_LIMIT:
                sk, v, nd = self.pool_fifo.pop(0)
                self.pool_desc -= nd
                if self.seen["pool"].get(sk, 0) < v:
                    self.eng["pool"].wait_ge(self.semh[sk], v)
                    self.seen["pool"][sk] = v
        deps = self._deps(queue, reads, writes, strict=True)
        if self.dcnt[idx]:
            deps[("d", idx)] = max(deps.get(("d", idx), 0), self.dcnt[idx])
        self._wait(queue, deps)
        inst = self.eng[queue].dma_start(out=out, in_=in_, **kw)
        self.dcnt[idx] += 16
        inst.then_inc(self.semh[("d", idx)], 16)
        self._record((("d", idx), self.dcnt[idx]), reads, writes)
        if queue == "pool":
            self.pool_fifo.append((("d", idx), self.dcnt[idx], ndesc))
            self.pool_desc += ndesc

    def barrier(self):
        for k in self.eng:
            for k2 in self.eng:
                if k2 != k and self.cnt[k2] > self.seen[k].get(("e", k2), 0):
                    self.eng[k].wait_ge(self.semh[("e", k2)], self.cnt[k2])
                    self.seen[k][("e", k2)] = self.cnt[k2]
            for i in range(len(self.dcnt)):
                if self.dcnt[i] > self.seen[k].get(("d", i), 0):
                    self.eng[k].wait_ge(self.semh[("d", i)], self.dcnt[i])
                    self.seen[k][("d", i)] = self.dcnt[i]

    def finish(self):
        e = self.eng["sp"]
        for i in range(len(self.dcnt)):
            if self.dcnt[i] and self.seen["sp"].get(("d", i), 0) < self.dcnt[i]:
                e.wait_ge(self.semh[("d", i)], self.dcnt[i])
        for k in self.eng:
            if k != "sp" and self.cnt[k] and self.seen["sp"].get(("e", k), 0) < self.cnt[k]:
                e.wait_ge(self.semh[("e", k)], self.cnt[k])


def host_consts():
    p = np.arange(128)[:, None]
    f = np.arange(128)[None, :]
    c = {}
    c["c_ident"] = (p == f).astype(np.float32)
    c["c_cb"] = np.where(p > f, NEG, 0.0).astype(np.float32)
    c["c_cb2"] = np.where(p <= f, NEG, 0.0).astype(np.float32)
    t = np.arange(S)[None, :]
    cm = np.where((16 * p + 31 > t) | (p == 127), NEG, 0.0)
    c["c_cmask"] = cm.astype(np.float32)
    fb = np.zeros((128, NT, 32), np.float32)
    for i in range(NT):
        cur = (128 * i + np.arange(128)) // 64
        for j in range(32):
            fb[:, i, j] = np.where((j == 0) | (j == cur) | (j == cur - 1), 100.0, 0.0)
    c["c_fb"] = fb.reshape(128, NT * 32)
    fbm = np.zeros((128, NT, 8), np.float32)
    for i in range(NT):
        own = i // 2
        for j in range(8):
            fbm[:, i, j] = 0.0 if j < own else (1e4 if j == own else -1e4)
    c["c_fbm"] = fbm.reshape(128, NT * 8)
    key = np.arange(S)[None, :]
    c["c_ind_nsa"] = np.where(key // 64 == np.arange(32)[:, None], NEG, 0.0).astype(np.float32)
    c["c_ind_moba"] = np.where(key // 256 == np.arange(8)[:, None], NEG, 0.0).astype(np.float32)
    cs = np.arange(127)[:, None] * 16
    ss = np.arange(32)[None, :] * 64
    shared = np.clip(np.minimum(cs + 32, ss + 64) - np.maximum(cs, ss), 0, None)
    sw = np.zeros((128, 32), np.float32)
    sw[:127] = shared / 32.0
    c["c_selw"] = sw
    sel = np.zeros((128, 2, 64), np.float32)
    for par in range(2):
        for u in range(64):
            sel[2 * u + par, par, u] = 1.0
    c["c_sel"] = sel.reshape(128, 128)
    inv = (np.float32(500000.0) ** (-np.arange(0, 16, 2, dtype=np.float32) / np.float32(16))).astype(np.float32)
    c["c_invf"] = np.tile(inv[None, :], (128, 1)).astype(np.float32)
    return c


CONST_SHAPES = {
    "c_ident": [128, 128], "c_cb": [128, 128], "c_cb2": [128, 128], "c_cmask": [128, S],
    "c_fb": [128, NT * 32], "c_fbm": [128, NT * 8], "c_ind_nsa": [32, S], "c_ind_moba": [8, S],
    "c_selw": [128, 32], "c_sel": [128, 128], "c_invf": [128, 8],
}

WEIGHT_SHAPES = {
    "norm_g": [4, D], "ada_w": [4, D, 3 * D], "ada_b": [4, 3 * D],
    "nsa_w_in": [2, D, NSA_IN], "nsa_w_out": [2, D, D], "nsa_q_norm": [2, 64], "nsa_k_norm": [2, 3, 64],
    "nsa_cmp_pe": [2, 2, 32, 64], "nsa_cmp_w1": [2, 2, 2048, 256], "nsa_cmp_b1": [2, 2, 256],
    "nsa_cmp_w2": [2, 2, 256, 64], "nsa_gate_b": [2, 48],
    "moba_w_in": [2, D, MOBA_IN], "moba_w_out": [2, D, D], "moba_q_norm": [2, 64], "moba_k_norm": [2, 64],
}


DBG_STAGE = [9]


def build(layers, n_seq, first_from_x=True):
    nc = bass.Bass("TRN2", target_bir_lowering=False, dynamic_dma_scratch_size=8192)
    dr = {}
    dr["x"] = nc.dram_tensor("x", [n_seq, S, D], F32, kind="ExternalInput").ap()
    dr["c"] = nc.dram_tensor("c", [n_seq, D], F32, kind="ExternalInput").ap()
    dr["positions"] = nc.dram_tensor("positions", [n_seq, S], I32, kind="ExternalInput").ap()
    for k, shp in WEIGHT_SHAPES.items():
        dr[k] = nc.dram_tensor(k, shp, F32, kind="ExternalInput").ap()
    for k, shp in CONST_SHAPES.items():
        dr[k] = nc.dram_tensor(k, shp, F32, kind="ExternalInput").ap()
    y_d = nc.dram_tensor("y", [n_seq, S, D], F32, kind="ExternalOutput").ap()
    mod_d = nc.dram_tensor("mod_scr", [n_seq, 4, 3 * D], F32, kind="Internal").ap()

    with ExitStack() as es:
        E = es.enter_context
        sch = Sched(nc, es)

        def V(fn, r=(), w=()):
            sch.op("dve", fn, r, w)

        def A(fn, r=(), w=()):
            sch.op("act", fn, r, w)

        def G(fn, r=(), w=()):
            sch.op("pool", fn, r, w)

        def P(fn, r=(), w=()):
            sch.op("pe", fn, r, w)

        def sb(name, shape, dt):
            return E(nc.sbuf_tensor(name, shape, dt))

        T = {}

        def sb(name, shape, dt, stack=None):
            t = (stack or es).enter_context(nc.sbuf_tensor(name, shape, dt))
            T[name] = t
            return t

        ident = sb("ident", [128, 128], BF16)
        cb = sb("cb", [128, 128], BF16)
        cb2 = sb("cb2", [128, 128], BF16)
        cmask = sb("cmask", [128, S], BF16)
        fb = sb("fb", [128, NT, 32], F32)
        fbm = sb("fbm", [128, NT, 8], F32)
        selm = sb("selm", [128, 2, 64], BF16)
        invf = sb("invf", [128, 8], F32)
        onesc = sb("onesc", [128, 1], BF16)
        W1 = sb("W1", [128, 20864], BF16)
        WO = sb("WO", [128, 8, D], BF16)
        KA = sb("KA", [128, NG, S], BF16)
        VA = sb("VA", [128, NT, NG, 65], BF16)
        KW = sb("KW", [128, NG, 4, 128], BF16)
        VW = sb("VW", [128, 4, NG, 65], BF16)
        onef = sb("onef", [128, 1], F32)
        kcmpT = sb("kcmpT", [128, NG, 128], BF16)
        vcmp = sb("vcmp", [128, NG, 97], BF16)
        modbc = sb("modbc", [128, 3, D], F32)
        cosT = sb("cosT", [128, NT, 8], F32)
        sinT = sb("sinT", [128, NT, 8], F32)
        nsinT = sb("nsinT", [128, NT, 8], F32)
        cosC = sb("cosC", [128, 8], F32)
        sinC = sb("sinC", [128, 8], F32)
        nsinC = sb("nsinC", [128, 8], F32)
        gq = sb("gq", [128, 64], F32)
        gk = sb("gk", [128, 3, 64], F32)
        gateb = sb("gateb", [128, 48], F32)
        kmT = sb("kmT", [64, NG, 8], BF16)
        kmtmp = sb("kmtmp", [64, NG], F32)
        ksum = sb("ksum", [64, NT, NG], F32)
        cw2 = sb("cw2", [128, 2, 2, 64], BF16)
        peT2 = sb("peT2", [128, 2, 16], BF16)
        peT2f = sb("peT2f", [128, 2, 16], F32)
        b1c = sb("b1c", [128, 2, 2], F32)
        b1p = sb("b1p", [128, 2, 2], F32)
        xt = [sb("xt%d" % i, [128, D], F32) for i in range(2)]
        st = sb("st", [128, 8], F32)
        hb = sb("hb", [128, D], BF16)
        hT = [sb("hT%d" % i, [128, 8, 128], BF16) for i in range(2)]
        sq = sb("sq", [128, 512], F32)
        ssq = sb("ssq", [128, 16], F32)
        rq = sb("rq", [128, 16], F32)
        qn = sb("qn", [128, 8, 64], F32)
        rtmp = sb("rtmp", [128, 8, 16], F32)
        rtmp2 = sb("rtmp2", [128, 8, 16], F32)
        kb = sb("kb", [128, NG, 64], BF16)
        ezs = sb("ezs", [128, D], F32)
        PJ = [E(nc.psum_tensor("PJ%d" % i, [128, 512], F32)) for i in range(2)]
        PS = [E(nc.psum_tensor("PS%d" % i, [128, 512], F32)) for i in range(2)]
        PO = [E(nc.psum_tensor("PO%d" % i, [128, 512], F32)) for i in range(2)]
        PT = [E(nc.psum_tensor("PT%d" % i, [128, 1024], BF16)) for i in range(2)]
        rr = {"PJ": 0, "PS": 0, "PO": 0, "PT": 0, "pT": 0, "xt": 0, "hT": 0}
        uid = [0]

        fixed_bank = [None]

        def nxt(k, n=2):
            if fixed_bank[0] is not None and k == "PT":
                return fixed_bank[0]
            if fixed_bank[0] == 1 and k == "PJ":
                raise RuntimeError("atomic-mode code must not allocate PJ banks (owned by the deferred front stage)")
            v = rr[k]
            rr[k] = (v + 1) % n
            return v

        def alloc_attn_tiles(stack):
            uid[0] += 1
            u = "_%d" % uid[0]
            for nm, shp, dt in [("qaug", [128, NH, 96], BF16), ("qT1", [128, NH, 128], BF16), ("qaT", [128, NH, 128], BF16),
                                ("szb", [128, D], BF16), ("gates", [128, NH, 3], F32), ("oacc", [128, NH, 64], F32)]:
                T[nm] = [stack.enter_context(nc.sbuf_tensor("%s%s_%d" % (nm, u, k), shp, dt)) for k in range(2)]
                if nm in ("qT1", "qaT"):
                    for k in range(2):
                        G(lambda e, t=T[nm][k]: e.memset(t[:], 0.0), w=[(nm, k)])
            for nm, shp, dt in [("gtmp", [128, 48], F32),
                                ("sg", [128, NH, 8], F32), ("m8", [128, NH, 8], F32), ("imp", [128, NG, 32], F32),
                                ("rs", [128, 4], F32), ("cc", [128, 4], F32), ("otmp", [128, 4, 64], F32),
                                ("og", [128, D], BF16), ("ogT", [128, 8, 128], BF16),
                                ("ytmp", [128, 512], F32), ("pT0", [128, 512], BF16), ("pT1", [128, 512], BF16),
                                ("pT2", [128, 512], BF16)]:
                T[nm] = stack.enter_context(nc.sbuf_tensor(nm + u, shp, dt))

        def alloc_cmp_tiles(stack):
            uid[0] += 1
            u = "_%d" % uid[0]
            for nm, shp, dt in [("KC2", [128, 2, NG, 1024], BF16), ("hidT", [128, 4, NG, 127], BF16),
                                ("hu", [128, 508], F32), ("hw", [128, 508], F32), ("kvb", [128, 512], BF16)]:
                T[nm] = stack.enter_context(nc.sbuf_tensor(nm + u, shp, dt))

        def cload(tile_ap, src, key, queue="pool"):
            sch.dma(queue, tile_ap, src, writes=[key])

        cload(ident[:], dr["c_ident"], "ident")
        cload(cb[:], dr["c_cb"], "cb")
        cload(cb2[:], dr["c_cb2"], "cb2")
        for h in range(2):
            cload(cmask[:, h * 1024:(h + 1) * 1024], dr["c_cmask"][:, h * 1024:(h + 1) * 1024], "cmask")
        cload(fb[:].rearrange("p a b -> p (a b)"), dr["c_fb"], "fb", "sp")
        cload(fbm[:].rearrange("p a b -> p (a b)"), dr["c_fbm"], "fbm", "sp")
        cload(selm[:].rearrange("p a b -> p (a b)"), dr["c_sel"], "selm")
        cload(invf[:], dr["c_invf"], "invf", "sp")
        V(lambda e: e.memset(onesc[:], 1.0), w=["onesc"])
        V(lambda e: e.memset(onef[:], 1.0), w=["onef"])
        V(lambda e: e.memset(VA[:, :, :, 64:65], 1.0), w=["VA1"])
        V(lambda e: e.memset(VW[:, :, :, 64:65], 1.0), w=["VW1"])
        V(lambda e: e.memset(kmT[:], 0.0), w=["kmT"])
        V(lambda e: e.memset(ksum[:], 0.0), w=["ksum"])
        V(lambda e: e.memset(kcmpT[:], 0.0), w=["kcmpT"])
        G(lambda e: e.memset(KA[64:128, :, :], 0.0), w=["KAind"])
        G(lambda e: e.memset(KW[:], 0.0), w=[("KW", 0), ("KW", 1), ("KW", 2), ("KW", 3)])
        V(lambda e: e.memset(vcmp[:], 0.0), w=["vcmp"])
        V(lambda e: e.memset(vcmp[:, :, 64:65], 1.0), w=["vcmp"])
        for g in range(NG):
            cload(vcmp[:, g, 65:97], dr["c_selw"], "vcmp")

        with ExitStack() as es2:
            E2 = es2.enter_context
            cT = E2(nc.sbuf_tensor("cT", [128, 8, 2], F32))
            cTe = E2(nc.sbuf_tensor("cTe", [128, 8, 2], F32))
            cTb = E2(nc.sbuf_tensor("cTb", [128, 8, 2], BF16))
            adaw = [E2(nc.sbuf_tensor("adaw%d" % i, [128, 8, 512], BF16)) for i in range(3)]
            adab = E2(nc.sbuf_tensor("adab", [2, 512], F32))
            grow = E2(nc.sbuf_tensor("grow", [2, D], F32))
            modr = [E2(nc.sbuf_tensor("modr%d" % i, [2, 512], F32)) for i in range(2)]
            V(lambda e: e.memset(cT[:], 0.0), w=["cT"])
            for b in range(n_seq):
                sch.dma("sp", cT[:, :, b:b + 1], dr["c"][b].rearrange("(k p o) -> p k o", p=128, o=1),
                        writes=["cT"], allow_slow_non_contiguous=True)
            A(lambda e: e.activation(out=cTe[:], in_=cT[:], func=AF.Exp, scale=-1.0), r=["cT"], w=["cTe"])
            V(lambda e: e.tensor_scalar_add(cTe[:], cTe[:], 1.0), r=["cTe"], w=["cTe"])
            V(lambda e: e.reciprocal(cTe[:], cTe[:]), r=["cTe"], w=["cTe"])
            V(lambda e: e.tensor_tensor(out=cTb[:], in0=cT[:], in1=cTe[:], op=ALU.mult), r=["cT", "cTe"], w=["cTb"])
            ci = 0
            for L in layers:
                for b in range(2):
                    sch.dma("sp", grow[b:b + 1, :], dr["norm_g"][L:L + 1, :], writes=["grow"])
                for ch in range(6):
                    slot = ci % 3
                    ms = ci % 2
                    ci += 1
                    sch.dma("pool", adaw[slot][:], dr["ada_w"][L, :, ch * 512:(ch + 1) * 512].rearrange("(k p) n -> p k n", p=128),
                            writes=[("adaw", slot)])
                    for b in range(2):
                        sch.dma("sp", adab[b:b + 1, :], dr["ada_b"][L:L + 1, ch * 512:(ch + 1) * 512], writes=["adab"])
                    pj = nxt("PJ")
                    for k in range(8):
                        P(lambda e, k=k, pj=pj, slot=slot: e.matmul(PJ[pj][0:2, :], cTb[:, k, :], adaw[slot][:, k, :],
                                                                     start=(k == 0), stop=(k == 7)),
                          r=["cTb", ("adaw", slot)], w=[("PJ", pj)])
                    V(lambda e, pj=pj, ms=ms: e.tensor_tensor(out=modr[ms][:], in0=PJ[pj][0:2, :], in1=adab[:], op=ALU.add),
                      r=[("PJ", pj), "adab"], w=[("modr", ms)])
                    if ch in (2, 3):
                        g0 = (ch - 2) * 512
                        V(lambda e, ms=ms, g0=g0: e.scalar_tensor_tensor(out=modr[ms][:], in0=modr[ms][:], scalar=1.0,
                                                                         in1=grow[:, g0:g0 + 512], op0=ALU.add, op1=ALU.mult),
                          r=[("modr", ms), "grow"], w=[("modr", ms)])
                        dst0 = g0
                    elif ch in (0, 1):
                        dst0 = D + ch * 512
                    else:
                        dst0 = 2 * D + (ch - 4) * 512
                    for b in range(n_seq):
                        sch.dma("sp", mod_d[b, L:L + 1, dst0:dst0 + 512], modr[ms][b:b + 1, :],
                                reads=[("modr", ms)], writes=[("mod", b, L)])
            sch.barrier()

        def rstd_from_ss(ss_ap, out_ap, n, keys_r, key_w):
            V(lambda e: e.tensor_scalar(out=out_ap, in0=ss_ap, scalar1=1.0 / n, scalar2=EPS, op0=ALU.mult, op1=ALU.add),
              r=keys_r, w=[key_w])
            A(lambda e: e.activation(out=out_ap, in_=out_ap, func=AF.Ln), r=[key_w], w=[key_w])
            A(lambda e: e.activation(out=out_ap, in_=out_ap, func=AF.Exp, scale=-0.5), r=[key_w], w=[key_w])

        def load_norm_h(b, i, src_ap):
            xs = nxt("xt")
            sch.dma("sp", xt[xs][:], src_ap, reads=[("res", b, i)], writes=[("xt", xs)])
            A(lambda e: e.activation(out=hb[:], in_=xt[xs][:], func=AF.Square, accum_out=st[:, 0:1]),
              r=[("xt", xs)], w=["hb", "st0"])
            rstd_from_ss(st[:, 0:1], st[:, 1:2], D, ["st0"], "st1")
            V(lambda e: e.scalar_tensor_tensor(out=ezs[:], in0=xt[xs][:], scalar=st[:, 1:2], in1=modbc[:, 0, :],
                                               op0=ALU.mult, op1=ALU.mult), r=[("xt", xs), "st1", ("modbc", 0)], w=[("ezs", 0), ("ezs", 512)])
            V(lambda e: e.tensor_tensor(out=hb[:], in0=ezs[:], in1=modbc[:, 1, :], op=ALU.add), r=[("ezs", 0), ("ezs", 512), ("modbc", 1)], w=["hb"])
            pt = nxt("PT")
            for k in range(8):
                P(lambda e, k=k, pt=pt: e.transpose(PT[pt][:, k * 128:(k + 1) * 128], hb[:, k * 128:(k + 1) * 128], ident[:]),
                  r=["hb", "ident"], w=[("PT", pt)])
            hs = nxt("hT")
            V(lambda e: e.tensor_copy(out=hT[hs][:].rearrange("p a b -> p (a b)"), in_=PT[pt][:]),
              r=[("PT", pt)], w=[("hT", hs)])
            return xs, hs

        def project(hs, wcol0, ncols, wstride):
            pj = nxt("PJ")
            for k in range(8):
                P(lambda e, k=k, pj=pj: e.matmul(PJ[pj][:, 0:ncols], hT[hs][:, k, :],
                                                 W1[:, k * wstride + wcol0:k * wstride + wcol0 + ncols],
                                                 start=(k == 0), stop=(k == 7)),
                  r=[("hT", hs)] + [("W1", c) for c in range(wcol0 // 512, (wcol0 + ncols - 1) // 512 + 1)], w=[("PJ", pj)])
            return pj

        def head_norm_rope(pj, c0, nh, g_ap, cos_ap, sin_ap, nsin_ap, out_ap, wkey):
            src = PJ[pj][:, c0:c0 + nh * 64]
            src3 = src.rearrange("p (a b) -> p a b", a=nh)
            A(lambda e: e.activation(out=sq[:, 0:nh * 64], in_=src, func=AF.Square), r=[("PJ", pj)], w=["sq"])
            V(lambda e: e.tensor_reduce(out=ssq[:, 0:nh], in_=sq[:, 0:nh * 64].rearrange("p (a b) -> p a b", a=nh),
                                        axis=AX.X, op=ALU.add), r=["sq"], w=["ssq"])
            rstd_from_ss(ssq[:, 0:nh], rq[:, 0:nh], HD, ["ssq"], "rq")
            V(lambda e: e.tensor_tensor(out=qn[:, 0:nh, :], in0=src3, in1=rq[:, 0:nh].unsqueeze(2).to_broadcast([128, nh, 64]),
                                        op=ALU.mult), r=[("PJ", pj), "rq"], w=["qn"])
            G(lambda e: e.tensor_tensor(out=qn[:, 0:nh, :], in0=qn[:, 0:nh, :], in1=g_ap.unsqueeze(1).to_broadcast([128, nh, 64]),
                                        op=ALU.mult), r=["qn", "gains"], w=["qn"])
            cosb = cos_ap.unsqueeze(1).to_broadcast([128, nh, 8])
            sinb = sin_ap.unsqueeze(1).to_broadcast([128, nh, 8])
            nsinb = nsin_ap.unsqueeze(1).to_broadcast([128, nh, 8])
            V(lambda e: e.tensor_tensor(out=rtmp[:, 0:nh, 0:8], in0=qn[:, 0:nh, 0:8], in1=cosb, op=ALU.mult),
              r=["qn", "rope", "ropeC"], w=["rtmp"])
            V(lambda e: e.tensor_tensor(out=rtmp[:, 0:nh, 8:16], in0=qn[:, 0:nh, 8:16], in1=cosb, op=ALU.mult),
              r=["qn", "rope", "ropeC"], w=["rtmp"])
            V(lambda e: e.tensor_tensor(out=rtmp2[:, 0:nh, 0:8], in0=qn[:, 0:nh, 8:16], in1=nsinb, op=ALU.mult),
              r=["qn", "rope", "ropeC"], w=["rtmp2"])
            V(lambda e: e.tensor_tensor(out=rtmp2[:, 0:nh, 8:16], in0=qn[:, 0:nh, 0:8], in1=sinb, op=ALU.mult),
              r=["qn", "rope", "ropeC"], w=["rtmp2"])
            V(lambda e: e.tensor_tensor(out=out_ap[:, :, 0:16], in0=rtmp[:, 0:nh, :], in1=rtmp2[:, 0:nh, :], op=ALU.add),
              r=["rtmp", "rtmp2"], w=[wkey])
            A(lambda e: e.activation(out=out_ap[:, :, 16:64], in_=qn[:, 0:nh, 16:64], func=AF.Copy),
              r=["qn"], w=[wkey])

        def silu_from_psum(pj, n, out_ap, c0, wkey):
            src = PJ[pj][:, 0:n]
            ek = ("ezs", c0)
            A(lambda e: e.activation(out=ezs[:, c0:c0 + n], in_=src, func=AF.Exp, scale=-1.0), r=[("PJ", pj)], w=[ek])
            A(lambda e: e.activation(out=ezs[:, c0:c0 + n], in_=ezs[:, c0:c0 + n], func=AF.Ln, bias=onef[:, 0:1]), r=[ek, "onef"], w=[ek])
            A(lambda e: e.activation(out=ezs[:, c0:c0 + n], in_=ezs[:, c0:c0 + n], func=AF.Exp, scale=-1.0), r=[ek], w=[ek])
            V(lambda e: e.tensor_tensor(out=out_ap, in0=src, in1=ezs[:, c0:c0 + n], op=ALU.mult), r=[("PJ", pj), ek], w=[wkey])

        def transpose_heads(src_tile, ncol, dst_tile, rkeys, wkey):
            for half in range(2):
                pt = nxt("PT")
                for hh in range(8):
                    h = half * 8 + hh
                    P(lambda e, h=h, hh=hh, pt=pt: e.transpose(PT[pt][0:ncol, hh * 128:(hh + 1) * 128], src_tile[:, h, 0:ncol], ident[:]),
                      r=rkeys + ["ident"], w=[("PT", pt)])
                if half == 0:
                    V(lambda e, pt=pt: e.tensor_copy(out=dst_tile[0:ncol, 0:8, :].rearrange("p a b -> p (a b)"), in_=PT[pt][0:ncol, :]),
                      r=[("PT", pt)], w=[wkey])
                else:
                    V(lambda e, pt=pt: e.tensor_copy(out=dst_tile[0:ncol, 8:16, :].rearrange("p a b -> p (a b)"), in_=PT[pt][0:ncol, :]),
                      r=[("PT", pt)], w=[wkey])

        def k_to_KT(dst_ap, wkey, ncopy=128):
            pt = nxt("PT")
            for g in range(NG):
                P(lambda e, g=g, pt=pt: e.transpose(PT[pt][0:64, g * 128:(g + 1) * 128], kb[:, g, :], ident[:]),
                  r=["kb", "ident"], w=[("PT", pt)])
            V(lambda e, pt=pt: e.tensor_copy(out=dst_ap, in_=PT[pt][0:64, 0:512].rearrange("p (a b) -> p a b", a=NG)[:, :, 0:ncopy]),
              r=[("PT", pt)], w=[wkey])

        class Attn:
            def __init__(self):
                self.pend = None

            def unit(self, lhsT_k, rhs_q, bias_ap, v_ap, vw, po, first, rk, on_done=None):
                ps = nxt("PS")
                P(lambda e: e.matmul(PS[ps][:], lhsT_k, rhs_q, start=True, stop=(bias_ap is None)),
                  r=rk, w=[("PS", ps)])
                if bias_ap is not None:
                    P(lambda e: e.matmul(PS[ps][:], ident[:], bias_ap, start=False, stop=True),
                      r=["ident", "cb", "cb2", "cmask"], w=[("PS", ps)])
                sl = nxt("pT", 3)
                pt_t = T["pT%d" % sl]
                A(lambda e: e.activation(out=pt_t[:], in_=PS[ps][:], func=AF.Exp, scale=0.125),
                  r=[("PS", ps)], w=[("pT", sl)])
                self.flush()
                self.pend = (pt_t, sl, v_ap, vw, po, first, rk, on_done)

            def flush(self):
                if self.pend is None:
                    return
                pt_t, sl, v_ap, vw, po, first, rk, on_done = self.pend
                self.pend = None
                for r in range(4):
                    P(lambda e, r=r: e.matmul(PO[po][:, r * vw:(r + 1) * vw], pt_t[:, r * 128:(r + 1) * 128], v_ap,
                                              start=(first and r == 0), stop=True, skip_group_check=True),
                      r=[("pT", sl)] + rk, w=[("PO", po)])
                if on_done is not None:
                    on_done()

        def branch_post(po, vw, g, gate_idx, first_branch, clamp=False, gates=None, sl=0):
            rs, cc, otmp, oacc = T["rs"], T["cc"], T["otmp"], T["oacc"][sl]
            pov = PO[po][:, 0:4 * vw].rearrange("p (a b) -> p a b", a=4)
            if clamp:
                V(lambda e: e.tensor_scalar(out=rs[:], in0=pov[:, :, 64:65].rearrange("p a b -> p (a b)"), scalar1=1e-30, scalar2=None, op0=ALU.max),
                  r=[("PO", po)], w=["rs"])
                V(lambda e: e.reciprocal(rs[:], rs[:]), r=["rs"], w=["rs"])
            else:
                V(lambda e: e.reciprocal(rs[:], pov[:, :, 64:65].rearrange("p a b -> p (a b)")), r=[("PO", po)], w=["rs"])
            if gate_idx is not None:
                V(lambda e: e.tensor_tensor(out=cc[:], in0=rs[:], in1=gates[:, 4 * g:4 * g + 4, gate_idx:gate_idx + 1].rearrange("p a b -> p (a b)"),
                                            op=ALU.mult), r=["rs", ("gates", 0), ("gates", 1)], w=["cc"])
                cap = cc
                ck = "cc"
            else:
                cap = rs
                ck = "rs"
            if first_branch:
                V(lambda e: e.tensor_tensor(out=oacc[:, 4 * g:4 * g + 4, :], in0=pov[:, :, 0:64],
                                            in1=cap[:].unsqueeze(2).to_broadcast([128, 4, 64]), op=ALU.mult),
                  r=[("PO", po), ck], w=[("oacc", sl, g)])
            else:
                V(lambda e: e.tensor_tensor(out=otmp[:], in0=pov[:, :, 0:64],
                                            in1=cap[:].unsqueeze(2).to_broadcast([128, 4, 64]), op=ALU.mult),
                  r=[("PO", po), ck], w=["otmp"])
                G(lambda e: e.tensor_tensor(out=oacc[:, 4 * g:4 * g + 4, :], in0=oacc[:, 4 * g:4 * g + 4, :], in1=otmp[:], op=ALU.add),
                  r=["otmp", ("oacc", sl, g)], w=[("oacc", sl, g)])

        def out_proj_store(b, i, xs, sl):
            oacc, og, ogT, ytmp, szb = T["oacc"][sl], T["og"], T["ogT"], T["ytmp"], T["szb"][sl]
            V(lambda e: e.tensor_tensor(out=og[:], in0=oacc[:].rearrange("p a b -> p (a b)"), in1=szb[:], op=ALU.mult),
              r=[("oacc", sl, g) for g in range(NG)] + [("szb", sl)], w=["og"])
            yield
            pt = nxt("PT")
            for k in range(8):
                P(lambda e, k=k, pt=pt: e.transpose(PT[pt][:, k * 128:(k + 1) * 128], og[:, k * 128:(k + 1) * 128], ident[:]),
                  r=["og", "ident"], w=[("PT", pt)])
            V(lambda e: e.tensor_copy(out=ogT[:].rearrange("p a b -> p (a b)"), in_=PT[pt][:]),
              r=[("PT", pt)], w=["ogT"])
            yield
            for half in range(2):
                pj = rr["PO"]
                for k in range(8):
                    P(lambda e, k=k, pj=pj, half=half: e.matmul(PO[pj][:], ogT[:, k, :], WO[:, k, half * 512:(half + 1) * 512],
                                                               start=(k == 0), stop=(k == 7)),
                      r=["ogT", ("WO", half)], w=[("PO", pj)])
                V(lambda e, pj=pj, half=half: e.tensor_tensor(out=ytmp[:], in0=PO[pj][:], in1=modbc[:, 2, half * 512:(half + 1) * 512], op=ALU.mult),
                  r=[("PO", pj), ("modbc", 2)], w=["ytmp"])
                G(lambda e, half=half: e.tensor_tensor(out=xt[xs][:, half * 512:(half + 1) * 512], in0=ytmp[:],
                                                       in1=xt[xs][:, half * 512:(half + 1) * 512], op=ALU.add),
                  r=["ytmp", ("xt", xs)], w=[("xt", xs)])
                if half == 0:
                    yield
            sch.dma("sp", y_d[b, i * 128:(i + 1) * 128, :], xt[xs][:], reads=[("xt", xs)], writes=[("res", b, i)])
            yield

        def load_w_cols(dst_col0, src_ap_fn, ncols, wstride, key="W1"):
            c = 0
            w1v = W1[:, 0:8 * wstride].rearrange("p (k n) -> p k n", k=8)
            while c < ncols:
                n = min(512 - (dst_col0 + c) % 512, ncols - c)
                sch.dma("pool", w1v[:, :, dst_col0 + c:dst_col0 + c + n], src_ap_fn(c, n), writes=[(key, (dst_col0 + c) // 512)])
                c += n

        def bload(dst_ap, src_row_ap, n, key):
            sch.dma("sp", dst_ap, src_row_ap.to_broadcast([128, n]), writes=[key])

        at_box = [None]

        def at_flush():
            at_box[0].flush()

        def run_pipelined(front, back_units, tail, n_tiles, prologue=None):
            def capture_front(i):
                fixed_bank[0] = 0
                sch.capture = []
                for _ in front(i):
                    pass
                ops = sch.capture
                sch.capture = None
                fixed_bank[0] = 1
                return ops

            for it in capture_front(0):
                sch.replay(it)
            if prologue is not None:
                prologue()
            tail_gen = iter(())
            DONE = object()
            for i in range(n_tiles):
                ops = capture_front(i + 1) if i + 1 < n_tiles else []
                pos = 0
                units = back_units(i)
                nf = units.index("need_front") if "need_front" in units else len(units)
                nu_front = sum(1 for u in units[:nf] if u != "need_front")
                ui = 0
                tail_done = False
                for u in units:
                    if u == "need_front":
                        for _ in tail_gen:
                            pass
                        tail_done = True
                        while pos < len(ops):
                            sch.replay(ops[pos])
                            pos += 1
                        continue
                    u()
                    if ui >= 1 and not tail_done:
                        if next(tail_gen, DONE) is DONE:
                            tail_done = True
                    if tail_done and pos < len(ops):
                        left = max(1, int(0.6 * nu_front) - ui)
                        k = -(-(len(ops) - pos) // left)
                        for _ in range(k):
                            if pos < len(ops):
                                sch.replay(ops[pos])
                                pos += 1
                    ui += 1
                for _ in tail_gen:
                    pass
                while pos < len(ops):
                    sch.replay(ops[pos])
                    pos += 1
                tail_gen = tail(i)
            at_flush()
            for _ in tail_gen:
                pass
            fixed_bank[0] = None

        def moba_layer(b, j, src_d):
            WS = MOBA_IN
            load_w_cols(0, lambda c, n: dr["moba_w_in"][j, :, c:c + n].rearrange("(k p) n -> p k n", p=128), MOBA_IN, WS)
            bload(gq[:], dr["moba_q_norm"][j:j + 1, :], 64, "gains")
            bload(gk[:, 0, :], dr["moba_k_norm"][j:j + 1, :], 64, "gains")
            for g in range(NG):
                sch.dma("pool", KA[64:72, g, :], dr["c_ind_moba"], writes=["KAind"])
            with ExitStack() as esc:
                alloc_attn_tiles(esc)
                sg, m8 = T["sg"], T["m8"]
                ctx = {}

                def front(i):
                    sl = i % 2
                    qaug, qT1, qaT, szb = T["qaug"][sl], T["qT1"][sl], T["qaT"][sl], T["szb"][sl]
                    xs, hs = load_norm_h(b, i, src_d[b, i * 128:(i + 1) * 128, :])
                    ctx[i] = xs
                    yield
                    for half in range(2):
                        pj = project(hs, half * 512, 512, WS)
                        head_norm_rope(pj, 0, 8, gq[:], cosT[:, i, :], sinT[:, i, :], nsinT[:, i, :],
                                       qaug[:, half * 8:(half + 1) * 8, 0:64], ("qaug", sl, half))
                        yield
                    pj = project(hs, 1024, 512, WS)
                    head_norm_rope(pj, 0, 4, gk[:, 0, :], cosT[:, i, :], sinT[:, i, :], nsinT[:, i, :], kb[:], "kb")
                    A(lambda e, pj=pj, i=i: e.activation(out=VA[:, i, :, 0:64], in_=PJ[pj][:, 256:512].rearrange("p (a b) -> p a b", a=NG), func=AF.Copy),
                      r=[("PJ", pj)], w=[("VA", i)])
                    yield
                    k_to_KT(KA[0:64, :, i * 128:(i + 1) * 128], ("KA", i))
                    pj2 = nxt("PJ")
                    for g in range(NG):
                        P(lambda e, g=g, pj2=pj2: e.matmul(PJ[pj2][0:64, g:g + 1], kb[:, g, :], onesc[:, 0:1], start=True, stop=True),
                          r=["kb", "onesc"], w=[("PJ", pj2)])
                    V(lambda e, pj2=pj2, i=i: e.tensor_copy(out=ksum[:, i, :], in_=PJ[pj2][0:64, 0:NG]), r=[("PJ", pj2)], w=["ksum"])
                    if i % 2 == 1:
                        jb = i // 2
                        V(lambda e, i=i: e.tensor_tensor(out=kmtmp[:], in0=ksum[:, i - 1, :], in1=ksum[:, i, :], op=ALU.add), r=["ksum"], w=["kmtmp"])
                        V(lambda e, jb=jb: e.tensor_scalar(out=kmT[:, :, jb:jb + 1].rearrange("p a b -> p (a b)"), in0=kmtmp[:], scalar1=1.0 / 256.0,
                                                           scalar2=None, op0=ALU.mult), r=["kmtmp"], w=["kmT"])
                    yield
                    for half in range(2):
                        pj = project(hs, 1536 + half * 512, 512, WS)
                        silu_from_psum(pj, 512, szb[:, half * 512:(half + 1) * 512], half * 512, ("szb", sl))
                        yield
                    transpose_heads(qaug, 64, qT1, [("qaug", sl, 0), ("qaug", sl, 1)], ("qT1", sl))
                    yield
                    pj3 = nxt("PJ")
                    for h in range(NH):
                        P(lambda e, h=h, pj3=pj3: e.matmul(PJ[pj3][:, h * 8:(h + 1) * 8], qT1[0:64, h, :], kmT[0:64, h // 4, :], start=True, stop=True),
                          r=[("qT1", sl), "kmT"], w=[("PJ", pj3)])
                    V(lambda e, pj3=pj3, i=i: e.tensor_tensor(out=sg[:], in0=PJ[pj3][:, 0:128].rearrange("p (a b) -> p a b", a=NH),
                                                             in1=fbm[:, i, :].unsqueeze(1).to_broadcast([128, NH, 8]), op=ALU.add),
                      r=[("PJ", pj3), "fbm"], w=["sg"])
                    for h in range(NH):
                        V(lambda e, h=h: e.max(out=m8[:, h, :], in_=sg[:, h, :]), r=["sg"], w=[("m8", h)])
                    V(lambda e: e.tensor_tensor(out=qaug[:, :, 64:72], in0=sg[:], in1=m8[:, :, 3:4].to_broadcast([128, NH, 8]), op=ALU.is_lt),
                      r=["sg"] + [("m8", h) for h in range(NH)], w=[("qaug", sl, 2)])
                    yield
                    transpose_heads(qaug, 72, qaT, [("qaug", sl, 0), ("qaug", sl, 1), ("qaug", sl, 2)], ("qaT", sl))
                    yield

                at = Attn()

                def back_units(i):
                    sl = i % 2
                    qaT = T["qaT"][sl]
                    units = []
                    for g in range(NG):
                        po_box = []
                        for jj in range(i + 1):
                            def u(g=g, jj=jj, po_box=po_box):
                                if jj == 0:
                                    po_box.append(nxt("PO"))
                                po = po_box[0]
                                bias = cb[:].unsqueeze(1).to_broadcast([128, 4, 128]) if jj == i else None
                                done = (lambda: branch_post(po, 65, g, None, True, sl=sl)) if jj == i else None
                                at.unit(KA[:, g, jj * 128:(jj + 1) * 128], qaT[:, 4 * g:4 * g + 4, :], bias, VA[:, jj, g, :], 65, po,
                                        jj == 0, [("KA", jj), "KAind", ("qaT", sl), ("VA", jj), "VA1"], done)
                            units.append(u)
                    return units

                def tail(i):
                    return out_proj_store(b, i, ctx[i], i % 2)

                at_box[0] = at
                run_pipelined(front, back_units, tail, NT)
                sch.barrier()

        def nsa_layer(b, j, src_d):
            w_in = dr["nsa_w_in"]
            load_w_cols(0, lambda c, n: w_in[j, :, 1024 + c:1024 + c + n].rearrange("(k p) n -> p k n", p=128), 1024, 1024)
            CW1 = W1[:, 8192:16384].rearrange("p (a m h) -> p a m h", a=2, m=16)
            for kv in range(2):
                for mh in range(2):
                    sch.dma("pool", CW1[:, kv, mh * 8:(mh + 1) * 8, :],
                            dr["nsa_cmp_w1"][j, kv, mh * 1024:(mh + 1) * 1024, :].rearrange("(m p) h -> p m h", p=128), writes=[("CW1", kv, mh)])
                sch.dma("pool", cw2[:, kv, :, :], dr["nsa_cmp_w2"][j, kv].rearrange("(h p) d -> p h d", p=128), writes=["cw2"])
                sch.dma("sp", peT2f[:, kv, :], dr["nsa_cmp_pe"][j, kv].rearrange("l d -> (l d)").rearrange("(m p) -> p m", p=128),
                        writes=["peT2f"], allow_slow_non_contiguous=True)
                sch.dma("sp", b1c[:, kv, :], dr["nsa_cmp_b1"][j, kv].rearrange("(h p) -> p h", p=128), writes=["b1c"],
                        allow_slow_non_contiguous=True)
            V(lambda e: e.tensor_copy(out=peT2[:], in_=peT2f[:]), r=["peT2f"], w=["peT2"])
            bload(gq[:], dr["nsa_q_norm"][j:j + 1, :], 64, "gains")
            for m in range(3):
                bload(gk[:, m, :], dr["nsa_k_norm"][j, m:m + 1, :], 64, "gains")
            bload(gateb[:], dr["nsa_gate_b"][j:j + 1, :], 48, "gateb")
            for g in range(NG):
                sch.dma("pool", KA[64:96, g, :], dr["c_ind_nsa"], writes=["KAind"])
            if DBG_STAGE[0] < 2:
                return
            with ExitStack() as esa:
                alloc_cmp_tiles(esa)
                KC2, hidT, hu, hw, kvb = T["KC2"], T["hidT"], T["hu"], T["hw"], T["kvb"]

                def a_stage2(i, hs):
                    pjA = project(hs, 0, 512, 1024)
                    pjB = project(hs, 512, 512, 1024)
                    A(lambda e, pjA=pjA: e.activation(out=kvb[:], in_=PJ[pjA][:], func=AF.Copy), r=[("PJ", pjA)], w=["kvb"])
                    head_norm_rope(pjB, 0, 4, gk[:, 1, :], cosT[:, i, :], sinT[:, i, :], nsinT[:, i, :], kb[:], "kb")
                    A(lambda e, pjB=pjB, i=i: e.activation(out=VA[:, i, :, 0:64], in_=PJ[pjB][:, 256:512].rearrange("p (a b) -> p a b", a=NG), func=AF.Copy),
                      r=[("PJ", pjB)], w=[("VA", i)])
                    pjs = nxt("PJ")
                    for kv in range(2):
                        for g in range(NG):
                            for par in range(2):
                                c0 = (kv * 4 + g) * 64
                                P(lambda e, kv=kv, g=g, par=par, c0=c0, pjs=pjs: e.matmul(
                                    PJ[pjs][par * 64:(par + 1) * 64, c0:c0 + 64], kvb[:, kv * 256 + g * 64:kv * 256 + (g + 1) * 64],
                                    selm[:, par, :], start=True, stop=True), r=["kvb", "selm"], w=[("PJ", pjs)])
                    V(lambda e, pjs=pjs, i=i: e.tensor_copy(out=KC2[:, :, :, i * 64:(i + 1) * 64],
                                                           in_=PJ[pjs][:].rearrange("p (a g u) -> p a g u", a=2, g=NG)),
                      r=[("PJ", pjs)], w=[("KC2", i)])
                    k_to_KT(KA[0:64, :, i * 128:(i + 1) * 128], ("KA", i))

                prev = None
                for i in range(NT):
                    xs, hs = load_norm_h(b, i, src_d[b, i * 128:(i + 1) * 128, :])
                    if prev is not None:
                        a_stage2(*prev)
                    prev = (i, hs)
                a_stage2(*prev)
                if DBG_STAGE[0] < 3:
                    sch.barrier()
                    return
                kc2keys = [("KC2", i) for i in range(NT)]
                pjb = nxt("PJ")
                for kv in range(2):
                    for half in range(2):
                        col = kv * 2 + half
                        for m in range(16):
                            P(lambda e, kv=kv, half=half, m=m, col=col, pjb=pjb: e.matmul(
                                PJ[pjb][:, col:col + 1], CW1[:, kv, m, half * 128:(half + 1) * 128], peT2[:, kv, m:m + 1],
                                start=(m == 0), stop=(m == 15)), r=[("CW1", kv, m // 8), "peT2"], w=[("PJ", pjb)])
                V(lambda e, pjb=pjb: e.tensor_tensor(out=b1p[:].rearrange("p a b -> p (a b)"), in0=PJ[pjb][:, 0:4],
                                                    in1=b1c[:].rearrange("p a b -> p (a b)"), op=ALU.add), r=[("PJ", pjb), "b1c"], w=["b1p"])
                for kv in range(2):
                    for half in range(2):
                        ps_ = nxt("PJ")
                        for m in range(16):
                            P(lambda e, kv=kv, half=half, m=m, ps_=ps_: e.matmul(
                                PJ[ps_][:, 0:508], CW1[:, kv, m, half * 128:(half + 1) * 128], KC2[:, kv, :, m:m + 1009:8],
                                start=(m == 0), stop=(m == 15)), r=[("CW1", kv, m // 8)] + kc2keys, w=[("PJ", ps_)])
                        V(lambda e, kv=kv, half=half, ps_=ps_: e.tensor_scalar(out=hu[:], in0=PJ[ps_][:, 0:508], scalar1=b1p[:, kv, half:half + 1],
                                                                              scalar2=None, op0=ALU.add), r=[("PJ", ps_), "b1p"], w=["hu"])
                        V(lambda e: e.tensor_tensor(out=hw[:], in0=hu[:], in1=hu[:], op=ALU.mult), r=["hu"], w=["hw"])
                        V(lambda e: e.tensor_scalar(out=hw[:], in0=hw[:], scalar1=0.044715, scalar2=1.0, op0=ALU.mult, op1=ALU.add), r=["hw"], w=["hw"])
                        V(lambda e: e.tensor_tensor(out=hw[:], in0=hw[:], in1=hu[:], op=ALU.mult), r=["hw", "hu"], w=["hw"])
                        A(lambda e: e.activation(out=hw[:], in_=hw[:], func=AF.Exp, scale=-1.5957691216057308), r=["hw"], w=["hw"])
                        V(lambda e: e.tensor_scalar_add(hw[:], hw[:], 1.0), r=["hw"], w=["hw"])
                        V(lambda e: e.reciprocal(hw[:], hw[:]), r=["hw"], w=["hw"])
                        V(lambda e, kv=kv, half=half: e.tensor_tensor(out=hidT[:, kv * 2 + half, :, :].rearrange("p a b -> p (a b)"), in0=hu[:], in1=hw[:],
                                                                      op=ALU.mult), r=["hu", "hw"], w=["hidT"])
                pj2 = nxt("PJ")
                for kv in range(2):
                    for g in range(NG):
                        c0 = (kv * 4 + g) * 64
                        for half in range(2):
                            P(lambda e, kv=kv, g=g, half=half, c0=c0, pj2=pj2: e.matmul(
                                PJ[pj2][0:127, c0:c0 + 64], hidT[:, kv * 2 + half, g, :], cw2[:, kv, half, :],
                                start=(half == 0), stop=(half == 1)), r=["hidT", "cw2"], w=[("PJ", pj2)])
                head_norm_rope(pj2, 0, 4, gk[:, 0, :], cosC[:], sinC[:], nsinC[:], kb[:], "kb")
                A(lambda e, pj2=pj2: e.activation(out=vcmp[0:127, :, 0:64], in_=PJ[pj2][0:127, 256:512].rearrange("p (a b) -> p a b", a=NG), func=AF.Copy),
                  r=[("PJ", pj2)], w=["vcmp"])
                k_to_KT(kcmpT[0:64, :, 0:127], "kcmpT", 127)
                sch.barrier()
            if DBG_STAGE[0] < 4:
                return
            WS = 2608
            load_w_cols(0, lambda c, n: w_in[j, :, c:c + n].rearrange("(k p) n -> p k n", p=128), 1024, WS)
            load_w_cols(1024, lambda c, n: w_in[j, :, 2048 + c:2048 + c + n].rearrange("(k p) n -> p k n", p=128), 1584, WS)
            with ExitStack() as esc:
                alloc_attn_tiles(esc)
                gtmp, imp, m8, rs = T["gtmp"], T["imp"], T["m8"], T["rs"]
                ctx = {}

                def front(i):
                    sl = i % 2
                    qaug, qT1, szb, gates = T["qaug"][sl], T["qT1"][sl], T["szb"][sl], T["gates"][sl]
                    xs, hs = load_norm_h(b, i, src_d[b, i * 128:(i + 1) * 128, :])
                    ctx[i] = xs
                    yield
                    for half in range(2):
                        pj = project(hs, half * 512, 512, WS)
                        head_norm_rope(pj, 0, 8, gq[:], cosT[:, i, :], sinT[:, i, :], nsinT[:, i, :],
                                       qaug[:, half * 8:(half + 1) * 8, 0:64], ("qaug", sl, half))
                        yield
                    slot = i % 4
                    pj = project(hs, 1024, 512, WS)
                    head_norm_rope(pj, 0, 4, gk[:, 2, :], cosT[:, i, :], sinT[:, i, :], nsinT[:, i, :], kb[:], "kb")
                    A(lambda e, pj=pj, slot=slot: e.activation(out=VW[:, slot, :, 0:64], in_=PJ[pj][:, 256:512].rearrange("p (a b) -> p a b", a=NG), func=AF.Copy),
                      r=[("PJ", pj)], w=[("VW", slot)])
                    yield
                    k_to_KT(KW[0:64, :, slot, :], ("KW", slot))
                    pj = project(hs, 1536, 48, WS)
                    V(lambda e, pj=pj: e.tensor_tensor(out=gtmp[:], in0=PJ[pj][:, 0:48], in1=gateb[:], op=ALU.add), r=[("PJ", pj), "gateb"], w=["gtmp"])
                    A(lambda e: e.activation(out=gtmp[:], in_=gtmp[:], func=AF.Exp, scale=-1.0), r=["gtmp"], w=["gtmp"])
                    A(lambda e: e.activation(out=gtmp[:], in_=gtmp[:], func=AF.Ln, bias=onef[:, 0:1]), r=["gtmp", "onef"], w=["gtmp"])
                    A(lambda e: e.activation(out=gates[:].rearrange("p a b -> p (a b)"), in_=gtmp[:], func=AF.Exp, scale=-1.0), r=["gtmp"], w=[("gates", sl)])
                    yield
                    for half in range(2):
                        pj = project(hs, 1584 + half * 512, 512, WS)
                        silu_from_psum(pj, 512, szb[:, half * 512:(half + 1) * 512], half * 512, ("szb", sl))
                        yield
                    transpose_heads(qaug, 64, qT1, [("qaug", sl, 0), ("qaug", sl, 1)], ("qT1", sl))
                    yield

                at = Attn()

                def cmp_post(po, g, i):
                    sl = i % 2
                    qaug = T["qaug"][sl]
                    branch_post(po, 97, g, 0, True, clamp=True, gates=T["gates"][sl], sl=sl)
                    pov = PO[po][:, 0:388].rearrange("p (a b) -> p a b", a=4)
                    V(lambda e: e.tensor_scalar(out=imp[:, g, :], in0=pov[:, 0, 65:97], scalar1=rs[:, 0:1], scalar2=None, op0=ALU.mult),
                      r=[("PO", po), "rs"], w=[("imp", g)])
                    for r_ in range(1, 4):
                        V(lambda e, r_=r_: e.scalar_tensor_tensor(out=imp[:, g, :], in0=pov[:, r_, 65:97], scalar=rs[:, r_:r_ + 1], in1=imp[:, g, :],
                                                                  op0=ALU.mult, op1=ALU.add), r=[("PO", po), "rs", ("imp", g)], w=[("imp", g)])
                    V(lambda e: e.tensor_tensor(out=imp[:, g, :], in0=imp[:, g, :], in1=fb[:, i, :], op=ALU.add), r=[("imp", g), "fb"], w=[("imp", g)])
                    V(lambda e: e.max(out=m8[:, g, :], in_=imp[:, g, :]), r=[("imp", g)], w=[("m8", g)])
                    V(lambda e: e.tensor_tensor(out=qaug[:, 4 * g:4 * g + 4, 64:96], in0=imp[:, g, :].unsqueeze(1).to_broadcast([128, 4, 32]),
                                                in1=m8[:, g, 7:8].unsqueeze(1).to_broadcast([128, 4, 32]), op=ALU.is_lt),
                      r=[("imp", g), ("m8", g)], w=[("qaug", sl, 2)])

                def cmp_unit(i, g):
                    sl = i % 2
                    qT1 = T["qT1"][sl]

                    def u():
                        po = nxt("PO")
                        at.unit(kcmpT[:, g, :], qT1[:, 4 * g:4 * g + 4, :], cmask[:, i * 128:(i + 1) * 128].unsqueeze(1).to_broadcast([128, 4, 128]),
                                vcmp[:, g, :], 97, po, True, ["kcmpT", ("qT1", sl), "vcmp"], (lambda: cmp_post(po, g, i)))
                    return u

                def back_units(i):
                    sl = i % 2
                    qaug, qT1, qaT, gates = T["qaug"][sl], T["qT1"][sl], T["qaT"][sl], T["gates"][sl]
                    units = []
                    tiles = [jj for jj in (i - 2, i - 1, i) if jj >= 0]
                    for g in range(NG):
                        po_box = []
                        for idx, jj in enumerate(tiles):
                            def u(g=g, jj=jj, idx=idx, po_box=po_box):
                                if idx == 0:
                                    po_box.append(nxt("PO"))
                                po = po_box[0]
                                if jj == i:
                                    bias = cb[:].unsqueeze(1).to_broadcast([128, 4, 128])
                                elif jj == i - 2:
                                    bias = cb2[:].unsqueeze(1).to_broadcast([128, 4, 128])
                                else:
                                    bias = None
                                done = (lambda: branch_post(po, 65, g, 2, False, gates=gates, sl=sl)) if jj == i else None
                                at.unit(KW[:, g, jj % 4, :], qT1[:, 4 * g:4 * g + 4, :], bias, VW[:, jj % 4, g, :], 65, po, idx == 0,
                                        [("KW", jj % 4), ("qT1", sl), ("VW", jj % 4), "VW1"], done)
                            units.append(u)

                    def tr():
                        at.flush()
                        transpose_heads(qaug, 96, qaT, [("qaug", sl, 0), ("qaug", sl, 1), ("qaug", sl, 2)], ("qaT", sl))
                    units.append(tr)
                    for g in range(NG):
                        po_box = []
                        for jj in range(i + 1):
                            def u(g=g, jj=jj, po_box=po_box):
                                if jj == 0:
                                    po_box.append(nxt("PO"))
                                po = po_box[0]
                                bias = cb[:].unsqueeze(1).to_broadcast([128, 4, 128]) if jj == i else None
                                done = (lambda: branch_post(po, 65, g, 1, False, gates=gates, sl=sl)) if jj == i else None
                                at.unit(KA[:, g, jj * 128:(jj + 1) * 128], qaT[:, 4 * g:4 * g + 4, :], bias, VA[:, jj, g, :], 65, po,
                                        jj == 0, [("KA", jj), "KAind", ("qaT", sl), ("VA", jj), "VA1"], done)
                            units.append(u)
                        if i + 1 < NT:
                            if g == 0:
                                units.append("need_front")
                            units.append(cmp_unit(i + 1, g))
                    return units

                def tail(i):
                    return out_proj_store(b, i, ctx[i], i % 2)

                def prologue():
                    for g in range(NG):
                        cmp_unit(0, g)()

                at_box[0] = at
                run_pipelined(front, back_units, tail, NT, prologue)
                sch.barrier()

        for b in range(n_seq):
            with ExitStack() as es3:
                E3 = es3.enter_context
                posi = E3(nc.sbuf_tensor("posi%d" % b, [128, NT + 1], I32))
                posf = E3(nc.sbuf_tensor("posf%d" % b, [128, NT + 1], F32))
                ang = E3(nc.sbuf_tensor("ang%d" % b, [128, NT + 1, 8], F32))
                kq = E3(nc.sbuf_tensor("kq%d" % b, [128, NT + 1, 8], F32))
                ki = E3(nc.sbuf_tensor("ki%d" % b, [128, NT + 1, 8], I32))
                red = E3(nc.sbuf_tensor("red%d" % b, [128, NT + 1, 8], F32))
                V(lambda e: e.memset(posi[:], 0), w=["posi"])
                sch.dma("sp", posi[:, 0:NT], dr["positions"][b].rearrange("(i p) -> p i", p=128), writes=["posi"],
                        allow_slow_non_contiguous=True)
                sch.dma("sp", posi[0:127, NT:NT + 1], dr["positions"][b, 16:16 + 16 * 127].rearrange("(n s) -> n s", s=16)[:, 15:16],
                        writes=["posi"], allow_slow_non_contiguous=True)
                V(lambda e: e.tensor_copy(out=posf[:], in_=posi[:]), r=["posi"], w=["posf"])
                V(lambda e: e.tensor_tensor(out=ang[:], in0=posf[:].unsqueeze(2).to_broadcast([128, NT + 1, 8]),
                                            in1=invf[:].unsqueeze(1).to_broadcast([128, NT + 1, 8]), op=ALU.mult),
                  r=["posf", "invf"], w=["ang"])
                C1 = 6.28125
                C2 = float(2.0 * np.pi - 6.28125)
                PI_LO = 3.1415925
                for which, off in (("sin", 0.0), ("cos", 0.25)):
                    V(lambda e, off=off: e.tensor_scalar(out=kq[:], in0=ang[:], scalar1=float(1.0 / (2.0 * np.pi)), scalar2=off,
                                                         op0=ALU.mult, op1=ALU.add), r=["ang"], w=["kq"])
                    V(lambda e: e.tensor_copy(out=ki[:], in_=kq[:]), r=["kq"], w=["ki"])
                    V(lambda e: e.tensor_copy(out=kq[:], in_=ki[:]), r=["ki"], w=["kq"])
                    V(lambda e: e.scalar_tensor_tensor(out=red[:], in0=kq[:], scalar=-C1, in1=ang[:], op0=ALU.mult, op1=ALU.add),
                      r=["kq", "ang"], w=["red"])
                    V(lambda e: e.scalar_tensor_tensor(out=red[:], in0=kq[:], scalar=-C2, in1=red[:], op0=ALU.mult, op1=ALU.add),
                      r=["kq", "red"], w=["red"])
                    if which == "cos":
                        V(lambda e: e.tensor_scalar(out=red[:], in0=red[:], scalar1=float(np.pi / 2), scalar2=PI_LO,
                                                    op0=ALU.add, op1=ALU.min), r=["red"], w=["red"])
                        V(lambda e: e.tensor_scalar(out=red[:], in0=red[:], scalar1=-PI_LO, scalar2=None, op0=ALU.max), r=["red"], w=["red"])
                        A(lambda e: e.activation(out=cosT[:], in_=red[:, 0:NT, :], func=AF.Sin), r=["red"], w=["rope"])
                        A(lambda e: e.activation(out=cosC[:], in_=red[:, NT, :], func=AF.Sin), r=["red"], w=["ropeC"])
                    else:
                        V(lambda e: e.tensor_scalar(out=red[:], in0=red[:], scalar1=PI_LO, scalar2=-PI_LO,
                                                    op0=ALU.min, op1=ALU.max), r=["red"], w=["red"])
                        A(lambda e: e.activation(out=sinT[:], in_=red[:, 0:NT, :], func=AF.Sin), r=["red"], w=["rope"])
                        A(lambda e: e.activation(out=sinC[:], in_=red[:, NT, :], func=AF.Sin), r=["red"], w=["ropeC"])
                V(lambda e: e.tensor_scalar(out=nsinT[:], in0=sinT[:], scalar1=-1.0, scalar2=None, op0=ALU.mult), r=["rope"], w=["rope"])
                V(lambda e: e.tensor_scalar(out=nsinC[:], in0=sinC[:], scalar1=-1.0, scalar2=None, op0=ALU.mult), r=["ropeC"], w=["ropeC"])
                sch.barrier()

            for li, L in enumerate(layers):
                src_d = dr["x"] if (li == 0 and first_from_x) else y_d
                is_nsa = (L % 2 == 0)
                j = L // 2
                for m in range(3):
                    sch.dma("sp", modbc[:, m, :], mod_d[b, L:L + 1, m * D:(m + 1) * D].to_broadcast([128, D]),
                            reads=[("mod", b, L)], writes=[("modbc", m)])
                wo_src = dr["nsa_w_out"][j] if is_nsa else dr["moba_w_out"][j]
                for hlf in range(2):
                    sch.dma("pool", WO[:, :, hlf * 512:(hlf + 1) * 512],
                            wo_src[:, hlf * 512:(hlf + 1) * 512].rearrange("(k p) n -> p k n", p=128), writes=[("WO", hlf)])
                if is_nsa:
                    nsa_layer(b, j, src_d)
                else:
                    moba_layer(b, j, src_d)
        sch.finish()
    return nc


_CONSTS = None


def _consts():
    global _CONSTS
    if _CONSTS is None:
        _CONSTS = host_consts()
    return _CONSTS


def run_layers(inputs, layers, n_seq, n_cores, first_from_x=True, trace=False):
    nc = build(layers, n_seq, first_from_x)
    cst = _consts()
    in_maps = []
    for c in range(n_cores):
        m = {}
        m["x"] = np.ascontiguousarray(inputs["x"][c * n_seq:(c + 1) * n_seq], dtype=np.float32)
        m["c"] = np.ascontiguousarray(inputs["c"][c * n_seq:(c + 1) * n_seq], dtype=np.float32)
        m["positions"] = np.ascontiguousarray(inputs["positions"][c * n_seq:(c + 1) * n_seq], dtype=np.int32)
        for k in WEIGHT_SHAPES:
            m[k] = np.ascontiguousarray(inputs[k], dtype=np.float32)
        for k in CONST_SHAPES:
            m[k] = cst[k]
        in_maps.append(m)
    res = run_bass_kernel_spmd(nc, in_maps, core_ids=list(range(n_cores)), trace=trace)
    out = np.concatenate([np.asarray(r["y"]) for r in res.results], axis=0)
    return out, res


def kernel(**inputs):
    out, _ = run_layers(inputs, [0, 1, 2, 3], 2, N_CORES)
    return out.astype(np.float32)
```

```python
import numpy as np
from contextlib import ExitStack
import concourse.bass as bass
import concourse.mybir as mybir
from concourse.bass_utils import run_bass_kernel_spmd

F32 = mybir.dt.float32
BF16 = mybir.dt.bfloat16
I32 = mybir.dt.int32
AF = mybir.ActivationFunctionType
ALU = mybir.AluOpType
AX = mybir.AxisListType

S = 2048
D = 1024
NT = 16
HD = 64
NH = 16
NG = 4
NEG = -30000.0
EPS = 1e-6
N_CORES = 8
NSA_IN = 3632
MOBA_IN = 2560
NDMA = 24
FRESH_POOL_SEMS = [False]
POOL_DESC_LIMIT = 4608


class Sched:
    def __init__(self, nc, es):
        self.nc = nc
        self.es = es
        self.nfresh = 0
        self.capture = None
        self.pool_fifo = []
        self.pool_desc = 0
        self.eng = {"pe": nc.tensor, "act": nc.scalar, "dve": nc.vector, "pool": nc.gpsimd, "sp": nc.sync}
        self.semh = {}
        for k in self.eng:
            self.semh[("e", k)] = es.enter_context(nc.semaphore("sem_" + k))
        for i in range(NDMA):
            self.semh[("d", i)] = es.enter_context(nc.semaphore("dsem%d" % i))
        self.cnt = {k: 0 for k in self.eng}
        self.dcnt = [0] * NDMA
        self.drr = 0
        self.seen = {k: {} for k in self.eng}
        self.lastw = {}
        self.readers = {}

    def _deps(self, eng, reads, writes, strict=False):
        deps = {}

        def add(t, same_ok):
            sk, v = t
            if same_ok and not strict and sk == ("e", eng) and eng == "pe":
                return
            if deps.get(sk, 0) < v:
                deps[sk] = v

        for k in reads:
            if k in self.lastw:
                add(self.lastw[k], False)
        for k in writes:
            if k in self.lastw:
                add(self.lastw[k], True)
            for sk, v in self.readers.get(k, {}).items():
                add((sk, v), True)
        return deps

    def _wait(self, eng, deps):
        for sk, v in deps.items():
            if eng == "pe" and sk == ("e", "pe"):
                continue
            if self.seen[eng].get(sk, 0) < v:
                self.eng[eng].wait_ge(self.semh[sk], v)
                self.seen[eng][sk] = v

    def _record(self, tok, reads, writes):
        for k in writes:
            self.lastw[k] = tok
            self.readers[k] = {}
        for k in reads:
            d = self.readers.setdefault(k, {})
            if d.get(tok[0], 0) < tok[1]:
                d[tok[0]] = tok[1]

    def op(self, eng, fn, reads=(), writes=()):
        if self.capture is not None:
            self.capture.append(("op", eng, fn, tuple(reads), tuple(writes), None))
            return
        self._wait(eng, self._deps(eng, reads, writes))
        inst = fn(self.eng[eng])
        self.cnt[eng] += 1
        inst.then_inc(self.semh[("e", eng)], 1)
        self._record((("e", eng), self.cnt[eng]), reads, writes)

    def replay(self, item):
        kind, a, b, reads, writes, kw = item
        if kind == "op":
            self.op(a, b, reads, writes)
        else:
            self.dma(a, b[0], b[1], reads, writes, **kw)

    def dma(self, queue, out, in_, reads=(), writes=(), **kw):
        if self.capture is not None:
            self.capture.append(("dma", queue, (out, in_), tuple(reads), tuple(writes), dict(kw)))
            return
        if FRESH_POOL_SEMS[0] and queue == "pool":
            kw.pop("ndesc", None)
            idx = NDMA + self.nfresh
            self.nfresh += 1
            self.semh[("d", idx)] = self.es.enter_context(self.nc.semaphore("fsem%d" % idx))
            self.dcnt.append(0)
            self._wait(queue, self._deps(queue, reads, writes, strict=True))
            inst = self.eng[queue].dma_start(out=out, in_=in_, **kw)
            self.dcnt[idx] += 16
            inst.then_inc(self.semh[("d", idx)], 16)
            self._record((("d", idx), self.dcnt[idx]), reads, writes)
            return
        ndesc = kw.pop("ndesc", 1024)
        idx = self.drr
        self.drr = (idx + 1) % NDMA
        if queue == "pool":
            while self.pool_fifo and self.pool_desc + ndesc > POOL_DESC_LIMIT:
                sk, v, nd = self.pool_fifo.pop(0)
                self.pool_desc -= nd
                if self.seen["pool"].get(sk, 0) < v:
                    self.eng["pool"].wait_ge(self.semh[sk], v)
                    self.seen["pool"][sk] = v
        deps = self._deps(queue, reads, writes, strict=True)
        if self.dcnt[idx]:
            deps[("d", idx)] = max(deps.get(("d", idx), 0), self.dcnt[idx])
        self._wait(queue, deps)
        inst = self.eng[queue].dma_start(out=out, in_=in_, **kw)
        self.dcnt[idx] += 16
        inst.then_inc(self.semh[("d", idx)], 16)
        self._record((("d", idx), self.dcnt[idx]), reads, writes)
        if queue == "pool":
            self.pool_fifo.append((("d", idx), self.dcnt[idx], ndesc))
            self.pool_desc += ndesc

    def barrier(self):
        for k in self.eng:
            for k2 in self.eng:
                if k2 != k and self.cnt[k2] > self.seen[k].get(("e", k2), 0):
                    self.eng[k].wait_ge(self.semh[("e", k2)], self.cnt[k2])
                    self.seen[k][("e", k2)] = self.cnt[k2]
            for i in range(len(self.dcnt)):
                if self.dcnt[i] > self.seen[k].get(("d", i), 0):
                    self.eng[k].wait_ge(self.semh[("d", i)], self.dcnt[i])
                    self.seen[k][("d", i)] = self.dcnt[i]

    def finish(self):
        e = self.eng["sp"]
        for i in range(len(self.dcnt)):
            if self.dcnt[i] and self.seen["sp"].get(("d", i), 0) < self.dcnt[i]:
                e.wait_ge(self.semh[("d", i)], self.dcnt[i])
        for k in self.eng:
            if k != "sp" and self.cnt[k] and self.seen["sp"].get(("e", k), 0) < self.cnt[k]:
                e.wait_ge(self.semh[("e", k)], self.cnt[k])


def host_consts():
    p = np.arange(128)[:, None]
    f = np.arange(128)[None, :]
    c = {}
    c["c_ident"] = (p == f).astype(np.float32)
    c["c_cb"] = np.where(p > f, NEG, 0.0).astype(np.float32)
    c["c_cb2"] = np.where(p <= f, NEG, 0.0).astype(np.float32)
    t = np.arange(S)[None, :]
    cm = np.where((16 * p + 31 > t) | (p == 127), NEG, 0.0)
    c["c_cmask"] = cm.astype(np.float32)
    fb = np.zeros((128, NT, 32), np.float32)
    for i in range(NT):
        cur = (128 * i + np.arange(128)) // 64
        for j in range(32):
            fb[:, i, j] = np.where((j == 0) | (j == cur) | (j == cur - 1), 100.0, 0.0)
    c["c_fb"] = fb.reshape(128, NT * 32)
    fbm = np.zeros((128, NT, 8), np.float32)
    for i in range(NT):
        own = i // 2
        for j in range(8):
            fbm[:, i, j] = 0.0 if j < own else (1e4 if j == own else -1e4)
    c["c_fbm"] = fbm.reshape(128, NT * 8)
    key = np.arange(S)[None, :]
    c["c_ind_nsa"] = np.where(key // 64 == np.arange(32)[:, None], NEG, 0.0).astype(np.float32)
    c["c_ind_moba"] = np.where(key // 256 == np.arange(8)[:, None], NEG, 0.0).astype(np.float32)
    cs = np.arange(127)[:, None] * 16
    ss = np.arange(32)[None, :] * 64
    shared = np.clip(np.minimum(cs + 32, ss + 64) - np.maximum(cs, ss), 0, None)
    sw = np.zeros((128, 32), np.float32)
    sw[:127] = shared / 32.0
    c["c_selw"] = sw
    sel = np.zeros((128, 2, 64), np.float32)
    for par in range(2):
        for u in range(64):
            sel[2 * u + par, par, u] = 1.0
    c["c_sel"] = sel.reshape(128, 128)
    inv = (np.float32(500000.0) ** (-np.arange(0, 16, 2, dtype=np.float32) / np.float32(16))).astype(np.float32)
    c["c_invf"] = np.tile(inv[None, :], (128, 1)).astype(np.float32)
    return c


CONST_SHAPES = {
    "c_ident": [128, 128], "c_cb": [128, 128], "c_cb2": [128, 128], "c_cmask": [128, S],
    "c_fb": [128, NT * 32], "c_fbm": [128, NT * 8], "c_ind_nsa": [32, S], "c_ind_moba": [8, S],
    "c_selw": [128, 32], "c_sel": [128, 128], "c_invf": [128, 8],
}

WEIGHT_SHAPES = {
    "norm_g": [4, D], "ada_w": [4, D, 3 * D], "ada_b": [4, 3 * D],
    "nsa_w_in": [2, D, NSA_IN], "nsa_w_out": [2, D, D], "nsa_q_norm": [2, 64], "nsa_k_norm": [2, 3, 64],
    "nsa_cmp_pe": [2, 2, 32, 64], "nsa_cmp_w1": [2, 2, 2048, 256], "nsa_cmp_b1": [2, 2, 256],
    "nsa_cmp_w2": [2, 2, 256, 64], "nsa_gate_b": [2, 48],
    "moba_w_in": [2, D, MOBA_IN], "moba_w_out": [2, D, D], "moba_q_norm": [2, 64], "moba_k_norm": [2, 64],
}


DBG_STAGE = [9]


def build(layers, n_seq, first_from_x=True):
    nc = bass.Bass("TRN2", target_bir_lowering=False, dynamic_dma_scratch_size=8192)
    dr = {}
    dr["x"] = nc.dram_tensor("x", [n_seq, S, D], F32, kind="ExternalInput").ap()
    dr["c"] = nc.dram_tensor("c", [n_seq, D], F32, kind="ExternalInput").ap()
    dr["positions"] = nc.dram_tensor("positions", [n_seq, S], I32, kind="ExternalInput").ap()
    for k, shp in WEIGHT_SHAPES.items():
        dr[k] = nc.dram_tensor(k, shp, F32, kind="ExternalInput").ap()
    for k, shp in CONST_SHAPES.items():
        dr[k] = nc.dram_tensor(k, shp, F32, kind="ExternalInput").ap()
    y_d = nc.dram_tensor("y", [n_seq, S, D], F32, kind="ExternalOutput").ap()
    mod_d = nc.dram_tensor("mod_scr", [n_seq, 4, 3 * D], F32, kind="Internal").ap()

    with ExitStack() as es:
        E = es.enter_context
        sch = Sched(nc, es)

        def V(fn, r=(), w=()):
            sch.op("dve", fn, r, w)

        def A(fn, r=(), w=()):
            sch.op("act", fn, r, w)

        def G(fn, r=(), w=()):
            sch.op("pool", fn, r, w)

        def P(fn, r=(), w=()):
            sch.op("pe", fn, r, w)

        def sb(name, shape, dt):
            return E(nc.sbuf_tensor(name, shape, dt))

        T = {}

        def sb(name, shape, dt, stack=None):
            t = (stack or es).enter_context(nc.sbuf_tensor(name, shape, dt))
            T[name] = t
            return t

        ident = sb("ident", [128, 128], BF16)
        cb = sb("cb", [128, 128], BF16)
        cb2 = sb("cb2", [128, 128], BF16)
        cmask = sb("cmask", [128, S], BF16)
        fb = sb("fb", [128, NT, 32], F32)
        fbm = sb("fbm", [128, NT, 8], F32)
        selm = sb("selm", [128, 2, 64], BF16)
        invf = sb("invf", [128, 8], F32)
        onesc = sb("onesc", [128, 1], BF16)
        W1 = sb("W1", [128, 20864], BF16)
        WO = sb("WO", [128, 8, D], BF16)
        KA = sb("KA", [128, NG, S], BF16)
        VA = sb("VA", [128, NT, NG, 65], BF16)
        KW = sb("KW", [128, NG, 4, 128], BF16)
        VW = sb("VW", [128, 4, NG, 65], BF16)
        onef = sb("onef", [128, 1], F32)
        epsf = sb("epsf", [128, 1], F32)
        kcmpT = sb("kcmpT", [128, NG, 128], BF16)
        vcmp = sb("vcmp", [128, NG, 97], BF16)
        modbc = sb("modbc", [128, 3, D], F32)
        cosT = sb("cosT", [128, NT, 8], F32)
        sinT = sb("sinT", [128, NT, 8], F32)
        nsinT = sb("nsinT", [128, NT, 8], F32)
        cosC = sb("cosC", [128, 8], F32)
        sinC = sb("sinC", [128, 8], F32)
        nsinC = sb("nsinC", [128, 8], F32)
        gq = sb("gq", [128, 64], F32)
        gk = sb("gk", [128, 3, 64], F32)
        gateb = sb("gateb", [128, 48], F32)
        kmT = sb("kmT", [64, NG, 8], BF16)
        kmtmp = sb("kmtmp", [64, NG], F32)
        ksum = sb("ksum", [64, NT, NG], F32)
        cw2 = sb("cw2", [128, 2, 2, 64], BF16)
        peT2 = sb("peT2", [128, 2, 16], BF16)
        peT2f = sb("peT2f", [128, 2, 16], F32)
        b1c = sb("b1c", [128, 2, 2], F32)
        b1p = sb("b1p", [128, 2, 2], F32)
        xt = [sb("xt%d" % i, [128, D], F32) for i in range(2)]
        st = sb("st", [128, 8], F32)
        hb = sb("hb", [128, D], BF16)
        hT = [sb("hT%d" % i, [128, 8, 128], BF16) for i in range(2)]
        sq = sb("sq", [128, 512], F32)
        ssq = sb("ssq", [128, 16], F32)
        rq = sb("rq", [128, 16], F32)
        qn = sb("qn", [128, 8, 64], F32)
        rtmp = sb("rtmp", [128, 8, 16], F32)
        rtmp2 = sb("rtmp2", [128, 8, 16], F32)
        kb = sb("kb", [128, NG, 64], BF16)
        ezs = sb("ezs", [128, D], F32)
        PJ = [E(nc.psum_tensor("PJ%d" % i, [128, 512], F32)) for i in range(2)]
        PS = [E(nc.psum_tensor("PS%d" % i, [128, 512], F32)) for i in range(2)]
        PO = [E(nc.psum_tensor("PO%d" % i, [128, 512], F32)) for i in range(2)]
        PT = [E(nc.psum_tensor("PT%d" % i, [128, 1024], BF16)) for i in range(2)]
        rr = {"PJ": 0, "PS": 0, "PO": 0, "PT": 0, "pT": 0, "xt": 0, "hT": 0}
        uid = [0]

        fixed_bank = [None]

        def nxt(k, n=2):
            if fixed_bank[0] is not None and k == "PT":
                return fixed_bank[0]
            if fixed_bank[0] == 1 and k == "PJ":
                raise RuntimeError("atomic-mode code must not allocate PJ banks (owned by the deferred front stage)")
            v = rr[k]
            rr[k] = (v + 1) % n
            return v

        def alloc_attn_tiles(stack):
            uid[0] += 1
            u = "_%d" % uid[0]
            for nm, shp, dt in [("qaug", [128, NH, 96], BF16), ("qT1", [128, NH, 128], BF16), ("qaT", [128, NH, 128], BF16),
                                ("szb", [128, D], BF16), ("gates", [128, NH, 3], F32), ("oacc", [128, NH, 64], F32)]:
                T[nm] = [stack.enter_context(nc.sbuf_tensor("%s%s_%d" % (nm, u, k), shp, dt)) for k in range(2)]
                if nm in ("qT1", "qaT"):
                    for k in range(2):
                        G(lambda e, t=T[nm][k]: e.memset(t[:], 0.0), w=[(nm, k)])
            for nm, shp, dt in [("gtmp", [128, 48], F32),
                                ("sg", [128, NH, 8], F32), ("m8", [128, NH, 8], F32), ("imp", [128, NG, 32], F32),
                                ("rs", [128, 4], F32), ("cc", [128, 4], F32), ("otmp", [128, 4, 64], F32),
                                ("og", [128, D], BF16), ("ogT", [128, 8, 128], BF16),
                                ("ytmp", [128, 512], F32), ("pT0", [128, 512], BF16), ("pT1", [128, 512], BF16),
                                ("pT2", [128, 512], BF16)]:
                T[nm] = stack.enter_context(nc.sbuf_tensor(nm + u, shp, dt))

        def alloc_cmp_tiles(stack):
            uid[0] += 1
            u = "_%d" % uid[0]
            for nm, shp, dt in [("KC2", [128, 2, NG, 1024], BF16), ("hidT", [128, 4, NG, 127], BF16),
                                ("hu", [128, 508], F32), ("hw", [128, 508], F32), ("kvb", [128, 512], BF16)]:
                T[nm] = stack.enter_context(nc.sbuf_tensor(nm + u, shp, dt))

        def cload(tile_ap, src, key, queue="pool"):
            sch.dma(queue, tile_ap, src, writes=[key])

        cload(ident[:], dr["c_ident"], "ident")
        cload(cb[:], dr["c_cb"], "cb")
        cload(cb2[:], dr["c_cb2"], "cb2")
        for h in range(2):
            cload(cmask[:, h * 1024:(h + 1) * 1024], dr["c_cmask"][:, h * 1024:(h + 1) * 1024], "cmask")
        cload(fb[:].rearrange("p a b -> p (a b)"), dr["c_fb"], "fb", "sp")
        cload(fbm[:].rearrange("p a b -> p (a b)"), dr["c_fbm"], "fbm", "sp")
        cload(selm[:].rearrange("p a b -> p (a b)"), dr["c_sel"], "selm")
        cload(invf[:], dr["c_invf"], "invf", "sp")
        V(lambda e: e.memset(onesc[:], 1.0), w=["onesc"])
        V(lambda e: e.memset(onef[:], 1.0), w=["onef"])
        V(lambda e: e.memset(epsf[:], EPS), w=["epsf"])
        V(lambda e: e.memset(VA[:, :, :, 64:65], 1.0), w=["VA1"])
        V(lambda e: e.memset(VW[:, :, :, 64:65], 1.0), w=["VW1"])
        V(lambda e: e.memset(kmT[:], 0.0), w=["kmT"])
        V(lambda e: e.memset(ksum[:], 0.0), w=["ksum"])
        V(lambda e: e.memset(kcmpT[:], 0.0), w=["kcmpT"])
        G(lambda e: e.memset(KA[64:128, :, :], 0.0), w=["KAind"])
        G(lambda e: e.memset(KW[:], 0.0), w=[("KW", 0), ("KW", 1), ("KW", 2), ("KW", 3)])
        V(lambda e: e.memset(vcmp[:], 0.0), w=["vcmp"])
        V(lambda e: e.memset(vcmp[:, :, 64:65], 1.0), w=["vcmp"])
        for g in range(NG):
            cload(vcmp[:, g, 65:97], dr["c_selw"], "vcmp")

        with ExitStack() as es2:
            E2 = es2.enter_context
            cT = E2(nc.sbuf_tensor("cT", [128, 8, 2], F32))
            cTe = E2(nc.sbuf_tensor("cTe", [128, 8, 2], F32))
            cTb = E2(nc.sbuf_tensor("cTb", [128, 8, 2], BF16))
            adaw = [E2(nc.sbuf_tensor("adaw%d" % i, [128, 8, 512], BF16)) for i in range(3)]
            adab = E2(nc.sbuf_tensor("adab", [2, 512], F32))
            grow = E2(nc.sbuf_tensor("grow", [2, D], F32))
            modr = [E2(nc.sbuf_tensor("modr%d" % i, [2, 512], F32)) for i in range(2)]
            V(lambda e: e.memset(cT[:], 0.0), w=["cT"])
            for b in range(n_seq):
                sch.dma("sp", cT[:, :, b:b + 1], dr["c"][b].rearrange("(k p o) -> p k o", p=128, o=1),
                        writes=["cT"], allow_slow_non_contiguous=True)
            A(lambda e: e.activation(out=cTe[:], in_=cT[:], func=AF.Exp, scale=-1.0), r=["cT"], w=["cTe"])
            V(lambda e: e.tensor_scalar_add(cTe[:], cTe[:], 1.0), r=["cTe"], w=["cTe"])
            V(lambda e: e.reciprocal(cTe[:], cTe[:]), r=["cTe"], w=["cTe"])
            V(lambda e: e.tensor_tensor(out=cTb[:], in0=cT[:], in1=cTe[:], op=ALU.mult), r=["cT", "cTe"], w=["cTb"])
            ci = 0
            for L in layers:
                for b in range(2):
                    sch.dma("sp", grow[b:b + 1, :], dr["norm_g"][L:L + 1, :], writes=["grow"])
                for ch in range(6):
                    slot = ci % 3
                    ms = ci % 2
                    ci += 1
                    sch.dma("pool", adaw[slot][:], dr["ada_w"][L, :, ch * 512:(ch + 1) * 512].rearrange("(k p) n -> p k n", p=128),
                            writes=[("adaw", slot)])
                    for b in range(2):
                        sch.dma("sp", adab[b:b + 1, :], dr["ada_b"][L:L + 1, ch * 512:(ch + 1) * 512], writes=["adab"])
                    pj = nxt("PJ")
                    for k in range(8):
                        P(lambda e, k=k, pj=pj, slot=slot: e.matmul(PJ[pj][0:2, :], cTb[:, k, :], adaw[slot][:, k, :],
                                                                     start=(k == 0), stop=(k == 7)),
                          r=["cTb", ("adaw", slot)], w=[("PJ", pj)])
                    V(lambda e, pj=pj, ms=ms: e.tensor_tensor(out=modr[ms][:], in0=PJ[pj][0:2, :], in1=adab[:], op=ALU.add),
                      r=[("PJ", pj), "adab"], w=[("modr", ms)])
                    if ch in (2, 3):
                        g0 = (ch - 2) * 512
                        V(lambda e, ms=ms, g0=g0: e.scalar_tensor_tensor(out=modr[ms][:], in0=modr[ms][:], scalar=1.0,
                                                                         in1=grow[:, g0:g0 + 512], op0=ALU.add, op1=ALU.mult),
                          r=[("modr", ms), "grow"], w=[("modr", ms)])
                        dst0 = g0
                    elif ch in (0, 1):
                        dst0 = D + ch * 512
                    else:
                        dst0 = 2 * D + (ch - 4) * 512
                    for b in range(n_seq):
                        sch.dma("sp", mod_d[b, L:L + 1, dst0:dst0 + 512], modr[ms][b:b + 1, :],
                                reads=[("modr", ms)], writes=[("mod", b, L)])
            sch.barrier()

        def rstd_from_ss(ss_ap, out_ap, n, keys_r, key_w):
            A(lambda e: e.activation(out=out_ap, in_=ss_ap, func=AF.Ln, scale=1.0 / n, bias=epsf[:, 0:1]), r=list(keys_r) + ["epsf"], w=[key_w])
            A(lambda e: e.activation(out=out_ap, in_=out_ap, func=AF.Exp, scale=-0.5), r=[key_w], w=[key_w])

        def load_norm_h(b, i, src_ap):
            xs = nxt("xt")
            sch.dma("sp", xt[xs][:], src_ap, reads=[("res", b, i)], writes=[("xt", xs)])
            A(lambda e: e.activation(out=hb[:], in_=xt[xs][:], func=AF.Square, accum_out=st[:, 0:1]),
              r=[("xt", xs)], w=["hb", "st0"])
            rstd_from_ss(st[:, 0:1], st[:, 1:2], D, ["st0"], "st1")
            V(lambda e: e.scalar_tensor_tensor(out=ezs[:], in0=xt[xs][:], scalar=st[:, 1:2], in1=modbc[:, 0, :],
                                               op0=ALU.mult, op1=ALU.mult), r=[("xt", xs), "st1", ("modbc", 0)], w=[("ezs", 0), ("ezs", 512)])
            V(lambda e: e.tensor_tensor(out=hb[:], in0=ezs[:], in1=modbc[:, 1, :], op=ALU.add), r=[("ezs", 0), ("ezs", 512), ("modbc", 1)], w=["hb"])
            pt = nxt("PT")
            for k in range(8):
                P(lambda e, k=k, pt=pt: e.transpose(PT[pt][:, k * 128:(k + 1) * 128], hb[:, k * 128:(k + 1) * 128], ident[:]),
                  r=["hb", "ident"], w=[("PT", pt)])
            hs = nxt("hT")
            V(lambda e: e.tensor_copy(out=hT[hs][:].rearrange("p a b -> p (a b)"), in_=PT[pt][:]),
              r=[("PT", pt)], w=[("hT", hs)])
            return xs, hs

        def project(hs, wcol0, ncols, wstride):
            pj = nxt("PJ")
            for k in range(8):
                P(lambda e, k=k, pj=pj: e.matmul(PJ[pj][:, 0:ncols], hT[hs][:, k, :],
                                                 W1[:, k * wstride + wcol0:k * wstride + wcol0 + ncols],
                                                 start=(k == 0), stop=(k == 7)),
                  r=[("hT", hs)] + [("W1", c) for c in range(wcol0 // 512, (wcol0 + ncols - 1) // 512 + 1)], w=[("PJ", pj)])
            return pj

        def head_norm_rope(pj, c0, nh, g_ap, cos_ap, sin_ap, nsin_ap, out_ap, wkey):
            src = PJ[pj][:, c0:c0 + nh * 64]
            src3 = src.rearrange("p (a b) -> p a b", a=nh)
            A(lambda e: e.activation(out=sq[:, 0:nh * 64], in_=src, func=AF.Square), r=[("PJ", pj)], w=["sq"])
            V(lambda e: e.tensor_reduce(out=ssq[:, 0:nh], in_=sq[:, 0:nh * 64].rearrange("p (a b) -> p a b", a=nh),
                                        axis=AX.X, op=ALU.add), r=["sq"], w=["ssq"])
            rstd_from_ss(ssq[:, 0:nh], rq[:, 0:nh], HD, ["ssq"], "rq")
            V(lambda e: e.tensor_tensor(out=qn[:, 0:nh, :], in0=src3, in1=rq[:, 0:nh].unsqueeze(2).to_broadcast([128, nh, 64]),
                                        op=ALU.mult), r=[("PJ", pj), "rq"], w=["qn"])
            V(lambda e: e.tensor_tensor(out=qn[:, 0:nh, :], in0=qn[:, 0:nh, :], in1=g_ap.unsqueeze(1).to_broadcast([128, nh, 64]),
                                        op=ALU.mult), r=["qn", "gains"], w=["qn"])
            cosb = cos_ap.unsqueeze(1).to_broadcast([128, nh, 8])
            sinb = sin_ap.unsqueeze(1).to_broadcast([128, nh, 8])
            nsinb = nsin_ap.unsqueeze(1).to_broadcast([128, nh, 8])
            V(lambda e: e.tensor_tensor(out=rtmp[:, 0:nh, 0:8], in0=qn[:, 0:nh, 0:8], in1=cosb, op=ALU.mult),
              r=["qn", "rope", "ropeC"], w=["rtmp"])
            V(lambda e: e.tensor_tensor(out=rtmp[:, 0:nh, 8:16], in0=qn[:, 0:nh, 8:16], in1=cosb, op=ALU.mult),
              r=["qn", "rope", "ropeC"], w=["rtmp"])
            V(lambda e: e.tensor_tensor(out=rtmp2[:, 0:nh, 0:8], in0=qn[:, 0:nh, 8:16], in1=nsinb, op=ALU.mult),
              r=["qn", "rope", "ropeC"], w=["rtmp2"])
            V(lambda e: e.tensor_tensor(out=rtmp2[:, 0:nh, 8:16], in0=qn[:, 0:nh, 0:8], in1=sinb, op=ALU.mult),
              r=["qn", "rope", "ropeC"], w=["rtmp2"])
            V(lambda e: e.tensor_tensor(out=out_ap[:, :, 0:16], in0=rtmp[:, 0:nh, :], in1=rtmp2[:, 0:nh, :], op=ALU.add),
              r=["rtmp", "rtmp2"], w=[wkey])
            A(lambda e: e.activation(out=out_ap[:, :, 16:64], in_=qn[:, 0:nh, 16:64], func=AF.Copy),
              r=["qn"], w=[wkey])

        def silu_from_psum(pj, n, out_ap, c0, wkey):
            src = PJ[pj][:, 0:n]
            ek = ("ezs", c0)
            A(lambda e: e.activation(out=ezs[:, c0:c0 + n], in_=src, func=AF.Exp, scale=-1.0), r=[("PJ", pj)], w=[ek])
            A(lambda e: e.activation(out=ezs[:, c0:c0 + n], in_=ezs[:, c0:c0 + n], func=AF.Ln, bias=onef[:, 0:1]), r=[ek, "onef"], w=[ek])
            A(lambda e: e.activation(out=ezs[:, c0:c0 + n], in_=ezs[:, c0:c0 + n], func=AF.Exp, scale=-1.0), r=[ek], w=[ek])
            V(lambda e: e.tensor_tensor(out=out_ap, in0=src, in1=ezs[:, c0:c0 + n], op=ALU.mult), r=[("PJ", pj), ek], w=[wkey])

        def transpose_heads(src_tile, ncol, dst_tile, rkeys, wkey):
            for half in range(2):
                pt = nxt("PT")
                for hh in range(8):
                    h = half * 8 + hh
                    P(lambda e, h=h, hh=hh, pt=pt: e.transpose(PT[pt][0:ncol, hh * 128:(hh + 1) * 128], src_tile[:, h, 0:ncol], ident[:]),
                      r=rkeys + ["ident"], w=[("PT", pt)])
                if half == 0:
                    V(lambda e, pt=pt: e.tensor_copy(out=dst_tile[0:ncol, 0:8, :].rearrange("p a b -> p (a b)"), in_=PT[pt][0:ncol, :]),
                      r=[("PT", pt)], w=[wkey])
                else:
                    V(lambda e, pt=pt: e.tensor_copy(out=dst_tile[0:ncol, 8:16, :].rearrange("p a b -> p (a b)"), in_=PT[pt][0:ncol, :]),
                      r=[("PT", pt)], w=[wkey])

        def k_to_KT(dst_ap, wkey, ncopy=128):
            pt = nxt("PT")
            for g in range(NG):
                P(lambda e, g=g, pt=pt: e.transpose(PT[pt][0:64, g * 128:(g + 1) * 128], kb[:, g, :], ident[:]),
                  r=["kb", "ident"], w=[("PT", pt)])
            V(lambda e, pt=pt: e.tensor_copy(out=dst_ap, in_=PT[pt][0:64, 0:512].rearrange("p (a b) -> p a b", a=NG)[:, :, 0:ncopy]),
              r=[("PT", pt)], w=[wkey])

        class Attn:
            def __init__(self):
                self.pend = None

            def unit(self, lhsT_k, rhs_q, bias_ap, v_ap, vw, po, first, rk, on_done=None):
                ps = nxt("PS")
                P(lambda e: e.matmul(PS[ps][:], lhsT_k, rhs_q, start=True, stop=(bias_ap is None)),
                  r=rk, w=[("PS", ps)])
                if bias_ap is not None:
                    P(lambda e: e.matmul(PS[ps][:], ident[:], bias_ap, start=False, stop=True),
                      r=["ident", "cb", "cb2", "cmask"], w=[("PS", ps)])
                sl = nxt("pT", 3)
                pt_t = T["pT%d" % sl]
                A(lambda e: e.activation(out=pt_t[:], in_=PS[ps][:], func=AF.Exp, scale=0.125),
                  r=[("PS", ps)], w=[("pT", sl)])
                self.flush()
                self.pend = (pt_t, sl, v_ap, vw, po, first, rk, on_done)

            def flush(self):
                if self.pend is None:
                    return
                pt_t, sl, v_ap, vw, po, first, rk, on_done = self.pend
                self.pend = None
                for r in range(4):
                    P(lambda e, r=r: e.matmul(PO[po][:, r * vw:(r + 1) * vw], pt_t[:, r * 128:(r + 1) * 128], v_ap,
                                              start=(first and r == 0), stop=True, skip_group_check=True),
                      r=[("pT", sl)] + rk, w=[("PO", po)])
                if on_done is not None:
                    on_done()

        def branch_post(po, vw, g, gate_idx, first_branch, clamp=False, gates=None, sl=0):
            rs, cc, otmp, oacc = T["rs"], T["cc"], T["otmp"], T["oacc"][sl]
            pov = PO[po][:, 0:4 * vw].rearrange("p (a b) -> p a b", a=4)
            if clamp:
                V(lambda e: e.tensor_scalar(out=rs[:], in0=pov[:, :, 64:65].rearrange("p a b -> p (a b)"), scalar1=1e-30, scalar2=None, op0=ALU.max),
                  r=[("PO", po)], w=["rs"])
                V(lambda e: e.reciprocal(rs[:], rs[:]), r=["rs"], w=["rs"])
            else:
                V(lambda e: e.reciprocal(rs[:], pov[:, :, 64:65].rearrange("p a b -> p (a b)")), r=[("PO", po)], w=["rs"])
            if gate_idx is not None:
                V(lambda e: e.tensor_tensor(out=cc[:], in0=rs[:], in1=gates[:, 4 * g:4 * g + 4, gate_idx:gate_idx + 1].rearrange("p a b -> p (a b)"),
                                            op=ALU.mult), r=["rs", ("gates", 0), ("gates", 1)], w=["cc"])
                cap = cc
                ck = "cc"
            else:
                cap = rs
                ck = "rs"
            if first_branch:
                V(lambda e: e.tensor_tensor(out=oacc[:, 4 * g:4 * g + 4, :], in0=pov[:, :, 0:64],
                                            in1=cap[:].unsqueeze(2).to_broadcast([128, 4, 64]), op=ALU.mult),
                  r=[("PO", po), ck], w=[("oacc", sl, g)])
            else:
                V(lambda e: e.tensor_tensor(out=otmp[:], in0=pov[:, :, 0:64],
                                            in1=cap[:].unsqueeze(2).to_broadcast([128, 4, 64]), op=ALU.mult),
                  r=[("PO", po), ck], w=["otmp"])
                G(lambda e: e.tensor_tensor(out=oacc[:, 4 * g:4 * g + 4, :], in0=oacc[:, 4 * g:4 * g + 4, :], in1=otmp[:], op=ALU.add),
                  r=["otmp", ("oacc", sl, g)], w=[("oacc", sl, g)])

        def out_proj_store(b, i, xs, sl):
            oacc, og, ogT, ytmp, szb = T["oacc"][sl], T["og"], T["ogT"], T["ytmp"], T["szb"][sl]
            V(lambda e: e.tensor_tensor(out=og[:], in0=oacc[:].rearrange("p a b -> p (a b)"), in1=szb[:], op=ALU.mult),
              r=[("oacc", sl, g) for g in range(NG)] + [("szb", sl)], w=["og"])
            yield
            pt = nxt("PT")
            for k in range(8):
                P(lambda e, k=k, pt=pt: e.transpose(PT[pt][:, k * 128:(k + 1) * 128], og[:, k * 128:(k + 1) * 128], ident[:]),
                  r=["og", "ident"], w=[("PT", pt)])
            V(lambda e: e.tensor_copy(out=ogT[:].rearrange("p a b -> p (a b)"), in_=PT[pt][:]),
              r=[("PT", pt)], w=["ogT"])
            yield
            for half in range(2):
                pj = rr["PO"]
                for k in range(8):
                    P(lambda e, k=k, pj=pj, half=half: e.matmul(PO[pj][:], ogT[:, k, :], WO[:, k, half * 512:(half + 1) * 512],
                                                               start=(k == 0), stop=(k == 7)),
                      r=["ogT", ("WO", half)], w=[("PO", pj)])
                V(lambda e, pj=pj, half=half: e.tensor_tensor(out=ytmp[:], in0=PO[pj][:], in1=modbc[:, 2, half * 512:(half + 1) * 512], op=ALU.mult),
                  r=[("PO", pj), ("modbc", 2)], w=["ytmp"])
                G(lambda e, half=half: e.tensor_tensor(out=xt[xs][:, half * 512:(half + 1) * 512], in0=ytmp[:],
                                                       in1=xt[xs][:, half * 512:(half + 1) * 512], op=ALU.add),
                  r=["ytmp", ("xt", xs)], w=[("xt", xs)])
                if half == 0:
                    yield
            sch.dma("sp", y_d[b, i * 128:(i + 1) * 128, :], xt[xs][:], reads=[("xt", xs)], writes=[("res", b, i)])
            yield

        def load_w_cols(dst_col0, src_ap_fn, ncols, wstride, key="W1"):
            c = 0
            w1v = W1[:, 0:8 * wstride].rearrange("p (k n) -> p k n", k=8)
            while c < ncols:
                n = min(512 - (dst_col0 + c) % 512, ncols - c)
                sch.dma("pool", w1v[:, :, dst_col0 + c:dst_col0 + c + n], src_ap_fn(c, n), writes=[(key, (dst_col0 + c) // 512)])
                c += n

        def bload(dst_ap, src_row_ap, n, key):
            sch.dma("sp", dst_ap, src_row_ap.to_broadcast([128, n]), writes=[key])

        at_box = [None]

        def at_flush():
            at_box[0].flush()

        def run_pipelined(front, back_units, tail, n_tiles, prologue=None):
            def capture_front(i):
                fixed_bank[0] = 0
                sch.capture = []
                crit = None
                for m in front(i):
                    if m == "crit":
                        crit = len(sch.capture)
                ops = sch.capture
                sch.capture = None
                fixed_bank[0] = 1
                return ops, (len(ops) if crit is None else crit)

            ops, crit_pos = capture_front(0)
            pos = 0
            while pos < crit_pos:
                sch.replay(ops[pos])
                pos += 1
            leftover = ops[pos:]
            if prologue is not None:
                prologue()
            tail_gen = iter(())
            DONE = object()
            for i in range(n_tiles):
                if i + 1 < n_tiles:
                    new_ops, new_crit = capture_front(i + 1)
                else:
                    new_ops, new_crit = [], 0
                n_left = len(leftover)
                ops = leftover + new_ops
                crit_pos = n_left + new_crit
                pos = 0
                units = back_units(i)
                nf = units.index("need_front") if "need_front" in units else len(units)
                nu_front = sum(1 for u in units[:nf] if u != "need_front")
                ui = 0
                tail_done = False
                for u in units:
                    if u == "need_front":
                        for _ in tail_gen:
                            pass
                        tail_done = True
                        while pos < crit_pos:
                            sch.replay(ops[pos])
                            pos += 1
                        continue
                    u()
                    if ui >= 1 and not tail_done:
                        if next(tail_gen, DONE) is DONE:
                            tail_done = True
                    lim = len(ops) if tail_done else n_left
                    if pos < lim:
                        if pos < crit_pos:
                            left = max(1, nu_front - ui - 1)
                            k = -(-(crit_pos - pos) // left)
                        else:
                            k = 3
                        for _ in range(k):
                            if pos < lim:
                                sch.replay(ops[pos])
                                pos += 1
                    ui += 1
                for _ in tail_gen:
                    pass
                while pos < crit_pos:
                    sch.replay(ops[pos])
                    pos += 1
                leftover = ops[pos:]
                tail_gen = tail(i)
            for it in leftover:
                sch.replay(it)
            at_flush()
            for _ in tail_gen:
                pass
            fixed_bank[0] = None

        def moba_layer(b, j, src_d):
            WS = MOBA_IN
            load_w_cols(0, lambda c, n: dr["moba_w_in"][j, :, c:c + n].rearrange("(k p) n -> p k n", p=128), MOBA_IN, WS)
            bload(gq[:], dr["moba_q_norm"][j:j + 1, :], 64, "gains")
            bload(gk[:, 0, :], dr["moba_k_norm"][j:j + 1, :], 64, "gains")
            for g in range(NG):
                sch.dma("pool", KA[64:72, g, :], dr["c_ind_moba"], writes=["KAind"])
            with ExitStack() as esc:
                alloc_attn_tiles(esc)
                sg, m8 = T["sg"], T["m8"]
                ctx = {}

                def front(i):
                    sl = i % 2
                    qaug, qT1, qaT, szb = T["qaug"][sl], T["qT1"][sl], T["qaT"][sl], T["szb"][sl]
                    xs, hs = load_norm_h(b, i, src_d[b, i * 128:(i + 1) * 128, :])
                    ctx[i] = xs
                    yield
                    cs, sn, nsn = cosT[:, i, :], sinT[:, i, :], nsinT[:, i, :]
                    pq0 = project(hs, 0, 512, WS)
                    pq1 = project(hs, 512, 512, WS)
                    head_norm_rope(pq0, 0, 8, gq[:], cs, sn, nsn, qaug[:, 0:8, 0:64], ("qaug", sl, 0))
                    pkv = project(hs, 1024, 512, WS)
                    head_norm_rope(pq1, 0, 8, gq[:], cs, sn, nsn, qaug[:, 8:16, 0:64], ("qaug", sl, 1))
                    transpose_heads(qaug, 64, qT1, [("qaug", sl, 0), ("qaug", sl, 1)], ("qT1", sl))
                    head_norm_rope(pkv, 0, 4, gk[:, 0, :], cs, sn, nsn, kb[:], "kb")
                    A(lambda e, pj=pkv, i=i: e.activation(out=VA[:, i, :, 0:64], in_=PJ[pj][:, 256:512].rearrange("p (a b) -> p a b", a=NG), func=AF.Copy),
                      r=[("PJ", pkv)], w=[("VA", i)])
                    k_to_KT(KA[0:64, :, i * 128:(i + 1) * 128], ("KA", i))
                    pj2 = nxt("PJ")
                    for g in range(NG):
                        P(lambda e, g=g, pj2=pj2: e.matmul(PJ[pj2][0:64, g:g + 1], kb[:, g, :], onesc[:, 0:1], start=True, stop=True),
                          r=["kb", "onesc"], w=[("PJ", pj2)])
                    V(lambda e, pj2=pj2, i=i: e.tensor_copy(out=ksum[:, i, :], in_=PJ[pj2][0:64, 0:NG]), r=[("PJ", pj2)], w=["ksum"])
                    if i % 2 == 1:
                        jb = i // 2
                        V(lambda e, i=i: e.tensor_tensor(out=kmtmp[:], in0=ksum[:, i - 1, :], in1=ksum[:, i, :], op=ALU.add), r=["ksum"], w=["kmtmp"])
                        V(lambda e, jb=jb: e.tensor_scalar(out=kmT[:, :, jb:jb + 1].rearrange("p a b -> p (a b)"), in0=kmtmp[:], scalar1=1.0 / 256.0,
                                                           scalar2=None, op0=ALU.mult), r=["kmtmp"], w=["kmT"])
                    pj3 = nxt("PJ")
                    for h in range(NH):
                        P(lambda e, h=h, pj3=pj3: e.matmul(PJ[pj3][:, h * 8:(h + 1) * 8], qT1[0:64, h, :], kmT[0:64, h // 4, :], start=True, stop=True),
                          r=[("qT1", sl), "kmT"], w=[("PJ", pj3)])
                    V(lambda e, pj3=pj3, i=i: e.tensor_tensor(out=sg[:], in0=PJ[pj3][:, 0:128].rearrange("p (a b) -> p a b", a=NH),
                                                             in1=fbm[:, i, :].unsqueeze(1).to_broadcast([128, NH, 8]), op=ALU.add),
                      r=[("PJ", pj3), "fbm"], w=["sg"])
                    for h in range(NH):
                        V(lambda e, h=h: e.max(out=m8[:, h, :], in_=sg[:, h, :]), r=["sg"], w=[("m8", h)])
                    V(lambda e: e.tensor_tensor(out=qaug[:, :, 64:72], in0=sg[:], in1=m8[:, :, 3:4].to_broadcast([128, NH, 8]), op=ALU.is_lt),
                      r=["sg"] + [("m8", h) for h in range(NH)], w=[("qaug", sl, 2)])
                    transpose_heads(qaug, 72, qaT, [("qaug", sl, 0), ("qaug", sl, 1), ("qaug", sl, 2)], ("qaT", sl))
                    yield "crit"
                    pz0 = project(hs, 1536, 512, WS)
                    pz1 = project(hs, 2048, 512, WS)
                    silu_from_psum(pz0, 512, szb[:, 0:512], 0, ("szb", sl))
                    silu_from_psum(pz1, 512, szb[:, 512:1024], 512, ("szb", sl))
                    yield

                at = Attn()

                def back_units(i):
                    sl = i % 2
                    qaT = T["qaT"][sl]
                    units = []
                    for g in range(NG):
                        po_box = []
                        for jj in range(i + 1):
                            def u(g=g, jj=jj, po_box=po_box):
                                if jj == 0:
                                    po_box.append(nxt("PO"))
                                po = po_box[0]
                                bias = cb[:].unsqueeze(1).to_broadcast([128, 4, 128]) if jj == i else None
                                done = (lambda: branch_post(po, 65, g, None, True, sl=sl)) if jj == i else None
                                at.unit(KA[:, g, jj * 128:(jj + 1) * 128], qaT[:, 4 * g:4 * g + 4, :], bias, VA[:, jj, g, :], 65, po,
                                        jj == 0, [("KA", jj), "KAind", ("qaT", sl), ("VA", jj), "VA1"], done)
                            units.append(u)
                    return units

                def tail(i):
                    return out_proj_store(b, i, ctx[i], i % 2)

                at_box[0] = at
                run_pipelined(front, back_units, tail, NT)
                sch.barrier()

        def nsa_layer(b, j, src_d):
            w_in = dr["nsa_w_in"]
            load_w_cols(0, lambda c, n: w_in[j, :, 1024 + c:1024 + c + n].rearrange("(k p) n -> p k n", p=128), 1024, 1024)
            CW1 = W1[:, 8192:16384].rearrange("p (a m h) -> p a m h", a=2, m=16)
            for kv in range(2):
                for mh in range(2):
                    sch.dma("pool", CW1[:, kv, mh * 8:(mh + 1) * 8, :],
                            dr["nsa_cmp_w1"][j, kv, mh * 1024:(mh + 1) * 1024, :].rearrange("(m p) h -> p m h", p=128), writes=[("CW1", kv, mh)])
                sch.dma("pool", cw2[:, kv, :, :], dr["nsa_cmp_w2"][j, kv].rearrange("(h p) d -> p h d", p=128), writes=["cw2"])
                sch.dma("sp", peT2f[:, kv, :], dr["nsa_cmp_pe"][j, kv].rearrange("l d -> (l d)").rearrange("(m p) -> p m", p=128),
                        writes=["peT2f"], allow_slow_non_contiguous=True)
                sch.dma("sp", b1c[:, kv, :], dr["nsa_cmp_b1"][j, kv].rearrange("(h p) -> p h", p=128), writes=["b1c"],
                        allow_slow_non_contiguous=True)
            V(lambda e: e.tensor_copy(out=peT2[:], in_=peT2f[:]), r=["peT2f"], w=["peT2"])
            bload(gq[:], dr["nsa_q_norm"][j:j + 1, :], 64, "gains")
            for m in range(3):
                bload(gk[:, m, :], dr["nsa_k_norm"][j, m:m + 1, :], 64, "gains")
            bload(gateb[:], dr["nsa_gate_b"][j:j + 1, :], 48, "gateb")
            for g in range(NG):
                sch.dma("pool", KA[64:96, g, :], dr["c_ind_nsa"], writes=["KAind"])
            if DBG_STAGE[0] < 2:
                return
            with ExitStack() as esa:
                alloc_cmp_tiles(esa)
                KC2, hidT, hu, hw, kvb = T["KC2"], T["hidT"], T["hu"], T["hw"], T["kvb"]

                def a_stage2(i, hs):
                    pjA = project(hs, 0, 512, 1024)
                    pjB = project(hs, 512, 512, 1024)
                    A(lambda e, pjA=pjA: e.activation(out=kvb[:], in_=PJ[pjA][:], func=AF.Copy), r=[("PJ", pjA)], w=["kvb"])
                    head_norm_rope(pjB, 0, 4, gk[:, 1, :], cosT[:, i, :], sinT[:, i, :], nsinT[:, i, :], kb[:], "kb")
                    A(lambda e, pjB=pjB, i=i: e.activation(out=VA[:, i, :, 0:64], in_=PJ[pjB][:, 256:512].rearrange("p (a b) -> p a b", a=NG), func=AF.Copy),
                      r=[("PJ", pjB)], w=[("VA", i)])
                    pjs = nxt("PJ")
                    for kv in range(2):
                        for g in range(NG):
                            for par in range(2):
                                c0 = (kv * 4 + g) * 64
                                P(lambda e, kv=kv, g=g, par=par, c0=c0, pjs=pjs: e.matmul(
                                    PJ[pjs][par * 64:(par + 1) * 64, c0:c0 + 64], kvb[:, kv * 256 + g * 64:kv * 256 + (g + 1) * 64],
                                    selm[:, par, :], start=True, stop=True), r=["kvb", "selm"], w=[("PJ", pjs)])
                    V(lambda e, pjs=pjs, i=i: e.tensor_copy(out=KC2[:, :, :, i * 64:(i + 1) * 64],
                                                           in_=PJ[pjs][:].rearrange("p (a g u) -> p a g u", a=2, g=NG)),
                      r=[("PJ", pjs)], w=[("KC2", i)])
                    k_to_KT(KA[0:64, :, i * 128:(i + 1) * 128], ("KA", i))

                prev = None
                for i in range(NT):
                    xs, hs = load_norm_h(b, i, src_d[b, i * 128:(i + 1) * 128, :])
                    if prev is not None:
                        a_stage2(*prev)
                    prev = (i, hs)
                a_stage2(*prev)
                if DBG_STAGE[0] < 3:
                    sch.barrier()
                    return
                kc2keys = [("KC2", i) for i in range(NT)]
                pjb = nxt("PJ")
                for kv in range(2):
                    for half in range(2):
                        col = kv * 2 + half
                        for m in range(16):
                            P(lambda e, kv=kv, half=half, m=m, col=col, pjb=pjb: e.matmul(
                                PJ[pjb][:, col:col + 1], CW1[:, kv, m, half * 128:(half + 1) * 128], peT2[:, kv, m:m + 1],
                                start=(m == 0), stop=(m == 15)), r=[("CW1", kv, m // 8), "peT2"], w=[("PJ", pjb)])
                V(lambda e, pjb=pjb: e.tensor_tensor(out=b1p[:].rearrange("p a b -> p (a b)"), in0=PJ[pjb][:, 0:4],
                                                    in1=b1c[:].rearrange("p a b -> p (a b)"), op=ALU.add), r=[("PJ", pjb), "b1c"], w=["b1p"])
                for kv in range(2):
                    for half in range(2):
                        ps_ = nxt("PJ")
                        for m in range(16):
                            P(lambda e, kv=kv, half=half, m=m, ps_=ps_: e.matmul(
                                PJ[ps_][:, 0:508], CW1[:, kv, m, half * 128:(half + 1) * 128], KC2[:, kv, :, m:m + 1009:8],
                                start=(m == 0), stop=(m == 15)), r=[("CW1", kv, m // 8)] + kc2keys, w=[("PJ", ps_)])
                        V(lambda e, kv=kv, half=half, ps_=ps_: e.tensor_scalar(out=hu[:], in0=PJ[ps_][:, 0:508], scalar1=b1p[:, kv, half:half + 1],
                                                                              scalar2=None, op0=ALU.add), r=[("PJ", ps_), "b1p"], w=["hu"])
                        V(lambda e: e.tensor_tensor(out=hw[:], in0=hu[:], in1=hu[:], op=ALU.mult), r=["hu"], w=["hw"])
                        V(lambda e: e.tensor_scalar(out=hw[:], in0=hw[:], scalar1=0.044715, scalar2=1.0, op0=ALU.mult, op1=ALU.add), r=["hw"], w=["hw"])
                        V(lambda e: e.tensor_tensor(out=hw[:], in0=hw[:], in1=hu[:], op=ALU.mult), r=["hw", "hu"], w=["hw"])
                        A(lambda e: e.activation(out=hw[:], in_=hw[:], func=AF.Exp, scale=-1.5957691216057308), r=["hw"], w=["hw"])
                        V(lambda e: e.tensor_scalar_add(hw[:], hw[:], 1.0), r=["hw"], w=["hw"])
                        V(lambda e: e.reciprocal(hw[:], hw[:]), r=["hw"], w=["hw"])
                        V(lambda e, kv=kv, half=half: e.tensor_tensor(out=hidT[:, kv * 2 + half, :, :].rearrange("p a b -> p (a b)"), in0=hu[:], in1=hw[:],
                                                                      op=ALU.mult), r=["hu", "hw"], w=["hidT"])
                pj2 = nxt("PJ")
                for kv in range(2):
                    for g in range(NG):
                        c0 = (kv * 4 + g) * 64
                        for half in range(2):
                            P(lambda e, kv=kv, g=g, half=half, c0=c0, pj2=pj2: e.matmul(
                                PJ[pj2][0:127, c0:c0 + 64], hidT[:, kv * 2 + half, g, :], cw2[:, kv, half, :],
                                start=(half == 0), stop=(half == 1)), r=["hidT", "cw2"], w=[("PJ", pj2)])
                head_norm_rope(pj2, 0, 4, gk[:, 0, :], cosC[:], sinC[:], nsinC[:], kb[:], "kb")
                A(lambda e, pj2=pj2: e.activation(out=vcmp[0:127, :, 0:64], in_=PJ[pj2][0:127, 256:512].rearrange("p (a b) -> p a b", a=NG), func=AF.Copy),
                  r=[("PJ", pj2)], w=["vcmp"])
                k_to_KT(kcmpT[0:64, :, 0:127], "kcmpT", 127)
                sch.barrier()
            if DBG_STAGE[0] < 4:
                return
            WS = 2608
            load_w_cols(0, lambda c, n: w_in[j, :, c:c + n].rearrange("(k p) n -> p k n", p=128), 1024, WS)
            load_w_cols(1024, lambda c, n: w_in[j, :, 2048 + c:2048 + c + n].rearrange("(k p) n -> p k n", p=128), 1584, WS)
            with ExitStack() as esc:
                alloc_attn_tiles(esc)
                gtmp, imp, m8, rs = T["gtmp"], T["imp"], T["m8"], T["rs"]
                ctx = {}

                def front(i):
                    sl = i % 2
                    qaug, qT1, szb, gates = T["qaug"][sl], T["qT1"][sl], T["szb"][sl], T["gates"][sl]
                    xs, hs = load_norm_h(b, i, src_d[b, i * 128:(i + 1) * 128, :])
                    ctx[i] = xs
                    yield
                    cs, sn, nsn = cosT[:, i, :], sinT[:, i, :], nsinT[:, i, :]
                    slot = i % 4
                    pq0 = project(hs, 0, 512, WS)
                    pq1 = project(hs, 512, 512, WS)
                    head_norm_rope(pq0, 0, 8, gq[:], cs, sn, nsn, qaug[:, 0:8, 0:64], ("qaug", sl, 0))
                    pkw = project(hs, 1024, 512, WS)
                    head_norm_rope(pq1, 0, 8, gq[:], cs, sn, nsn, qaug[:, 8:16, 0:64], ("qaug", sl, 1))
                    transpose_heads(qaug, 64, qT1, [("qaug", sl, 0), ("qaug", sl, 1)], ("qT1", sl))
                    pgl = project(hs, 1536, 48, WS)
                    head_norm_rope(pkw, 0, 4, gk[:, 2, :], cs, sn, nsn, kb[:], "kb")
                    A(lambda e, pj=pkw, slot=slot: e.activation(out=VW[:, slot, :, 0:64], in_=PJ[pj][:, 256:512].rearrange("p (a b) -> p a b", a=NG), func=AF.Copy),
                      r=[("PJ", pkw)], w=[("VW", slot)])
                    k_to_KT(KW[0:64, :, slot, :], ("KW", slot))
                    V(lambda e, pj=pgl: e.tensor_tensor(out=gtmp[:], in0=PJ[pj][:, 0:48], in1=gateb[:], op=ALU.add), r=[("PJ", pgl), "gateb"], w=["gtmp"])
                    A(lambda e: e.activation(out=gtmp[:], in_=gtmp[:], func=AF.Exp, scale=-1.0), r=["gtmp"], w=["gtmp"])
                    A(lambda e: e.activation(out=gtmp[:], in_=gtmp[:], func=AF.Ln, bias=onef[:, 0:1]), r=["gtmp", "onef"], w=["gtmp"])
                    A(lambda e: e.activation(out=gates[:].rearrange("p a b -> p (a b)"), in_=gtmp[:], func=AF.Exp, scale=-1.0), r=["gtmp"], w=[("gates", sl)])
                    yield "crit"
                    pz0 = project(hs, 1584, 512, WS)
                    pz1 = project(hs, 2096, 512, WS)
                    silu_from_psum(pz0, 512, szb[:, 0:512], 0, ("szb", sl))
                    silu_from_psum(pz1, 512, szb[:, 512:1024], 512, ("szb", sl))
                    yield

                at = Attn()

                def cmp_post(po, g, i):
                    sl = i % 2
                    qaug = T["qaug"][sl]
                    branch_post(po, 97, g, 0, True, clamp=True, gates=T["gates"][sl], sl=sl)
                    pov = PO[po][:, 0:388].rearrange("p (a b) -> p a b", a=4)
                    V(lambda e: e.tensor_scalar(out=imp[:, g, :], in0=pov[:, 0, 65:97], scalar1=rs[:, 0:1], scalar2=None, op0=ALU.mult),
                      r=[("PO", po), "rs"], w=[("imp", g)])
                    for r_ in range(1, 4):
                        V(lambda e, r_=r_: e.scalar_tensor_tensor(out=imp[:, g, :], in0=pov[:, r_, 65:97], scalar=rs[:, r_:r_ + 1], in1=imp[:, g, :],
                                                                  op0=ALU.mult, op1=ALU.add), r=[("PO", po), "rs", ("imp", g)], w=[("imp", g)])
                    V(lambda e: e.tensor_tensor(out=imp[:, g, :], in0=imp[:, g, :], in1=fb[:, i, :], op=ALU.add), r=[("imp", g), "fb"], w=[("imp", g)])
                    V(lambda e: e.max(out=m8[:, g, :], in_=imp[:, g, :]), r=[("imp", g)], w=[("m8", g)])
                    V(lambda e: e.tensor_tensor(out=qaug[:, 4 * g:4 * g + 4, 64:96], in0=imp[:, g, :].unsqueeze(1).to_broadcast([128, 4, 32]),
                                                in1=m8[:, g, 7:8].unsqueeze(1).to_broadcast([128, 4, 32]), op=ALU.is_lt),
                      r=[("imp", g), ("m8", g)], w=[("qaug", sl, 2)])

                def cmp_unit(i, g):
                    sl = i % 2
                    qT1 = T["qT1"][sl]

                    def u():
                        po = nxt("PO")
                        at.unit(kcmpT[:, g, :], qT1[:, 4 * g:4 * g + 4, :], cmask[:, i * 128:(i + 1) * 128].unsqueeze(1).to_broadcast([128, 4, 128]),
                                vcmp[:, g, :], 97, po, True, ["kcmpT", ("qT1", sl), "vcmp"], (lambda: cmp_post(po, g, i)))
                    return u

                def back_units(i):
                    sl = i % 2
                    qaug, qT1, qaT, gates = T["qaug"][sl], T["qT1"][sl], T["qaT"][sl], T["gates"][sl]
                    units = []
                    tiles = [jj for jj in (i - 2, i - 1, i) if jj >= 0]
                    for g in range(NG):
                        po_box = []
                        for idx, jj in enumerate(tiles):
                            def u(g=g, jj=jj, idx=idx, po_box=po_box):
                                if idx == 0:
                                    po_box.append(nxt("PO"))
                                po = po_box[0]
                                if jj == i:
                                    bias = cb[:].unsqueeze(1).to_broadcast([128, 4, 128])
                                elif jj == i - 2:
                                    bias = cb2[:].unsqueeze(1).to_broadcast([128, 4, 128])
                                else:
                                    bias = None
                                done = (lambda: branch_post(po, 65, g, 2, False, gates=gates, sl=sl)) if jj == i else None
                                at.unit(KW[:, g, jj % 4, :], qT1[:, 4 * g:4 * g + 4, :], bias, VW[:, jj % 4, g, :], 65, po, idx == 0,
                                        [("KW", jj % 4), ("qT1", sl), ("VW", jj % 4), "VW1"], done)
                            units.append(u)

                    def tr():
                        at.flush()
                        transpose_heads(qaug, 96, qaT, [("qaug", sl, 0), ("qaug", sl, 1), ("qaug", sl, 2)], ("qaT", sl))
                    units.append(tr)
                    for g in range(NG):
                        po_box = []
                        for jj in range(i + 1):
                            def u(g=g, jj=jj, po_box=po_box):
                                if jj == 0:
                                    po_box.append(nxt("PO"))
                                po = po_box[0]
                                bias = cb[:].unsqueeze(1).to_broadcast([128, 4, 128]) if jj == i else None
                                done = (lambda: branch_post(po, 65, g, 1, False, gates=gates, sl=sl)) if jj == i else None
                                at.unit(KA[:, g, jj * 128:(jj + 1) * 128], qaT[:, 4 * g:4 * g + 4, :], bias, VA[:, jj, g, :], 65, po,
                                        jj == 0, [("KA", jj), "KAind", ("qaT", sl), ("VA", jj), "VA1"], done)
                            units.append(u)
                        if i + 1 < NT:
                            if g == 0:
                                units.append("need_front")
                            units.append(cmp_unit(i + 1, g))
                    return units

                def tail(i):
                    return out_proj_store(b, i, ctx[i], i % 2)

                def prologue():
                    for g in range(NG):
                        cmp_unit(0, g)()

                at_box[0] = at
                run_pipelined(front, back_units, tail, NT, prologue)
                sch.barrier()

        for b in range(n_seq):
            with ExitStack() as es3:
                E3 = es3.enter_context
                posi = E3(nc.sbuf_tensor("posi%d" % b, [128, NT + 1], I32))
                posf = E3(nc.sbuf_tensor("posf%d" % b, [128, NT + 1], F32))
                ang = E3(nc.sbuf_tensor("ang%d" % b, [128, NT + 1, 8], F32))
                kq = E3(nc.sbuf_tensor("kq%d" % b, [128, NT + 1, 8], F32))
                ki = E3(nc.sbuf_tensor("ki%d" % b, [128, NT + 1, 8], I32))
                red = E3(nc.sbuf_tensor("red%d" % b, [128, NT + 1, 8], F32))
                V(lambda e: e.memset(posi[:], 0), w=["posi"])
                sch.dma("sp", posi[:, 0:NT], dr["positions"][b].rearrange("(i p) -> p i", p=128), writes=["posi"],
                        allow_slow_non_contiguous=True)
                sch.dma("sp", posi[0:127, NT:NT + 1], dr["positions"][b, 16:16 + 16 * 127].rearrange("(n s) -> n s", s=16)[:, 15:16],
                        writes=["posi"], allow_slow_non_contiguous=True)
                V(lambda e: e.tensor_copy(out=posf[:], in_=posi[:]), r=["posi"], w=["posf"])
                V(lambda e: e.tensor_tensor(out=ang[:], in0=posf[:].unsqueeze(2).to_broadcast([128, NT + 1, 8]),
                                            in1=invf[:].unsqueeze(1).to_broadcast([128, NT + 1, 8]), op=ALU.mult),
                  r=["posf", "invf"], w=["ang"])
                C1 = 6.28125
                C2 = float(2.0 * np.pi - 6.28125)
                PI_LO = 3.1415925
                for which, off in (("sin", 0.0), ("cos", 0.25)):
                    V(lambda e, off=off: e.tensor_scalar(out=kq[:], in0=ang[:], scalar1=float(1.0 / (2.0 * np.pi)), scalar2=off,
                                                         op0=ALU.mult, op1=ALU.add), r=["ang"], w=["kq"])
                    V(lambda e: e.tensor_copy(out=ki[:], in_=kq[:]), r=["kq"], w=["ki"])
                    V(lambda e: e.tensor_copy(out=kq[:], in_=ki[:]), r=["ki"], w=["kq"])
                    V(lambda e: e.scalar_tensor_tensor(out=red[:], in0=kq[:], scalar=-C1, in1=ang[:], op0=ALU.mult, op1=ALU.add),
                      r=["kq", "ang"], w=["red"])
                    V(lambda e: e.scalar_tensor_tensor(out=red[:], in0=kq[:], scalar=-C2, in1=red[:], op0=ALU.mult, op1=ALU.add),
                      r=["kq", "red"], w=["red"])
                    if which == "cos":
                        V(lambda e: e.tensor_scalar(out=red[:], in0=red[:], scalar1=float(np.pi / 2), scalar2=PI_LO,
                                                    op0=ALU.add, op1=ALU.min), r=["red"], w=["red"])
                        V(lambda e: e.tensor_scalar(out=red[:], in0=red[:], scalar1=-PI_LO, scalar2=None, op0=ALU.max), r=["red"], w=["red"])
                        A(lambda e: e.activation(out=cosT[:], in_=red[:, 0:NT, :], func=AF.Sin), r=["red"], w=["rope"])
                        A(lambda e: e.activation(out=cosC[:], in_=red[:, NT, :], func=AF.Sin), r=["red"], w=["ropeC"])
                    else:
                        V(lambda e: e.tensor_scalar(out=red[:], in0=red[:], scalar1=PI_LO, scalar2=-PI_LO,
                                                    op0=ALU.min, op1=ALU.max), r=["red"], w=["red"])
                        A(lambda e: e.activation(out=sinT[:], in_=red[:, 0:NT, :], func=AF.Sin), r=["red"], w=["rope"])
                        A(lambda e: e.activation(out=sinC[:], in_=red[:, NT, :], func=AF.Sin), r=["red"], w=["ropeC"])
                V(lambda e: e.tensor_scalar(out=nsinT[:], in0=sinT[:], scalar1=-1.0, scalar2=None, op0=ALU.mult), r=["rope"], w=["rope"])
                V(lambda e: e.tensor_scalar(out=nsinC[:], in0=sinC[:], scalar1=-1.0, scalar2=None, op0=ALU.mult), r=["ropeC"], w=["ropeC"])
                sch.barrier()

            for li, L in enumerate(layers):
                src_d = dr["x"] if (li == 0 and first_from_x) else y_d
                is_nsa = (L % 2 == 0)
                j = L // 2
                for m in range(3):
                    sch.dma("sp", modbc[:, m, :], mod_d[b, L:L + 1, m * D:(m + 1) * D].to_broadcast([128, D]),
                            reads=[("mod", b, L)], writes=[("modbc", m)])
                wo_src = dr["nsa_w_out"][j] if is_nsa else dr["moba_w_out"][j]
                for hlf in range(2):
                    sch.dma("pool", WO[:, :, hlf * 512:(hlf + 1) * 512],
                            wo_src[:, hlf * 512:(hlf + 1) * 512].rearrange("(k p) n -> p k n", p=128), writes=[("WO", hlf)])
                if is_nsa:
                    nsa_layer(b, j, src_d)
                else:
                    moba_layer(b, j, src_d)
        sch.finish()
    return nc


_CONSTS = None


def _consts():
    global _CONSTS
    if _CONSTS is None:
        _CONSTS = host_consts()
    return _CONSTS


def run_layers(inputs, layers, n_seq, n_cores, first_from_x=True, trace=False):
    nc = build(layers, n_seq, first_from_x)
    cst = _consts()
    in_maps = []
    for c in range(n_cores):
        m = {}
        m["x"] = np.ascontiguousarray(inputs["x"][c * n_seq:(c + 1) * n_seq], dtype=np.float32)
        m["c"] = np.ascontiguousarray(inputs["c"][c * n_seq:(c + 1) * n_seq], dtype=np.float32)
        m["positions"] = np.ascontiguousarray(inputs["positions"][c * n_seq:(c + 1) * n_seq], dtype=np.int32)
        for k in WEIGHT_SHAPES:
            m[k] = np.ascontiguousarray(inputs[k], dtype=np.float32)
        for k in CONST_SHAPES:
            m[k] = cst[k]
        in_maps.append(m)
    res = run_bass_kernel_spmd(nc, in_maps, core_ids=list(range(n_cores)), trace=trace)
    out = np.concatenate([np.asarray(r["y"]) for r in res.results], axis=0)
    return out, res


def kernel(**inputs):
    out, _ = run_layers(inputs, [0, 1, 2, 3], 2, N_CORES)
    return out.astype(np.float32)
```

```python
import numpy as np
from contextlib import ExitStack
import concourse.bass as bass
import concourse.mybir as mybir
from concourse.bass_utils import run_bass_kernel_spmd

F32 = mybir.dt.float32
BF16 = mybir.dt.bfloat16
I32 = mybir.dt.int32
AF = mybir.ActivationFunctionType
ALU = mybir.AluOpType
AX = mybir.AxisListType

S = 2048
D = 1024
NT = 16
HD = 64
NH = 16
NG = 4
NEG = -30000.0
EPS = 1e-6
N_CORES = 8
NSA_IN = 3632
MOBA_IN = 2560
NDMA = 24
FRESH_POOL_SEMS = [False]
POOL_DESC_LIMIT = 4608


class Sched:
    def __init__(self, nc, es):
        self.nc = nc
        self.es = es
        self.nfresh = 0
        self.capture = None
        self.pool_fifo = []
        self.pool_desc = 0
        self.eng = {"pe": nc.tensor, "act": nc.scalar, "dve": nc.vector, "pool": nc.gpsimd, "sp": nc.sync}
        self.semh = {}
        for k in self.eng:
            self.semh[("e", k)] = es.enter_context(nc.semaphore("sem_" + k))
        for i in range(NDMA):
            self.semh[("d", i)] = es.enter_context(nc.semaphore("dsem%d" % i))
        self.cnt = {k: 0 for k in self.eng}
        self.dcnt = [0] * NDMA
        self.drr = 0
        self.seen = {k: {} for k in self.eng}
        self.lastw = {}
        self.readers = {}

    def _deps(self, eng, reads, writes, strict=False):
        deps = {}

        def add(t, same_ok):
            sk, v = t
            if same_ok and not strict and sk == ("e", eng) and eng == "pe":
                return
            if deps.get(sk, 0) < v:
                deps[sk] = v

        for k in reads:
            if k in self.lastw:
                add(self.lastw[k], False)
        for k in writes:
            if k in self.lastw:
                add(self.lastw[k], True)
            for sk, v in self.readers.get(k, {}).items():
                add((sk, v), True)
        return deps

    def _wait(self, eng, deps):
        for sk, v in deps.items():
            if eng == "pe" and sk == ("e", "pe"):
                continue
            if self.seen[eng].get(sk, 0) < v:
                self.eng[eng].wait_ge(self.semh[sk], v)
                self.seen[eng][sk] = v

    def _record(self, tok, reads, writes):
        for k in writes:
            self.lastw[k] = tok
            self.readers[k] = {}
        for k in reads:
            d = self.readers.setdefault(k, {})
            if d.get(tok[0], 0) < tok[1]:
                d[tok[0]] = tok[1]

    def op(self, eng, fn, reads=(), writes=()):
        if self.capture is not None:
            self.capture.append(("op", eng, fn, tuple(reads), tuple(writes), None))
            return
        self._wait(eng, self._deps(eng, reads, writes))
        inst = fn(self.eng[eng])
        self.cnt[eng] += 1
        inst.then_inc(self.semh[("e", eng)], 1)
        self._record((("e", eng), self.cnt[eng]), reads, writes)

    def replay(self, item):
        kind, a, b, reads, writes, kw = item
        if kind == "op":
            self.op(a, b, reads, writes)
        else:
            self.dma(a, b[0], b[1], reads, writes, **kw)

    def dma(self, queue, out, in_, reads=(), writes=(), **kw):
        if self.capture is not None:
            self.capture.append(("dma", queue, (out, in_), tuple(reads), tuple(writes), dict(kw)))
            return
        if FRESH_POOL_SEMS[0] and queue == "pool":
            kw.pop("ndesc", None)
            idx = NDMA + self.nfresh
            self.nfresh += 1
            self.semh[("d", idx)] = self.es.enter_context(self.nc.semaphore("fsem%d" % idx))
            self.dcnt.append(0)
            self._wait(queue, self._deps(queue, reads, writes, strict=True))
            inst = self.eng[queue].dma_start(out=out, in_=in_, **kw)
            self.dcnt[idx] += 16
            inst.then_inc(self.semh[("d", idx)], 16)
            self._record((("d", idx), self.dcnt[idx]), reads, writes)
            return
        ndesc = kw.pop("ndesc", 1024)
        idx = self.drr
        self.drr = (idx + 1) % NDMA
        if queue == "pool":
            while self.pool_fifo and self.pool_desc + ndesc > POOL_DESC_LIMIT:
                sk, v, nd = self.pool_fifo.pop(0)
                self.pool_desc -= nd
                if self.seen["pool"].get(sk, 0) < v:
                    self.eng["pool"].wait_ge(self.semh[sk], v)
                    self.seen["pool"][sk] = v
        deps = self._deps(queue, reads, writes, strict=True)
        if self.dcnt[idx]:
            deps[("d", idx)] = max(deps.get(("d", idx), 0), self.dcnt[idx])
        self._wait(queue, deps)
        inst = self.eng[queue].dma_start(out=out, in_=in_, **kw)
        self.dcnt[idx] += 16
        inst.then_inc(self.semh[("d", idx)], 16)
        self._record((("d", idx), self.dcnt[idx]), reads, writes)
        if queue == "pool":
            self.pool_fifo.append((("d", idx), self.dcnt[idx], ndesc))
            self.pool_desc += ndesc

    def barrier(self):
        for k in self.eng:
            for k2 in self.eng:
                if k2 != k and self.cnt[k2] > self.seen[k].get(("e", k2), 0):
                    self.eng[k].wait_ge(self.semh[("e", k2)], self.cnt[k2])
                    self.seen[k][("e", k2)] = self.cnt[k2]
            for i in range(len(self.dcnt)):
                if self.dcnt[i] > self.seen[k].get(("d", i), 0):
                    self.eng[k].wait_ge(self.semh[("d", i)], self.dcnt[i])
                    self.seen[k][("d", i)] = self.dcnt[i]

    def finish(self):
        e = self.eng["sp"]
        for i in range(len(self.dcnt)):
            if self.dcnt[i] and self.seen["sp"].get(("d", i), 0) < self.dcnt[i]:
                e.wait_ge(self.semh[("d", i)], self.dcnt[i])
        for k in self.eng:
            if k != "sp" and self.cnt[k] and self.seen["sp"].get(("e", k), 0) < self.cnt[k]:
                e.wait_ge(self.semh[("e", k)], self.cnt[k])


def host_consts():
    p = np.arange(128)[:, None]
    f = np.arange(128)[None, :]
    c = {}
    c["c_ident"] = (p == f).astype(np.float32)
    c["c_cb"] = np.where(p > f, NEG, 0.0).astype(np.float32)
    c["c_cb2"] = np.where(p <= f, NEG, 0.0).astype(np.float32)
    t = np.arange(S)[None, :]
    cm = np.where((16 * p + 31 > t) | (p == 127), NEG, 0.0)
    c["c_cmask"] = cm.astype(np.float32)
    fb = np.zeros((128, NT, 32), np.float32)
    for i in range(NT):
        cur = (128 * i + np.arange(128)) // 64
        for j in range(32):
            fb[:, i, j] = np.where((j == 0) | (j == cur) | (j == cur - 1), 100.0, 0.0)
    c["c_fb"] = fb.reshape(128, NT * 32)
    fbm = np.zeros((128, NT, 8), np.float32)
    for i in range(NT):
        own = i // 2
        for j in range(8):
            fbm[:, i, j] = 0.0 if j < own else (1e4 if j == own else -1e4)
    c["c_fbm"] = fbm.reshape(128, NT * 8)
    key = np.arange(S)[None, :]
    c["c_ind_nsa"] = np.where(key // 64 == np.arange(32)[:, None], NEG, 0.0).astype(np.float32)
    c["c_ind_moba"] = np.where(key // 256 == np.arange(8)[:, None], NEG, 0.0).astype(np.float32)
    cs = np.arange(127)[:, None] * 16
    ss = np.arange(32)[None, :] * 64
    shared = np.clip(np.minimum(cs + 32, ss + 64) - np.maximum(cs, ss), 0, None)
    sw = np.zeros((128, 32), np.float32)
    sw[:127] = shared / 32.0
    c["c_selw"] = sw
    sel = np.zeros((128, 2, 64), np.float32)
    for par in range(2):
        for u in range(64):
            sel[2 * u + par, par, u] = 1.0
    c["c_sel"] = sel.reshape(128, 128)
    inv = (np.float32(500000.0) ** (-np.arange(0, 16, 2, dtype=np.float32) / np.float32(16))).astype(np.float32)
    c["c_invf"] = np.tile(inv[None, :], (128, 1)).astype(np.float32)
    return c


CONST_SHAPES = {
    "c_ident": [128, 128], "c_cb": [128, 128], "c_cb2": [128, 128], "c_cmask": [128, S],
    "c_fb": [128, NT * 32], "c_fbm": [128, NT * 8], "c_ind_nsa": [32, S], "c_ind_moba": [8, S],
    "c_selw": [128, 32], "c_sel": [128, 128], "c_invf": [128, 8],
}

WEIGHT_SHAPES = {
    "norm_g": [4, D], "ada_w": [4, D, 3 * D], "ada_b": [4, 3 * D],
    "nsa_w_in": [2, D, NSA_IN], "nsa_w_out": [2, D, D], "nsa_q_norm": [2, 64], "nsa_k_norm": [2, 3, 64],
    "nsa_cmp_pe": [2, 2, 32, 64], "nsa_cmp_w1": [2, 2, 2048, 256], "nsa_cmp_b1": [2, 2, 256],
    "nsa_cmp_w2": [2, 2, 256, 64], "nsa_gate_b": [2, 48],
    "moba_w_in": [2, D, MOBA_IN], "moba_w_out": [2, D, D], "moba_q_norm": [2, 64], "moba_k_norm": [2, 64],
}


DBG_STAGE = [9]


def build(layers, n_seq, first_from_x=True):
    nc = bass.Bass("TRN2", target_bir_lowering=False, dynamic_dma_scratch_size=8192)
    dr = {}
    dr["x"] = nc.dram_tensor("x", [n_seq, S, D], F32, kind="ExternalInput").ap()
    dr["c"] = nc.dram_tensor("c", [n_seq, D], F32, kind="ExternalInput").ap()
    dr["positions"] = nc.dram_tensor("positions", [n_seq, S], I32, kind="ExternalInput").ap()
    for k, shp in WEIGHT_SHAPES.items():
        dr[k] = nc.dram_tensor(k, shp, F32, kind="ExternalInput").ap()
    for k, shp in CONST_SHAPES.items():
        dr[k] = nc.dram_tensor(k, shp, F32, kind="ExternalInput").ap()
    y_d = nc.dram_tensor("y", [n_seq, S, D], F32, kind="ExternalOutput").ap()
    mod_d = nc.dram_tensor("mod_scr", [n_seq, 4, 3 * D], F32, kind="Internal").ap()

    with ExitStack() as es:
        E = es.enter_context
        sch = Sched(nc, es)

        def V(fn, r=(), w=()):
            sch.op("dve", fn, r, w)

        def A(fn, r=(), w=()):
            sch.op("act", fn, r, w)

        def G(fn, r=(), w=()):
            sch.op("pool", fn, r, w)

        def P(fn, r=(), w=()):
            sch.op("pe", fn, r, w)

        def sb(name, shape, dt):
            return E(nc.sbuf_tensor(name, shape, dt))

        T = {}

        def sb(name, shape, dt, stack=None):
            t = (stack or es).enter_context(nc.sbuf_tensor(name, shape, dt))
            T[name] = t
            return t

        ident = sb("ident", [128, 128], BF16)
        cb = sb("cb", [128, 128], BF16)
        cb2 = sb("cb2", [128, 128], BF16)
        cmask = sb("cmask", [128, S], BF16)
        fb = sb("fb", [128, NT, 32], F32)
        fbm = sb("fbm", [128, NT, 8], F32)
        selm = sb("selm", [128, 2, 64], BF16)
        invf = sb("invf", [128, 8], F32)
        onesc = sb("onesc", [128, 1], BF16)
        W1 = sb("W1", [128, 20864], BF16)
        WO = sb("WO", [128, 8, D], BF16)
        KA = sb("KA", [128, NG, S], BF16)
        VA = sb("VA", [128, NT, NG, 65], BF16)
        KW = sb("KW", [128, NG, 4, 128], BF16)
        VW = sb("VW", [128, 4, NG, 65], BF16)
        onef = sb("onef", [128, 1], F32)
        epsf = sb("epsf", [128, 1], F32)
        kcmpT = sb("kcmpT", [128, NG, 128], BF16)
        vcmp = sb("vcmp", [128, NG, 97], BF16)
        modbc = sb("modbc", [128, 3, D], F32)
        cosT = sb("cosT", [128, NT, 8], F32)
        sinT = sb("sinT", [128, NT, 8], F32)
        nsinT = sb("nsinT", [128, NT, 8], F32)
        cosC = sb("cosC", [128, 8], F32)
        sinC = sb("sinC", [128, 8], F32)
        nsinC = sb("nsinC", [128, 8], F32)
        gq = sb("gq", [128, 64], F32)
        gk = sb("gk", [128, 3, 64], F32)
        gateb = sb("gateb", [128, 48], F32)
        kmT = sb("kmT", [64, NG, 8], BF16)
        kmtmp = sb("kmtmp", [64, NG], F32)
        ksum = sb("ksum", [64, NT, NG], F32)
        cw2 = sb("cw2", [128, 2, 2, 64], BF16)
        peT2 = sb("peT2", [128, 2, 16], BF16)
        peT2f = sb("peT2f", [128, 2, 16], F32)
        b1c = sb("b1c", [128, 2, 2], F32)
        b1p = sb("b1p", [128, 2, 2], F32)
        xt = [sb("xt%d" % i, [128, D], F32) for i in range(2)]
        st = sb("st", [128, 8], F32)
        hb = sb("hb", [128, D], BF16)
        hT = [sb("hT%d" % i, [128, 8, 128], BF16) for i in range(2)]
        sq = sb("sq", [128, 512], F32)
        ssq = sb("ssq", [128, 16], F32)
        rq = sb("rq", [128, 16], F32)
        qn = sb("qn", [128, 8, 64], F32)
        rtmp = sb("rtmp", [128, 8, 16], F32)
        rtmp2 = sb("rtmp2", [128, 8, 16], F32)
        kb = sb("kb", [128, NG, 64], BF16)
        ezs = sb("ezs", [128, D], F32)
        PJ = [E(nc.psum_tensor("PJ%d" % i, [128, 512], F32)) for i in range(2)]
        PS = [E(nc.psum_tensor("PS%d" % i, [128, 512], F32)) for i in range(2)]
        PO = [E(nc.psum_tensor("PO%d" % i, [128, 512], F32)) for i in range(2)]
        PT = [E(nc.psum_tensor("PT%d" % i, [128, 1024], BF16)) for i in range(2)]
        rr = {"PJ": 0, "PS": 0, "PO": 0, "PT": 0, "pT": 0, "xt": 0, "hT": 0}
        uid = [0]

        fixed_bank = [None]
        pin_pt = [None]

        def nxt(k, n=2):
            if pin_pt[0] is not None and k == "PT":
                return pin_pt[0]
            if fixed_bank[0] is not None and k == "PT":
                return fixed_bank[0]
            if fixed_bank[0] == 1 and k == "PJ":
                raise RuntimeError("atomic-mode code must not allocate PJ banks (owned by the deferred front stage)")
            v = rr[k]
            rr[k] = (v + 1) % n
            return v

        def alloc_attn_tiles(stack):
            uid[0] += 1
            u = "_%d" % uid[0]
            for nm, shp, dt in [("qaug", [128, NH, 96], BF16), ("qT1", [128, NH, 128], BF16), ("qaT", [128, NH, 128], BF16),
                                ("szb", [128, D], BF16), ("gates", [128, NH, 3], F32), ("oacc", [128, NH, 64], F32)]:
                T[nm] = [stack.enter_context(nc.sbuf_tensor("%s%s_%d" % (nm, u, k), shp, dt)) for k in range(2)]
                if nm in ("qT1", "qaT"):
                    for k in range(2):
                        G(lambda e, t=T[nm][k]: e.memset(t[:], 0.0), w=[(nm, k)])
            for nm, shp, dt in [("gtmp", [128, 48], F32),
                                ("sg", [128, NH, 8], F32), ("m8", [128, NH, 8], F32), ("imp", [128, NG, 32], F32),
                                ("rs", [128, 4], F32), ("cc", [128, 4], F32), ("otmp", [128, 4, 64], F32),
                                ("og", [128, D], BF16), ("ogT", [128, 8, 128], BF16),
                                ("ytmp", [128, 512], F32), ("pT0", [128, 512], BF16), ("pT1", [128, 512], BF16),
                                ("pT2", [128, 512], BF16)]:
                T[nm] = stack.enter_context(nc.sbuf_tensor(nm + u, shp, dt))

        def alloc_cmp_tiles(stack):
            uid[0] += 1
            u = "_%d" % uid[0]
            for nm, shp, dt in [("KC2", [128, 2, NG, 1024], BF16), ("hidT", [128, 4, NG, 127], BF16),
                                ("hu", [128, 508], F32), ("hw", [128, 508], F32), ("kvb", [128, 512], BF16)]:
                T[nm] = stack.enter_context(nc.sbuf_tensor(nm + u, shp, dt))

        def cload(tile_ap, src, key, queue="pool"):
            sch.dma(queue, tile_ap, src, writes=[key])

        cload(ident[:], dr["c_ident"], "ident")
        cload(cb[:], dr["c_cb"], "cb")
        cload(cb2[:], dr["c_cb2"], "cb2")
        for h in range(2):
            cload(cmask[:, h * 1024:(h + 1) * 1024], dr["c_cmask"][:, h * 1024:(h + 1) * 1024], "cmask")
        cload(fb[:].rearrange("p a b -> p (a b)"), dr["c_fb"], "fb", "sp")
        cload(fbm[:].rearrange("p a b -> p (a b)"), dr["c_fbm"], "fbm", "sp")
        cload(selm[:].rearrange("p a b -> p (a b)"), dr["c_sel"], "selm")
        cload(invf[:], dr["c_invf"], "invf", "sp")
        V(lambda e: e.memset(onesc[:], 1.0), w=["onesc"])
        V(lambda e: e.memset(onef[:], 1.0), w=["onef"])
        V(lambda e: e.memset(epsf[:], EPS), w=["epsf"])
        V(lambda e: e.memset(VA[:, :, :, 64:65], 1.0), w=["VA1"])
        V(lambda e: e.memset(VW[:, :, :, 64:65], 1.0), w=["VW1"])
        V(lambda e: e.memset(kmT[:], 0.0), w=["kmT"])
        V(lambda e: e.memset(ksum[:], 0.0), w=["ksum"])
        V(lambda e: e.memset(kcmpT[:], 0.0), w=["kcmpT"])
        G(lambda e: e.memset(KA[64:128, :, :], 0.0), w=["KAind"])
        G(lambda e: e.memset(KW[:], 0.0), w=[("KW", 0), ("KW", 1), ("KW", 2), ("KW", 3)])
        V(lambda e: e.memset(vcmp[:], 0.0), w=["vcmp"])
        V(lambda e: e.memset(vcmp[:, :, 64:65], 1.0), w=["vcmp"])
        for g in range(NG):
            cload(vcmp[:, g, 65:97], dr["c_selw"], "vcmp")

        with ExitStack() as es2:
            E2 = es2.enter_context
            cT = E2(nc.sbuf_tensor("cT", [128, 8, 2], F32))
            cTe = E2(nc.sbuf_tensor("cTe", [128, 8, 2], F32))
            cTb = E2(nc.sbuf_tensor("cTb", [128, 8, 2], BF16))
            adaw = [E2(nc.sbuf_tensor("adaw%d" % i, [128, 8, 512], BF16)) for i in range(3)]
            adab = E2(nc.sbuf_tensor("adab", [2, 512], F32))
            grow = E2(nc.sbuf_tensor("grow", [2, D], F32))
            modr = [E2(nc.sbuf_tensor("modr%d" % i, [2, 512], F32)) for i in range(2)]
            V(lambda e: e.memset(cT[:], 0.0), w=["cT"])
            for b in range(n_seq):
                sch.dma("sp", cT[:, :, b:b + 1], dr["c"][b].rearrange("(k p o) -> p k o", p=128, o=1),
                        writes=["cT"], allow_slow_non_contiguous=True)
            A(lambda e: e.activation(out=cTe[:], in_=cT[:], func=AF.Exp, scale=-1.0), r=["cT"], w=["cTe"])
            V(lambda e: e.tensor_scalar_add(cTe[:], cTe[:], 1.0), r=["cTe"], w=["cTe"])
            V(lambda e: e.reciprocal(cTe[:], cTe[:]), r=["cTe"], w=["cTe"])
            V(lambda e: e.tensor_tensor(out=cTb[:], in0=cT[:], in1=cTe[:], op=ALU.mult), r=["cT", "cTe"], w=["cTb"])
            ci = 0
            for L in layers:
                for b in range(2):
                    sch.dma("sp", grow[b:b + 1, :], dr["norm_g"][L:L + 1, :], writes=["grow"])
                for ch in range(6):
                    slot = ci % 3
                    ms = ci % 2
                    ci += 1
                    sch.dma("pool", adaw[slot][:], dr["ada_w"][L, :, ch * 512:(ch + 1) * 512].rearrange("(k p) n -> p k n", p=128),
                            writes=[("adaw", slot)])
                    for b in range(2):
                        sch.dma("sp", adab[b:b + 1, :], dr["ada_b"][L:L + 1, ch * 512:(ch + 1) * 512], writes=["adab"])
                    pj = nxt("PJ")
                    for k in range(8):
                        P(lambda e, k=k, pj=pj, slot=slot: e.matmul(PJ[pj][0:2, :], cTb[:, k, :], adaw[slot][:, k, :],
                                                                     start=(k == 0), stop=(k == 7)),
                          r=["cTb", ("adaw", slot)], w=[("PJ", pj)])
                    V(lambda e, pj=pj, ms=ms: e.tensor_tensor(out=modr[ms][:], in0=PJ[pj][0:2, :], in1=adab[:], op=ALU.add),
                      r=[("PJ", pj), "adab"], w=[("modr", ms)])
                    if ch in (2, 3):
                        g0 = (ch - 2) * 512
                        V(lambda e, ms=ms, g0=g0: e.scalar_tensor_tensor(out=modr[ms][:], in0=modr[ms][:], scalar=1.0,
                                                                         in1=grow[:, g0:g0 + 512], op0=ALU.add, op1=ALU.mult),
                          r=[("modr", ms), "grow"], w=[("modr", ms)])
                        dst0 = g0
                    elif ch in (0, 1):
                        dst0 = D + ch * 512
                    else:
                        dst0 = 2 * D + (ch - 4) * 512
                    for b in range(n_seq):
                        sch.dma("sp", mod_d[b, L:L + 1, dst0:dst0 + 512], modr[ms][b:b + 1, :],
                                reads=[("modr", ms)], writes=[("mod", b, L)])
            sch.barrier()

        def rstd_from_ss(ss_ap, out_ap, n, keys_r, key_w):
            A(lambda e: e.activation(out=out_ap, in_=ss_ap, func=AF.Ln, scale=1.0 / n, bias=epsf[:, 0:1]), r=list(keys_r) + ["epsf"], w=[key_w])
            A(lambda e: e.activation(out=out_ap, in_=out_ap, func=AF.Exp, scale=-0.5), r=[key_w], w=[key_w])

        def load_norm_h(b, i, src_ap):
            xs = nxt("xt")
            sch.dma("sp", xt[xs][:], src_ap, reads=[("res", b, i)], writes=[("xt", xs)])
            A(lambda e: e.activation(out=hb[:], in_=xt[xs][:], func=AF.Square, accum_out=st[:, 0:1]),
              r=[("xt", xs)], w=["hb", "st0"])
            rstd_from_ss(st[:, 0:1], st[:, 1:2], D, ["st0"], "st1")
            V(lambda e: e.scalar_tensor_tensor(out=ezs[:], in0=xt[xs][:], scalar=st[:, 1:2], in1=modbc[:, 0, :],
                                               op0=ALU.mult, op1=ALU.mult), r=[("xt", xs), "st1", ("modbc", 0)], w=[("ezs", 0), ("ezs", 512)])
            V(lambda e: e.tensor_tensor(out=hb[:], in0=ezs[:], in1=modbc[:, 1, :], op=ALU.add), r=[("ezs", 0), ("ezs", 512), ("modbc", 1)], w=["hb"])
            pt = nxt("PT")
            for k in range(8):
                P(lambda e, k=k, pt=pt: e.transpose(PT[pt][:, k * 128:(k + 1) * 128], hb[:, k * 128:(k + 1) * 128], ident[:]),
                  r=["hb", "ident"], w=[("PT", pt)])
            hs = nxt("hT")
            V(lambda e: e.tensor_copy(out=hT[hs][:].rearrange("p a b -> p (a b)"), in_=PT[pt][:]),
              r=[("PT", pt)], w=[("hT", hs)])
            return xs, hs

        def project(hs, wcol0, ncols, wstride):
            pj = nxt("PJ")
            for k in range(8):
                P(lambda e, k=k, pj=pj: e.matmul(PJ[pj][:, 0:ncols], hT[hs][:, k, :],
                                                 W1[:, k * wstride + wcol0:k * wstride + wcol0 + ncols],
                                                 start=(k == 0), stop=(k == 7)),
                  r=[("hT", hs)] + [("W1", c) for c in range(wcol0 // 512, (wcol0 + ncols - 1) // 512 + 1)], w=[("PJ", pj)])
            return pj

        def head_norm_rope(pj, c0, nh, g_ap, cos_ap, sin_ap, nsin_ap, out_ap, wkey, alt=False):
            src = PJ[pj][:, c0:c0 + nh * 64]
            src3 = src.rearrange("p (a b) -> p a b", a=nh)
            if alt:
                sq_ = ezs[:, 0:nh * 64]
                qn_ = ezs[:, 256:256 + nh * 64].rearrange("p (a b) -> p a b", a=nh)
                rt_ = ezs[:, 512:512 + nh * 16].rearrange("p (a b) -> p a b", a=nh)
                rt2_ = ezs[:, 576:576 + nh * 16].rearrange("p (a b) -> p a b", a=nh)
                ssq_ = ezs[:, 640:640 + nh]
                rq_ = ezs[:, 648:648 + nh]
                ka = [("ezs", 0), ("ezs", 512)]
                k_sq = k_ssq = k_rq = k_qn = k_rt = k_rt2 = None
                ksq, kssq, krq, kqn, krt, krt2 = ka, ka, ka, ka, ka, ka
            else:
                sq_ = sq[:, 0:nh * 64]
                qn_ = qn[:, 0:nh, :]
                rt_ = rtmp[:, 0:nh, :]
                rt2_ = rtmp2[:, 0:nh, :]
                ssq_ = ssq[:, 0:nh]
                rq_ = rq[:, 0:nh]
                ksq, kssq, krq, kqn, krt, krt2 = ["sq"], ["ssq"], ["rq"], ["qn"], ["rtmp"], ["rtmp2"]
            A(lambda e: e.activation(out=sq_, in_=src, func=AF.Square), r=[("PJ", pj)], w=ksq)
            V(lambda e: e.tensor_reduce(out=ssq_, in_=sq_.rearrange("p (a b) -> p a b", a=nh), axis=AX.X, op=ALU.add), r=ksq, w=kssq)
            A(lambda e: e.activation(out=rq_, in_=ssq_, func=AF.Ln, scale=1.0 / HD, bias=epsf[:, 0:1]), r=kssq + ["epsf"], w=krq)
            A(lambda e: e.activation(out=rq_, in_=rq_, func=AF.Exp, scale=-0.5), r=krq, w=krq)
            V(lambda e: e.tensor_tensor(out=qn_, in0=src3, in1=rq_.unsqueeze(2).to_broadcast([128, nh, 64]), op=ALU.mult),
              r=[("PJ", pj)] + krq, w=kqn)
            V(lambda e: e.tensor_tensor(out=qn_, in0=qn_, in1=g_ap.unsqueeze(1).to_broadcast([128, nh, 64]), op=ALU.mult),
              r=kqn + ["gains"], w=kqn)
            cosb = cos_ap.unsqueeze(1).to_broadcast([128, nh, 8])
            sinb = sin_ap.unsqueeze(1).to_broadcast([128, nh, 8])
            nsinb = nsin_ap.unsqueeze(1).to_broadcast([128, nh, 8])
            V(lambda e: e.tensor_tensor(out=rt_[:, :, 0:8], in0=qn_[:, :, 0:8], in1=cosb, op=ALU.mult), r=kqn + ["rope", "ropeC"], w=krt)
            V(lambda e: e.tensor_tensor(out=rt_[:, :, 8:16], in0=qn_[:, :, 8:16], in1=cosb, op=ALU.mult), r=kqn + ["rope", "ropeC"], w=krt)
            V(lambda e: e.tensor_tensor(out=rt2_[:, :, 0:8], in0=qn_[:, :, 8:16], in1=nsinb, op=ALU.mult), r=kqn + ["rope", "ropeC"], w=krt2)
            V(lambda e: e.tensor_tensor(out=rt2_[:, :, 8:16], in0=qn_[:, :, 0:8], in1=sinb, op=ALU.mult), r=kqn + ["rope", "ropeC"], w=krt2)
            V(lambda e: e.tensor_tensor(out=out_ap[:, :, 0:16], in0=rt_, in1=rt2_, op=ALU.add), r=krt + krt2, w=[wkey])
            A(lambda e: e.activation(out=out_ap[:, :, 16:64], in_=qn_[:, :, 16:64], func=AF.Copy), r=kqn, w=[wkey])

        def merge_streams(fa, fb):
            outer = sch.capture
            sch.capture = []
            fa()
            la = sch.capture
            sch.capture = []
            fb()
            lb = sch.capture
            sch.capture = outer
            na, nb = len(la), len(lb)
            ia = ib = 0
            while ia < na or ib < nb:
                if ib >= nb or (ia < na and ia * nb <= ib * na):
                    it = la[ia]
                    ia += 1
                else:
                    it = lb[ib]
                    ib += 1
                if outer is not None:
                    outer.append(it)
                else:
                    sch.replay(it)

        def silu_from_psum(pj, n, out_ap, c0, wkey):
            src = PJ[pj][:, 0:n]
            ek = ("ezs", c0)
            A(lambda e: e.activation(out=ezs[:, c0:c0 + n], in_=src, func=AF.Exp, scale=-1.0), r=[("PJ", pj)], w=[ek])
            A(lambda e: e.activation(out=ezs[:, c0:c0 + n], in_=ezs[:, c0:c0 + n], func=AF.Ln, bias=onef[:, 0:1]), r=[ek, "onef"], w=[ek])
            A(lambda e: e.activation(out=ezs[:, c0:c0 + n], in_=ezs[:, c0:c0 + n], func=AF.Exp, scale=-1.0), r=[ek], w=[ek])
            V(lambda e: e.tensor_tensor(out=out_ap, in0=src, in1=ezs[:, c0:c0 + n], op=ALU.mult), r=[("PJ", pj), ek], w=[wkey])

        def transpose_heads(src_tile, ncol, dst_tile, rkeys, wkey):
            for half in range(2):
                pt = nxt("PT")
                for hh in range(8):
                    h = half * 8 + hh
                    P(lambda e, h=h, hh=hh, pt=pt: e.transpose(PT[pt][0:ncol, hh * 128:(hh + 1) * 128], src_tile[:, h, 0:ncol], ident[:]),
                      r=rkeys + ["ident"], w=[("PT", pt)])
                if half == 0:
                    V(lambda e, pt=pt: e.tensor_copy(out=dst_tile[0:ncol, 0:8, :].rearrange("p a b -> p (a b)"), in_=PT[pt][0:ncol, :]),
                      r=[("PT", pt)], w=[wkey])
                else:
                    V(lambda e, pt=pt: e.tensor_copy(out=dst_tile[0:ncol, 8:16, :].rearrange("p a b -> p (a b)"), in_=PT[pt][0:ncol, :]),
                      r=[("PT", pt)], w=[wkey])

        def k_to_KT(dst_ap, wkey, ncopy=128):
            pt = nxt("PT")
            for g in range(NG):
                P(lambda e, g=g, pt=pt: e.transpose(PT[pt][0:64, g * 128:(g + 1) * 128], kb[:, g, :], ident[:]),
                  r=["kb", "ident"], w=[("PT", pt)])
            V(lambda e, pt=pt: e.tensor_copy(out=dst_ap, in_=PT[pt][0:64, 0:512].rearrange("p (a b) -> p a b", a=NG)[:, :, 0:ncopy]),
              r=[("PT", pt)], w=[wkey])

        class Attn:
            def __init__(self):
                self.pend = None

            def unit(self, lhsT_k, rhs_q, bias_ap, v_ap, vw, po, first, rk, on_done=None):
                ps = nxt("PS")
                P(lambda e: e.matmul(PS[ps][:], lhsT_k, rhs_q, start=True, stop=(bias_ap is None)),
                  r=rk, w=[("PS", ps)])
                if bias_ap is not None:
                    P(lambda e: e.matmul(PS[ps][:], ident[:], bias_ap, start=False, stop=True),
                      r=["ident", "cb", "cb2", "cmask"], w=[("PS", ps)])
                sl = nxt("pT", 3)
                pt_t = T["pT%d" % sl]
                A(lambda e: e.activation(out=pt_t[:], in_=PS[ps][:], func=AF.Exp, scale=0.125),
                  r=[("PS", ps)], w=[("pT", sl)])
                self.flush()
                self.pend = (pt_t, sl, v_ap, vw, po, first, rk, on_done)

            def flush(self):
                if self.pend is None:
                    return
                pt_t, sl, v_ap, vw, po, first, rk, on_done = self.pend
                self.pend = None
                for r in range(4):
                    P(lambda e, r=r: e.matmul(PO[po][:, r * vw:(r + 1) * vw], pt_t[:, r * 128:(r + 1) * 128], v_ap,
                                              start=(first and r == 0), stop=True, skip_group_check=True),
                      r=[("pT", sl)] + rk, w=[("PO", po)])
                if on_done is not None:
                    on_done()

        def branch_post(po, vw, g, gate_idx, first_branch, clamp=False, gates=None, sl=0):
            rs, cc, otmp, oacc = T["rs"], T["cc"], T["otmp"], T["oacc"][sl]
            pov = PO[po][:, 0:4 * vw].rearrange("p (a b) -> p a b", a=4)
            if clamp:
                V(lambda e: e.tensor_scalar(out=rs[:], in0=pov[:, :, 64:65].rearrange("p a b -> p (a b)"), scalar1=1e-30, scalar2=None, op0=ALU.max),
                  r=[("PO", po)], w=["rs"])
                V(lambda e: e.reciprocal(rs[:], rs[:]), r=["rs"], w=["rs"])
            else:
                V(lambda e: e.reciprocal(rs[:], pov[:, :, 64:65].rearrange("p a b -> p (a b)")), r=[("PO", po)], w=["rs"])
            if gate_idx is not None:
                V(lambda e: e.tensor_tensor(out=cc[:], in0=rs[:], in1=gates[:, 4 * g:4 * g + 4, gate_idx:gate_idx + 1].rearrange("p a b -> p (a b)"),
                                            op=ALU.mult), r=["rs", ("gates", 0), ("gates", 1)], w=["cc"])
                cap = cc
                ck = "cc"
            else:
                cap = rs
                ck = "rs"
            if first_branch:
                V(lambda e: e.tensor_tensor(out=oacc[:, 4 * g:4 * g + 4, :], in0=pov[:, :, 0:64],
                                            in1=cap[:].unsqueeze(2).to_broadcast([128, 4, 64]), op=ALU.mult),
                  r=[("PO", po), ck], w=[("oacc", sl, g)])
            else:
                V(lambda e: e.tensor_tensor(out=otmp[:], in0=pov[:, :, 0:64],
                                            in1=cap[:].unsqueeze(2).to_broadcast([128, 4, 64]), op=ALU.mult),
                  r=[("PO", po), ck], w=["otmp"])
                G(lambda e: e.tensor_tensor(out=oacc[:, 4 * g:4 * g + 4, :], in0=oacc[:, 4 * g:4 * g + 4, :], in1=otmp[:], op=ALU.add),
                  r=["otmp", ("oacc", sl, g)], w=[("oacc", sl, g)])

        def out_proj_store(b, i, xs, sl):
            oacc, og, ogT, ytmp, szb = T["oacc"][sl], T["og"], T["ogT"], T["ytmp"], T["szb"][sl]
            V(lambda e: e.tensor_tensor(out=og[:], in0=oacc[:].rearrange("p a b -> p (a b)"), in1=szb[:], op=ALU.mult),
              r=[("oacc", sl, g) for g in range(NG)] + [("szb", sl)], w=["og"])
            yield
            pt = nxt("PT")
            for k in range(8):
                P(lambda e, k=k, pt=pt: e.transpose(PT[pt][:, k * 128:(k + 1) * 128], og[:, k * 128:(k + 1) * 128], ident[:]),
                  r=["og", "ident"], w=[("PT", pt)])
            V(lambda e: e.tensor_copy(out=ogT[:].rearrange("p a b -> p (a b)"), in_=PT[pt][:]),
              r=[("PT", pt)], w=["ogT"])
            yield
            for half in range(2):
                pj = rr["PO"]
                for k in range(8):
                    P(lambda e, k=k, pj=pj, half=half: e.matmul(PO[pj][:], ogT[:, k, :], WO[:, k, half * 512:(half + 1) * 512],
                                                               start=(k == 0), stop=(k == 7)),
                      r=["ogT", ("WO", half)], w=[("PO", pj)])
                V(lambda e, pj=pj, half=half: e.tensor_tensor(out=ytmp[:], in0=PO[pj][:], in1=modbc[:, 2, half * 512:(half + 1) * 512], op=ALU.mult),
                  r=[("PO", pj), ("modbc", 2)], w=["ytmp"])
                G(lambda e, half=half: e.tensor_tensor(out=xt[xs][:, half * 512:(half + 1) * 512], in0=ytmp[:],
                                                       in1=xt[xs][:, half * 512:(half + 1) * 512], op=ALU.add),
                  r=["ytmp", ("xt", xs)], w=[("xt", xs)])
                if half == 0:
                    yield
            sch.dma("sp", y_d[b, i * 128:(i + 1) * 128, :], xt[xs][:], reads=[("xt", xs)], writes=[("res", b, i)])
            yield

        def load_w_cols(dst_col0, src_ap_fn, ncols, wstride, key="W1"):
            c = 0
            w1v = W1[:, 0:8 * wstride].rearrange("p (k n) -> p k n", k=8)
            while c < ncols:
                n = min(512 - (dst_col0 + c) % 512, ncols - c)
                sch.dma("pool", w1v[:, :, dst_col0 + c:dst_col0 + c + n], src_ap_fn(c, n), writes=[(key, (dst_col0 + c) // 512)])
                c += n

        def bload(dst_ap, src_row_ap, n, key):
            sch.dma("sp", dst_ap, src_row_ap.to_broadcast([128, n]), writes=[key])

        at_box = [None]

        def at_flush():
            at_box[0].flush()

        def run_pipelined(front, back_units, tail, n_tiles, prologue=None):
            def capture_front(i):
                fixed_bank[0] = 0
                sch.capture = []
                crit = None
                for m in front(i):
                    if m == "crit":
                        crit = len(sch.capture)
                ops = sch.capture
                sch.capture = None
                fixed_bank[0] = 1
                return ops, (len(ops) if crit is None else crit)

            ops, crit_pos = capture_front(0)
            pos = 0
            while pos < crit_pos:
                sch.replay(ops[pos])
                pos += 1
            leftover = ops[pos:]
            if prologue is not None:
                prologue()
            tail_gen = iter(())
            DONE = object()
            for i in range(n_tiles):
                if i + 1 < n_tiles:
                    new_ops, new_crit = capture_front(i + 1)
                else:
                    new_ops, new_crit = [], 0
                n_left = len(leftover)
                ops = leftover + new_ops
                crit_pos = n_left + new_crit
                pos = 0
                units = back_units(i)
                nf = units.index("need_front") if "need_front" in units else len(units)
                nu_front = sum(1 for u in units[:nf] if u != "need_front")
                ui = 0
                tail_done = False
                for u in units:
                    if u == "need_front":
                        for _ in tail_gen:
                            pass
                        tail_done = True
                        while pos < crit_pos:
                            sch.replay(ops[pos])
                            pos += 1
                        continue
                    u()
                    if ui >= 1 and not tail_done:
                        if next(tail_gen, DONE) is DONE:
                            tail_done = True
                    lim = len(ops) if tail_done else n_left
                    if pos < lim:
                        if pos < crit_pos:
                            left = max(1, nu_front - ui - 1)
                            k = -(-(crit_pos - pos) // left)
                        else:
                            k = 3
                        for _ in range(k):
                            if pos < lim:
                                sch.replay(ops[pos])
                                pos += 1
                    ui += 1
                for _ in tail_gen:
                    pass
                while pos < crit_pos:
                    sch.replay(ops[pos])
                    pos += 1
                leftover = ops[pos:]
                tail_gen = tail(i)
            for it in leftover:
                sch.replay(it)
            at_flush()
            for _ in tail_gen:
                pass
            fixed_bank[0] = None

        def moba_layer(b, j, src_d):
            WS = MOBA_IN
            load_w_cols(0, lambda c, n: dr["moba_w_in"][j, :, c:c + n].rearrange("(k p) n -> p k n", p=128), MOBA_IN, WS)
            bload(gq[:], dr["moba_q_norm"][j:j + 1, :], 64, "gains")
            bload(gk[:, 0, :], dr["moba_k_norm"][j:j + 1, :], 64, "gains")
            for g in range(NG):
                sch.dma("pool", KA[64:72, g, :], dr["c_ind_moba"], writes=["KAind"])
            with ExitStack() as esc:
                alloc_attn_tiles(esc)
                sg, m8 = T["sg"], T["m8"]
                ctx = {}

                def front(i):
                    sl = i % 2
                    qaug, qT1, qaT, szb = T["qaug"][sl], T["qT1"][sl], T["qaT"][sl], T["szb"][sl]
                    xs, hs = load_norm_h(b, i, src_d[b, i * 128:(i + 1) * 128, :])
                    ctx[i] = xs
                    yield
                    cs, sn, nsn = cosT[:, i, :], sinT[:, i, :], nsinT[:, i, :]
                    pq0 = project(hs, 0, 512, WS)
                    pq1 = project(hs, 512, 512, WS)
                    head_norm_rope(pq0, 0, 8, gq[:], cs, sn, nsn, qaug[:, 0:8, 0:64], ("qaug", sl, 0))
                    pkv = project(hs, 1024, 512, WS)
                    merge_streams(lambda: head_norm_rope(pq1, 0, 8, gq[:], cs, sn, nsn, qaug[:, 8:16, 0:64], ("qaug", sl, 1)),
                                  lambda: head_norm_rope(pkv, 0, 4, gk[:, 0, :], cs, sn, nsn, kb[:], "kb", alt=True))
                    transpose_heads(qaug, 64, qT1, [("qaug", sl, 0), ("qaug", sl, 1)], ("qT1", sl))
                    A(lambda e, pj=pkv, i=i: e.activation(out=VA[:, i, :, 0:64], in_=PJ[pj][:, 256:512].rearrange("p (a b) -> p a b", a=NG), func=AF.Copy),
                      r=[("PJ", pkv)], w=[("VA", i)])
                    k_to_KT(KA[0:64, :, i * 128:(i + 1) * 128], ("KA", i))
                    pj2 = nxt("PJ")
                    for g in range(NG):
                        P(lambda e, g=g, pj2=pj2: e.matmul(PJ[pj2][0:64, g:g + 1], kb[:, g, :], onesc[:, 0:1], start=True, stop=True),
                          r=["kb", "onesc"], w=[("PJ", pj2)])
                    V(lambda e, pj2=pj2, i=i: e.tensor_copy(out=ksum[:, i, :], in_=PJ[pj2][0:64, 0:NG]), r=[("PJ", pj2)], w=["ksum"])
                    if i % 2 == 1:
                        jb = i // 2
                        V(lambda e, i=i: e.tensor_tensor(out=kmtmp[:], in0=ksum[:, i - 1, :], in1=ksum[:, i, :], op=ALU.add), r=["ksum"], w=["kmtmp"])
                        V(lambda e, jb=jb: e.tensor_scalar(out=kmT[:, :, jb:jb + 1].rearrange("p a b -> p (a b)"), in0=kmtmp[:], scalar1=1.0 / 256.0,
                                                           scalar2=None, op0=ALU.mult), r=["kmtmp"], w=["kmT"])
                    pj3 = nxt("PJ")
                    for h in range(NH):
                        P(lambda e, h=h, pj3=pj3: e.matmul(PJ[pj3][:, h * 8:(h + 1) * 8], qT1[0:64, h, :], kmT[0:64, h // 4, :], start=True, stop=True),
                          r=[("qT1", sl), "kmT"], w=[("PJ", pj3)])
                    V(lambda e, pj3=pj3, i=i: e.tensor_tensor(out=sg[:], in0=PJ[pj3][:, 0:128].rearrange("p (a b) -> p a b", a=NH),
                                                             in1=fbm[:, i, :].unsqueeze(1).to_broadcast([128, NH, 8]), op=ALU.add),
                      r=[("PJ", pj3), "fbm"], w=["sg"])
                    for h in range(NH):
                        V(lambda e, h=h: e.max(out=m8[:, h, :], in_=sg[:, h, :]), r=["sg"], w=[("m8", h)])
                    V(lambda e: e.tensor_tensor(out=qaug[:, :, 64:72], in0=sg[:], in1=m8[:, :, 3:4].to_broadcast([128, NH, 8]), op=ALU.is_lt),
                      r=["sg"] + [("m8", h) for h in range(NH)], w=[("qaug", sl, 2)])
                    transpose_heads(qaug, 72, qaT, [("qaug", sl, 0), ("qaug", sl, 1), ("qaug", sl, 2)], ("qaT", sl))
                    yield "crit"
                    pz0 = project(hs, 1536, 512, WS)
                    pz1 = project(hs, 2048, 512, WS)
                    silu_from_psum(pz0, 512, szb[:, 0:512], 0, ("szb", sl))
                    silu_from_psum(pz1, 512, szb[:, 512:1024], 512, ("szb", sl))
                    yield

                at = Attn()

                def back_units(i):
                    sl = i % 2
                    qaT = T["qaT"][sl]
                    units = []
                    for g in range(NG):
                        po_box = []
                        for jj in range(i + 1):
                            def u(g=g, jj=jj, po_box=po_box):
                                if jj == 0:
                                    po_box.append(nxt("PO"))
                                po = po_box[0]
                                bias = cb[:].unsqueeze(1).to_broadcast([128, 4, 128]) if jj == i else None
                                done = (lambda: branch_post(po, 65, g, None, True, sl=sl)) if jj == i else None
                                at.unit(KA[:, g, jj * 128:(jj + 1) * 128], qaT[:, 4 * g:4 * g + 4, :], bias, VA[:, jj, g, :], 65, po,
                                        jj == 0, [("KA", jj), "KAind", ("qaT", sl), ("VA", jj), "VA1"], done)
                            units.append(u)
                    return units

                def tail(i):
                    return out_proj_store(b, i, ctx[i], i % 2)

                at_box[0] = at
                run_pipelined(front, back_units, tail, NT)
                sch.barrier()

        def nsa_layer(b, j, src_d):
            w_in = dr["nsa_w_in"]
            load_w_cols(0, lambda c, n: w_in[j, :, 1024 + c:1024 + c + n].rearrange("(k p) n -> p k n", p=128), 1024, 1024)
            CW1 = W1[:, 8192:16384].rearrange("p (a m h) -> p a m h", a=2, m=16)
            for kv in range(2):
                for mh in range(2):
                    sch.dma("pool", CW1[:, kv, mh * 8:(mh + 1) * 8, :],
                            dr["nsa_cmp_w1"][j, kv, mh * 1024:(mh + 1) * 1024, :].rearrange("(m p) h -> p m h", p=128), writes=[("CW1", kv, mh)])
                sch.dma("pool", cw2[:, kv, :, :], dr["nsa_cmp_w2"][j, kv].rearrange("(h p) d -> p h d", p=128), writes=["cw2"])
                sch.dma("sp", peT2f[:, kv, :], dr["nsa_cmp_pe"][j, kv].rearrange("l d -> (l d)").rearrange("(m p) -> p m", p=128),
                        writes=["peT2f"], allow_slow_non_contiguous=True)
                sch.dma("sp", b1c[:, kv, :], dr["nsa_cmp_b1"][j, kv].rearrange("(h p) -> p h", p=128), writes=["b1c"],
                        allow_slow_non_contiguous=True)
            V(lambda e: e.tensor_copy(out=peT2[:], in_=peT2f[:]), r=["peT2f"], w=["peT2"])
            bload(gq[:], dr["nsa_q_norm"][j:j + 1, :], 64, "gains")
            for m in range(3):
                bload(gk[:, m, :], dr["nsa_k_norm"][j, m:m + 1, :], 64, "gains")
            bload(gateb[:], dr["nsa_gate_b"][j:j + 1, :], 48, "gateb")
            for g in range(NG):
                sch.dma("pool", KA[64:96, g, :], dr["c_ind_nsa"], writes=["KAind"])
            if DBG_STAGE[0] < 2:
                return
            with ExitStack() as esa:
                alloc_cmp_tiles(esa)
                KC2, hidT, hu, hw, kvb = T["KC2"], T["hidT"], T["hu"], T["hw"], T["kvb"]

                def a_stage2(i, hs):
                    pjA = project(hs, 0, 512, 1024)
                    pjB = project(hs, 512, 512, 1024)
                    A(lambda e, pjA=pjA: e.activation(out=kvb[:], in_=PJ[pjA][:], func=AF.Copy), r=[("PJ", pjA)], w=["kvb"])
                    head_norm_rope(pjB, 0, 4, gk[:, 1, :], cosT[:, i, :], sinT[:, i, :], nsinT[:, i, :], kb[:], "kb")
                    A(lambda e, pjB=pjB, i=i: e.activation(out=VA[:, i, :, 0:64], in_=PJ[pjB][:, 256:512].rearrange("p (a b) -> p a b", a=NG), func=AF.Copy),
                      r=[("PJ", pjB)], w=[("VA", i)])
                    pjs = nxt("PJ")
                    for kv in range(2):
                        for g in range(NG):
                            for par in range(2):
                                c0 = (kv * 4 + g) * 64
                                P(lambda e, kv=kv, g=g, par=par, c0=c0, pjs=pjs: e.matmul(
                                    PJ[pjs][par * 64:(par + 1) * 64, c0:c0 + 64], kvb[:, kv * 256 + g * 64:kv * 256 + (g + 1) * 64],
                                    selm[:, par, :], start=True, stop=True), r=["kvb", "selm"], w=[("PJ", pjs)])
                    V(lambda e, pjs=pjs, i=i: e.tensor_copy(out=KC2[:, :, :, i * 64:(i + 1) * 64],
                                                           in_=PJ[pjs][:].rearrange("p (a g u) -> p a g u", a=2, g=NG)),
                      r=[("PJ", pjs)], w=[("KC2", i)])
                    k_to_KT(KA[0:64, :, i * 128:(i + 1) * 128], ("KA", i))

                def cap(fn, pt):
                    pin_pt[0] = pt
                    sch.capture = []
                    r = fn()
                    ops_ = sch.capture
                    sch.capture = None
                    pin_pt[0] = None
                    return ops_, r

                def merge(o1, o2):
                    n1, n2 = len(o1), len(o2)
                    i1 = i2 = 0
                    while i1 < n1 or i2 < n2:
                        if i2 >= n2 or (i1 < n1 and i1 * n2 <= i2 * n1):
                            sch.replay(o1[i1])
                            i1 += 1
                        else:
                            sch.replay(o2[i2])
                            i2 += 1

                o1, (xs0, hs_prev) = cap(lambda: load_norm_h(b, 0, src_d[b, 0:128, :]), 0)
                merge(o1, [])
                for i in range(NT):
                    if i + 1 < NT:
                        o1, (xs1, hs_next) = cap(lambda i=i: load_norm_h(b, i + 1, src_d[b, (i + 1) * 128:(i + 2) * 128, :]), 0)
                    else:
                        o1, hs_next = [], None
                    o2, _ = cap(lambda i=i, hs_prev=hs_prev: a_stage2(i, hs_prev), 1)
                    merge(o1, o2)
                    hs_prev = hs_next
                if DBG_STAGE[0] < 3:
                    sch.barrier()
                    return
                kc2keys = [("KC2", i) for i in range(NT)]
                pjb = nxt("PJ")
                for kv in range(2):
                    for half in range(2):
                        col = kv * 2 + half
                        for m in range(16):
                            P(lambda e, kv=kv, half=half, m=m, col=col, pjb=pjb: e.matmul(
                                PJ[pjb][:, col:col + 1], CW1[:, kv, m, half * 128:(half + 1) * 128], peT2[:, kv, m:m + 1],
                                start=(m == 0), stop=(m == 15)), r=[("CW1", kv, m // 8), "peT2"], w=[("PJ", pjb)])
                V(lambda e, pjb=pjb: e.tensor_tensor(out=b1p[:].rearrange("p a b -> p (a b)"), in0=PJ[pjb][:, 0:4],
                                                    in1=b1c[:].rearrange("p a b -> p (a b)"), op=ALU.add), r=[("PJ", pjb), "b1c"], w=["b1p"])
                for kv in range(2):
                    for half in range(2):
                        ps_ = nxt("PJ")
                        for m in range(16):
                            P(lambda e, kv=kv, half=half, m=m, ps_=ps_: e.matmul(
                                PJ[ps_][:, 0:508], CW1[:, kv, m, half * 128:(half + 1) * 128], KC2[:, kv, :, m:m + 1009:8],
                                start=(m == 0), stop=(m == 15)), r=[("CW1", kv, m // 8)] + kc2keys, w=[("PJ", ps_)])
                        V(lambda e, kv=kv, half=half, ps_=ps_: e.tensor_scalar(out=hu[:], in0=PJ[ps_][:, 0:508], scalar1=b1p[:, kv, half:half + 1],
                                                                              scalar2=None, op0=ALU.add), r=[("PJ", ps_), "b1p"], w=["hu"])
                        V(lambda e: e.tensor_tensor(out=hw[:], in0=hu[:], in1=hu[:], op=ALU.mult), r=["hu"], w=["hw"])
                        V(lambda e: e.tensor_scalar(out=hw[:], in0=hw[:], scalar1=0.044715, scalar2=1.0, op0=ALU.mult, op1=ALU.add), r=["hw"], w=["hw"])
                        V(lambda e: e.tensor_tensor(out=hw[:], in0=hw[:], in1=hu[:], op=ALU.mult), r=["hw", "hu"], w=["hw"])
                        A(lambda e: e.activation(out=hw[:], in_=hw[:], func=AF.Exp, scale=-1.5957691216057308), r=["hw"], w=["hw"])
                        V(lambda e: e.tensor_scalar_add(hw[:], hw[:], 1.0), r=["hw"], w=["hw"])
                        V(lambda e: e.reciprocal(hw[:], hw[:]), r=["hw"], w=["hw"])
                        V(lambda e, kv=kv, half=half: e.tensor_tensor(out=hidT[:, kv * 2 + half, :, :].rearrange("p a b -> p (a b)"), in0=hu[:], in1=hw[:],
                                                                      op=ALU.mult), r=["hu", "hw"], w=["hidT"])
                pj2 = nxt("PJ")
                for kv in range(2):
                    for g in range(NG):
                        c0 = (kv * 4 + g) * 64
                        for half in range(2):
                            P(lambda e, kv=kv, g=g, half=half, c0=c0, pj2=pj2: e.matmul(
                                PJ[pj2][0:127, c0:c0 + 64], hidT[:, kv * 2 + half, g, :], cw2[:, kv, half, :],
                                start=(half == 0), stop=(half == 1)), r=["hidT", "cw2"], w=[("PJ", pj2)])
                head_norm_rope(pj2, 0, 4, gk[:, 0, :], cosC[:], sinC[:], nsinC[:], kb[:], "kb")
                A(lambda e, pj2=pj2: e.activation(out=vcmp[0:127, :, 0:64], in_=PJ[pj2][0:127, 256:512].rearrange("p (a b) -> p a b", a=NG), func=AF.Copy),
                  r=[("PJ", pj2)], w=["vcmp"])
                k_to_KT(kcmpT[0:64, :, 0:127], "kcmpT", 127)
                sch.barrier()
            if DBG_STAGE[0] < 4:
                return
            WS = 2608
            load_w_cols(0, lambda c, n: w_in[j, :, c:c + n].rearrange("(k p) n -> p k n", p=128), 1024, WS)
            load_w_cols(1024, lambda c, n: w_in[j, :, 2048 + c:2048 + c + n].rearrange("(k p) n -> p k n", p=128), 1584, WS)
            with ExitStack() as esc:
                alloc_attn_tiles(esc)
                gtmp, imp, m8, rs = T["gtmp"], T["imp"], T["m8"], T["rs"]
                ctx = {}

                def front(i):
                    sl = i % 2
                    qaug, qT1, szb, gates = T["qaug"][sl], T["qT1"][sl], T["szb"][sl], T["gates"][sl]
                    xs, hs = load_norm_h(b, i, src_d[b, i * 128:(i + 1) * 128, :])
                    ctx[i] = xs
                    yield
                    cs, sn, nsn = cosT[:, i, :], sinT[:, i, :], nsinT[:, i, :]
                    slot = i % 4
                    pq0 = project(hs, 0, 512, WS)
                    pq1 = project(hs, 512, 512, WS)
                    head_norm_rope(pq0, 0, 8, gq[:], cs, sn, nsn, qaug[:, 0:8, 0:64], ("qaug", sl, 0))
                    pkw = project(hs, 1024, 512, WS)
                    merge_streams(lambda: head_norm_rope(pq1, 0, 8, gq[:], cs, sn, nsn, qaug[:, 8:16, 0:64], ("qaug", sl, 1)),
                                  lambda: head_norm_rope(pkw, 0, 4, gk[:, 2, :], cs, sn, nsn, kb[:], "kb", alt=True))
                    transpose_heads(qaug, 64, qT1, [("qaug", sl, 0), ("qaug", sl, 1)], ("qT1", sl))
                    pgl = project(hs, 1536, 48, WS)
                    A(lambda e, pj=pkw, slot=slot: e.activation(out=VW[:, slot, :, 0:64], in_=PJ[pj][:, 256:512].rearrange("p (a b) -> p a b", a=NG), func=AF.Copy),
                      r=[("PJ", pkw)], w=[("VW", slot)])
                    k_to_KT(KW[0:64, :, slot, :], ("KW", slot))
                    V(lambda e, pj=pgl: e.tensor_tensor(out=gtmp[:], in0=PJ[pj][:, 0:48], in1=gateb[:], op=ALU.add), r=[("PJ", pgl), "gateb"], w=["gtmp"])
                    A(lambda e: e.activation(out=gtmp[:], in_=gtmp[:], func=AF.Exp, scale=-1.0), r=["gtmp"], w=["gtmp"])
                    A(lambda e: e.activation(out=gtmp[:], in_=gtmp[:], func=AF.Ln, bias=onef[:, 0:1]), r=["gtmp", "onef"], w=["gtmp"])
                    A(lambda e: e.activation(out=gates[:].rearrange("p a b -> p (a b)"), in_=gtmp[:], func=AF.Exp, scale=-1.0), r=["gtmp"], w=[("gates", sl)])
                    yield "crit"
                    pz0 = project(hs, 1584, 512, WS)
                    pz1 = project(hs, 2096, 512, WS)
                    silu_from_psum(pz0, 512, szb[:, 0:512], 0, ("szb", sl))
                    silu_from_psum(pz1, 512, szb[:, 512:1024], 512, ("szb", sl))
                    yield

                at = Attn()

                def cmp_post(po, g, i):
                    sl = i % 2
                    qaug = T["qaug"][sl]
                    branch_post(po, 97, g, 0, True, clamp=True, gates=T["gates"][sl], sl=sl)
                    pov = PO[po][:, 0:388].rearrange("p (a b) -> p a b", a=4)
                    V(lambda e: e.tensor_scalar(out=imp[:, g, :], in0=pov[:, 0, 65:97], scalar1=rs[:, 0:1], scalar2=None, op0=ALU.mult),
                      r=[("PO", po), "rs"], w=[("imp", g)])
                    for r_ in range(1, 4):
                        V(lambda e, r_=r_: e.scalar_tensor_tensor(out=imp[:, g, :], in0=pov[:, r_, 65:97], scalar=rs[:, r_:r_ + 1], in1=imp[:, g, :],
                                                                  op0=ALU.mult, op1=ALU.add), r=[("PO", po), "rs", ("imp", g)], w=[("imp", g)])
                    V(lambda e: e.tensor_tensor(out=imp[:, g, :], in0=imp[:, g, :], in1=fb[:, i, :], op=ALU.add), r=[("imp", g), "fb"], w=[("imp", g)])
                    V(lambda e: e.max(out=m8[:, g, :], in_=imp[:, g, :]), r=[("imp", g)], w=[("m8", g)])
                    V(lambda e: e.tensor_tensor(out=qaug[:, 4 * g:4 * g + 4, 64:96], in0=imp[:, g, :].unsqueeze(1).to_broadcast([128, 4, 32]),
                                                in1=m8[:, g, 7:8].unsqueeze(1).to_broadcast([128, 4, 32]), op=ALU.is_lt),
                      r=[("imp", g), ("m8", g)], w=[("qaug", sl, 2)])

                def cmp_unit(i, g):
                    sl = i % 2
                    qT1 = T["qT1"][sl]

                    def u():
                        po = nxt("PO")
                        at.unit(kcmpT[:, g, :], qT1[:, 4 * g:4 * g + 4, :], cmask[:, i * 128:(i + 1) * 128].unsqueeze(1).to_broadcast([128, 4, 128]),
                                vcmp[:, g, :], 97, po, True, ["kcmpT", ("qT1", sl), "vcmp"], (lambda: cmp_post(po, g, i)))
                    return u

                def back_units(i):
                    sl = i % 2
                    qaug, qT1, qaT, gates = T["qaug"][sl], T["qT1"][sl], T["qaT"][sl], T["gates"][sl]
                    units = []
                    tiles = [jj for jj in (i - 2, i - 1, i) if jj >= 0]
                    for g in range(NG):
                        po_box = []
                        for idx, jj in enumerate(tiles):
                            def u(g=g, jj=jj, idx=idx, po_box=po_box):
                                if idx == 0:
                                    po_box.append(nxt("PO"))
                                po = po_box[0]
                                if jj == i:
                                    bias = cb[:].unsqueeze(1).to_broadcast([128, 4, 128])
                                elif jj == i - 2:
                                    bias = cb2[:].unsqueeze(1).to_broadcast([128, 4, 128])
                                else:
                                    bias = None
                                done = (lambda: branch_post(po, 65, g, 2, False, gates=gates, sl=sl)) if jj == i else None
                                at.unit(KW[:, g, jj % 4, :], qT1[:, 4 * g:4 * g + 4, :], bias, VW[:, jj % 4, g, :], 65, po, idx == 0,
                                        [("KW", jj % 4), ("qT1", sl), ("VW", jj % 4), "VW1"], done)
                            units.append(u)

                    def tr():
                        at.flush()
                        transpose_heads(qaug, 96, qaT, [("qaug", sl, 0), ("qaug", sl, 1), ("qaug", sl, 2)], ("qaT", sl))
                    units.append(tr)
                    for g in range(NG):
                        po_box = []
                        for jj in range(i + 1):
                            def u(g=g, jj=jj, po_box=po_box):
                                if jj == 0:
                                    po_box.append(nxt("PO"))
                                po = po_box[0]
                                bias = cb[:].unsqueeze(1).to_broadcast([128, 4, 128]) if jj == i else None
                                done = (lambda: branch_post(po, 65, g, 1, False, gates=gates, sl=sl)) if jj == i else None
                                at.unit(KA[:, g, jj * 128:(jj + 1) * 128], qaT[:, 4 * g:4 * g + 4, :], bias, VA[:, jj, g, :], 65, po,
                                        jj == 0, [("KA", jj), "KAind", ("qaT", sl), ("VA", jj), "VA1"], done)
                            units.append(u)
                        if i + 1 < NT:
                            if g == 0:
                                units.append("need_front")
                            units.append(cmp_unit(i + 1, g))
                    return units

                def tail(i):
                    return out_proj_store(b, i, ctx[i], i % 2)

                def prologue():
                    for g in range(NG):
                        cmp_unit(0, g)()

                at_box[0] = at
                run_pipelined(front, back_units, tail, NT, prologue)
                sch.barrier()

        for b in range(n_seq):
            with ExitStack() as es3:
                E3 = es3.enter_context
                posi = E3(nc.sbuf_tensor("posi%d" % b, [128, NT + 1], I32))
                posf = E3(nc.sbuf_tensor("posf%d" % b, [128, NT + 1], F32))
                ang = E3(nc.sbuf_tensor("ang%d" % b, [128, NT + 1, 8], F32))
                kq = E3(nc.sbuf_tensor("kq%d" % b, [128, NT + 1, 8], F32))
                ki = E3(nc.sbuf_tensor("ki%d" % b, [128, NT + 1, 8], I32))
                red = E3(nc.sbuf_tensor("red%d" % b, [128, NT + 1, 8], F32))
                V(lambda e: e.memset(posi[:], 0), w=["posi"])
                sch.dma("sp", posi[:, 0:NT], dr["positions"][b].rearrange("(i p) -> p i", p=128), writes=["posi"],
                        allow_slow_non_contiguous=True)
                sch.dma("sp", posi[0:127, NT:NT + 1], dr["positions"][b, 16:16 + 16 * 127].rearrange("(n s) -> n s", s=16)[:, 15:16],
                        writes=["posi"], allow_slow_non_contiguous=True)
                V(lambda e: e.tensor_copy(out=posf[:], in_=posi[:]), r=["posi"], w=["posf"])
                V(lambda e: e.tensor_tensor(out=ang[:], in0=posf[:].unsqueeze(2).to_broadcast([128, NT + 1, 8]),
                                            in1=invf[:].unsqueeze(1).to_broadcast([128, NT + 1, 8]), op=ALU.mult),
                  r=["posf", "invf"], w=["ang"])
                C1 = 6.28125
                C2 = float(2.0 * np.pi - 6.28125)
                PI_LO = 3.1415925
                for which, off in (("sin", 0.0), ("cos", 0.25)):
                    V(lambda e, off=off: e.tensor_scalar(out=kq[:], in0=ang[:], scalar1=float(1.0 / (2.0 * np.pi)), scalar2=off,
                                                         op0=ALU.mult, op1=ALU.add), r=["ang"], w=["kq"])
                    V(lambda e: e.tensor_copy(out=ki[:], in_=kq[:]), r=["kq"], w=["ki"])
                    V(lambda e: e.tensor_copy(out=kq[:], in_=ki[:]), r=["ki"], w=["kq"])
                    V(lambda e: e.scalar_tensor_tensor(out=red[:], in0=kq[:], scalar=-C1, in1=ang[:], op0=ALU.mult, op1=ALU.add),
                      r=["kq", "ang"], w=["red"])
                    V(lambda e: e.scalar_tensor_tensor(out=red[:], in0=kq[:], scalar=-C2, in1=red[:], op0=ALU.mult, op1=ALU.add),
                      r=["kq", "red"], w=["red"])
                    if which == "cos":
                        V(lambda e: e.tensor_scalar(out=red[:], in0=red[:], scalar1=float(np.pi / 2), scalar2=PI_LO,
                                                    op0=ALU.add, op1=ALU.min), r=["red"], w=["red"])
                        V(lambda e: e.tensor_scalar(out=red[:], in0=red[:], scalar1=-PI_LO, scalar2=None, op0=ALU.max), r=["red"], w=["red"])
                        A(lambda e: e.activation(out=cosT[:], in_=red[:, 0:NT, :], func=AF.Sin), r=["red"], w=["rope"])
                        A(lambda e: e.activation(out=cosC[:], in_=red[:, NT, :], func=AF.Sin), r=["red"], w=["ropeC"])
                    else:
                        V(lambda e: e.tensor_scalar(out=red[:], in0=red[:], scalar1=PI_LO, scalar2=-PI_LO,
                                                    op0=ALU.min, op1=ALU.max), r=["red"], w=["red"])
                        A(lambda e: e.activation(out=sinT[:], in_=red[:, 0:NT, :], func=AF.Sin), r=["red"], w=["rope"])
                        A(lambda e: e.activation(out=sinC[:], in_=red[:, NT, :], func=AF.Sin), r=["red"], w=["ropeC"])
                V(lambda e: e.tensor_scalar(out=nsinT[:], in0=sinT[:], scalar1=-1.0, scalar2=None, op0=ALU.mult), r=["rope"], w=["rope"])
                V(lambda e: e.tensor_scalar(out=nsinC[:], in0=sinC[:], scalar1=-1.0, scalar2=None, op0=ALU.mult), r=["ropeC"], w=["ropeC"])
                sch.barrier()

            for li, L in enumerate(layers):
                src_d = dr["x"] if (li == 0 and first_from_x) else y_d
                is_nsa = (L % 2 == 0)
                j = L // 2
                for m in range(3):
                    sch.dma("sp", modbc[:, m, :], mod_d[b, L:L + 1, m * D:(m + 1) * D].to_broadcast([128, D]),
                            reads=[("mod", b, L)], writes=[("modbc", m)])
                wo_src = dr["nsa_w_out"][j] if is_nsa else dr["moba_w_out"][j]
                for hlf in range(2):
                    sch.dma("pool", WO[:, :, hlf * 512:(hlf + 1) * 512],
                            wo_src[:, hlf * 512:(hlf + 1) * 512].rearrange("(k p) n -> p k n", p=128), writes=[("WO", hlf)])
                if is_nsa:
                    nsa_layer(b, j, src_d)
                else:
                    moba_layer(b, j, src_d)
        sch.finish()
    return nc


_CONSTS = None


def _consts():
    global _CONSTS
    if _CONSTS is None:
        _CONSTS = host_consts()
    return _CONSTS


def run_layers(inputs, layers, n_seq, n_cores, first_from_x=True, trace=False):
    nc = build(layers, n_seq, first_from_x)
    cst = _consts()
    in_maps = []
    for c in range(n_cores):
        m = {}
        m["x"] = np.ascontiguousarray(inputs["x"][c * n_seq:(c + 1) * n_seq], dtype=np.float32)
        m["c"] = np.ascontiguousarray(inputs["c"][c * n_seq:(c + 1) * n_seq], dtype=np.float32)
        m["positions"] = np.ascontiguousarray(inputs["positions"][c * n_seq:(c + 1) * n_seq], dtype=np.int32)
        for k in WEIGHT_SHAPES:
            m[k] = np.ascontiguousarray(inputs[k], dtype=np.float32)
        for k in CONST_SHAPES:
            m[k] = cst[k]
        in_maps.append(m)
    res = run_bass_kernel_spmd(nc, in_maps, core_ids=list(range(n_cores)), trace=trace)
    out = np.concatenate([np.asarray(r["y"]) for r in res.results], axis=0)
    return out, res


def kernel(**inputs):
    out, _ = run_layers(inputs, [0, 1, 2, 3], 2, N_CORES)
    return out.astype(np.float32)
```

```python
import numpy as np
from contextlib import ExitStack
import concourse.bass as bass
import concourse.mybir as mybir
from concourse.bass_utils import run_bass_kernel_spmd

F32 = mybir.dt.float32
BF16 = mybir.dt.bfloat16
I32 = mybir.dt.int32
AF = mybir.ActivationFunctionType
ALU = mybir.AluOpType
AX = mybir.AxisListType

S = 2048
D = 1024
NT = 16
HD = 64
NH = 16
NG = 4
NEG = -30000.0
EPS = 1e-6
N_CORES = 8
NSA_IN = 3632
MOBA_IN = 2560
NDMA = 24
FRESH_POOL_SEMS = [False]
POOL_DESC_LIMIT = 4608


class Sched:
    def __init__(self, nc, es):
        self.nc = nc
        self.es = es
        self.nfresh = 0
        self.capture = None
        self.pool_fifo = []
        self.pool_desc = 0
        self.eng = {"pe": nc.tensor, "act": nc.scalar, "dve": nc.vector, "pool": nc.gpsimd, "sp": nc.sync}
        self.semh = {}
        for k in self.eng:
            self.semh[("e", k)] = es.enter_context(nc.semaphore("sem_" + k))
        for i in range(NDMA):
            self.semh[("d", i)] = es.enter_context(nc.semaphore("dsem%d" % i))
        self.cnt = {k: 0 for k in self.eng}
        self.dcnt = [0] * NDMA
        self.drr = 0
        self.seen = {k: {} for k in self.eng}
        self.lastw = {}
        self.readers = {}

    def _deps(self, eng, reads, writes, strict=False):
        deps = {}

        def add(t, same_ok):
            sk, v = t
            if same_ok and not strict and sk == ("e", eng) and eng == "pe":
                return
            if deps.get(sk, 0) < v:
                deps[sk] = v

        for k in reads:
            if k in self.lastw:
                add(self.lastw[k], False)
        for k in writes:
            if k in self.lastw:
                add(self.lastw[k], True)
            for sk, v in self.readers.get(k, {}).items():
                add((sk, v), True)
        return deps

    def _wait(self, eng, deps):
        for sk, v in deps.items():
            if eng == "pe" and sk == ("e", "pe"):
                continue
            if self.seen[eng].get(sk, 0) < v:
                self.eng[eng].wait_ge(self.semh[sk], v)
                self.seen[eng][sk] = v

    def _record(self, tok, reads, writes):
        for k in writes:
            self.lastw[k] = tok
            self.readers[k] = {}
        for k in reads:
            d = self.readers.setdefault(k, {})
            if d.get(tok[0], 0) < tok[1]:
                d[tok[0]] = tok[1]

    def op(self, eng, fn, reads=(), writes=()):
        if self.capture is not None:
            self.capture.append(("op", eng, fn, tuple(reads), tuple(writes), None))
            return
        self._wait(eng, self._deps(eng, reads, writes))
        inst = fn(self.eng[eng])
        self.cnt[eng] += 1
        inst.then_inc(self.semh[("e", eng)], 1)
        self._record((("e", eng), self.cnt[eng]), reads, writes)

    def replay(self, item):
        kind, a, b, reads, writes, kw = item
        if kind == "op":
            self.op(a, b, reads, writes)
        else:
            self.dma(a, b[0], b[1], reads, writes, **kw)

    def dma(self, queue, out, in_, reads=(), writes=(), **kw):
        if self.capture is not None:
            self.capture.append(("dma", queue, (out, in_), tuple(reads), tuple(writes), dict(kw)))
            return
        if FRESH_POOL_SEMS[0] and queue == "pool":
            kw.pop("ndesc", None)
            idx = NDMA + self.nfresh
            self.nfresh += 1
            self.semh[("d", idx)] = self.es.enter_context(self.nc.semaphore("fsem%d" % idx))
            self.dcnt.append(0)
            self._wait(queue, self._deps(queue, reads, writes, strict=True))
            inst = self.eng[queue].dma_start(out=out, in_=in_, **kw)
            self.dcnt[idx] += 16
            inst.then_inc(self.semh[("d", idx)], 16)
            self._record((("d", idx), self.dcnt[idx]), reads, writes)
            return
        ndesc = kw.pop("ndesc", 1024)
        idx = self.drr
        self.drr = (idx + 1) % NDMA
        if queue == "pool":
            while self.pool_fifo and self.pool_desc + ndesc > POOL_DESC_LIMIT:
                sk, v, nd = self.pool_fifo.pop(0)
                self.pool_desc -= nd
                if self.seen["pool"].get(sk, 0) < v:
                    self.eng["pool"].wait_ge(self.semh[sk], v)
                    self.seen["pool"][sk] = v
        deps = self._deps(queue, reads, writes, strict=True)
        if self.dcnt[idx]:
            deps[("d", idx)] = max(deps.get(("d", idx), 0), self.dcnt[idx])
        self._wait(queue, deps)
        inst = self.eng[queue].dma_start(out=out, in_=in_, **kw)
        self.dcnt[idx] += 16
        inst.then_inc(self.semh[("d", idx)], 16)
        self._record((("d", idx), self.dcnt[idx]), reads, writes)
        if queue == "pool":
            self.pool_fifo.append((("d", idx), self.dcnt[idx], ndesc))
            self.pool_desc += ndesc

    def barrier(self):
        for k in self.eng:
            for k2 in self.eng:
                if k2 != k and self.cnt[k2] > self.seen[k].get(("e", k2), 0):
                    self.eng[k].wait_ge(self.semh[("e", k2)], self.cnt[k2])
                    self.seen[k][("e", k2)] = self.cnt[k2]
            for i in range(len(self.dcnt)):
                if self.dcnt[i] > self.seen[k].get(("d", i), 0):
                    self.eng[k].wait_ge(self.semh[("d", i)], self.dcnt[i])
                    self.seen[k][("d", i)] = self.dcnt[i]

    def finish(self):
        e = self.eng["sp"]
        for i in range(len(self.dcnt)):
            if self.dcnt[i] and self.seen["sp"].get(("d", i), 0) < self.dcnt[i]:
                e.wait_ge(self.semh[("d", i)], self.dcnt[i])
        for k in self.eng:
            if k != "sp" and self.cnt[k] and self.seen["sp"].get(("e", k), 0) < self.cnt[k]:
                e.wait_ge(self.semh[("e", k)], self.cnt[k])


def host_consts():
    p = np.arange(128)[:, None]
    f = np.arange(128)[None, :]
    c = {}
    c["c_ident"] = (p == f).astype(np.float32)
    c["c_cb"] = np.where(p > f, NEG, 0.0).astype(np.float32)
    c["c_cb2"] = np.where(p <= f, NEG, 0.0).astype(np.float32)
    t = np.arange(S)[None, :]
    cm = np.where((16 * p + 31 > t) | (p == 127), NEG, 0.0)
    c["c_cmask"] = cm.astype(np.float32)
    fb = np.zeros((128, NT, 32), np.float32)
    for i in range(NT):
        cur = (128 * i + np.arange(128)) // 64
        for j in range(32):
            fb[:, i, j] = np.where((j == 0) | (j == cur) | (j == cur - 1), 100.0, 0.0)
    c["c_fb"] = fb.reshape(128, NT * 32)
    fbm = np.zeros((128, NT, 8), np.float32)
    for i in range(NT):
        own = i // 2
        for j in range(8):
            fbm[:, i, j] = 0.0 if j < own else (1e4 if j == own else -1e4)
    c["c_fbm"] = fbm.reshape(128, NT * 8)
    key = np.arange(S)[None, :]
    c["c_ind_nsa"] = np.where(key // 64 == np.arange(32)[:, None], NEG, 0.0).astype(np.float32)
    c["c_ind_moba"] = np.where(key // 256 == np.arange(8)[:, None], NEG, 0.0).astype(np.float32)
    cs = np.arange(127)[:, None] * 16
    ss = np.arange(32)[None, :] * 64
    shared = np.clip(np.minimum(cs + 32, ss + 64) - np.maximum(cs, ss), 0, None)
    sw = np.zeros((128, 32), np.float32)
    sw[:127] = shared / 32.0
    c["c_selw"] = sw
    sel = np.zeros((128, 2, 64), np.float32)
    for par in range(2):
        for u in range(64):
            sel[2 * u + par, par, u] = 1.0
    c["c_sel"] = sel.reshape(128, 128)
    inv = (np.float32(500000.0) ** (-np.arange(0, 16, 2, dtype=np.float32) / np.float32(16))).astype(np.float32)
    c["c_invf"] = np.tile(inv[None, :], (128, 1)).astype(np.float32)
    return c


CONST_SHAPES = {
    "c_ident": [128, 128], "c_cb": [128, 128], "c_cb2": [128, 128], "c_cmask": [128, S],
    "c_fb": [128, NT * 32], "c_fbm": [128, NT * 8], "c_ind_nsa": [32, S], "c_ind_moba": [8, S],
    "c_selw": [128, 32], "c_sel": [128, 128], "c_invf": [128, 8],
}

WEIGHT_SHAPES = {
    "norm_g": [4, D], "ada_w": [4, D, 3 * D], "ada_b": [4, 3 * D],
    "nsa_w_in": [2, D, NSA_IN], "nsa_w_out": [2, D, D], "nsa_q_norm": [2, 64], "nsa_k_norm": [2, 3, 64],
    "nsa_cmp_pe": [2, 2, 32, 64], "nsa_cmp_w1": [2, 2, 2048, 256], "nsa_cmp_b1": [2, 2, 256],
    "nsa_cmp_w2": [2, 2, 256, 64], "nsa_gate_b": [2, 48],
    "moba_w_in": [2, D, MOBA_IN], "moba_w_out": [2, D, D], "moba_q_norm": [2, 64], "moba_k_norm": [2, 64],
}


DBG_STAGE = [9]


def build(layers, n_seq, first_from_x=True):
    nc = bass.Bass("TRN2", target_bir_lowering=False, dynamic_dma_scratch_size=8192)
    dr = {}
    dr["x"] = nc.dram_tensor("x", [n_seq, S, D], F32, kind="ExternalInput").ap()
    dr["c"] = nc.dram_tensor("c", [n_seq, D], F32, kind="ExternalInput").ap()
    dr["positions"] = nc.dram_tensor("positions", [n_seq, S], I32, kind="ExternalInput").ap()
    for k, shp in WEIGHT_SHAPES.items():
        dr[k] = nc.dram_tensor(k, shp, F32, kind="ExternalInput").ap()
    for k, shp in CONST_SHAPES.items():
        dr[k] = nc.dram_tensor(k, shp, F32, kind="ExternalInput").ap()
    y_d = nc.dram_tensor("y", [n_seq, S, D], F32, kind="ExternalOutput").ap()
    mod_d = nc.dram_tensor("mod_scr", [n_seq, 4, 3 * D], F32, kind="Internal").ap()

    with ExitStack() as es:
        E = es.enter_context
        sch = Sched(nc, es)

        def V(fn, r=(), w=()):
            sch.op("dve", fn, r, w)

        def A(fn, r=(), w=()):
            sch.op("act", fn, r, w)

        def G(fn, r=(), w=()):
            sch.op("pool", fn, r, w)

        def P(fn, r=(), w=()):
            sch.op("pe", fn, r, w)

        def sb(name, shape, dt):
            return E(nc.sbuf_tensor(name, shape, dt))

        T = {}

        def sb(name, shape, dt, stack=None):
            t = (stack or es).enter_context(nc.sbuf_tensor(name, shape, dt))
            T[name] = t
            return t

        ident = sb("ident", [128, 128], BF16)
        cb = sb("cb", [128, 128], BF16)
        cb2 = sb("cb2", [128, 128], BF16)
        cmask = sb("cmask", [128, S], BF16)
        fb = sb("fb", [128, NT, 32], F32)
        fbm = sb("fbm", [128, NT, 8], F32)
        selm = sb("selm", [128, 2, 64], BF16)
        invf = sb("invf", [128, 8], F32)
        onesc = sb("onesc", [128, 1], BF16)
        W1 = sb("W1", [128, 20864], BF16)
        WO = sb("WO", [128, 8, D], BF16)
        KA = sb("KA", [128, NG, S], BF16)
        VA = sb("VA", [128, NT, NG, 65], BF16)
        KW = sb("KW", [128, NG, 4, 128], BF16)
        VW = sb("VW", [128, 4, NG, 65], BF16)
        onef = sb("onef", [128, 1], F32)
        epsf = sb("epsf", [128, 1], F32)
        kcmpT = sb("kcmpT", [128, NG, 128], BF16)
        vcmp = sb("vcmp", [128, NG, 97], BF16)
        modbc = sb("modbc", [128, 3, D], F32)
        cosT = sb("cosT", [128, NT, 8], F32)
        sinT = sb("sinT", [128, NT, 8], F32)
        nsinT = sb("nsinT", [128, NT, 8], F32)
        cosC = sb("cosC", [128, 8], F32)
        sinC = sb("sinC", [128, 8], F32)
        nsinC = sb("nsinC", [128, 8], F32)
        gq = sb("gq", [128, 64], F32)
        gk = sb("gk", [128, 3, 64], F32)
        gateb = sb("gateb", [128, 48], F32)
        kmT = sb("kmT", [64, NG, 8], BF16)
        kmtmp = sb("kmtmp", [64, NG], F32)
        ksum = sb("ksum", [64, NT, NG], F32)
        cw2 = sb("cw2", [128, 2, 2, 64], BF16)
        peT2 = sb("peT2", [128, 2, 16], BF16)
        peT2f = sb("peT2f", [128, 2, 16], F32)
        b1c = sb("b1c", [128, 2, 2], F32)
        b1p = sb("b1p", [128, 2, 2], F32)
        xt = [sb("xt%d" % i, [128, D], F32) for i in range(2)]
        st = sb("st", [128, 8], F32)
        hb = sb("hb", [128, D], BF16)
        hT = [sb("hT%d" % i, [128, 8, 128], BF16) for i in range(2)]
        sq = sb("sq", [128, 512], F32)
        ssq = sb("ssq", [128, 16], F32)
        rq = sb("rq", [128, 16], F32)
        qn = sb("qn", [128, 8, 64], F32)
        rtmp = sb("rtmp", [128, 8, 16], F32)
        rtmp2 = sb("rtmp2", [128, 8, 16], F32)
        kb = sb("kb", [128, NG, 64], BF16)
        ezs = sb("ezs", [128, D], F32)
        PJ = [E(nc.psum_tensor("PJ%d" % i, [128, 512], F32)) for i in range(2)]
        PS = [E(nc.psum_tensor("PS%d" % i, [128, 512], F32)) for i in range(2)]
        PO = [E(nc.psum_tensor("PO%d" % i, [128, 512], F32)) for i in range(2)]
        PT = [E(nc.psum_tensor("PT%d" % i, [128, 1024], BF16)) for i in range(2)]
        rr = {"PJ": 0, "PS": 0, "PO": 0, "PT": 0, "pT": 0, "xt": 0, "hT": 0}
        uid = [0]

        fixed_bank = [None]
        pin_pt = [None]

        def nxt(k, n=2):
            if pin_pt[0] is not None and k == "PT":
                return pin_pt[0]
            if fixed_bank[0] is not None and k == "PT":
                return fixed_bank[0]
            if fixed_bank[0] == 1 and k == "PJ":
                raise RuntimeError("atomic-mode code must not allocate PJ banks (owned by the deferred front stage)")
            v = rr[k]
            rr[k] = (v + 1) % n
            return v

        def alloc_attn_tiles(stack):
            uid[0] += 1
            u = "_%d" % uid[0]
            for nm, shp, dt in [("qaug", [128, NH, 96], BF16), ("qT1", [128, NH, 128], BF16), ("qaT", [128, NH, 128], BF16),
                                ("szb", [128, D], BF16), ("gates", [128, NH, 3], F32), ("oacc", [128, NH, 64], F32)]:
                T[nm] = [stack.enter_context(nc.sbuf_tensor("%s%s_%d" % (nm, u, k), shp, dt)) for k in range(2)]
                if nm in ("qT1", "qaT"):
                    for k in range(2):
                        G(lambda e, t=T[nm][k]: e.memset(t[:], 0.0), w=[(nm, k)])
            for nm, shp, dt in [("gtmp", [128, 48], F32),
                                ("sg", [128, NH, 8], F32), ("m8", [128, NH, 8], F32), ("imp", [128, NG, 32], F32),
                                ("rs", [128, 4], F32), ("cc", [128, 4], F32), ("otmp", [128, 4, 64], F32),
                                ("og", [128, D], BF16), ("ogT", [128, 8, 128], BF16),
                                ("ytmp", [128, 512], F32), ("pT0", [128, 512], BF16), ("pT1", [128, 512], BF16),
                                ("pT2", [128, 512], BF16)]:
                T[nm] = stack.enter_context(nc.sbuf_tensor(nm + u, shp, dt))

        def alloc_cmp_tiles(stack):
            uid[0] += 1
            u = "_%d" % uid[0]
            for nm, shp, dt in [("KC2", [128, 2, NG, 1024], BF16), ("hidT", [128, 4, NG, 127], BF16),
                                ("hu", [128, 508], F32), ("hw", [128, 508], F32), ("kvb", [128, 512], BF16)]:
                T[nm] = stack.enter_context(nc.sbuf_tensor(nm + u, shp, dt))

        def cload(tile_ap, src, key, queue="pool"):
            sch.dma(queue, tile_ap, src, writes=[key])

        cload(ident[:], dr["c_ident"], "ident")
        cload(cb[:], dr["c_cb"], "cb")
        cload(cb2[:], dr["c_cb2"], "cb2")
        for h in range(2):
            cload(cmask[:, h * 1024:(h + 1) * 1024], dr["c_cmask"][:, h * 1024:(h + 1) * 1024], "cmask")
        cload(fb[:].rearrange("p a b -> p (a b)"), dr["c_fb"], "fb", "sp")
        cload(fbm[:].rearrange("p a b -> p (a b)"), dr["c_fbm"], "fbm", "sp")
        cload(selm[:].rearrange("p a b -> p (a b)"), dr["c_sel"], "selm")
        cload(invf[:], dr["c_invf"], "invf", "sp")
        V(lambda e: e.memset(onesc[:], 1.0), w=["onesc"])
        V(lambda e: e.memset(onef[:], 1.0), w=["onef"])
        V(lambda e: e.memset(epsf[:], EPS), w=["epsf"])
        V(lambda e: e.memset(VA[:, :, :, 64:65], 1.0), w=["VA1"])
        V(lambda e: e.memset(VW[:, :, :, 64:65], 1.0), w=["VW1"])
        V(lambda e: e.memset(kmT[:], 0.0), w=["kmT"])
        V(lambda e: e.memset(ksum[:], 0.0), w=["ksum"])
        V(lambda e: e.memset(kcmpT[:], 0.0), w=["kcmpT"])
        G(lambda e: e.memset(KA[64:128, :, :], 0.0), w=["KAind"])
        G(lambda e: e.memset(KW[:], 0.0), w=[("KW", 0), ("KW", 1), ("KW", 2), ("KW", 3)])
        V(lambda e: e.memset(vcmp[:], 0.0), w=["vcmp"])
        V(lambda e: e.memset(vcmp[:, :, 64:65], 1.0), w=["vcmp"])
        for g in range(NG):
            cload(vcmp[:, g, 65:97], dr["c_selw"], "vcmp")

        with ExitStack() as es2:
            E2 = es2.enter_context
            cT = E2(nc.sbuf_tensor("cT", [128, 8, 2], F32))
            cTe = E2(nc.sbuf_tensor("cTe", [128, 8, 2], F32))
            cTb = E2(nc.sbuf_tensor("cTb", [128, 8, 2], BF16))
            adaw = [E2(nc.sbuf_tensor("adaw%d" % i, [128, 8, 512], BF16)) for i in range(3)]
            adab = E2(nc.sbuf_tensor("adab", [2, 512], F32))
            grow = E2(nc.sbuf_tensor("grow", [2, D], F32))
            modr = [E2(nc.sbuf_tensor("modr%d" % i, [2, 512], F32)) for i in range(2)]
            V(lambda e: e.memset(cT[:], 0.0), w=["cT"])
            for b in range(n_seq):
                sch.dma("sp", cT[:, :, b:b + 1], dr["c"][b].rearrange("(k p o) -> p k o", p=128, o=1),
                        writes=["cT"], allow_slow_non_contiguous=True)
            A(lambda e: e.activation(out=cTe[:], in_=cT[:], func=AF.Exp, scale=-1.0), r=["cT"], w=["cTe"])
            V(lambda e: e.tensor_scalar_add(cTe[:], cTe[:], 1.0), r=["cTe"], w=["cTe"])
            V(lambda e: e.reciprocal(cTe[:], cTe[:]), r=["cTe"], w=["cTe"])
            V(lambda e: e.tensor_tensor(out=cTb[:], in0=cT[:], in1=cTe[:], op=ALU.mult), r=["cT", "cTe"], w=["cTb"])
            ci = 0
            for L in layers:
                for b in range(2):
                    sch.dma("sp", grow[b:b + 1, :], dr["norm_g"][L:L + 1, :], writes=["grow"])
                for ch in range(6):
                    slot = ci % 3
                    ms = ci % 2
                    ci += 1
                    sch.dma("pool", adaw[slot][:], dr["ada_w"][L, :, ch * 512:(ch + 1) * 512].rearrange("(k p) n -> p k n", p=128),
                            writes=[("adaw", slot)])
                    for b in range(2):
                        sch.dma("sp", adab[b:b + 1, :], dr["ada_b"][L:L + 1, ch * 512:(ch + 1) * 512], writes=["adab"])
                    pj = nxt("PJ")
                    for k in range(8):
                        P(lambda e, k=k, pj=pj, slot=slot: e.matmul(PJ[pj][0:2, :], cTb[:, k, :], adaw[slot][:, k, :],
                                                                     start=(k == 0), stop=(k == 7)),
                          r=["cTb", ("adaw", slot)], w=[("PJ", pj)])
                    V(lambda e, pj=pj, ms=ms: e.tensor_tensor(out=modr[ms][:], in0=PJ[pj][0:2, :], in1=adab[:], op=ALU.add),
                      r=[("PJ", pj), "adab"], w=[("modr", ms)])
                    if ch in (2, 3):
                        g0 = (ch - 2) * 512
                        V(lambda e, ms=ms, g0=g0: e.scalar_tensor_tensor(out=modr[ms][:], in0=modr[ms][:], scalar=1.0,
                                                                         in1=grow[:, g0:g0 + 512], op0=ALU.add, op1=ALU.mult),
                          r=[("modr", ms), "grow"], w=[("modr", ms)])
                        dst0 = g0
                    elif ch in (0, 1):
                        dst0 = D + ch * 512
                    else:
                        dst0 = 2 * D + (ch - 4) * 512
                    for b in range(n_seq):
                        sch.dma("sp", mod_d[b, L:L + 1, dst0:dst0 + 512], modr[ms][b:b + 1, :],
                                reads=[("modr", ms)], writes=[("mod", b, L)])
            sch.barrier()

        def rstd_from_ss(ss_ap, out_ap, n, keys_r, key_w):
            A(lambda e: e.activation(out=out_ap, in_=ss_ap, func=AF.Ln, scale=1.0 / n, bias=epsf[:, 0:1]), r=list(keys_r) + ["epsf"], w=[key_w])
            A(lambda e: e.activation(out=out_ap, in_=out_ap, func=AF.Exp, scale=-0.5), r=[key_w], w=[key_w])

        def load_norm_h(b, i, src_ap):
            xs = nxt("xt")
            sch.dma("sp", xt[xs][:], src_ap, reads=[("res", b, i)], writes=[("xt", xs)])
            A(lambda e: e.activation(out=hb[:], in_=xt[xs][:], func=AF.Square, accum_out=st[:, 0:1]),
              r=[("xt", xs)], w=["hb", "st0"])
            rstd_from_ss(st[:, 0:1], st[:, 1:2], D, ["st0"], "st1")
            V(lambda e: e.scalar_tensor_tensor(out=ezs[:], in0=xt[xs][:], scalar=st[:, 1:2], in1=modbc[:, 0, :],
                                               op0=ALU.mult, op1=ALU.mult), r=[("xt", xs), "st1", ("modbc", 0)], w=[("ezs", 0), ("ezs", 512)])
            V(lambda e: e.tensor_tensor(out=hb[:], in0=ezs[:], in1=modbc[:, 1, :], op=ALU.add), r=[("ezs", 0), ("ezs", 512), ("modbc", 1)], w=["hb"])
            pt = nxt("PT")
            for k in range(8):
                P(lambda e, k=k, pt=pt: e.transpose(PT[pt][:, k * 128:(k + 1) * 128], hb[:, k * 128:(k + 1) * 128], ident[:]),
                  r=["hb", "ident"], w=[("PT", pt)])
            hs = nxt("hT")
            V(lambda e: e.tensor_copy(out=hT[hs][:].rearrange("p a b -> p (a b)"), in_=PT[pt][:]),
              r=[("PT", pt)], w=[("hT", hs)])
            return xs, hs

        def project(hs, wcol0, ncols, wstride):
            pj = nxt("PJ")
            for k in range(8):
                P(lambda e, k=k, pj=pj: e.matmul(PJ[pj][:, 0:ncols], hT[hs][:, k, :],
                                                 W1[:, k * wstride + wcol0:k * wstride + wcol0 + ncols],
                                                 start=(k == 0), stop=(k == 7)),
                  r=[("hT", hs)] + [("W1", c) for c in range(wcol0 // 512, (wcol0 + ncols - 1) // 512 + 1)], w=[("PJ", pj)])
            return pj

        def head_norm_rope(pj, c0, nh, g_ap, cos_ap, sin_ap, nsin_ap, out_ap, wkey, alt=False):
            src = PJ[pj][:, c0:c0 + nh * 64]
            src3 = src.rearrange("p (a b) -> p a b", a=nh)
            if alt:
                sq_ = ezs[:, 0:nh * 64]
                qn_ = ezs[:, 256:256 + nh * 64].rearrange("p (a b) -> p a b", a=nh)
                rt_ = ezs[:, 512:512 + nh * 16].rearrange("p (a b) -> p a b", a=nh)
                rt2_ = ezs[:, 576:576 + nh * 16].rearrange("p (a b) -> p a b", a=nh)
                ssq_ = ezs[:, 640:640 + nh]
                rq_ = ezs[:, 648:648 + nh]
                ka = [("ezs", 0), ("ezs", 512)]
                k_sq = k_ssq = k_rq = k_qn = k_rt = k_rt2 = None
                ksq, kssq, krq, kqn, krt, krt2 = ka, ka, ka, ka, ka, ka
            else:
                sq_ = sq[:, 0:nh * 64]
                qn_ = qn[:, 0:nh, :]
                rt_ = rtmp[:, 0:nh, :]
                rt2_ = rtmp2[:, 0:nh, :]
                ssq_ = ssq[:, 0:nh]
                rq_ = rq[:, 0:nh]
                ksq, kssq, krq, kqn, krt, krt2 = ["sq"], ["ssq"], ["rq"], ["qn"], ["rtmp"], ["rtmp2"]
            A(lambda e: e.activation(out=sq_, in_=src, func=AF.Square), r=[("PJ", pj)], w=ksq)
            V(lambda e: e.tensor_reduce(out=ssq_, in_=sq_.rearrange("p (a b) -> p a b", a=nh), axis=AX.X, op=ALU.add), r=ksq, w=kssq)
            A(lambda e: e.activation(out=rq_, in_=ssq_, func=AF.Ln, scale=1.0 / HD, bias=epsf[:, 0:1]), r=kssq + ["epsf"], w=krq)
            A(lambda e: e.activation(out=rq_, in_=rq_, func=AF.Exp, scale=-0.5), r=krq, w=krq)
            V(lambda e: e.tensor_tensor(out=qn_, in0=src3, in1=rq_.unsqueeze(2).to_broadcast([128, nh, 64]), op=ALU.mult),
              r=[("PJ", pj)] + krq, w=kqn)
            V(lambda e: e.tensor_tensor(out=qn_, in0=qn_, in1=g_ap.unsqueeze(1).to_broadcast([128, nh, 64]), op=ALU.mult),
              r=kqn + ["gains"], w=kqn)
            cosb = cos_ap.unsqueeze(1).to_broadcast([128, nh, 8])
            sinb = sin_ap.unsqueeze(1).to_broadcast([128, nh, 8])
            nsinb = nsin_ap.unsqueeze(1).to_broadcast([128, nh, 8])
            V(lambda e: e.tensor_tensor(out=rt_[:, :, 0:8], in0=qn_[:, :, 0:8], in1=cosb, op=ALU.mult), r=kqn + ["rope", "ropeC"], w=krt)
            V(lambda e: e.tensor_tensor(out=rt_[:, :, 8:16], in0=qn_[:, :, 8:16], in1=cosb, op=ALU.mult), r=kqn + ["rope", "ropeC"], w=krt)
            V(lambda e: e.tensor_tensor(out=rt2_[:, :, 0:8], in0=qn_[:, :, 8:16], in1=nsinb, op=ALU.mult), r=kqn + ["rope", "ropeC"], w=krt2)
            V(lambda e: e.tensor_tensor(out=rt2_[:, :, 8:16], in0=qn_[:, :, 0:8], in1=sinb, op=ALU.mult), r=kqn + ["rope", "ropeC"], w=krt2)
            V(lambda e: e.tensor_tensor(out=out_ap[:, :, 0:16], in0=rt_, in1=rt2_, op=ALU.add), r=krt + krt2, w=[wkey])
            A(lambda e: e.activation(out=out_ap[:, :, 16:64], in_=qn_[:, :, 16:64], func=AF.Copy), r=kqn, w=[wkey])

        def merge_streams(fa, fb):
            outer = sch.capture
            sch.capture = []
            fa()
            la = sch.capture
            sch.capture = []
            fb()
            lb = sch.capture
            sch.capture = outer
            na, nb = len(la), len(lb)
            ia = ib = 0
            while ia < na or ib < nb:
                if ib >= nb or (ia < na and ia * nb <= ib * na):
                    it = la[ia]
                    ia += 1
                else:
                    it = lb[ib]
                    ib += 1
                if outer is not None:
                    outer.append(it)
                else:
                    sch.replay(it)

        def silu_from_psum(pj, n, out_ap, c0, wkey):
            src = PJ[pj][:, 0:n]
            ek = ("ezs", c0)
            A(lambda e: e.activation(out=ezs[:, c0:c0 + n], in_=src, func=AF.Exp, scale=-1.0), r=[("PJ", pj)], w=[ek])
            A(lambda e: e.activation(out=ezs[:, c0:c0 + n], in_=ezs[:, c0:c0 + n], func=AF.Ln, bias=onef[:, 0:1]), r=[ek, "onef"], w=[ek])
            A(lambda e: e.activation(out=ezs[:, c0:c0 + n], in_=ezs[:, c0:c0 + n], func=AF.Exp, scale=-1.0), r=[ek], w=[ek])
            V(lambda e: e.tensor_tensor(out=out_ap, in0=src, in1=ezs[:, c0:c0 + n], op=ALU.mult), r=[("PJ", pj), ek], w=[wkey])

        def transpose_heads(src_tile, ncol, dst_tile, rkeys, wkey, halves=(0, 1)):
            for half in halves:
                pt = nxt("PT")
                for hh in range(8):
                    h = half * 8 + hh
                    P(lambda e, h=h, hh=hh, pt=pt: e.transpose(PT[pt][0:ncol, hh * 128:(hh + 1) * 128], src_tile[:, h, 0:ncol], ident[:]),
                      r=rkeys + ["ident"], w=[("PT", pt)])
                if half == 0:
                    V(lambda e, pt=pt: e.tensor_copy(out=dst_tile[0:ncol, 0:8, :].rearrange("p a b -> p (a b)"), in_=PT[pt][0:ncol, :]),
                      r=[("PT", pt)], w=[wkey])
                else:
                    V(lambda e, pt=pt: e.tensor_copy(out=dst_tile[0:ncol, 8:16, :].rearrange("p a b -> p (a b)"), in_=PT[pt][0:ncol, :]),
                      r=[("PT", pt)], w=[wkey])

        def k_to_KT(dst_ap, wkey, ncopy=128):
            pt = nxt("PT")
            for g in range(NG):
                P(lambda e, g=g, pt=pt: e.transpose(PT[pt][0:64, g * 128:(g + 1) * 128], kb[:, g, :], ident[:]),
                  r=["kb", "ident"], w=[("PT", pt)])
            V(lambda e, pt=pt: e.tensor_copy(out=dst_ap, in_=PT[pt][0:64, 0:512].rearrange("p (a b) -> p a b", a=NG)[:, :, 0:ncopy]),
              r=[("PT", pt)], w=[wkey])

        class Attn:
            def __init__(self):
                self.pend = None

            def unit(self, lhsT_k, rhs_q, bias_ap, v_ap, vw, po, first, rk, on_done=None):
                ps = nxt("PS")
                P(lambda e: e.matmul(PS[ps][:], lhsT_k, rhs_q, start=True, stop=(bias_ap is None)),
                  r=rk, w=[("PS", ps)])
                if bias_ap is not None:
                    P(lambda e: e.matmul(PS[ps][:], ident[:], bias_ap, start=False, stop=True),
                      r=["ident", "cb", "cb2", "cmask"], w=[("PS", ps)])
                sl = nxt("pT", 3)
                pt_t = T["pT%d" % sl]
                A(lambda e: e.activation(out=pt_t[:], in_=PS[ps][:], func=AF.Exp, scale=0.125),
                  r=[("PS", ps)], w=[("pT", sl)])
                self.flush()
                self.pend = (pt_t, sl, v_ap, vw, po, first, rk, on_done)

            def flush(self):
                if self.pend is None:
                    return
                pt_t, sl, v_ap, vw, po, first, rk, on_done = self.pend
                self.pend = None
                for r in range(4):
                    P(lambda e, r=r: e.matmul(PO[po][:, r * vw:(r + 1) * vw], pt_t[:, r * 128:(r + 1) * 128], v_ap,
                                              start=(first and r == 0), stop=True, skip_group_check=True),
                      r=[("pT", sl)] + rk, w=[("PO", po)])
                if on_done is not None:
                    on_done()

        def branch_post(po, vw, g, gate_idx, first_branch, clamp=False, gates=None, sl=0):
            rs, cc, otmp, oacc = T["rs"], T["cc"], T["otmp"], T["oacc"][sl]
            pov = PO[po][:, 0:4 * vw].rearrange("p (a b) -> p a b", a=4)
            if clamp:
                V(lambda e: e.tensor_scalar(out=rs[:], in0=pov[:, :, 64:65].rearrange("p a b -> p (a b)"), scalar1=1e-30, scalar2=None, op0=ALU.max),
                  r=[("PO", po)], w=["rs"])
                V(lambda e: e.reciprocal(rs[:], rs[:]), r=["rs"], w=["rs"])
            else:
                V(lambda e: e.reciprocal(rs[:], pov[:, :, 64:65].rearrange("p a b -> p (a b)")), r=[("PO", po)], w=["rs"])
            if gate_idx is not None:
                V(lambda e: e.tensor_tensor(out=cc[:], in0=rs[:], in1=gates[:, 4 * g:4 * g + 4, gate_idx:gate_idx + 1].rearrange("p a b -> p (a b)"),
                                            op=ALU.mult), r=["rs", ("gates", 0), ("gates", 1)], w=["cc"])
                cap = cc
                ck = "cc"
            else:
                cap = rs
                ck = "rs"
            if first_branch:
                V(lambda e: e.tensor_tensor(out=oacc[:, 4 * g:4 * g + 4, :], in0=pov[:, :, 0:64],
                                            in1=cap[:].unsqueeze(2).to_broadcast([128, 4, 64]), op=ALU.mult),
                  r=[("PO", po), ck], w=[("oacc", sl, g)])
            else:
                V(lambda e: e.tensor_tensor(out=otmp[:], in0=pov[:, :, 0:64],
                                            in1=cap[:].unsqueeze(2).to_broadcast([128, 4, 64]), op=ALU.mult),
                  r=[("PO", po), ck], w=["otmp"])
                G(lambda e: e.tensor_tensor(out=oacc[:, 4 * g:4 * g + 4, :], in0=oacc[:, 4 * g:4 * g + 4, :], in1=otmp[:], op=ALU.add),
                  r=["otmp", ("oacc", sl, g)], w=[("oacc", sl, g)])

        def out_proj_store(b, i, xs, sl):
            oacc, og, ogT, ytmp, szb = T["oacc"][sl], T["og"], T["ogT"], T["ytmp"], T["szb"][sl]
            V(lambda e: e.tensor_tensor(out=og[:], in0=oacc[:].rearrange("p a b -> p (a b)"), in1=szb[:], op=ALU.mult),
              r=[("oacc", sl, g) for g in range(NG)] + [("szb", sl)], w=["og"])
            yield
            pt = nxt("PT")
            for k in range(8):
                P(lambda e, k=k, pt=pt: e.transpose(PT[pt][:, k * 128:(k + 1) * 128], og[:, k * 128:(k + 1) * 128], ident[:]),
                  r=["og", "ident"], w=[("PT", pt)])
            V(lambda e: e.tensor_copy(out=ogT[:].rearrange("p a b -> p (a b)"), in_=PT[pt][:]),
              r=[("PT", pt)], w=["ogT"])
            yield
            for half in range(2):
                pj = rr["PO"]
                for k in range(8):
                    P(lambda e, k=k, pj=pj, half=half: e.matmul(PO[pj][:], ogT[:, k, :], WO[:, k, half * 512:(half + 1) * 512],
                                                               start=(k == 0), stop=(k == 7)),
                      r=["ogT", ("WO", half)], w=[("PO", pj)])
                V(lambda e, pj=pj, half=half: e.tensor_tensor(out=ytmp[:], in0=PO[pj][:], in1=modbc[:, 2, half * 512:(half + 1) * 512], op=ALU.mult),
                  r=[("PO", pj), ("modbc", 2)], w=["ytmp"])
                G(lambda e, half=half: e.tensor_tensor(out=xt[xs][:, half * 512:(half + 1) * 512], in0=ytmp[:],
                                                       in1=xt[xs][:, half * 512:(half + 1) * 512], op=ALU.add),
                  r=["ytmp", ("xt", xs)], w=[("xt", xs)])
                if half == 0:
                    yield
            sch.dma("sp", y_d[b, i * 128:(i + 1) * 128, :], xt[xs][:], reads=[("xt", xs)], writes=[("res", b, i)])
            yield

        def load_w_cols(dst_col0, src_ap_fn, ncols, wstride, key="W1"):
            c = 0
            w1v = W1[:, 0:8 * wstride].rearrange("p (k n) -> p k n", k=8)
            while c < ncols:
                n = min(512 - (dst_col0 + c) % 512, ncols - c)
                sch.dma("pool", w1v[:, :, dst_col0 + c:dst_col0 + c + n], src_ap_fn(c, n), writes=[(key, (dst_col0 + c) // 512)])
                c += n

        def bload(dst_ap, src_row_ap, n, key):
            sch.dma("sp", dst_ap, src_row_ap.to_broadcast([128, n]), writes=[key])

        at_box = [None]

        def at_flush():
            at_box[0].flush()

        def run_pipelined(front, back_units, tail, n_tiles, prologue=None):
            def capture_front(i):
                fixed_bank[0] = 0
                sch.capture = []
                crit = None
                for m in front(i):
                    if m == "crit":
                        crit = len(sch.capture)
                ops = sch.capture
                sch.capture = None
                fixed_bank[0] = 1
                return ops, (len(ops) if crit is None else crit)

            ops, crit_pos = capture_front(0)
            pos = 0
            while pos < crit_pos:
                sch.replay(ops[pos])
                pos += 1
            leftover = ops[pos:]
            if prologue is not None:
                prologue()
            tail_gen = iter(())
            DONE = object()
            for i in range(n_tiles):
                if i + 1 < n_tiles:
                    new_ops, new_crit = capture_front(i + 1)
                else:
                    new_ops, new_crit = [], 0
                n_left = len(leftover)
                ops = leftover + new_ops
                crit_pos = n_left + new_crit
                pos = 0
                units = back_units(i)
                nf = units.index("need_front") if "need_front" in units else len(units)
                nu_front = sum(1 for u in units[:nf] if u != "need_front")
                ui = 0
                tail_done = False
                for u in units:
                    if u == "need_front":
                        for _ in tail_gen:
                            pass
                        tail_done = True
                        while pos < crit_pos:
                            sch.replay(ops[pos])
                            pos += 1
                        continue
                    u()
                    if ui >= 1 and not tail_done:
                        if next(tail_gen, DONE) is DONE:
                            tail_done = True
                    lim = len(ops) if tail_done else n_left
                    if pos < lim:
                        if pos < crit_pos:
                            left = max(1, nu_front - ui - 1)
                            k = -(-(crit_pos - pos) // left)
                        else:
                            k = 3
                        for _ in range(k):
                            if pos < lim:
                                sch.replay(ops[pos])
                                pos += 1
                    ui += 1
                for _ in tail_gen:
                    pass
                while pos < crit_pos:
                    sch.replay(ops[pos])
                    pos += 1
                leftover = ops[pos:]
                tail_gen = tail(i)
            for it in leftover:
                sch.replay(it)
            at_flush()
            for _ in tail_gen:
                pass
            fixed_bank[0] = None

        def moba_layer(b, j, src_d):
            WS = MOBA_IN
            load_w_cols(0, lambda c, n: dr["moba_w_in"][j, :, c:c + n].rearrange("(k p) n -> p k n", p=128), MOBA_IN, WS)
            bload(gq[:], dr["moba_q_norm"][j:j + 1, :], 64, "gains")
            bload(gk[:, 0, :], dr["moba_k_norm"][j:j + 1, :], 64, "gains")
            for g in range(NG):
                sch.dma("pool", KA[64:72, g, :], dr["c_ind_moba"], writes=["KAind"])
            with ExitStack() as esc:
                alloc_attn_tiles(esc)
                sg, m8 = T["sg"], T["m8"]
                ctx = {}

                def front(i):
                    sl = i % 2
                    qaug, qT1, qaT, szb = T["qaug"][sl], T["qT1"][sl], T["qaT"][sl], T["szb"][sl]
                    xs, hs = load_norm_h(b, i, src_d[b, i * 128:(i + 1) * 128, :])
                    ctx[i] = xs
                    yield
                    cs, sn, nsn = cosT[:, i, :], sinT[:, i, :], nsinT[:, i, :]
                    pq0 = project(hs, 0, 512, WS)
                    pq1 = project(hs, 512, 512, WS)
                    head_norm_rope(pq0, 0, 8, gq[:], cs, sn, nsn, qaug[:, 0:8, 0:64], ("qaug", sl, 0))
                    pkv = project(hs, 1024, 512, WS)
                    transpose_heads(qaug, 64, qT1, [("qaug", sl, 0)], ("qT1", sl), halves=(0,))
                    merge_streams(lambda: head_norm_rope(pq1, 0, 8, gq[:], cs, sn, nsn, qaug[:, 8:16, 0:64], ("qaug", sl, 1)),
                                  lambda: head_norm_rope(pkv, 0, 4, gk[:, 0, :], cs, sn, nsn, kb[:], "kb", alt=True))
                    transpose_heads(qaug, 64, qT1, [("qaug", sl, 1)], ("qT1", sl), halves=(1,))
                    A(lambda e, pj=pkv, i=i: e.activation(out=VA[:, i, :, 0:64], in_=PJ[pj][:, 256:512].rearrange("p (a b) -> p a b", a=NG), func=AF.Copy),
                      r=[("PJ", pkv)], w=[("VA", i)])
                    k_to_KT(KA[0:64, :, i * 128:(i + 1) * 128], ("KA", i))
                    pj2 = nxt("PJ")
                    for g in range(NG):
                        P(lambda e, g=g, pj2=pj2: e.matmul(PJ[pj2][0:64, g:g + 1], kb[:, g, :], onesc[:, 0:1], start=True, stop=True),
                          r=["kb", "onesc"], w=[("PJ", pj2)])
                    V(lambda e, pj2=pj2, i=i: e.tensor_copy(out=ksum[:, i, :], in_=PJ[pj2][0:64, 0:NG]), r=[("PJ", pj2)], w=["ksum"])
                    if i % 2 == 1:
                        jb = i // 2
                        V(lambda e, i=i: e.tensor_tensor(out=kmtmp[:], in0=ksum[:, i - 1, :], in1=ksum[:, i, :], op=ALU.add), r=["ksum"], w=["kmtmp"])
                        V(lambda e, jb=jb: e.tensor_scalar(out=kmT[:, :, jb:jb + 1].rearrange("p a b -> p (a b)"), in0=kmtmp[:], scalar1=1.0 / 256.0,
                                                           scalar2=None, op0=ALU.mult), r=["kmtmp"], w=["kmT"])
                    pj3 = nxt("PJ")
                    for h in range(NH):
                        P(lambda e, h=h, pj3=pj3: e.matmul(PJ[pj3][:, h * 8:(h + 1) * 8], qT1[0:64, h, :], kmT[0:64, h // 4, :], start=True, stop=True),
                          r=[("qT1", sl), "kmT"], w=[("PJ", pj3)])
                    V(lambda e, pj3=pj3, i=i: e.tensor_tensor(out=sg[:], in0=PJ[pj3][:, 0:128].rearrange("p (a b) -> p a b", a=NH),
                                                             in1=fbm[:, i, :].unsqueeze(1).to_broadcast([128, NH, 8]), op=ALU.add),
                      r=[("PJ", pj3), "fbm"], w=["sg"])
                    for h in range(NH):
                        V(lambda e, h=h: e.max(out=m8[:, h, :], in_=sg[:, h, :]), r=["sg"], w=[("m8", h)])
                    V(lambda e: e.tensor_tensor(out=qaug[:, :, 64:72], in0=sg[:], in1=m8[:, :, 3:4].to_broadcast([128, NH, 8]), op=ALU.is_lt),
                      r=["sg"] + [("m8", h) for h in range(NH)], w=[("qaug", sl, 2)])
                    transpose_heads(qaug, 72, qaT, [("qaug", sl, 0), ("qaug", sl, 1), ("qaug", sl, 2)], ("qaT", sl))
                    yield "crit"
                    pz0 = project(hs, 1536, 512, WS)
                    pz1 = project(hs, 2048, 512, WS)
                    silu_from_psum(pz0, 512, szb[:, 0:512], 0, ("szb", sl))
                    silu_from_psum(pz1, 512, szb[:, 512:1024], 512, ("szb", sl))
                    yield

                at = Attn()

                def back_units(i):
                    sl = i % 2
                    qaT = T["qaT"][sl]
                    units = []
                    for g in range(NG):
                        po_box = []
                        for jj in range(i + 1):
                            def u(g=g, jj=jj, po_box=po_box):
                                if jj == 0:
                                    po_box.append(nxt("PO"))
                                po = po_box[0]
                                bias = cb[:].unsqueeze(1).to_broadcast([128, 4, 128]) if jj == i else None
                                done = (lambda: branch_post(po, 65, g, None, True, sl=sl)) if jj == i else None
                                at.unit(KA[:, g, jj * 128:(jj + 1) * 128], qaT[:, 4 * g:4 * g + 4, :], bias, VA[:, jj, g, :], 65, po,
                                        jj == 0, [("KA", jj), "KAind", ("qaT", sl), ("VA", jj), "VA1"], done)
                            units.append(u)
                    return units

                def tail(i):
                    return out_proj_store(b, i, ctx[i], i % 2)

                at_box[0] = at
                run_pipelined(front, back_units, tail, NT)
                sch.barrier()

        def nsa_layer(b, j, src_d):
            w_in = dr["nsa_w_in"]
            load_w_cols(0, lambda c, n: w_in[j, :, 1024 + c:1024 + c + n].rearrange("(k p) n -> p k n", p=128), 1024, 1024)
            CW1 = W1[:, 8192:16384].rearrange("p (a m h) -> p a m h", a=2, m=16)
            for kv in range(2):
                for mh in range(2):
                    sch.dma("pool", CW1[:, kv, mh * 8:(mh + 1) * 8, :],
                            dr["nsa_cmp_w1"][j, kv, mh * 1024:(mh + 1) * 1024, :].rearrange("(m p) h -> p m h", p=128), writes=[("CW1", kv, mh)])
                sch.dma("pool", cw2[:, kv, :, :], dr["nsa_cmp_w2"][j, kv].rearrange("(h p) d -> p h d", p=128), writes=["cw2"])
                sch.dma("sp", peT2f[:, kv, :], dr["nsa_cmp_pe"][j, kv].rearrange("l d -> (l d)").rearrange("(m p) -> p m", p=128),
                        writes=["peT2f"], allow_slow_non_contiguous=True)
                sch.dma("sp", b1c[:, kv, :], dr["nsa_cmp_b1"][j, kv].rearrange("(h p) -> p h", p=128), writes=["b1c"],
                        allow_slow_non_contiguous=True)
            V(lambda e: e.tensor_copy(out=peT2[:], in_=peT2f[:]), r=["peT2f"], w=["peT2"])
            bload(gq[:], dr["nsa_q_norm"][j:j + 1, :], 64, "gains")
            for m in range(3):
                bload(gk[:, m, :], dr["nsa_k_norm"][j, m:m + 1, :], 64, "gains")
            bload(gateb[:], dr["nsa_gate_b"][j:j + 1, :], 48, "gateb")
            for g in range(NG):
                sch.dma("pool", KA[64:96, g, :], dr["c_ind_nsa"], writes=["KAind"])
            if DBG_STAGE[0] < 2:
                return
            with ExitStack() as esa:
                alloc_cmp_tiles(esa)
                KC2, hidT, hu, hw, kvb = T["KC2"], T["hidT"], T["hu"], T["hw"], T["kvb"]

                def a_stage2(i, hs):
                    pjA = project(hs, 0, 512, 1024)
                    pjB = project(hs, 512, 512, 1024)
                    A(lambda e, pjA=pjA: e.activation(out=kvb[:], in_=PJ[pjA][:], func=AF.Copy), r=[("PJ", pjA)], w=["kvb"])
                    head_norm_rope(pjB, 0, 4, gk[:, 1, :], cosT[:, i, :], sinT[:, i, :], nsinT[:, i, :], kb[:], "kb")
                    A(lambda e, pjB=pjB, i=i: e.activation(out=VA[:, i, :, 0:64], in_=PJ[pjB][:, 256:512].rearrange("p (a b) -> p a b", a=NG), func=AF.Copy),
                      r=[("PJ", pjB)], w=[("VA", i)])
                    pjs = nxt("PJ")
                    for kv in range(2):
                        for g in range(NG):
                            for par in range(2):
                                c0 = (kv * 4 + g) * 64
                                P(lambda e, kv=kv, g=g, par=par, c0=c0, pjs=pjs: e.matmul(
                                    PJ[pjs][par * 64:(par + 1) * 64, c0:c0 + 64], kvb[:, kv * 256 + g * 64:kv * 256 + (g + 1) * 64],
                                    selm[:, par, :], start=True, stop=True), r=["kvb", "selm"], w=[("PJ", pjs)])
                    V(lambda e, pjs=pjs, i=i: e.tensor_copy(out=KC2[:, :, :, i * 64:(i + 1) * 64],
                                                           in_=PJ[pjs][:].rearrange("p (a g u) -> p a g u", a=2, g=NG)),
                      r=[("PJ", pjs)], w=[("KC2", i)])
                    k_to_KT(KA[0:64, :, i * 128:(i + 1) * 128], ("KA", i))

                def cap(fn, pt):
                    pin_pt[0] = pt
                    sch.capture = []
                    r = fn()
                    ops_ = sch.capture
                    sch.capture = None
                    pin_pt[0] = None
                    return ops_, r

                def merge(o1, o2):
                    n1, n2 = len(o1), len(o2)
                    i1 = i2 = 0
                    while i1 < n1 or i2 < n2:
                        if i2 >= n2 or (i1 < n1 and i1 * n2 <= i2 * n1):
                            sch.replay(o1[i1])
                            i1 += 1
                        else:
                            sch.replay(o2[i2])
                            i2 += 1

                o1, (xs0, hs_prev) = cap(lambda: load_norm_h(b, 0, src_d[b, 0:128, :]), 0)
                merge(o1, [])
                for i in range(NT):
                    if i + 1 < NT:
                        o1, (xs1, hs_next) = cap(lambda i=i: load_norm_h(b, i + 1, src_d[b, (i + 1) * 128:(i + 2) * 128, :]), 0)
                    else:
                        o1, hs_next = [], None
                    o2, _ = cap(lambda i=i, hs_prev=hs_prev: a_stage2(i, hs_prev), 1)
                    merge(o1, o2)
                    hs_prev = hs_next
                if DBG_STAGE[0] < 3:
                    sch.barrier()
                    return
                kc2keys = [("KC2", i) for i in range(NT)]
                pjb = nxt("PJ")
                for kv in range(2):
                    for half in range(2):
                        col = kv * 2 + half
                        for m in range(16):
                            P(lambda e, kv=kv, half=half, m=m, col=col, pjb=pjb: e.matmul(
                                PJ[pjb][:, col:col + 1], CW1[:, kv, m, half * 128:(half + 1) * 128], peT2[:, kv, m:m + 1],
                                start=(m == 0), stop=(m == 15)), r=[("CW1", kv, m // 8), "peT2"], w=[("PJ", pjb)])
                V(lambda e, pjb=pjb: e.tensor_tensor(out=b1p[:].rearrange("p a b -> p (a b)"), in0=PJ[pjb][:, 0:4],
                                                    in1=b1c[:].rearrange("p a b -> p (a b)"), op=ALU.add), r=[("PJ", pjb), "b1c"], w=["b1p"])
                for kv in range(2):
                    for half in range(2):
                        ps_ = nxt("PJ")
                        for m in range(16):
                            P(lambda e, kv=kv, half=half, m=m, ps_=ps_: e.matmul(
                                PJ[ps_][:, 0:508], CW1[:, kv, m, half * 128:(half + 1) * 128], KC2[:, kv, :, m:m + 1009:8],
                                start=(m == 0), stop=(m == 15)), r=[("CW1", kv, m // 8)] + kc2keys, w=[("PJ", ps_)])
                        V(lambda e, kv=kv, half=half, ps_=ps_: e.tensor_scalar(out=hu[:], in0=PJ[ps_][:, 0:508], scalar1=b1p[:, kv, half:half + 1],
                                                                              scalar2=None, op0=ALU.add), r=[("PJ", ps_), "b1p"], w=["hu"])
                        V(lambda e: e.tensor_tensor(out=hw[:], in0=hu[:], in1=hu[:], op=ALU.mult), r=["hu"], w=["hw"])
                        V(lambda e: e.tensor_scalar(out=hw[:], in0=hw[:], scalar1=0.044715, scalar2=1.0, op0=ALU.mult, op1=ALU.add), r=["hw"], w=["hw"])
                        V(lambda e: e.tensor_tensor(out=hw[:], in0=hw[:], in1=hu[:], op=ALU.mult), r=["hw", "hu"], w=["hw"])
                        A(lambda e: e.activation(out=hw[:], in_=hw[:], func=AF.Exp, scale=-1.5957691216057308), r=["hw"], w=["hw"])
                        V(lambda e: e.tensor_scalar_add(hw[:], hw[:], 1.0), r=["hw"], w=["hw"])
                        V(lambda e: e.reciprocal(hw[:], hw[:]), r=["hw"], w=["hw"])
                        V(lambda e, kv=kv, half=half: e.tensor_tensor(out=hidT[:, kv * 2 + half, :, :].rearrange("p a b -> p (a b)"), in0=hu[:], in1=hw[:],
                                                                      op=ALU.mult), r=["hu", "hw"], w=["hidT"])
                pj2 = nxt("PJ")
                for kv in range(2):
                    for g in range(NG):
                        c0 = (kv * 4 + g) * 64
                        for half in range(2):
                            P(lambda e, kv=kv, g=g, half=half, c0=c0, pj2=pj2: e.matmul(
                                PJ[pj2][0:127, c0:c0 + 64], hidT[:, kv * 2 + half, g, :], cw2[:, kv, half, :],
                                start=(half == 0), stop=(half == 1)), r=["hidT", "cw2"], w=[("PJ", pj2)])
                head_norm_rope(pj2, 0, 4, gk[:, 0, :], cosC[:], sinC[:], nsinC[:], kb[:], "kb")
                A(lambda e, pj2=pj2: e.activation(out=vcmp[0:127, :, 0:64], in_=PJ[pj2][0:127, 256:512].rearrange("p (a b) -> p a b", a=NG), func=AF.Copy),
                  r=[("PJ", pj2)], w=["vcmp"])
                k_to_KT(kcmpT[0:64, :, 0:127], "kcmpT", 127)
                sch.barrier()
            if DBG_STAGE[0] < 4:
                return
            WS = 2608
            load_w_cols(0, lambda c, n: w_in[j, :, c:c + n].rearrange("(k p) n -> p k n", p=128), 1024, WS)
            load_w_cols(1024, lambda c, n: w_in[j, :, 2048 + c:2048 + c + n].rearrange("(k p) n -> p k n", p=128), 1584, WS)
            with ExitStack() as esc:
                alloc_attn_tiles(esc)
                gtmp, imp, m8, rs = T["gtmp"], T["imp"], T["m8"], T["rs"]
                ctx = {}

                def front(i):
                    sl = i % 2
                    qaug, qT1, szb, gates = T["qaug"][sl], T["qT1"][sl], T["szb"][sl], T["gates"][sl]
                    xs, hs = load_norm_h(b, i, src_d[b, i * 128:(i + 1) * 128, :])
                    ctx[i] = xs
                    yield
                    cs, sn, nsn = cosT[:, i, :], sinT[:, i, :], nsinT[:, i, :]
                    slot = i % 4
                    pq0 = project(hs, 0, 512, WS)
                    pq1 = project(hs, 512, 512, WS)
                    head_norm_rope(pq0, 0, 8, gq[:], cs, sn, nsn, qaug[:, 0:8, 0:64], ("qaug", sl, 0))
                    pkw = project(hs, 1024, 512, WS)
                    transpose_heads(qaug, 64, qT1, [("qaug", sl, 0)], ("qT1", sl), halves=(0,))
                    merge_streams(lambda: head_norm_rope(pq1, 0, 8, gq[:], cs, sn, nsn, qaug[:, 8:16, 0:64], ("qaug", sl, 1)),
                                  lambda: head_norm_rope(pkw, 0, 4, gk[:, 2, :], cs, sn, nsn, kb[:], "kb", alt=True))
                    transpose_heads(qaug, 64, qT1, [("qaug", sl, 1)], ("qT1", sl), halves=(1,))
                    pgl = project(hs, 1536, 48, WS)
                    A(lambda e, pj=pkw, slot=slot: e.activation(out=VW[:, slot, :, 0:64], in_=PJ[pj][:, 256:512].rearrange("p (a b) -> p a b", a=NG), func=AF.Copy),
                      r=[("PJ", pkw)], w=[("VW", slot)])
                    k_to_KT(KW[0:64, :, slot, :], ("KW", slot))
                    V(lambda e, pj=pgl: e.tensor_tensor(out=gtmp[:], in0=PJ[pj][:, 0:48], in1=gateb[:], op=ALU.add), r=[("PJ", pgl), "gateb"], w=["gtmp"])
                    A(lambda e: e.activation(out=gtmp[:], in_=gtmp[:], func=AF.Exp, scale=-1.0), r=["gtmp"], w=["gtmp"])
                    A(lambda e: e.activation(out=gtmp[:], in_=gtmp[:], func=AF.Ln, bias=onef[:, 0:1]), r=["gtmp", "onef"], w=["gtmp"])
                    A(lambda e: e.activation(out=gates[:].rearrange("p a b -> p (a b)"), in_=gtmp[:], func=AF.Exp, scale=-1.0), r=["gtmp"], w=[("gates", sl)])
                    yield "crit"
                    pz0 = project(hs, 1584, 512, WS)
                    pz1 = project(hs, 2096, 512, WS)
                    silu_from_psum(pz0, 512, szb[:, 0:512], 0, ("szb", sl))
                    silu_from_psum(pz1, 512, szb[:, 512:1024], 512, ("szb", sl))
                    yield

                at = Attn()

                def cmp_post(po, g, i):
                    sl = i % 2
                    qaug = T["qaug"][sl]
                    branch_post(po, 97, g, 0, True, clamp=True, gates=T["gates"][sl], sl=sl)
                    pov = PO[po][:, 0:388].rearrange("p (a b) -> p a b", a=4)
                    V(lambda e: e.tensor_scalar(out=imp[:, g, :], in0=pov[:, 0, 65:97], scalar1=rs[:, 0:1], scalar2=None, op0=ALU.mult),
                      r=[("PO", po), "rs"], w=[("imp", g)])
                    for r_ in range(1, 4):
                        V(lambda e, r_=r_: e.scalar_tensor_tensor(out=imp[:, g, :], in0=pov[:, r_, 65:97], scalar=rs[:, r_:r_ + 1], in1=imp[:, g, :],
                                                                  op0=ALU.mult, op1=ALU.add), r=[("PO", po), "rs", ("imp", g)], w=[("imp", g)])
                    V(lambda e: e.tensor_tensor(out=imp[:, g, :], in0=imp[:, g, :], in1=fb[:, i, :], op=ALU.add), r=[("imp", g), "fb"], w=[("imp", g)])
                    V(lambda e: e.max(out=m8[:, g, :], in_=imp[:, g, :]), r=[("imp", g)], w=[("m8", g)])
                    V(lambda e: e.tensor_tensor(out=qaug[:, 4 * g:4 * g + 4, 64:96], in0=imp[:, g, :].unsqueeze(1).to_broadcast([128, 4, 32]),
                                                in1=m8[:, g, 7:8].unsqueeze(1).to_broadcast([128, 4, 32]), op=ALU.is_lt),
                      r=[("imp", g), ("m8", g)], w=[("qaug", sl, 2)])

                def cmp_unit(i, g):
                    sl = i % 2
                    qT1 = T["qT1"][sl]

                    def u():
                        po = nxt("PO")
                        at.unit(kcmpT[:, g, :], qT1[:, 4 * g:4 * g + 4, :], cmask[:, i * 128:(i + 1) * 128].unsqueeze(1).to_broadcast([128, 4, 128]),
                                vcmp[:, g, :], 97, po, True, ["kcmpT", ("qT1", sl), "vcmp"], (lambda: cmp_post(po, g, i)))
                    return u

                def back_units(i):
                    sl = i % 2
                    qaug, qT1, qaT, gates = T["qaug"][sl], T["qT1"][sl], T["qaT"][sl], T["gates"][sl]
                    units = []
                    tiles = [jj for jj in (i - 2, i - 1, i) if jj >= 0]
                    for g in range(NG):
                        po_box = []
                        for idx, jj in enumerate(tiles):
                            def u(g=g, jj=jj, idx=idx, po_box=po_box):
                                if idx == 0:
                                    po_box.append(nxt("PO"))
                                po = po_box[0]
                                if jj == i:
                                    bias = cb[:].unsqueeze(1).to_broadcast([128, 4, 128])
                                elif jj == i - 2:
                                    bias = cb2[:].unsqueeze(1).to_broadcast([128, 4, 128])
                                else:
                                    bias = None
                                done = (lambda: branch_post(po, 65, g, 2, False, gates=gates, sl=sl)) if jj == i else None
                                at.unit(KW[:, g, jj % 4, :], qT1[:, 4 * g:4 * g + 4, :], bias, VW[:, jj % 4, g, :], 65, po, idx == 0,
                                        [("KW", jj % 4), ("qT1", sl), ("VW", jj % 4), "VW1"], done)
                            units.append(u)

                    def tr():
                        at.flush()
                        transpose_heads(qaug, 96, qaT, [("qaug", sl, 0), ("qaug", sl, 1), ("qaug", sl, 2)], ("qaT", sl))
                    units.append(tr)
                    for g in range(NG):
                        po_box = []
                        for jj in range(i + 1):
                            def u(g=g, jj=jj, po_box=po_box):
                                if jj == 0:
                                    po_box.append(nxt("PO"))
                                po = po_box[0]
                                bias = cb[:].unsqueeze(1).to_broadcast([128, 4, 128]) if jj == i else None
                                done = (lambda: branch_post(po, 65, g, 1, False, gates=gates, sl=sl)) if jj == i else None
                                at.unit(KA[:, g, jj * 128:(jj + 1) * 128], qaT[:, 4 * g:4 * g + 4, :], bias, VA[:, jj, g, :], 65, po,
                                        jj == 0, [("KA", jj), "KAind", ("qaT", sl), ("VA", jj), "VA1"], done)
                            units.append(u)
                        if i + 1 < NT:
                            if g == 0:
                                units.append("need_front")
                            units.append(cmp_unit(i + 1, g))
                    return units

                def tail(i):
                    return out_proj_store(b, i, ctx[i], i % 2)

                def prologue():
                    for g in range(NG):
                        cmp_unit(0, g)()

                at_box[0] = at
                run_pipelined(front, back_units, tail, NT, prologue)
                sch.barrier()

        for b in range(n_seq):
            with ExitStack() as es3:
                E3 = es3.enter_context
                posi = E3(nc.sbuf_tensor("posi%d" % b, [128, NT + 1], I32))
                posf = E3(nc.sbuf_tensor("posf%d" % b, [128, NT + 1], F32))
                ang = E3(nc.sbuf_tensor("ang%d" % b, [128, NT + 1, 8], F32))
                kq = E3(nc.sbuf_tensor("kq%d" % b, [128, NT + 1, 8], F32))
                ki = E3(nc.sbuf_tensor("ki%d" % b, [128, NT + 1, 8], I32))
                red = E3(nc.sbuf_tensor("red%d" % b, [128, NT + 1, 8], F32))
                V(lambda e: e.memset(posi[:], 0), w=["posi"])
                sch.dma("sp", posi[:, 0:NT], dr["positions"][b].rearrange("(i p) -> p i", p=128), writes=["posi"],
                        allow_slow_non_contiguous=True)
                sch.dma("sp", posi[0:127, NT:NT + 1], dr["positions"][b, 16:16 + 16 * 127].rearrange("(n s) -> n s", s=16)[:, 15:16],
                        writes=["posi"], allow_slow_non_contiguous=True)
                V(lambda e: e.tensor_copy(out=posf[:], in_=posi[:]), r=["posi"], w=["posf"])
                V(lambda e: e.tensor_tensor(out=ang[:], in0=posf[:].unsqueeze(2).to_broadcast([128, NT + 1, 8]),
                                            in1=invf[:].unsqueeze(1).to_broadcast([128, NT + 1, 8]), op=ALU.mult),
                  r=["posf", "invf"], w=["ang"])
                C1 = 6.28125
                C2 = float(2.0 * np.pi - 6.28125)
                PI_LO = 3.1415925
                for which, off in (("sin", 0.0), ("cos", 0.25)):
                    V(lambda e, off=off: e.tensor_scalar(out=kq[:], in0=ang[:], scalar1=float(1.0 / (2.0 * np.pi)), scalar2=off,
                                                         op0=ALU.mult, op1=ALU.add), r=["ang"], w=["kq"])
                    V(lambda e: e.tensor_copy(out=ki[:], in_=kq[:]), r=["kq"], w=["ki"])
                    V(lambda e: e.tensor_copy(out=kq[:], in_=ki[:]), r=["ki"], w=["kq"])
                    V(lambda e: e.scalar_tensor_tensor(out=red[:], in0=kq[:], scalar=-C1, in1=ang[:], op0=ALU.mult, op1=ALU.add),
                      r=["kq", "ang"], w=["red"])
                    V(lambda e: e.scalar_tensor_tensor(out=red[:], in0=kq[:], scalar=-C2, in1=red[:], op0=ALU.mult, op1=ALU.add),
                      r=["kq", "red"], w=["red"])
                    if which == "cos":
                        V(lambda e: e.tensor_scalar(out=red[:], in0=red[:], scalar1=float(np.pi / 2), scalar2=PI_LO,
                                                    op0=ALU.add, op1=ALU.min), r=["red"], w=["red"])
                        V(lambda e: e.tensor_scalar(out=red[:], in0=red[:], scalar1=-PI_LO, scalar2=None, op0=ALU.max), r=["red"], w=["red"])
                        A(lambda e: e.activation(out=cosT[:], in_=red[:, 0:NT, :], func=AF.Sin), r=["red"], w=["rope"])
                        A(lambda e: e.activation(out=cosC[:], in_=red[:, NT, :], func=AF.Sin), r=["red"], w=["ropeC"])
                    else:
                        V(lambda e: e.tensor_scalar(out=red[:], in0=red[:], scalar1=PI_LO, scalar2=-PI_LO,
                                                    op0=ALU.min, op1=ALU.max), r=["red"], w=["red"])
                        A(lambda e: e.activation(out=sinT[:], in_=red[:, 0:NT, :], func=AF.Sin), r=["red"], w=["rope"])
                        A(lambda e: e.activation(out=sinC[:], in_=red[:, NT, :], func=AF.Sin), r=["red"], w=["ropeC"])
                V(lambda e: e.tensor_scalar(out=nsinT[:], in0=sinT[:], scalar1=-1.0, scalar2=None, op0=ALU.mult), r=["rope"], w=["rope"])
                V(lambda e: e.tensor_scalar(out=nsinC[:], in0=sinC[:], scalar1=-1.0, scalar2=None, op0=ALU.mult), r=["ropeC"], w=["ropeC"])
                sch.barrier()

            for li, L in enumerate(layers):
                src_d = dr["x"] if (li == 0 and first_from_x) else y_d
                is_nsa = (L % 2 == 0)
                j = L // 2
                for m in range(3):
                    sch.dma("sp", modbc[:, m, :], mod_d[b, L:L + 1, m * D:(m + 1) * D].to_broadcast([128, D]),
                            reads=[("mod", b, L)], writes=[("modbc", m)])
                wo_src = dr["nsa_w_out"][j] if is_nsa else dr["moba_w_out"][j]
                for hlf in range(2):
                    sch.dma("pool", WO[:, :, hlf * 512:(hlf + 1) * 512],
                            wo_src[:, hlf * 512:(hlf + 1) * 512].rearrange("(k p) n -> p k n", p=128), writes=[("WO", hlf)])
                if is_nsa:
                    nsa_layer(b, j, src_d)
                else:
                    moba_layer(b, j, src_d)
        sch.finish()
    return nc


_CONSTS = None


def _consts():
    global _CONSTS
    if _CONSTS is None:
        _CONSTS = host_consts()
    return _CONSTS


def run_layers(inputs, layers, n_seq, n_cores, first_from_x=True, trace=False):
    nc = build(layers, n_seq, first_from_x)
    cst = _consts()
    in_maps = []
    for c in range(n_cores):
        m = {}
        m["x"] = np.ascontiguousarray(inputs["x"][c * n_seq:(c + 1) * n_seq], dtype=np.float32)
        m["c"] = np.ascontiguousarray(inputs["c"][c * n_seq:(c + 1) * n_seq], dtype=np.float32)
        m["positions"] = np.ascontiguousarray(inputs["positions"][c * n_seq:(c + 1) * n_seq], dtype=np.int32)
        for k in WEIGHT_SHAPES:
            m[k] = np.ascontiguousarray(inputs[k], dtype=np.float32)
        for k in CONST_SHAPES:
            m[k] = cst[k]
        in_maps.append(m)
    res = run_bass_kernel_spmd(nc, in_maps, core_ids=list(range(n_cores)), trace=trace)
    out = np.concatenate([np.asarray(r["y"]) for r in res.results], axis=0)
    return out, res


def kernel(**inputs):
    out, _ = run_layers(inputs, [0, 1, 2, 3], 2, N_CORES)
    return out.astype(np.float32)
```

```python
import numpy as np
from contextlib import ExitStack
import concourse.bass as bass
import concourse.mybir as mybir
from concourse.bass_utils import run_bass_kernel_spmd

F32 = mybir.dt.float32
BF16 = mybir.dt.bfloat16
I32 = mybir.dt.int32
AF = mybir.ActivationFunctionType
ALU = mybir.AluOpType
AX = mybir.AxisListType

S = 2048
D = 1024
NT = 16
HD = 64
NH = 16
NG = 4
NEG = -30000.0
EPS = 1e-6
N_CORES = 8
NSA_IN = 3632
MOBA_IN = 2560
NDMA = 24
FRESH_POOL_SEMS = [False]
POOL_DESC_LIMIT = 6144


class Sched:
    def __init__(self, nc, es):
        self.nc = nc
        self.es = es
        self.nfresh = 0
        self.capture = None
        self.pool_fifo = []
        self.pool_desc = 0
        self.eng = {"pe": nc.tensor, "act": nc.scalar, "dve": nc.vector, "pool": nc.gpsimd, "sp": nc.sync}
        self.semh = {}
        for k in self.eng:
            self.semh[("e", k)] = es.enter_context(nc.semaphore("sem_" + k))
        for i in range(NDMA):
            self.semh[("d", i)] = es.enter_context(nc.semaphore("dsem%d" % i))
        self.cnt = {k: 0 for k in self.eng}
        self.dcnt = [0] * NDMA
        self.drr = 0
        self.seen = {k: {} for k in self.eng}
        self.lastw = {}
        self.readers = {}

    def _deps(self, eng, reads, writes, strict=False):
        deps = {}

        def add(t, same_ok):
            sk, v = t
            if same_ok and not strict and sk == ("e", eng) and eng == "pe":
                return
            if deps.get(sk, 0) < v:
                deps[sk] = v

        for k in reads:
            if k in self.lastw:
                add(self.lastw[k], False)
        for k in writes:
            if k in self.lastw:
                add(self.lastw[k], True)
            for sk, v in self.readers.get(k, {}).items():
                add((sk, v), True)
        return deps

    def _wait(self, eng, deps):
        for sk, v in deps.items():
            if eng == "pe" and sk == ("e", "pe"):
                continue
            if self.seen[eng].get(sk, 0) < v:
                self.eng[eng].wait_ge(self.semh[sk], v)
                self.seen[eng][sk] = v

    def _record(self, tok, reads, writes):
        for k in writes:
            self.lastw[k] = tok
            self.readers[k] = {}
        for k in reads:
            d = self.readers.setdefault(k, {})
            if d.get(tok[0], 0) < tok[1]:
                d[tok[0]] = tok[1]

    def op(self, eng, fn, reads=(), writes=()):
        if self.capture is not None:
            self.capture.append(("op", eng, fn, tuple(reads), tuple(writes), None))
            return
        self._wait(eng, self._deps(eng, reads, writes))
        inst = fn(self.eng[eng])
        self.cnt[eng] += 1
        inst.then_inc(self.semh[("e", eng)], 1)
        self._record((("e", eng), self.cnt[eng]), reads, writes)

    def replay(self, item):
        kind, a, b, reads, writes, kw = item
        if kind == "op":
            self.op(a, b, reads, writes)
        else:
            self.dma(a, b[0], b[1], reads, writes, **kw)

    def dma(self, queue, out, in_, reads=(), writes=(), **kw):
        if self.capture is not None:
            self.capture.append(("dma", queue, (out, in_), tuple(reads), tuple(writes), dict(kw)))
            return
        if FRESH_POOL_SEMS[0] and queue == "pool":
            kw.pop("ndesc", None)
            idx = NDMA + self.nfresh
            self.nfresh += 1
            self.semh[("d", idx)] = self.es.enter_context(self.nc.semaphore("fsem%d" % idx))
            self.dcnt.append(0)
            self._wait(queue, self._deps(queue, reads, writes, strict=True))
            inst = self.eng[queue].dma_start(out=out, in_=in_, **kw)
            self.dcnt[idx] += 16
            inst.then_inc(self.semh[("d", idx)], 16)
            self._record((("d", idx), self.dcnt[idx]), reads, writes)
            return
        ndesc = kw.pop("ndesc", 1024)
        idx = self.drr
        self.drr = (idx + 1) % NDMA
        if queue == "pool":
            while self.pool_fifo and self.pool_desc + ndesc > POOL_DESC_LIMIT:
                sk, v, nd = self.pool_fifo.pop(0)
                self.pool_desc -= nd
                if self.seen["pool"].get(sk, 0) < v:
                    self.eng["pool"].wait_ge(self.semh[sk], v)
                    self.seen["pool"][sk] = v
        deps = self._deps(queue, reads, writes, strict=True)
        if self.dcnt[idx]:
            deps[("d", idx)] = max(deps.get(("d", idx), 0), self.dcnt[idx])
        self._wait(queue, deps)
        inst = self.eng[queue].dma_start(out=out, in_=in_, **kw)
        self.dcnt[idx] += 16
        inst.then_inc(self.semh[("d", idx)], 16)
        self._record((("d", idx), self.dcnt[idx]), reads, writes)
        if queue == "pool":
            self.pool_fifo.append((("d", idx), self.dcnt[idx], ndesc))
            self.pool_desc += ndesc

    def barrier(self):
        for k in self.eng:
            for k2 in self.eng:
                if k2 != k and self.cnt[k2] > self.seen[k].get(("e", k2), 0):
                    self.eng[k].wait_ge(self.semh[("e", k2)], self.cnt[k2])
                    self.seen[k][("e", k2)] = self.cnt[k2]
            for i in range(len(self.dcnt)):
                if self.dcnt[i] > self.seen[k].get(("d", i), 0):
                    self.eng[k].wait_ge(self.semh[("d", i)], self.dcnt[i])
                    self.seen[k][("d", i)] = self.dcnt[i]

    def finish(self):
        e = self.eng["sp"]
        for i in range(len(self.dcnt)):
            if self.dcnt[i] and self.seen["sp"].get(("d", i), 0) < self.dcnt[i]:
                e.wait_ge(self.semh[("d", i)], self.dcnt[i])
        for k in self.eng:
            if k != "sp" and self.cnt[k] and self.seen["sp"].get(("e", k), 0) < self.cnt[k]:
                e.wait_ge(self.semh[("e", k)], self.cnt[k])


def host_consts():
    p = np.arange(128)[:, None]
    f = np.arange(128)[None, :]
    c = {}
    c["c_ident"] = (p == f).astype(np.float32)
    c["c_cb"] = np.where(p > f, NEG, 0.0).astype(np.float32)
    c["c_cb2"] = np.where(p <= f, NEG, 0.0).astype(np.float32)
    t = np.arange(S)[None, :]
    cm = np.where((16 * p + 31 > t) | (p == 127), NEG, 0.0)
    c["c_cmask"] = cm.astype(np.float32)
    fb = np.zeros((128, NT, 32), np.float32)
    for i in range(NT):
        cur = (128 * i + np.arange(128)) // 64
        for j in range(32):
            fb[:, i, j] = np.where((j == 0) | (j == cur) | (j == cur - 1), 100.0, 0.0)
    c["c_fb"] = fb.reshape(128, NT * 32)
    fbm = np.zeros((128, NT, 8), np.float32)
    for i in range(NT):
        own = i // 2
        for j in range(8):
            fbm[:, i, j] = 0.0 if j < own else (1e4 if j == own else -1e4)
    c["c_fbm"] = fbm.reshape(128, NT * 8)
    key = np.arange(S)[None, :]
    c["c_ind_nsa"] = np.where(key // 64 == np.arange(32)[:, None], NEG, 0.0).astype(np.float32)
    c["c_ind_moba"] = np.where(key // 256 == np.arange(8)[:, None], NEG, 0.0).astype(np.float32)
    cs = np.arange(127)[:, None] * 16
    ss = np.arange(32)[None, :] * 64
    shared = np.clip(np.minimum(cs + 32, ss + 64) - np.maximum(cs, ss), 0, None)
    sw = np.zeros((128, 32), np.float32)
    sw[:127] = shared / 32.0
    c["c_selw"] = sw
    sel = np.zeros((128, 2, 64), np.float32)
    for par in range(2):
        for u in range(64):
            sel[2 * u + par, par, u] = 1.0
    c["c_sel"] = sel.reshape(128, 128)
    inv = (np.float32(500000.0) ** (-np.arange(0, 16, 2, dtype=np.float32) / np.float32(16))).astype(np.float32)
    c["c_invf"] = np.tile(inv[None, :], (128, 1)).astype(np.float32)
    return c


CONST_SHAPES = {
    "c_ident": [128, 128], "c_cb": [128, 128], "c_cb2": [128, 128], "c_cmask": [128, S],
    "c_fb": [128, NT * 32], "c_fbm": [128, NT * 8], "c_ind_nsa": [32, S], "c_ind_moba": [8, S],
    "c_selw": [128, 32], "c_sel": [128, 128], "c_invf": [128, 8],
}

WEIGHT_SHAPES = {
    "norm_g": [4, D], "ada_w": [4, D, 3 * D], "ada_b": [4, 3 * D],
    "nsa_w_in": [2, D, NSA_IN], "nsa_w_out": [2, D, D], "nsa_q_norm": [2, 64], "nsa_k_norm": [2, 3, 64],
    "nsa_cmp_pe": [2, 2, 32, 64], "nsa_cmp_w1": [2, 2, 2048, 256], "nsa_cmp_b1": [2, 2, 256],
    "nsa_cmp_w2": [2, 2, 256, 64], "nsa_gate_b": [2, 48],
    "moba_w_in": [2, D, MOBA_IN], "moba_w_out": [2, D, D], "moba_q_norm": [2, 64], "moba_k_norm": [2, 64],
}


DBG_STAGE = [9]


def build(layers, n_seq, first_from_x=True):
    nc = bass.Bass("TRN2", target_bir_lowering=False, dynamic_dma_scratch_size=8192)
    dr = {}
    dr["x"] = nc.dram_tensor("x", [n_seq, S, D], F32, kind="ExternalInput").ap()
    dr["c"] = nc.dram_tensor("c", [n_seq, D], F32, kind="ExternalInput").ap()
    dr["positions"] = nc.dram_tensor("positions", [n_seq, S], I32, kind="ExternalInput").ap()
    for k, shp in WEIGHT_SHAPES.items():
        dr[k] = nc.dram_tensor(k, shp, F32, kind="ExternalInput").ap()
    for k, shp in CONST_SHAPES.items():
        dr[k] = nc.dram_tensor(k, shp, F32, kind="ExternalInput").ap()
    y_d = nc.dram_tensor("y", [n_seq, S, D], F32, kind="ExternalOutput").ap()
    mod_d = nc.dram_tensor("mod_scr", [n_seq, 4, 3 * D], F32, kind="Internal").ap()

    with ExitStack() as es:
        E = es.enter_context
        sch = Sched(nc, es)

        def V(fn, r=(), w=()):
            sch.op("dve", fn, r, w)

        def A(fn, r=(), w=()):
            sch.op("act", fn, r, w)

        def G(fn, r=(), w=()):
            sch.op("pool", fn, r, w)

        def P(fn, r=(), w=()):
            sch.op("pe", fn, r, w)

        def sb(name, shape, dt):
            return E(nc.sbuf_tensor(name, shape, dt))

        T = {}

        def sb(name, shape, dt, stack=None):
            t = (stack or es).enter_context(nc.sbuf_tensor(name, shape, dt))
            T[name] = t
            return t

        ident = sb("ident", [128, 128], BF16)
        cb = sb("cb", [128, 128], BF16)
        cb2 = sb("cb2", [128, 128], BF16)
        cmask = sb("cmask", [128, S], BF16)
        fb = sb("fb", [128, NT, 32], F32)
        fbm = sb("fbm", [128, NT, 8], F32)
        selm = sb("selm", [128, 2, 64], BF16)
        invf = sb("invf", [128, 8], F32)
        onesc = sb("onesc", [128, 1], BF16)
        W1 = sb("W1", [128, 20864], BF16)
        WO = sb("WO", [128, 8, D], BF16)
        KA = sb("KA", [128, NG, S], BF16)
        VA = sb("VA", [128, NT, NG, 65], BF16)
        KW = sb("KW", [128, NG, 4, 128], BF16)
        VW = sb("VW", [128, 4, NG, 65], BF16)
        onef = sb("onef", [128, 1], F32)
        epsf = sb("epsf", [128, 1], F32)
        kcmpT = sb("kcmpT", [128, NG, 128], BF16)
        vcmp = sb("vcmp", [128, NG, 97], BF16)
        modbc = sb("modbc", [128, 3, D], F32)
        cosT = sb("cosT", [128, NT, 8], F32)
        sinT = sb("sinT", [128, NT, 8], F32)
        nsinT = sb("nsinT", [128, NT, 8], F32)
        cosC = sb("cosC", [128, 8], F32)
        sinC = sb("sinC", [128, 8], F32)
        nsinC = sb("nsinC", [128, 8], F32)
        gq = sb("gq", [128, 64], F32)
        gk = sb("gk", [128, 3, 64], F32)
        gateb = sb("gateb", [128, 48], F32)
        kmT = sb("kmT", [64, NG, 8], BF16)
        kmtmp = sb("kmtmp", [64, NG], F32)
        ksum = sb("ksum", [64, NT, NG], F32)
        cw2 = sb("cw2", [128, 2, 2, 64], BF16)
        peT2 = sb("peT2", [128, 2, 16], BF16)
        peT2f = sb("peT2f", [128, 2, 16], F32)
        b1c = sb("b1c", [128, 2, 2], F32)
        b1p = sb("b1p", [128, 2, 2], F32)
        xt = [sb("xt%d" % i, [128, D], F32) for i in range(2)]
        st = sb("st", [128, 8], F32)
        hb = sb("hb", [128, D], BF16)
        hT = [sb("hT%d" % i, [128, 8, 128], BF16) for i in range(2)]
        sq = sb("sq", [128, 512], F32)
        ssq = sb("ssq", [128, 16], F32)
        rq = sb("rq", [128, 16], F32)
        qn = sb("qn", [128, 8, 64], F32)
        rtmp = sb("rtmp", [128, 8, 16], F32)
        rtmp2 = sb("rtmp2", [128, 8, 16], F32)
        kb = sb("kb", [128, NG, 64], BF16)
        ezs = sb("ezs", [128, D], F32)
        PJ = [E(nc.psum_tensor("PJ%d" % i, [128, 512], F32)) for i in range(2)]
        PS = [E(nc.psum_tensor("PS%d" % i, [128, 512], F32)) for i in range(2)]
        PO = [E(nc.psum_tensor("PO%d" % i, [128, 512], F32)) for i in range(2)]
        PT = [E(nc.psum_tensor("PT%d" % i, [128, 1024], BF16)) for i in range(2)]
        rr = {"PJ": 0, "PS": 0, "PO": 0, "PT": 0, "pT": 0, "xt": 0, "hT": 0}
        uid = [0]

        fixed_bank = [None]
        pin_pt = [None]

        def nxt(k, n=2):
            if pin_pt[0] is not None and k == "PT":
                return pin_pt[0]
            if fixed_bank[0] is not None and k == "PT":
                return fixed_bank[0]
            if fixed_bank[0] == 1 and k == "PJ":
                raise RuntimeError("atomic-mode code must not allocate PJ banks (owned by the deferred front stage)")
            v = rr[k]
            rr[k] = (v + 1) % n
            return v

        def alloc_attn_tiles(stack):
            uid[0] += 1
            u = "_%d" % uid[0]
            for nm, shp, dt in [("qaug", [128, NH, 96], BF16), ("qT1", [128, NH, 128], BF16), ("qaT", [128, NH, 128], BF16),
                                ("szb", [128, D], BF16), ("gates", [128, NH, 3], F32), ("oacc", [128, NH, 64], F32)]:
                T[nm] = [stack.enter_context(nc.sbuf_tensor("%s%s_%d" % (nm, u, k), shp, dt)) for k in range(2)]
                if nm in ("qT1", "qaT"):
                    for k in range(2):
                        G(lambda e, t=T[nm][k]: e.memset(t[:], 0.0), w=[(nm, k)])
            for nm, shp, dt in [("gtmp", [128, 48], F32),
                                ("sg", [128, NH, 8], F32), ("m8", [128, NH, 8], F32), ("imp", [128, NG, 32], F32),
                                ("rs", [128, 4], F32), ("cc", [128, 4], F32), ("otmp", [128, 4, 64], F32),
                                ("og", [128, D], BF16), ("ogT", [128, 8, 128], BF16),
                                ("ytmp", [128, 512], F32), ("pT0", [128, 512], BF16), ("pT1", [128, 512], BF16),
                                ("pT2", [128, 512], BF16)]:
                T[nm] = stack.enter_context(nc.sbuf_tensor(nm + u, shp, dt))

        def alloc_cmp_tiles(stack):
            uid[0] += 1
            u = "_%d" % uid[0]
            for nm, shp, dt in [("KC2", [128, 2, NG, 1024], BF16), ("hidT", [128, 4, NG, 127], BF16),
                                ("hu", [128, 508], F32), ("hw", [128, 508], F32), ("kvb", [128, 512], BF16)]:
                T[nm] = stack.enter_context(nc.sbuf_tensor(nm + u, shp, dt))

        def cload(tile_ap, src, key, queue="pool"):
            sch.dma(queue, tile_ap, src, writes=[key])

        cload(ident[:], dr["c_ident"], "ident")
        cload(cb[:], dr["c_cb"], "cb")
        cload(cb2[:], dr["c_cb2"], "cb2")
        for h in range(2):
            cload(cmask[:, h * 1024:(h + 1) * 1024], dr["c_cmask"][:, h * 1024:(h + 1) * 1024], "cmask")
        cload(fb[:].rearrange("p a b -> p (a b)"), dr["c_fb"], "fb", "sp")
        cload(fbm[:].rearrange("p a b -> p (a b)"), dr["c_fbm"], "fbm", "sp")
        cload(selm[:].rearrange("p a b -> p (a b)"), dr["c_sel"], "selm")
        cload(invf[:], dr["c_invf"], "invf", "sp")
        V(lambda e: e.memset(onesc[:], 1.0), w=["onesc"])
        V(lambda e: e.memset(onef[:], 1.0), w=["onef"])
        V(lambda e: e.memset(epsf[:], EPS), w=["epsf"])
        V(lambda e: e.memset(VA[:, :, :, 64:65], 1.0), w=["VA1"])
        V(lambda e: e.memset(VW[:, :, :, 64:65], 1.0), w=["VW1"])
        V(lambda e: e.memset(kmT[:], 0.0), w=["kmT"])
        V(lambda e: e.memset(ksum[:], 0.0), w=["ksum"])
        V(lambda e: e.memset(kcmpT[:], 0.0), w=["kcmpT"])
        G(lambda e: e.memset(KA[64:128, :, :], 0.0), w=["KAind"])
        G(lambda e: e.memset(KW[:], 0.0), w=[("KW", 0), ("KW", 1), ("KW", 2), ("KW", 3)])
        V(lambda e: e.memset(vcmp[:], 0.0), w=["vcmp"])
        V(lambda e: e.memset(vcmp[:, :, 64:65], 1.0), w=["vcmp"])
        for g in range(NG):
            cload(vcmp[:, g, 65:97], dr["c_selw"], "vcmp")

        with ExitStack() as es2:
            E2 = es2.enter_context
            cT = E2(nc.sbuf_tensor("cT", [128, 8, 2], F32))
            cTe = E2(nc.sbuf_tensor("cTe", [128, 8, 2], F32))
            cTb = E2(nc.sbuf_tensor("cTb", [128, 8, 2], BF16))
            adaw = [E2(nc.sbuf_tensor("adaw%d" % i, [128, 8, 512], BF16)) for i in range(3)]
            adab = E2(nc.sbuf_tensor("adab", [2, 512], F32))
            grow = E2(nc.sbuf_tensor("grow", [2, D], F32))
            modr = [E2(nc.sbuf_tensor("modr%d" % i, [2, 512], F32)) for i in range(2)]
            V(lambda e: e.memset(cT[:], 0.0), w=["cT"])
            for b in range(n_seq):
                sch.dma("sp", cT[:, :, b:b + 1], dr["c"][b].rearrange("(k p o) -> p k o", p=128, o=1),
                        writes=["cT"], allow_slow_non_contiguous=True)
            A(lambda e: e.activation(out=cTe[:], in_=cT[:], func=AF.Exp, scale=-1.0), r=["cT"], w=["cTe"])
            V(lambda e: e.tensor_scalar_add(cTe[:], cTe[:], 1.0), r=["cTe"], w=["cTe"])
            V(lambda e: e.reciprocal(cTe[:], cTe[:]), r=["cTe"], w=["cTe"])
            V(lambda e: e.tensor_tensor(out=cTb[:], in0=cT[:], in1=cTe[:], op=ALU.mult), r=["cT", "cTe"], w=["cTb"])
            ci = 0
            for L in layers:
                for b in range(2):
                    sch.dma("sp", grow[b:b + 1, :], dr["norm_g"][L:L + 1, :], writes=["grow"])
                for ch in range(6):
                    slot = ci % 3
                    ms = ci % 2
                    ci += 1
                    sch.dma("pool", adaw[slot][:], dr["ada_w"][L, :, ch * 512:(ch + 1) * 512].rearrange("(k p) n -> p k n", p=128),
                            writes=[("adaw", slot)])
                    for b in range(2):
                        sch.dma("sp", adab[b:b + 1, :], dr["ada_b"][L:L + 1, ch * 512:(ch + 1) * 512], writes=["adab"])
                    pj = nxt("PJ")
                    for k in range(8):
                        P(lambda e, k=k, pj=pj, slot=slot: e.matmul(PJ[pj][0:2, :], cTb[:, k, :], adaw[slot][:, k, :],
                                                                     start=(k == 0), stop=(k == 7)),
                          r=["cTb", ("adaw", slot)], w=[("PJ", pj)])
                    V(lambda e, pj=pj, ms=ms: e.tensor_tensor(out=modr[ms][:], in0=PJ[pj][0:2, :], in1=adab[:], op=ALU.add),
                      r=[("PJ", pj), "adab"], w=[("modr", ms)])
                    if ch in (2, 3):
                        g0 = (ch - 2) * 512
                        V(lambda e, ms=ms, g0=g0: e.scalar_tensor_tensor(out=modr[ms][:], in0=modr[ms][:], scalar=1.0,
                                                                         in1=grow[:, g0:g0 + 512], op0=ALU.add, op1=ALU.mult),
                          r=[("modr", ms), "grow"], w=[("modr", ms)])
                        dst0 = g0
                    elif ch in (0, 1):
                        dst0 = D + ch * 512
                    else:
                        dst0 = 2 * D + (ch - 4) * 512
                    for b in range(n_seq):
                        sch.dma("sp", mod_d[b, L:L + 1, dst0:dst0 + 512], modr[ms][b:b + 1, :],
                                reads=[("modr", ms)], writes=[("mod", b, L)])
            sch.barrier()

        def rstd_from_ss(ss_ap, out_ap, n, keys_r, key_w):
            A(lambda e: e.activation(out=out_ap, in_=ss_ap, func=AF.Ln, scale=1.0 / n, bias=epsf[:, 0:1]), r=list(keys_r) + ["epsf"], w=[key_w])
            A(lambda e: e.activation(out=out_ap, in_=out_ap, func=AF.Exp, scale=-0.5), r=[key_w], w=[key_w])

        def load_norm_h(b, i, src_ap):
            xs = nxt("xt")
            sch.dma("sp", xt[xs][:], src_ap, reads=[("res", b, i)], writes=[("xt", xs)])
            A(lambda e: e.activation(out=hb[:], in_=xt[xs][:], func=AF.Square, accum_out=st[:, 0:1]),
              r=[("xt", xs)], w=["hb", "st0"])
            rstd_from_ss(st[:, 0:1], st[:, 1:2], D, ["st0"], "st1")
            V(lambda e: e.scalar_tensor_tensor(out=ezs[:], in0=xt[xs][:], scalar=st[:, 1:2], in1=modbc[:, 0, :],
                                               op0=ALU.mult, op1=ALU.mult), r=[("xt", xs), "st1", ("modbc", 0)], w=[("ezs", 0), ("ezs", 512)])
            V(lambda e: e.tensor_tensor(out=hb[:], in0=ezs[:], in1=modbc[:, 1, :], op=ALU.add), r=[("ezs", 0), ("ezs", 512), ("modbc", 1)], w=["hb"])
            pt = nxt("PT")
            for k in range(8):
                P(lambda e, k=k, pt=pt: e.transpose(PT[pt][:, k * 128:(k + 1) * 128], hb[:, k * 128:(k + 1) * 128], ident[:]),
                  r=["hb", "ident"], w=[("PT", pt)])
            hs = nxt("hT")
            V(lambda e: e.tensor_copy(out=hT[hs][:].rearrange("p a b -> p (a b)"), in_=PT[pt][:]),
              r=[("PT", pt)], w=[("hT", hs)])
            return xs, hs

        def project(hs, wcol0, ncols, wstride):
            pj = nxt("PJ")
            for k in range(8):
                P(lambda e, k=k, pj=pj: e.matmul(PJ[pj][:, 0:ncols], hT[hs][:, k, :],
                                                 W1[:, k * wstride + wcol0:k * wstride + wcol0 + ncols],
                                                 start=(k == 0), stop=(k == 7)),
                  r=[("hT", hs)] + [("W1", c) for c in range(wcol0 // 512, (wcol0 + ncols - 1) // 512 + 1)], w=[("PJ", pj)])
            return pj

        def head_norm_rope(pj, c0, nh, g_ap, cos_ap, sin_ap, nsin_ap, out_ap, wkey, alt=False):
            src = PJ[pj][:, c0:c0 + nh * 64]
            src3 = src.rearrange("p (a b) -> p a b", a=nh)
            if alt:
                sq_ = ezs[:, 0:nh * 64]
                qn_ = ezs[:, 256:256 + nh * 64].rearrange("p (a b) -> p a b", a=nh)
                rt_ = ezs[:, 512:512 + nh * 16].rearrange("p (a b) -> p a b", a=nh)
                rt2_ = ezs[:, 576:576 + nh * 16].rearrange("p (a b) -> p a b", a=nh)
                ssq_ = ezs[:, 640:640 + nh]
                rq_ = ezs[:, 648:648 + nh]
                ka = [("ezs", 0), ("ezs", 512)]
                k_sq = k_ssq = k_rq = k_qn = k_rt = k_rt2 = None
                ksq, kssq, krq, kqn, krt, krt2 = ka, ka, ka, ka, ka, ka
            else:
                sq_ = sq[:, 0:nh * 64]
                qn_ = qn[:, 0:nh, :]
                rt_ = rtmp[:, 0:nh, :]
                rt2_ = rtmp2[:, 0:nh, :]
                ssq_ = ssq[:, 0:nh]
                rq_ = rq[:, 0:nh]
                ksq, kssq, krq, kqn, krt, krt2 = ["sq"], ["ssq"], ["rq"], ["qn"], ["rtmp"], ["rtmp2"]
            A(lambda e: e.activation(out=sq_, in_=src, func=AF.Square), r=[("PJ", pj)], w=ksq)
            V(lambda e: e.tensor_reduce(out=ssq_, in_=sq_.rearrange("p (a b) -> p a b", a=nh), axis=AX.X, op=ALU.add), r=ksq, w=kssq)
            A(lambda e: e.activation(out=rq_, in_=ssq_, func=AF.Ln, scale=1.0 / HD, bias=epsf[:, 0:1]), r=kssq + ["epsf"], w=krq)
            A(lambda e: e.activation(out=rq_, in_=rq_, func=AF.Exp, scale=-0.5), r=krq, w=krq)
            V(lambda e: e.tensor_tensor(out=qn_, in0=src3, in1=rq_.unsqueeze(2).to_broadcast([128, nh, 64]), op=ALU.mult),
              r=[("PJ", pj)] + krq, w=kqn)
            V(lambda e: e.tensor_tensor(out=qn_, in0=qn_, in1=g_ap.unsqueeze(1).to_broadcast([128, nh, 64]), op=ALU.mult),
              r=kqn + ["gains"], w=kqn)
            cosb = cos_ap.unsqueeze(1).to_broadcast([128, nh, 8])
            sinb = sin_ap.unsqueeze(1).to_broadcast([128, nh, 8])
            nsinb = nsin_ap.unsqueeze(1).to_broadcast([128, nh, 8])
            V(lambda e: e.tensor_tensor(out=rt_[:, :, 0:8], in0=qn_[:, :, 0:8], in1=cosb, op=ALU.mult), r=kqn + ["rope", "ropeC"], w=krt)
            V(lambda e: e.tensor_tensor(out=rt_[:, :, 8:16], in0=qn_[:, :, 8:16], in1=cosb, op=ALU.mult), r=kqn + ["rope", "ropeC"], w=krt)
            V(lambda e: e.tensor_tensor(out=rt2_[:, :, 0:8], in0=qn_[:, :, 8:16], in1=nsinb, op=ALU.mult), r=kqn + ["rope", "ropeC"], w=krt2)
            V(lambda e: e.tensor_tensor(out=rt2_[:, :, 8:16], in0=qn_[:, :, 0:8], in1=sinb, op=ALU.mult), r=kqn + ["rope", "ropeC"], w=krt2)
            V(lambda e: e.tensor_tensor(out=out_ap[:, :, 0:16], in0=rt_, in1=rt2_, op=ALU.add), r=krt + krt2, w=[wkey])
            A(lambda e: e.activation(out=out_ap[:, :, 16:64], in_=qn_[:, :, 16:64], func=AF.Copy), r=kqn, w=[wkey])

        def merge_streams(fa, fb):
            outer = sch.capture
            sch.capture = []
            fa()
            la = sch.capture
            sch.capture = []
            fb()
            lb = sch.capture
            sch.capture = outer
            na, nb = len(la), len(lb)
            ia = ib = 0
            while ia < na or ib < nb:
                if ib >= nb or (ia < na and ia * nb <= ib * na):
                    it = la[ia]
                    ia += 1
                else:
                    it = lb[ib]
                    ib += 1
                if outer is not None:
                    outer.append(it)
                else:
                    sch.replay(it)

        def silu_from_psum(pj, n, out_ap, c0, wkey):
            src = PJ[pj][:, 0:n]
            ek = ("ezs", c0)
            A(lambda e: e.activation(out=ezs[:, c0:c0 + n], in_=src, func=AF.Exp, scale=-1.0), r=[("PJ", pj)], w=[ek])
            A(lambda e: e.activation(out=ezs[:, c0:c0 + n], in_=ezs[:, c0:c0 + n], func=AF.Ln, bias=onef[:, 0:1]), r=[ek, "onef"], w=[ek])
            A(lambda e: e.activation(out=ezs[:, c0:c0 + n], in_=ezs[:, c0:c0 + n], func=AF.Exp, scale=-1.0), r=[ek], w=[ek])
            V(lambda e: e.tensor_tensor(out=out_ap, in0=src, in1=ezs[:, c0:c0 + n], op=ALU.mult), r=[("PJ", pj), ek], w=[wkey])

        def transpose_heads(src_tile, ncol, dst_tile, rkeys, wkey, halves=(0, 1)):
            for half in halves:
                pt = nxt("PT")
                for hh in range(8):
                    h = half * 8 + hh
                    P(lambda e, h=h, hh=hh, pt=pt: e.transpose(PT[pt][0:ncol, hh * 128:(hh + 1) * 128], src_tile[:, h, 0:ncol], ident[:]),
                      r=rkeys + ["ident"], w=[("PT", pt)])
                if half == 0:
                    V(lambda e, pt=pt: e.tensor_copy(out=dst_tile[0:ncol, 0:8, :].rearrange("p a b -> p (a b)"), in_=PT[pt][0:ncol, :]),
                      r=[("PT", pt)], w=[wkey])
                else:
                    V(lambda e, pt=pt: e.tensor_copy(out=dst_tile[0:ncol, 8:16, :].rearrange("p a b -> p (a b)"), in_=PT[pt][0:ncol, :]),
                      r=[("PT", pt)], w=[wkey])

        def k_to_KT(dst_ap, wkey, ncopy=128):
            pt = nxt("PT")
            for g in range(NG):
                P(lambda e, g=g, pt=pt: e.transpose(PT[pt][0:64, g * 128:(g + 1) * 128], kb[:, g, :], ident[:]),
                  r=["kb", "ident"], w=[("PT", pt)])
            V(lambda e, pt=pt: e.tensor_copy(out=dst_ap, in_=PT[pt][0:64, 0:512].rearrange("p (a b) -> p a b", a=NG)[:, :, 0:ncopy]),
              r=[("PT", pt)], w=[wkey])

        class Attn:
            def __init__(self):
                self.pend = None

            def unit(self, lhsT_k, rhs_q, bias_ap, v_ap, vw, po, first, rk, on_done=None):
                ps = nxt("PS")
                P(lambda e: e.matmul(PS[ps][:], lhsT_k, rhs_q, start=True, stop=(bias_ap is None)),
                  r=rk, w=[("PS", ps)])
                if bias_ap is not None:
                    P(lambda e: e.matmul(PS[ps][:], ident[:], bias_ap, start=False, stop=True),
                      r=["ident", "cb", "cb2", "cmask"], w=[("PS", ps)])
                sl = nxt("pT", 3)
                pt_t = T["pT%d" % sl]
                A(lambda e: e.activation(out=pt_t[:], in_=PS[ps][:], func=AF.Exp, scale=0.125),
                  r=[("PS", ps)], w=[("pT", sl)])
                self.flush()
                self.pend = (pt_t, sl, v_ap, vw, po, first, rk, on_done)

            def flush(self):
                if self.pend is None:
                    return
                pt_t, sl, v_ap, vw, po, first, rk, on_done = self.pend
                self.pend = None
                for r in range(4):
                    P(lambda e, r=r: e.matmul(PO[po][:, r * vw:(r + 1) * vw], pt_t[:, r * 128:(r + 1) * 128], v_ap,
                                              start=(first and r == 0), stop=True, skip_group_check=True),
                      r=[("pT", sl)] + rk, w=[("PO", po)])
                if on_done is not None:
                    on_done()

        def branch_post(po, vw, g, gate_idx, first_branch, clamp=False, gates=None, sl=0):
            rs, cc, otmp, oacc = T["rs"], T["cc"], T["otmp"], T["oacc"][sl]
            pov = PO[po][:, 0:4 * vw].rearrange("p (a b) -> p a b", a=4)
            if clamp:
                V(lambda e: e.tensor_scalar(out=rs[:], in0=pov[:, :, 64:65].rearrange("p a b -> p (a b)"), scalar1=1e-30, scalar2=None, op0=ALU.max),
                  r=[("PO", po)], w=["rs"])
                V(lambda e: e.reciprocal(rs[:], rs[:]), r=["rs"], w=["rs"])
            else:
                V(lambda e: e.reciprocal(rs[:], pov[:, :, 64:65].rearrange("p a b -> p (a b)")), r=[("PO", po)], w=["rs"])
            if gate_idx is not None:
                V(lambda e: e.tensor_tensor(out=cc[:], in0=rs[:], in1=gates[:, 4 * g:4 * g + 4, gate_idx:gate_idx + 1].rearrange("p a b -> p (a b)"),
                                            op=ALU.mult), r=["rs", ("gates", 0), ("gates", 1)], w=["cc"])
                cap = cc
                ck = "cc"
            else:
                cap = rs
                ck = "rs"
            if first_branch:
                V(lambda e: e.tensor_tensor(out=oacc[:, 4 * g:4 * g + 4, :], in0=pov[:, :, 0:64],
                                            in1=cap[:].unsqueeze(2).to_broadcast([128, 4, 64]), op=ALU.mult),
                  r=[("PO", po), ck], w=[("oacc", sl, g)])
            else:
                V(lambda e: e.tensor_tensor(out=otmp[:], in0=pov[:, :, 0:64],
                                            in1=cap[:].unsqueeze(2).to_broadcast([128, 4, 64]), op=ALU.mult),
                  r=[("PO", po), ck], w=["otmp"])
                G(lambda e: e.tensor_tensor(out=oacc[:, 4 * g:4 * g + 4, :], in0=oacc[:, 4 * g:4 * g + 4, :], in1=otmp[:], op=ALU.add),
                  r=["otmp", ("oacc", sl, g)], w=[("oacc", sl, g)])

        def out_proj_store(b, i, xs, sl):
            oacc, og, ogT, ytmp, szb = T["oacc"][sl], T["og"], T["ogT"], T["ytmp"], T["szb"][sl]
            V(lambda e: e.tensor_tensor(out=og[:], in0=oacc[:].rearrange("p a b -> p (a b)"), in1=szb[:], op=ALU.mult),
              r=[("oacc", sl, g) for g in range(NG)] + [("szb", sl)], w=["og"])
            yield
            pt = nxt("PT")
            for k in range(8):
                P(lambda e, k=k, pt=pt: e.transpose(PT[pt][:, k * 128:(k + 1) * 128], og[:, k * 128:(k + 1) * 128], ident[:]),
                  r=["og", "ident"], w=[("PT", pt)])
            V(lambda e: e.tensor_copy(out=ogT[:].rearrange("p a b -> p (a b)"), in_=PT[pt][:]),
              r=[("PT", pt)], w=["ogT"])
            yield
            for half in range(2):
                pj = rr["PO"]
                for k in range(8):
                    P(lambda e, k=k, pj=pj, half=half: e.matmul(PO[pj][:], ogT[:, k, :], WO[:, k, half * 512:(half + 1) * 512],
                                                               start=(k == 0), stop=(k == 7)),
                      r=["ogT", ("WO", half)], w=[("PO", pj)])
                V(lambda e, pj=pj, half=half: e.tensor_tensor(out=ytmp[:], in0=PO[pj][:], in1=modbc[:, 2, half * 512:(half + 1) * 512], op=ALU.mult),
                  r=[("PO", pj), ("modbc", 2)], w=["ytmp"])
                G(lambda e, half=half: e.tensor_tensor(out=xt[xs][:, half * 512:(half + 1) * 512], in0=ytmp[:],
                                                       in1=xt[xs][:, half * 512:(half + 1) * 512], op=ALU.add),
                  r=["ytmp", ("xt", xs)], w=[("xt", xs)])
                if half == 0:
                    yield
            sch.dma("sp", y_d[b, i * 128:(i + 1) * 128, :], xt[xs][:], reads=[("xt", xs)], writes=[("res", b, i)])
            yield

        def load_w_cols(dst_col0, src_ap_fn, ncols, wstride, key="W1"):
            c = 0
            w1v = W1[:, 0:8 * wstride].rearrange("p (k n) -> p k n", k=8)
            while c < ncols:
                n = min(512 - (dst_col0 + c) % 512, ncols - c)
                sch.dma("pool", w1v[:, :, dst_col0 + c:dst_col0 + c + n], src_ap_fn(c, n), writes=[(key, (dst_col0 + c) // 512)])
                c += n

        def load_wo(wo_src):
            for hlf in range(2):
                sch.dma("pool", WO[:, :, hlf * 512:(hlf + 1) * 512],
                        wo_src[:, hlf * 512:(hlf + 1) * 512].rearrange("(k p) n -> p k n", p=128), writes=[("WO", hlf)])

        def bload(dst_ap, src_row_ap, n, key):
            sch.dma("sp", dst_ap, src_row_ap.to_broadcast([128, n]), writes=[key])

        at_box = [None]

        def at_flush():
            at_box[0].flush()

        def run_pipelined(front, back_units, tail, n_tiles, prologue=None):
            def capture_front(i):
                fixed_bank[0] = 0
                sch.capture = []
                crit = None
                for m in front(i):
                    if m == "crit":
                        crit = len(sch.capture)
                ops = sch.capture
                sch.capture = None
                fixed_bank[0] = 1
                return ops, (len(ops) if crit is None else crit)

            ops, crit_pos = capture_front(0)
            pos = 0
            while pos < crit_pos:
                sch.replay(ops[pos])
                pos += 1
            leftover = ops[pos:]
            if prologue is not None:
                prologue()
            tail_gen = iter(())
            DONE = object()
            for i in range(n_tiles):
                if i + 1 < n_tiles:
                    new_ops, new_crit = capture_front(i + 1)
                else:
                    new_ops, new_crit = [], 0
                n_left = len(leftover)
                ops = leftover + new_ops
                crit_pos = n_left + new_crit
                pos = 0
                units = back_units(i)
                nf = units.index("need_front") if "need_front" in units else len(units)
                nu_front = sum(1 for u in units[:nf] if u != "need_front")
                ui = 0
                tail_done = False
                for u in units:
                    if u == "need_front":
                        for _ in tail_gen:
                            pass
                        tail_done = True
                        while pos < crit_pos:
                            sch.replay(ops[pos])
                            pos += 1
                        continue
                    u()
                    if ui >= 1 and not tail_done:
                        if next(tail_gen, DONE) is DONE:
                            tail_done = True
                    lim = len(ops) if tail_done else n_left
                    if pos < lim:
                        if pos < crit_pos:
                            left = max(1, nu_front - ui - 1)
                            k = -(-(crit_pos - pos) // left)
                        else:
                            k = 3
                        for _ in range(k):
                            if pos < lim:
                                sch.replay(ops[pos])
                                pos += 1
                    ui += 1
                for _ in tail_gen:
                    pass
                while pos < crit_pos:
                    sch.replay(ops[pos])
                    pos += 1
                leftover = ops[pos:]
                tail_gen = tail(i)
            for it in leftover:
                sch.replay(it)
            at_flush()
            for _ in tail_gen:
                pass
            fixed_bank[0] = None

        def moba_layer(b, j, src_d):
            WS = MOBA_IN
            load_w_cols(0, lambda c, n: dr["moba_w_in"][j, :, c:c + n].rearrange("(k p) n -> p k n", p=128), MOBA_IN, WS)
            bload(gq[:], dr["moba_q_norm"][j:j + 1, :], 64, "gains")
            bload(gk[:, 0, :], dr["moba_k_norm"][j:j + 1, :], 64, "gains")
            for g in range(NG):
                sch.dma("pool", KA[64:72, g, :], dr["c_ind_moba"], writes=["KAind"])
            load_wo(dr["moba_w_out"][j])
            with ExitStack() as esc:
                alloc_attn_tiles(esc)
                sg, m8 = T["sg"], T["m8"]
                ctx = {}

                def front(i):
                    sl = i % 2
                    qaug, qT1, qaT, szb = T["qaug"][sl], T["qT1"][sl], T["qaT"][sl], T["szb"][sl]
                    xs, hs = load_norm_h(b, i, src_d[b, i * 128:(i + 1) * 128, :])
                    ctx[i] = xs
                    yield
                    cs, sn, nsn = cosT[:, i, :], sinT[:, i, :], nsinT[:, i, :]
                    pq0 = project(hs, 0, 512, WS)
                    pq1 = project(hs, 512, 512, WS)
                    head_norm_rope(pq0, 0, 8, gq[:], cs, sn, nsn, qaug[:, 0:8, 0:64], ("qaug", sl, 0))
                    pkv = project(hs, 1024, 512, WS)
                    transpose_heads(qaug, 64, qT1, [("qaug", sl, 0)], ("qT1", sl), halves=(0,))
                    merge_streams(lambda: head_norm_rope(pq1, 0, 8, gq[:], cs, sn, nsn, qaug[:, 8:16, 0:64], ("qaug", sl, 1)),
                                  lambda: head_norm_rope(pkv, 0, 4, gk[:, 0, :], cs, sn, nsn, kb[:], "kb", alt=True))
                    transpose_heads(qaug, 64, qT1, [("qaug", sl, 1)], ("qT1", sl), halves=(1,))
                    A(lambda e, pj=pkv, i=i: e.activation(out=VA[:, i, :, 0:64], in_=PJ[pj][:, 256:512].rearrange("p (a b) -> p a b", a=NG), func=AF.Copy),
                      r=[("PJ", pkv)], w=[("VA", i)])
                    k_to_KT(KA[0:64, :, i * 128:(i + 1) * 128], ("KA", i))
                    pj3 = nxt("PJ")
                    for h in range(NH):
                        P(lambda e, h=h, pj3=pj3: e.matmul(PJ[pj3][:, h * 8:(h + 1) * 8], qT1[0:64, h, :], kmT[0:64, h // 4, :], start=True, stop=True),
                          r=[("qT1", sl), "kmT"], w=[("PJ", pj3)])
                    V(lambda e, pj3=pj3, i=i: e.tensor_tensor(out=sg[:], in0=PJ[pj3][:, 0:128].rearrange("p (a b) -> p a b", a=NH),
                                                             in1=fbm[:, i, :].unsqueeze(1).to_broadcast([128, NH, 8]), op=ALU.add),
                      r=[("PJ", pj3), "fbm"], w=["sg"])
                    for h in range(NH):
                        V(lambda e, h=h: e.max(out=m8[:, h, :], in_=sg[:, h, :]), r=["sg"], w=[("m8", h)])
                    V(lambda e: e.tensor_tensor(out=qaug[:, :, 64:72], in0=sg[:], in1=m8[:, :, 3:4].to_broadcast([128, NH, 8]), op=ALU.is_lt),
                      r=["sg"] + [("m8", h) for h in range(NH)], w=[("qaug", sl, 2)])
                    transpose_heads(qaug, 72, qaT, [("qaug", sl, 0), ("qaug", sl, 1), ("qaug", sl, 2)], ("qaT", sl))
                    yield "crit"
                    pj2 = nxt("PJ")
                    for g in range(NG):
                        P(lambda e, g=g, pj2=pj2: e.matmul(PJ[pj2][0:64, g:g + 1], kb[:, g, :], onesc[:, 0:1], start=True, stop=True),
                          r=["kb", "onesc"], w=[("PJ", pj2)])
                    V(lambda e, pj2=pj2, i=i: e.tensor_copy(out=ksum[:, i, :], in_=PJ[pj2][0:64, 0:NG]), r=[("PJ", pj2)], w=["ksum"])
                    if i % 2 == 1:
                        jb = i // 2
                        V(lambda e, i=i: e.tensor_tensor(out=kmtmp[:], in0=ksum[:, i - 1, :], in1=ksum[:, i, :], op=ALU.add), r=["ksum"], w=["kmtmp"])
                        V(lambda e, jb=jb: e.tensor_scalar(out=kmT[:, :, jb:jb + 1].rearrange("p a b -> p (a b)"), in0=kmtmp[:], scalar1=1.0 / 256.0,
                                                           scalar2=None, op0=ALU.mult), r=["kmtmp"], w=["kmT"])
                    pz0 = project(hs, 1536, 512, WS)
                    pz1 = project(hs, 2048, 512, WS)
                    silu_from_psum(pz0, 512, szb[:, 0:512], 0, ("szb", sl))
                    silu_from_psum(pz1, 512, szb[:, 512:1024], 512, ("szb", sl))
                    yield

                at = Attn()

                def back_units(i):
                    sl = i % 2
                    qaT = T["qaT"][sl]
                    units = []
                    for g in range(NG):
                        po_box = []
                        for jj in range(i + 1):
                            def u(g=g, jj=jj, po_box=po_box):
                                if jj == 0:
                                    po_box.append(nxt("PO"))
                                po = po_box[0]
                                bias = cb[:].unsqueeze(1).to_broadcast([128, 4, 128]) if jj == i else None
                                done = (lambda: branch_post(po, 65, g, None, True, sl=sl)) if jj == i else None
                                at.unit(KA[:, g, jj * 128:(jj + 1) * 128], qaT[:, 4 * g:4 * g + 4, :], bias, VA[:, jj, g, :], 65, po,
                                        jj == 0, [("KA", jj), "KAind", ("qaT", sl), ("VA", jj), "VA1"], done)
                            units.append(u)
                    return units

                def tail(i):
                    return out_proj_store(b, i, ctx[i], i % 2)

                at_box[0] = at
                run_pipelined(front, back_units, tail, NT)
                sch.barrier()

        def nsa_layer(b, j, src_d):
            w_in = dr["nsa_w_in"]
            load_w_cols(0, lambda c, n: w_in[j, :, 1024 + c:1024 + c + n].rearrange("(k p) n -> p k n", p=128), 1024, 1024)
            CW1 = W1[:, 8192:16384].rearrange("p (a m h) -> p a m h", a=2, m=16)
            for kv in range(2):
                for mh in range(2):
                    sch.dma("pool", CW1[:, kv, mh * 8:(mh + 1) * 8, :],
                            dr["nsa_cmp_w1"][j, kv, mh * 1024:(mh + 1) * 1024, :].rearrange("(m p) h -> p m h", p=128), writes=[("CW1", kv, mh)])
                sch.dma("pool", cw2[:, kv, :, :], dr["nsa_cmp_w2"][j, kv].rearrange("(h p) d -> p h d", p=128), writes=["cw2"])
                sch.dma("sp", peT2f[:, kv, :], dr["nsa_cmp_pe"][j, kv].rearrange("l d -> (l d)").rearrange("(m p) -> p m", p=128),
                        writes=["peT2f"], allow_slow_non_contiguous=True)
                sch.dma("sp", b1c[:, kv, :], dr["nsa_cmp_b1"][j, kv].rearrange("(h p) -> p h", p=128), writes=["b1c"],
                        allow_slow_non_contiguous=True)
            V(lambda e: e.tensor_copy(out=peT2[:], in_=peT2f[:]), r=["peT2f"], w=["peT2"])
            bload(gq[:], dr["nsa_q_norm"][j:j + 1, :], 64, "gains")
            for m in range(3):
                bload(gk[:, m, :], dr["nsa_k_norm"][j, m:m + 1, :], 64, "gains")
            bload(gateb[:], dr["nsa_gate_b"][j:j + 1, :], 48, "gateb")
            for g in range(NG):
                sch.dma("pool", KA[64:96, g, :], dr["c_ind_nsa"], writes=["KAind"])
            load_wo(dr["nsa_w_out"][j])
            if DBG_STAGE[0] < 2:
                return
            with ExitStack() as esa:
                alloc_cmp_tiles(esa)
                KC2, hidT, hu, hw, kvb = T["KC2"], T["hidT"], T["hu"], T["hw"], T["kvb"]

                def a_stage2(i, hs):
                    pjA = project(hs, 0, 512, 1024)
                    pjB = project(hs, 512, 512, 1024)
                    A(lambda e, pjA=pjA: e.activation(out=kvb[:], in_=PJ[pjA][:], func=AF.Copy), r=[("PJ", pjA)], w=["kvb"])
                    head_norm_rope(pjB, 0, 4, gk[:, 1, :], cosT[:, i, :], sinT[:, i, :], nsinT[:, i, :], kb[:], "kb")
                    A(lambda e, pjB=pjB, i=i: e.activation(out=VA[:, i, :, 0:64], in_=PJ[pjB][:, 256:512].rearrange("p (a b) -> p a b", a=NG), func=AF.Copy),
                      r=[("PJ", pjB)], w=[("VA", i)])
                    pjs = nxt("PJ")
                    for kv in range(2):
                        for g in range(NG):
                            for par in range(2):
                                c0 = (kv * 4 + g) * 64
                                P(lambda e, kv=kv, g=g, par=par, c0=c0, pjs=pjs: e.matmul(
                                    PJ[pjs][par * 64:(par + 1) * 64, c0:c0 + 64], kvb[:, kv * 256 + g * 64:kv * 256 + (g + 1) * 64],
                                    selm[:, par, :], start=True, stop=True), r=["kvb", "selm"], w=[("PJ", pjs)])
                    V(lambda e, pjs=pjs, i=i: e.tensor_copy(out=KC2[:, :, :, i * 64:(i + 1) * 64],
                                                           in_=PJ[pjs][:].rearrange("p (a g u) -> p a g u", a=2, g=NG)),
                      r=[("PJ", pjs)], w=[("KC2", i)])
                    k_to_KT(KA[0:64, :, i * 128:(i + 1) * 128], ("KA", i))

                def cap(fn, pt):
                    pin_pt[0] = pt
                    sch.capture = []
                    r = fn()
                    ops_ = sch.capture
                    sch.capture = None
                    pin_pt[0] = None
                    return ops_, r

                def merge(o1, o2):
                    n1, n2 = len(o1), len(o2)
                    i1 = i2 = 0
                    while i1 < n1 or i2 < n2:
                        if i2 >= n2 or (i1 < n1 and i1 * n2 <= i2 * n1):
                            sch.replay(o1[i1])
                            i1 += 1
                        else:
                            sch.replay(o2[i2])
                            i2 += 1

                o1, (xs0, hs_prev) = cap(lambda: load_norm_h(b, 0, src_d[b, 0:128, :]), 0)
                merge(o1, [])
                for i in range(NT):
                    if i + 1 < NT:
                        o1, (xs1, hs_next) = cap(lambda i=i: load_norm_h(b, i + 1, src_d[b, (i + 1) * 128:(i + 2) * 128, :]), 0)
                    else:
                        o1, hs_next = [], None
                    o2, _ = cap(lambda i=i, hs_prev=hs_prev: a_stage2(i, hs_prev), 1)
                    merge(o1, o2)
                    hs_prev = hs_next
                if DBG_STAGE[0] < 3:
                    sch.barrier()
                    return
                kc2keys = [("KC2", i) for i in range(NT)]
                pjb = nxt("PJ")
                for kv in range(2):
                    for half in range(2):
                        col = kv * 2 + half
                        for m in range(16):
                            P(lambda e, kv=kv, half=half, m=m, col=col, pjb=pjb: e.matmul(
                                PJ[pjb][:, col:col + 1], CW1[:, kv, m, half * 128:(half + 1) * 128], peT2[:, kv, m:m + 1],
                                start=(m == 0), stop=(m == 15)), r=[("CW1", kv, m // 8), "peT2"], w=[("PJ", pjb)])
                V(lambda e, pjb=pjb: e.tensor_tensor(out=b1p[:].rearrange("p a b -> p (a b)"), in0=PJ[pjb][:, 0:4],
                                                    in1=b1c[:].rearrange("p a b -> p (a b)"), op=ALU.add), r=[("PJ", pjb), "b1c"], w=["b1p"])
                for kv in range(2):
                    for half in range(2):
                        ps_ = nxt("PJ")
                        for m in range(16):
                            P(lambda e, kv=kv, half=half, m=m, ps_=ps_: e.matmul(
                                PJ[ps_][:, 0:508], CW1[:, kv, m, half * 128:(half + 1) * 128], KC2[:, kv, :, m:m + 1009:8],
                                start=(m == 0), stop=(m == 15)), r=[("CW1", kv, m // 8)] + kc2keys, w=[("PJ", ps_)])
                        V(lambda e, kv=kv, half=half, ps_=ps_: e.tensor_scalar(out=hu[:], in0=PJ[ps_][:, 0:508], scalar1=b1p[:, kv, half:half + 1],
                                                                              scalar2=None, op0=ALU.add), r=[("PJ", ps_), "b1p"], w=["hu"])
                        V(lambda e: e.tensor_tensor(out=hw[:], in0=hu[:], in1=hu[:], op=ALU.mult), r=["hu"], w=["hw"])
                        V(lambda e: e.tensor_scalar(out=hw[:], in0=hw[:], scalar1=0.044715, scalar2=1.0, op0=ALU.mult, op1=ALU.add), r=["hw"], w=["hw"])
                        V(lambda e: e.tensor_tensor(out=hw[:], in0=hw[:], in1=hu[:], op=ALU.mult), r=["hw", "hu"], w=["hw"])
                        A(lambda e: e.activation(out=hw[:], in_=hw[:], func=AF.Exp, scale=-1.5957691216057308), r=["hw"], w=["hw"])
                        V(lambda e: e.tensor_scalar_add(hw[:], hw[:], 1.0), r=["hw"], w=["hw"])
                        V(lambda e: e.reciprocal(hw[:], hw[:]), r=["hw"], w=["hw"])
                        V(lambda e, kv=kv, half=half: e.tensor_tensor(out=hidT[:, kv * 2 + half, :, :].rearrange("p a b -> p (a b)"), in0=hu[:], in1=hw[:],
                                                                      op=ALU.mult), r=["hu", "hw"], w=["hidT"])
                pj2 = nxt("PJ")
                for kv in range(2):
                    for g in range(NG):
                        c0 = (kv * 4 + g) * 64
                        for half in range(2):
                            P(lambda e, kv=kv, g=g, half=half, c0=c0, pj2=pj2: e.matmul(
                                PJ[pj2][0:127, c0:c0 + 64], hidT[:, kv * 2 + half, g, :], cw2[:, kv, half, :],
                                start=(half == 0), stop=(half == 1)), r=["hidT", "cw2"], w=[("PJ", pj2)])
                head_norm_rope(pj2, 0, 4, gk[:, 0, :], cosC[:], sinC[:], nsinC[:], kb[:], "kb")
                A(lambda e, pj2=pj2: e.activation(out=vcmp[0:127, :, 0:64], in_=PJ[pj2][0:127, 256:512].rearrange("p (a b) -> p a b", a=NG), func=AF.Copy),
                  r=[("PJ", pj2)], w=["vcmp"])
                k_to_KT(kcmpT[0:64, :, 0:127], "kcmpT", 127)
                sch.barrier()
            if DBG_STAGE[0] < 4:
                return
            WS = 2608
            load_w_cols(0, lambda c, n: w_in[j, :, c:c + n].rearrange("(k p) n -> p k n", p=128), 1024, WS)
            load_w_cols(1024, lambda c, n: w_in[j, :, 2048 + c:2048 + c + n].rearrange("(k p) n -> p k n", p=128), 1584, WS)
            with ExitStack() as esc:
                alloc_attn_tiles(esc)
                gtmp, imp, m8, rs = T["gtmp"], T["imp"], T["m8"], T["rs"]
                ctx = {}

                def front(i):
                    sl = i % 2
                    qaug, qT1, szb, gates = T["qaug"][sl], T["qT1"][sl], T["szb"][sl], T["gates"][sl]
                    xs, hs = load_norm_h(b, i, src_d[b, i * 128:(i + 1) * 128, :])
                    ctx[i] = xs
                    yield
                    cs, sn, nsn = cosT[:, i, :], sinT[:, i, :], nsinT[:, i, :]
                    slot = i % 4
                    pq0 = project(hs, 0, 512, WS)
                    pq1 = project(hs, 512, 512, WS)
                    head_norm_rope(pq0, 0, 8, gq[:], cs, sn, nsn, qaug[:, 0:8, 0:64], ("qaug", sl, 0))
                    pkw = project(hs, 1024, 512, WS)
                    transpose_heads(qaug, 64, qT1, [("qaug", sl, 0)], ("qT1", sl), halves=(0,))
                    merge_streams(lambda: head_norm_rope(pq1, 0, 8, gq[:], cs, sn, nsn, qaug[:, 8:16, 0:64], ("qaug", sl, 1)),
                                  lambda: head_norm_rope(pkw, 0, 4, gk[:, 2, :], cs, sn, nsn, kb[:], "kb", alt=True))
                    transpose_heads(qaug, 64, qT1, [("qaug", sl, 1)], ("qT1", sl), halves=(1,))
                    pgl = project(hs, 1536, 48, WS)
                    A(lambda e, pj=pkw, slot=slot: e.activation(out=VW[:, slot, :, 0:64], in_=PJ[pj][:, 256:512].rearrange("p (a b) -> p a b", a=NG), func=AF.Copy),
                      r=[("PJ", pkw)], w=[("VW", slot)])
                    k_to_KT(KW[0:64, :, slot, :], ("KW", slot))
                    V(lambda e, pj=pgl: e.tensor_tensor(out=gtmp[:], in0=PJ[pj][:, 0:48], in1=gateb[:], op=ALU.add), r=[("PJ", pgl), "gateb"], w=["gtmp"])
                    A(lambda e: e.activation(out=gtmp[:], in_=gtmp[:], func=AF.Exp, scale=-1.0), r=["gtmp"], w=["gtmp"])
                    A(lambda e: e.activation(out=gtmp[:], in_=gtmp[:], func=AF.Ln, bias=onef[:, 0:1]), r=["gtmp", "onef"], w=["gtmp"])
                    A(lambda e: e.activation(out=gates[:].rearrange("p a b -> p (a b)"), in_=gtmp[:], func=AF.Exp, scale=-1.0), r=["gtmp"], w=[("gates", sl)])
                    yield "crit"
                    pz0 = project(hs, 1584, 512, WS)
                    pz1 = project(hs, 2096, 512, WS)
                    silu_from_psum(pz0, 512, szb[:, 0:512], 0, ("szb", sl))
                    silu_from_psum(pz1, 512, szb[:, 512:1024], 512, ("szb", sl))
                    yield

                at = Attn()

                def cmp_post(po, g, i):
                    sl = i % 2
                    qaug = T["qaug"][sl]
                    branch_post(po, 97, g, 0, True, clamp=True, gates=T["gates"][sl], sl=sl)
                    pov = PO[po][:, 0:388].rearrange("p (a b) -> p a b", a=4)
                    V(lambda e: e.tensor_scalar(out=imp[:, g, :], in0=pov[:, 0, 65:97], scalar1=rs[:, 0:1], scalar2=None, op0=ALU.mult),
                      r=[("PO", po), "rs"], w=[("imp", g)])
                    for r_ in range(1, 4):
                        V(lambda e, r_=r_: e.scalar_tensor_tensor(out=imp[:, g, :], in0=pov[:, r_, 65:97], scalar=rs[:, r_:r_ + 1], in1=imp[:, g, :],
                                                                  op0=ALU.mult, op1=ALU.add), r=[("PO", po), "rs", ("imp", g)], w=[("imp", g)])
                    V(lambda e: e.tensor_tensor(out=imp[:, g, :], in0=imp[:, g, :], in1=fb[:, i, :], op=ALU.add), r=[("imp", g), "fb"], w=[("imp", g)])
                    V(lambda e: e.max(out=m8[:, g, :], in_=imp[:, g, :]), r=[("imp", g)], w=[("m8", g)])
                    V(lambda e: e.tensor_tensor(out=qaug[:, 4 * g:4 * g + 4, 64:96], in0=imp[:, g, :].unsqueeze(1).to_broadcast([128, 4, 32]),
                                                in1=m8[:, g, 7:8].unsqueeze(1).to_broadcast([128, 4, 32]), op=ALU.is_lt),
                      r=[("imp", g), ("m8", g)], w=[("qaug", sl, 2)])

                def cmp_unit(i, g):
                    sl = i % 2
                    qT1 = T["qT1"][sl]

                    def u():
                        po = nxt("PO")
                        at.unit(kcmpT[:, g, :], qT1[:, 4 * g:4 * g + 4, :], cmask[:, i * 128:(i + 1) * 128].unsqueeze(1).to_broadcast([128, 4, 128]),
                                vcmp[:, g, :], 97, po, True, ["kcmpT", ("qT1", sl), "vcmp"], (lambda: cmp_post(po, g, i)))
                    return u

                def back_units(i):
                    sl = i % 2
                    qaug, qT1, qaT, gates = T["qaug"][sl], T["qT1"][sl], T["qaT"][sl], T["gates"][sl]
                    units = []
                    tiles = [jj for jj in (i - 2, i - 1, i) if jj >= 0]
                    for g in range(NG):
                        po_box = []
                        for idx, jj in enumerate(tiles):
                            def u(g=g, jj=jj, idx=idx, po_box=po_box):
                                if idx == 0:
                                    po_box.append(nxt("PO"))
                                po = po_box[0]
                                if jj == i:
                                    bias = cb[:].unsqueeze(1).to_broadcast([128, 4, 128])
                                elif jj == i - 2:
                                    bias = cb2[:].unsqueeze(1).to_broadcast([128, 4, 128])
                                else:
                                    bias = None
                                done = (lambda: branch_post(po, 65, g, 2, False, gates=gates, sl=sl)) if jj == i else None
                                at.unit(KW[:, g, jj % 4, :], qT1[:, 4 * g:4 * g + 4, :], bias, VW[:, jj % 4, g, :], 65, po, idx == 0,
                                        [("KW", jj % 4), ("qT1", sl), ("VW", jj % 4), "VW1"], done)
                            units.append(u)

                    def tr():
                        at.flush()
                        transpose_heads(qaug, 96, qaT, [("qaug", sl, 0), ("qaug", sl, 1), ("qaug", sl, 2)], ("qaT", sl))
                    units.append(tr)
                    for g in range(NG):
                        po_box = []
                        for jj in range(i + 1):
                            def u(g=g, jj=jj, po_box=po_box):
                                if jj == 0:
                                    po_box.append(nxt("PO"))
                                po = po_box[0]
                                bias = cb[:].unsqueeze(1).to_broadcast([128, 4, 128]) if jj == i else None
                                done = (lambda: branch_post(po, 65, g, 1, False, gates=gates, sl=sl)) if jj == i else None
                                at.unit(KA[:, g, jj * 128:(jj + 1) * 128], qaT[:, 4 * g:4 * g + 4, :], bias, VA[:, jj, g, :], 65, po,
                                        jj == 0, [("KA", jj), "KAind", ("qaT", sl), ("VA", jj), "VA1"], done)
                            units.append(u)
                        if i + 1 < NT:
                            if g == 0:
                                units.append("need_front")
                            units.append(cmp_unit(i + 1, g))
                    return units

                def tail(i):
                    return out_proj_store(b, i, ctx[i], i % 2)

                def prologue():
                    for g in range(NG):
                        cmp_unit(0, g)()

                at_box[0] = at
                run_pipelined(front, back_units, tail, NT, prologue)
                sch.barrier()

        for b in range(n_seq):
            with ExitStack() as es3:
                E3 = es3.enter_context
                posi = E3(nc.sbuf_tensor("posi%d" % b, [128, NT + 1], I32))
                posf = E3(nc.sbuf_tensor("posf%d" % b, [128, NT + 1], F32))
                ang = E3(nc.sbuf_tensor("ang%d" % b, [128, NT + 1, 8], F32))
                kq = E3(nc.sbuf_tensor("kq%d" % b, [128, NT + 1, 8], F32))
                ki = E3(nc.sbuf_tensor("ki%d" % b, [128, NT + 1, 8], I32))
                red = E3(nc.sbuf_tensor("red%d" % b, [128, NT + 1, 8], F32))
                V(lambda e: e.memset(posi[:], 0), w=["posi"])
                sch.dma("sp", posi[:, 0:NT], dr["positions"][b].rearrange("(i p) -> p i", p=128), writes=["posi"],
                        allow_slow_non_contiguous=True)
                sch.dma("sp", posi[0:127, NT:NT + 1], dr["positions"][b, 16:16 + 16 * 127].rearrange("(n s) -> n s", s=16)[:, 15:16],
                        writes=["posi"], allow_slow_non_contiguous=True)
                V(lambda e: e.tensor_copy(out=posf[:], in_=posi[:]), r=["posi"], w=["posf"])
                V(lambda e: e.tensor_tensor(out=ang[:], in0=posf[:].unsqueeze(2).to_broadcast([128, NT + 1, 8]),
                                            in1=invf[:].unsqueeze(1).to_broadcast([128, NT + 1, 8]), op=ALU.mult),
                  r=["posf", "invf"], w=["ang"])
                C1 = 6.28125
                C2 = float(2.0 * np.pi - 6.28125)
                PI_LO = 3.1415925
                for which, off in (("sin", 0.0), ("cos", 0.25)):
                    V(lambda e, off=off: e.tensor_scalar(out=kq[:], in0=ang[:], scalar1=float(1.0 / (2.0 * np.pi)), scalar2=off,
                                                         op0=ALU.mult, op1=ALU.add), r=["ang"], w=["kq"])
                    V(lambda e: e.tensor_copy(out=ki[:], in_=kq[:]), r=["kq"], w=["ki"])
                    V(lambda e: e.tensor_copy(out=kq[:], in_=ki[:]), r=["ki"], w=["kq"])
                    V(lambda e: e.scalar_tensor_tensor(out=red[:], in0=kq[:], scalar=-C1, in1=ang[:], op0=ALU.mult, op1=ALU.add),
                      r=["kq", "ang"], w=["red"])
                    V(lambda e: e.scalar_tensor_tensor(out=red[:], in0=kq[:], scalar=-C2, in1=red[:], op0=ALU.mult, op1=ALU.add),
                      r=["kq", "red"], w=["red"])
                    if which == "cos":
                        V(lambda e: e.tensor_scalar(out=red[:], in0=red[:], scalar1=float(np.pi / 2), scalar2=PI_LO,
                                                    op0=ALU.add, op1=ALU.min), r=["red"], w=["red"])
                        V(lambda e: e.tensor_scalar(out=red[:], in0=red[:], scalar1=-PI_LO, scalar2=None, op0=ALU.max), r=["red"], w=["red"])
                        A(lambda e: e.activation(out=cosT[:], in_=red[:, 0:NT, :], func=AF.Sin), r=["red"], w=["rope"])
                        A(lambda e: e.activation(out=cosC[:], in_=red[:, NT, :], func=AF.Sin), r=["red"], w=["ropeC"])
                    else:
                        V(lambda e: e.tensor_scalar(out=red[:], in0=red[:], scalar1=PI_LO, scalar2=-PI_LO,
                                                    op0=ALU.min, op1=ALU.max), r=["red"], w=["red"])
                        A(lambda e: e.activation(out=sinT[:], in_=red[:, 0:NT, :], func=AF.Sin), r=["red"], w=["rope"])
                        A(lambda e: e.activation(out=sinC[:], in_=red[:, NT, :], func=AF.Sin), r=["red"], w=["ropeC"])
                V(lambda e: e.tensor_scalar(out=nsinT[:], in0=sinT[:], scalar1=-1.0, scalar2=None, op0=ALU.mult), r=["rope"], w=["rope"])
                V(lambda e: e.tensor_scalar(out=nsinC[:], in0=sinC[:], scalar1=-1.0, scalar2=None, op0=ALU.mult), r=["ropeC"], w=["ropeC"])
                sch.barrier()

            for li, L in enumerate(layers):
                src_d = dr["x"] if (li == 0 and first_from_x) else y_d
                is_nsa = (L % 2 == 0)
                j = L // 2
                for m in range(3):
                    sch.dma("sp", modbc[:, m, :], mod_d[b, L:L + 1, m * D:(m + 1) * D].to_broadcast([128, D]),
                            reads=[("mod", b, L)], writes=[("modbc", m)])
                if is_nsa:
                    nsa_layer(b, j, src_d)
                else:
                    moba_layer(b, j, src_d)
        sch.finish()
    return nc


_CONSTS = None


def _consts():
    global _CONSTS
    if _CONSTS is None:
        _CONSTS = host_consts()
    return _CONSTS


def run_layers(inputs, layers, n_seq, n_cores, first_from_x=True, trace=False):
    nc = build(layers, n_seq, first_from_x)
    cst = _consts()
    in_maps = []
    for c in range(n_cores):
        m = {}
        m["x"] = np.ascontiguousarray(inputs["x"][c * n_seq:(c + 1) * n_seq], dtype=np.float32)
        m["c"] = np.ascontiguousarray(inputs["c"][c * n_seq:(c + 1) * n_seq], dtype=np.float32)
        m["positions"] = np.ascontiguousarray(inputs["positions"][c * n_seq:(c + 1) * n_seq], dtype=np.int32)
        for k in WEIGHT_SHAPES:
            m[k] = np.ascontiguousarray(inputs[k], dtype=np.float32)
        for k in CONST_SHAPES:
            m[k] = cst[k]
        in_maps.append(m)
    res = run_bass_kernel_spmd(nc, in_maps, core_ids=list(range(n_cores)), trace=trace)
    out = np.concatenate([np.asarray(r["y"]) for r in res.results], axis=0)
    return out, res


def kernel(**inputs):
    out, _ = run_layers(inputs, [0, 1, 2, 3], 2, N_CORES)
    return out.astype(np.float32)
```

```python
import numpy as np
from contextlib import ExitStack
import concourse.bass as bass
import concourse.mybir as mybir
from concourse.bass_utils import run_bass_kernel_spmd

F32 = mybir.dt.float32
BF16 = mybir.dt.bfloat16
I32 = mybir.dt.int32
AF = mybir.ActivationFunctionType
ALU = mybir.AluOpType
AX = mybir.AxisListType

S = 2048
D = 1024
NT = 16
HD = 64
NH = 16
NG = 4
NEG = -30000.0
EPS = 1e-6
N_CORES = 8
NSA_IN = 3632
MOBA_IN = 2560
NDMA = 24
FRESH_POOL_SEMS = [False]
POOL_DESC_LIMIT = 6144


class Sched:
    def __init__(self, nc, es):
        self.nc = nc
        self.es = es
        self.nfresh = 0
        self.capture = None
        self.pool_fifo = []
        self.pool_desc = 0
        self.eng = {"pe": nc.tensor, "act": nc.scalar, "dve": nc.vector, "pool": nc.gpsimd, "sp": nc.sync}
        self.semh = {}
        for k in self.eng:
            self.semh[("e", k)] = es.enter_context(nc.semaphore("sem_" + k))
        for i in range(NDMA):
            self.semh[("d", i)] = es.enter_context(nc.semaphore("dsem%d" % i))
        self.cnt = {k: 0 for k in self.eng}
        self.dcnt = [0] * NDMA
        self.drr = 0
        self.seen = {k: {} for k in self.eng}
        self.lastw = {}
        self.readers = {}

    def _deps(self, eng, reads, writes, strict=False):
        deps = {}

        def add(t, same_ok):
            sk, v = t
            if same_ok and not strict and sk == ("e", eng):
                return
            if deps.get(sk, 0) < v:
                deps[sk] = v

        for k in reads:
            if k in self.lastw:
                add(self.lastw[k], False)
        for k in writes:
            if k in self.lastw:
                add(self.lastw[k], True)
            for sk, v in self.readers.get(k, {}).items():
                add((sk, v), True)
        return deps

    def _wait(self, eng, deps):
        for sk, v in deps.items():
            if eng == "pe" and sk == ("e", "pe"):
                continue
            if self.seen[eng].get(sk, 0) < v:
                self.eng[eng].wait_ge(self.semh[sk], v)
                self.seen[eng][sk] = v

    def _record(self, tok, reads, writes):
        for k in writes:
            self.lastw[k] = tok
            self.readers[k] = {}
        for k in reads:
            d = self.readers.setdefault(k, {})
            if d.get(tok[0], 0) < tok[1]:
                d[tok[0]] = tok[1]

    def op(self, eng, fn, reads=(), writes=()):
        if self.capture is not None:
            self.capture.append(("op", eng, fn, tuple(reads), tuple(writes), None))
            return
        self._wait(eng, self._deps(eng, reads, writes))
        inst = fn(self.eng[eng])
        self.cnt[eng] += 1
        inst.then_inc(self.semh[("e", eng)], 1)
        self._record((("e", eng), self.cnt[eng]), reads, writes)

    def replay(self, item):
        kind, a, b, reads, writes, kw = item
        if kind == "op":
            self.op(a, b, reads, writes)
        else:
            self.dma(a, b[0], b[1], reads, writes, **kw)

    def dma(self, queue, out, in_, reads=(), writes=(), **kw):
        if self.capture is not None:
            self.capture.append(("dma", queue, (out, in_), tuple(reads), tuple(writes), dict(kw)))
            return
        if FRESH_POOL_SEMS[0] and queue == "pool":
            kw.pop("ndesc", None)
            idx = NDMA + self.nfresh
            self.nfresh += 1
            self.semh[("d", idx)] = self.es.enter_context(self.nc.semaphore("fsem%d" % idx))
            self.dcnt.append(0)
            self._wait(queue, self._deps(queue, reads, writes, strict=True))
            inst = self.eng[queue].dma_start(out=out, in_=in_, **kw)
            self.dcnt[idx] += 16
            inst.then_inc(self.semh[("d", idx)], 16)
            self._record((("d", idx), self.dcnt[idx]), reads, writes)
            return
        ndesc = kw.pop("ndesc", 1024)
        idx = self.drr
        self.drr = (idx + 1) % NDMA
        if queue == "pool":
            while self.pool_fifo and self.pool_desc + ndesc > POOL_DESC_LIMIT:
                sk, v, nd = self.pool_fifo.pop(0)
                self.pool_desc -= nd
                if self.seen["pool"].get(sk, 0) < v:
                    self.eng["pool"].wait_ge(self.semh[sk], v)
                    self.seen["pool"][sk] = v
        deps = self._deps(queue, reads, writes, strict=True)
        if self.dcnt[idx]:
            deps[("d", idx)] = max(deps.get(("d", idx), 0), self.dcnt[idx])
        self._wait(queue, deps)
        inst = self.eng[queue].dma_start(out=out, in_=in_, **kw)
        self.dcnt[idx] += 16
        inst.then_inc(self.semh[("d", idx)], 16)
        self._record((("d", idx), self.dcnt[idx]), reads, writes)
        if queue == "pool":
            self.pool_fifo.append((("d", idx), self.dcnt[idx], ndesc))
            self.pool_desc += ndesc

    def barrier(self):
        for k in self.eng:
            for k2 in self.eng:
                if k2 != k and self.cnt[k2] > self.seen[k].get(("e", k2), 0):
                    self.eng[k].wait_ge(self.semh[("e", k2)], self.cnt[k2])
                    self.seen[k][("e", k2)] = self.cnt[k2]
            for i in range(len(self.dcnt)):
                if self.dcnt[i] > self.seen[k].get(("d", i), 0):
                    self.eng[k].wait_ge(self.semh[("d", i)], self.dcnt[i])
                    self.seen[k][("d", i)] = self.dcnt[i]

    def finish(self):
        e = self.eng["sp"]
        for i in range(len(self.dcnt)):
            if self.dcnt[i] and self.seen["sp"].get(("d", i), 0) < self.dcnt[i]:
                e.wait_ge(self.semh[("d", i)], self.dcnt[i])
        for k in self.eng:
            if k != "sp" and self.cnt[k] and self.seen["sp"].get(("e", k), 0) < self.cnt[k]:
                e.wait_ge(self.semh[("e", k)], self.cnt[k])


def host_consts():
    p = np.arange(128)[:, None]
    f = np.arange(128)[None, :]
    c = {}
    c["c_ident"] = (p == f).astype(np.float32)
    c["c_cb"] = np.where(p > f, NEG, 0.0).astype(np.float32)
    c["c_cb2"] = np.where(p <= f, NEG, 0.0).astype(np.float32)
    t = np.arange(S)[None, :]
    cm = np.where((16 * p + 31 > t) | (p == 127), NEG, 0.0)
    c["c_cmask"] = cm.astype(np.float32)
    fb = np.zeros((128, NT, 32), np.float32)
    for i in range(NT):
        cur = (128 * i + np.arange(128)) // 64
        for j in range(32):
            fb[:, i, j] = np.where((j == 0) | (j == cur) | (j == cur - 1), 100.0, 0.0)
    c["c_fb"] = fb.reshape(128, NT * 32)
    fbm = np.zeros((128, NT, 8), np.float32)
    for i in range(NT):
        own = i // 2
        for j in range(8):
            fbm[:, i, j] = 0.0 if j < own else (1e4 if j == own else -1e4)
    c["c_fbm"] = fbm.reshape(128, NT * 8)
    key = np.arange(S)[None, :]
    c["c_ind_nsa"] = np.where(key // 64 == np.arange(32)[:, None], NEG, 0.0).astype(np.float32)
    c["c_ind_moba"] = np.where(key // 256 == np.arange(8)[:, None], NEG, 0.0).astype(np.float32)
    cs = np.arange(127)[:, None] * 16
    ss = np.arange(32)[None, :] * 64
    shared = np.clip(np.minimum(cs + 32, ss + 64) - np.maximum(cs, ss), 0, None)
    sw = np.zeros((128, 32), np.float32)
    sw[:127] = shared / 32.0
    c["c_selw"] = sw
    sel = np.zeros((128, 2, 64), np.float32)
    for par in range(2):
        for u in range(64):
            sel[2 * u + par, par, u] = 1.0
    c["c_sel"] = sel.reshape(128, 128)
    inv = (np.float32(500000.0) ** (-np.arange(0, 16, 2, dtype=np.float32) / np.float32(16))).astype(np.float32)
    c["c_invf"] = np.tile(inv[None, :], (128, 1)).astype(np.float32)
    return c


CONST_SHAPES = {
    "c_ident": [128, 128], "c_cb": [128, 128], "c_cb2": [128, 128], "c_cmask": [128, S],
    "c_fb": [128, NT * 32], "c_fbm": [128, NT * 8], "c_ind_nsa": [32, S], "c_ind_moba": [8, S],
    "c_selw": [128, 32], "c_sel": [128, 128], "c_invf": [128, 8],
}

WEIGHT_SHAPES = {
    "norm_g": [4, D], "ada_w": [4, D, 3 * D], "ada_b": [4, 3 * D],
    "nsa_w_in": [2, D, NSA_IN], "nsa_w_out": [2, D, D], "nsa_q_norm": [2, 64], "nsa_k_norm": [2, 3, 64],
    "nsa_cmp_pe": [2, 2, 32, 64], "nsa_cmp_w1": [2, 2, 2048, 256], "nsa_cmp_b1": [2, 2, 256],
    "nsa_cmp_w2": [2, 2, 256, 64], "nsa_gate_b": [2, 48],
    "moba_w_in": [2, D, MOBA_IN], "moba_w_out": [2, D, D], "moba_q_norm": [2, 64], "moba_k_norm": [2, 64],
}


DBG_STAGE = [9]


def build(layers, n_seq, first_from_x=True):
    nc = bass.Bass("TRN2", target_bir_lowering=False, dynamic_dma_scratch_size=8192)
    dr = {}
    dr["x"] = nc.dram_tensor("x", [n_seq, S, D], F32, kind="ExternalInput").ap()
    dr["c"] = nc.dram_tensor("c", [n_seq, D], F32, kind="ExternalInput").ap()
    dr["positions"] = nc.dram_tensor("positions", [n_seq, S], I32, kind="ExternalInput").ap()
    for k, shp in WEIGHT_SHAPES.items():
        dr[k] = nc.dram_tensor(k, shp, F32, kind="ExternalInput").ap()
    for k, shp in CONST_SHAPES.items():
        dr[k] = nc.dram_tensor(k, shp, F32, kind="ExternalInput").ap()
    y_d = nc.dram_tensor("y", [n_seq, S, D], F32, kind="ExternalOutput").ap()
    mod_d = nc.dram_tensor("mod_scr", [n_seq, 4, 3 * D], F32, kind="Internal").ap()

    with ExitStack() as es:
        E = es.enter_context
        sch = Sched(nc, es)

        def V(fn, r=(), w=()):
            sch.op("dve", fn, r, w)

        def A(fn, r=(), w=()):
            sch.op("act", fn, r, w)

        def G(fn, r=(), w=()):
            sch.op("pool", fn, r, w)

        def P(fn, r=(), w=()):
            sch.op("pe", fn, r, w)

        def sb(name, shape, dt):
            return E(nc.sbuf_tensor(name, shape, dt))

        T = {}

        def sb(name, shape, dt, stack=None):
            t = (stack or es).enter_context(nc.sbuf_tensor(name, shape, dt))
            T[name] = t
            return t

        ident = sb("ident", [128, 128], BF16)
        cb = sb("cb", [128, 128], BF16)
        cb2 = sb("cb2", [128, 128], BF16)
        cmask = sb("cmask", [128, S], BF16)
        fb = sb("fb", [128, NT, 32], F32)
        fbm = sb("fbm", [128, NT, 8], F32)
        selm = sb("selm", [128, 2, 64], BF16)
        invf = sb("invf", [128, 8], F32)
        onesc = sb("onesc", [128, 1], BF16)
        W1 = sb("W1", [128, 20864], BF16)
        WO = sb("WO", [128, 8, D], BF16)
        KA = sb("KA", [128, NG, S], BF16)
        VA = sb("VA", [128, NT, NG, 65], BF16)
        KW = sb("KW", [128, NG, 4, 128], BF16)
        VW = sb("VW", [128, 4, NG, 65], BF16)
        onef = sb("onef", [128, 1], F32)
        epsf = sb("epsf", [128, 1], F32)
        kcmpT = sb("kcmpT", [128, NG, 128], BF16)
        vcmp = sb("vcmp", [128, NG, 97], BF16)
        modbc = sb("modbc", [128, 3, D], F32)
        cosT = sb("cosT", [128, NT, 8], F32)
        sinT = sb("sinT", [128, NT, 8], F32)
        nsinT = sb("nsinT", [128, NT, 8], F32)
        cosC = sb("cosC", [128, 8], F32)
        sinC = sb("sinC", [128, 8], F32)
        nsinC = sb("nsinC", [128, 8], F32)
        gq = sb("gq", [128, 64], F32)
        gk = sb("gk", [128, 3, 64], F32)
        gateb = sb("gateb", [128, 48], F32)
        kmT = sb("kmT", [64, NG, 8], BF16)
        kmtmp = sb("kmtmp", [64, NG], F32)
        ksum = sb("ksum", [64, NT, NG], F32)
        cw2 = sb("cw2", [128, 2, 2, 64], BF16)
        peT2 = sb("peT2", [128, 2, 16], BF16)
        peT2f = sb("peT2f", [128, 2, 16], F32)
        b1c = sb("b1c", [128, 2, 2], F32)
        b1p = sb("b1p", [128, 2, 2], F32)
        xt = [sb("xt%d" % i, [128, D], F32) for i in range(2)]
        st = sb("st", [128, 8], F32)
        hb = sb("hb", [128, D], BF16)
        hT = [sb("hT%d" % i, [128, 8, 128], BF16) for i in range(2)]
        sq = sb("sq", [128, 512], F32)
        ssq = sb("ssq", [128, 16], F32)
        rq = sb("rq", [128, 16], F32)
        qn = sb("qn", [128, 8, 64], F32)
        rtmp = sb("rtmp", [128, 8, 16], F32)
        rtmp2 = sb("rtmp2", [128, 8, 16], F32)
        kb = sb("kb", [128, NG, 64], BF16)
        ezs = sb("ezs", [128, D], F32)
        PJ = [E(nc.psum_tensor("PJ%d" % i, [128, 512], F32)) for i in range(2)]
        PS = [E(nc.psum_tensor("PS%d" % i, [128, 512], F32)) for i in range(2)]
        PO = [E(nc.psum_tensor("PO%d" % i, [128, 512], F32)) for i in range(2)]
        PT = [E(nc.psum_tensor("PT%d" % i, [128, 1024], BF16)) for i in range(2)]
        rr = {"PJ": 0, "PS": 0, "PO": 0, "PT": 0, "pT": 0, "xt": 0, "hT": 0}
        uid = [0]

        fixed_bank = [None]
        pin_pt = [None]

        def nxt(k, n=2):
            if pin_pt[0] is not None and k == "PT":
                return pin_pt[0]
            if fixed_bank[0] is not None and k == "PT":
                return fixed_bank[0]
            if fixed_bank[0] == 1 and k == "PJ":
                raise RuntimeError("atomic-mode code must not allocate PJ banks (owned by the deferred front stage)")
            v = rr[k]
            rr[k] = (v + 1) % n
            return v

        def alloc_attn_tiles(stack):
            uid[0] += 1
            u = "_%d" % uid[0]
            for nm, shp, dt in [("qaug", [128, NH, 96], BF16), ("qT1", [128, NH, 128], BF16), ("qaT", [128, NH, 128], BF16),
                                ("szb", [128, D], BF16), ("gates", [128, NH, 3], F32), ("oacc", [128, NH, 64], F32)]:
                T[nm] = [stack.enter_context(nc.sbuf_tensor("%s%s_%d" % (nm, u, k), shp, dt)) for k in range(2)]
                if nm in ("qT1", "qaT"):
                    for k in range(2):
                        G(lambda e, t=T[nm][k]: e.memset(t[:], 0.0), w=[(nm, k)])
            for nm, shp, dt in [("gtmp", [128, 48], F32),
                                ("sg", [128, NH, 8], F32), ("m8", [128, NH, 8], F32), ("imp", [128, NG, 32], F32),
                                ("rs", [128, 4], F32), ("cc", [128, 4], F32), ("otmp", [128, 4, 64], F32),
                                ("og", [128, D], BF16), ("ogT", [128, 8, 128], BF16),
                                ("ytmp", [128, 512], F32), ("pT0", [128, 512], BF16), ("pT1", [128, 512], BF16),
                                ("pT2", [128, 512], BF16)]:
                T[nm] = stack.enter_context(nc.sbuf_tensor(nm + u, shp, dt))

        def alloc_cmp_tiles(stack):
            uid[0] += 1
            u = "_%d" % uid[0]
            for nm, shp, dt in [("KC2", [128, 2, NG, 1024], BF16), ("hidT", [128, 4, NG, 127], BF16),
                                ("hu", [128, 508], F32), ("hw", [128, 508], F32), ("kvb", [128, 512], BF16)]:
                T[nm] = stack.enter_context(nc.sbuf_tensor(nm + u, shp, dt))

        def cload(tile_ap, src, key, queue="pool"):
            sch.dma(queue, tile_ap, src, writes=[key])

        cload(ident[:], dr["c_ident"], "ident")
        cload(cb[:], dr["c_cb"], "cb")
        cload(cb2[:], dr["c_cb2"], "cb2")
        for h in range(2):
            cload(cmask[:, h * 1024:(h + 1) * 1024], dr["c_cmask"][:, h * 1024:(h + 1) * 1024], "cmask")
        cload(fb[:].rearrange("p a b -> p (a b)"), dr["c_fb"], "fb", "sp")
        cload(fbm[:].rearrange("p a b -> p (a b)"), dr["c_fbm"], "fbm", "sp")
        cload(selm[:].rearrange("p a b -> p (a b)"), dr["c_sel"], "selm")
        cload(invf[:], dr["c_invf"], "invf", "sp")
        V(lambda e: e.memset(onesc[:], 1.0), w=["onesc"])
        V(lambda e: e.memset(onef[:], 1.0), w=["onef"])
        V(lambda e: e.memset(epsf[:], EPS), w=["epsf"])
        V(lambda e: e.memset(VA[:, :, :, 64:65], 1.0), w=["VA1"])
        V(lambda e: e.memset(VW[:, :, :, 64:65], 1.0), w=["VW1"])
        V(lambda e: e.memset(kmT[:], 0.0), w=["kmT"])
        V(lambda e: e.memset(ksum[:], 0.0), w=["ksum"])
        V(lambda e: e.memset(kcmpT[:], 0.0), w=["kcmpT"])
        G(lambda e: e.memset(KA[64:128, :, :], 0.0), w=["KAind"])
        G(lambda e: e.memset(KW[:], 0.0), w=[("KW", 0), ("KW", 1), ("KW", 2), ("KW", 3)])
        V(lambda e: e.memset(vcmp[:], 0.0), w=["vcmp"])
        V(lambda e: e.memset(vcmp[:, :, 64:65], 1.0), w=["vcmp"])
        for g in range(NG):
            cload(vcmp[:, g, 65:97], dr["c_selw"], "vcmp")

        with ExitStack() as es2:
            E2 = es2.enter_context
            cT = E2(nc.sbuf_tensor("cT", [128, 8, 2], F32))
            cTe = E2(nc.sbuf_tensor("cTe", [128, 8, 2], F32))
            cTb = E2(nc.sbuf_tensor("cTb", [128, 8, 2], BF16))
            adaw = [E2(nc.sbuf_tensor("adaw%d" % i, [128, 8, 512], BF16)) for i in range(3)]
            adab = E2(nc.sbuf_tensor("adab", [2, 512], F32))
            grow = E2(nc.sbuf_tensor("grow", [2, D], F32))
            modr = [E2(nc.sbuf_tensor("modr%d" % i, [2, 512], F32)) for i in range(2)]
            V(lambda e: e.memset(cT[:], 0.0), w=["cT"])
            for b in range(n_seq):
                sch.dma("sp", cT[:, :, b:b + 1], dr["c"][b].rearrange("(k p o) -> p k o", p=128, o=1),
                        writes=["cT"], allow_slow_non_contiguous=True)
            A(lambda e: e.activation(out=cTe[:], in_=cT[:], func=AF.Exp, scale=-1.0), r=["cT"], w=["cTe"])
            V(lambda e: e.tensor_scalar_add(cTe[:], cTe[:], 1.0), r=["cTe"], w=["cTe"])
            V(lambda e: e.reciprocal(cTe[:], cTe[:]), r=["cTe"], w=["cTe"])
            V(lambda e: e.tensor_tensor(out=cTb[:], in0=cT[:], in1=cTe[:], op=ALU.mult), r=["cT", "cTe"], w=["cTb"])
            ci = 0
            for L in layers:
                for b in range(2):
                    sch.dma("sp", grow[b:b + 1, :], dr["norm_g"][L:L + 1, :], writes=["grow"])
                for ch in range(6):
                    slot = ci % 3
                    ms = ci % 2
                    ci += 1
                    sch.dma("pool", adaw[slot][:], dr["ada_w"][L, :, ch * 512:(ch + 1) * 512].rearrange("(k p) n -> p k n", p=128),
                            writes=[("adaw", slot)])
                    for b in range(2):
                        sch.dma("sp", adab[b:b + 1, :], dr["ada_b"][L:L + 1, ch * 512:(ch + 1) * 512], writes=["adab"])
                    pj = nxt("PJ")
                    for k in range(8):
                        P(lambda e, k=k, pj=pj, slot=slot: e.matmul(PJ[pj][0:2, :], cTb[:, k, :], adaw[slot][:, k, :],
                                                                     start=(k == 0), stop=(k == 7)),
                          r=["cTb", ("adaw", slot)], w=[("PJ", pj)])
                    V(lambda e, pj=pj, ms=ms: e.tensor_tensor(out=modr[ms][:], in0=PJ[pj][0:2, :], in1=adab[:], op=ALU.add),
                      r=[("PJ", pj), "adab"], w=[("modr", ms)])
                    if ch in (2, 3):
                        g0 = (ch - 2) * 512
                        V(lambda e, ms=ms, g0=g0: e.scalar_tensor_tensor(out=modr[ms][:], in0=modr[ms][:], scalar=1.0,
                                                                         in1=grow[:, g0:g0 + 512], op0=ALU.add, op1=ALU.mult),
                          r=[("modr", ms), "grow"], w=[("modr", ms)])
                        dst0 = g0
                    elif ch in (0, 1):
                        dst0 = D + ch * 512
                    else:
                        dst0 = 2 * D + (ch - 4) * 512
                    for b in range(n_seq):
                        sch.dma("sp", mod_d[b, L:L + 1, dst0:dst0 + 512], modr[ms][b:b + 1, :],
                                reads=[("modr", ms)], writes=[("mod", b, L)])
            sch.barrier()

        def rstd_from_ss(ss_ap, out_ap, n, keys_r, key_w):
            A(lambda e: e.activation(out=out_ap, in_=ss_ap, func=AF.Ln, scale=1.0 / n, bias=epsf[:, 0:1]), r=list(keys_r) + ["epsf"], w=[key_w])
            A(lambda e: e.activation(out=out_ap, in_=out_ap, func=AF.Exp, scale=-0.5), r=[key_w], w=[key_w])

        def load_norm_h(b, i, src_ap):
            xs = nxt("xt")
            sch.dma("sp", xt[xs][:], src_ap, reads=[("res", b, i)], writes=[("xt", xs)])
            A(lambda e: e.activation(out=hb[:], in_=xt[xs][:], func=AF.Square, accum_out=st[:, 0:1]),
              r=[("xt", xs)], w=["hb", "st0"])
            rstd_from_ss(st[:, 0:1], st[:, 1:2], D, ["st0"], "st1")
            V(lambda e: e.scalar_tensor_tensor(out=ezs[:], in0=xt[xs][:], scalar=st[:, 1:2], in1=modbc[:, 0, :],
                                               op0=ALU.mult, op1=ALU.mult), r=[("xt", xs), "st1", ("modbc", 0)], w=[("ezs", 0), ("ezs", 512)])
            V(lambda e: e.tensor_tensor(out=hb[:], in0=ezs[:], in1=modbc[:, 1, :], op=ALU.add), r=[("ezs", 0), ("ezs", 512), ("modbc", 1)], w=["hb"])
            pt = nxt("PT")
            for k in range(8):
                P(lambda e, k=k, pt=pt: e.transpose(PT[pt][:, k * 128:(k + 1) * 128], hb[:, k * 128:(k + 1) * 128], ident[:]),
                  r=["hb", "ident"], w=[("PT", pt)])
            hs = nxt("hT")
            V(lambda e: e.tensor_copy(out=hT[hs][:].rearrange("p a b -> p (a b)"), in_=PT[pt][:]),
              r=[("PT", pt)], w=[("hT", hs)])
            return xs, hs

        def project(hs, wcol0, ncols, wstride):
            pj = nxt("PJ")
            for k in range(8):
                P(lambda e, k=k, pj=pj: e.matmul(PJ[pj][:, 0:ncols], hT[hs][:, k, :],
                                                 W1[:, k * wstride + wcol0:k * wstride + wcol0 + ncols],
                                                 start=(k == 0), stop=(k == 7)),
                  r=[("hT", hs)] + [("W1", c) for c in range(wcol0 // 512, (wcol0 + ncols - 1) // 512 + 1)], w=[("PJ", pj)])
            return pj

        def head_norm_rope(pj, c0, nh, g_ap, cos_ap, sin_ap, nsin_ap, out_ap, wkey, alt=False):
            src = PJ[pj][:, c0:c0 + nh * 64]
            src3 = src.rearrange("p (a b) -> p a b", a=nh)
            if alt:
                sq_ = ezs[:, 0:nh * 64]
                qn_ = ezs[:, 256:256 + nh * 64].rearrange("p (a b) -> p a b", a=nh)
                rt_ = ezs[:, 512:512 + nh * 16].rearrange("p (a b) -> p a b", a=nh)
                rt2_ = ezs[:, 576:576 + nh * 16].rearrange("p (a b) -> p a b", a=nh)
                ssq_ = ezs[:, 640:640 + nh]
                rq_ = ezs[:, 648:648 + nh]
                ka = [("ezs", 0), ("ezs", 512)]
                k_sq = k_ssq = k_rq = k_qn = k_rt = k_rt2 = None
                ksq, kssq, krq, kqn, krt, krt2 = ka, ka, ka, ka, ka, ka
            else:
                sq_ = sq[:, 0:nh * 64]
                qn_ = qn[:, 0:nh, :]
                rt_ = rtmp[:, 0:nh, :]
                rt2_ = rtmp2[:, 0:nh, :]
                ssq_ = ssq[:, 0:nh]
                rq_ = rq[:, 0:nh]
                ksq, kssq, krq, kqn, krt, krt2 = ["sq"], ["ssq"], ["rq"], ["qn"], ["rtmp"], ["rtmp2"]
            A(lambda e: e.activation(out=sq_, in_=src, func=AF.Square), r=[("PJ", pj)], w=ksq)
            V(lambda e: e.tensor_reduce(out=ssq_, in_=sq_.rearrange("p (a b) -> p a b", a=nh), axis=AX.X, op=ALU.add), r=ksq, w=kssq)
            A(lambda e: e.activation(out=rq_, in_=ssq_, func=AF.Ln, scale=1.0 / HD, bias=epsf[:, 0:1]), r=kssq + ["epsf"], w=krq)
            A(lambda e: e.activation(out=rq_, in_=rq_, func=AF.Exp, scale=-0.5), r=krq, w=krq)
            V(lambda e: e.tensor_tensor(out=qn_, in0=src3, in1=rq_.unsqueeze(2).to_broadcast([128, nh, 64]), op=ALU.mult),
              r=[("PJ", pj)] + krq, w=kqn)
            V(lambda e: e.tensor_tensor(out=qn_, in0=qn_, in1=g_ap.unsqueeze(1).to_broadcast([128, nh, 64]), op=ALU.mult),
              r=kqn + ["gains"], w=kqn)
            cosb = cos_ap.unsqueeze(1).to_broadcast([128, nh, 8])
            sinb = sin_ap.unsqueeze(1).to_broadcast([128, nh, 8])
            nsinb = nsin_ap.unsqueeze(1).to_broadcast([128, nh, 8])
            V(lambda e: e.tensor_tensor(out=rt_[:, :, 0:8], in0=qn_[:, :, 0:8], in1=cosb, op=ALU.mult), r=kqn + ["rope", "ropeC"], w=krt)
            V(lambda e: e.tensor_tensor(out=rt_[:, :, 8:16], in0=qn_[:, :, 8:16], in1=cosb, op=ALU.mult), r=kqn + ["rope", "ropeC"], w=krt)
            V(lambda e: e.tensor_tensor(out=rt2_[:, :, 0:8], in0=qn_[:, :, 8:16], in1=nsinb, op=ALU.mult), r=kqn + ["rope", "ropeC"], w=krt2)
            V(lambda e: e.tensor_tensor(out=rt2_[:, :, 8:16], in0=qn_[:, :, 0:8], in1=sinb, op=ALU.mult), r=kqn + ["rope", "ropeC"], w=krt2)
            V(lambda e: e.tensor_tensor(out=out_ap[:, :, 0:16], in0=rt_, in1=rt2_, op=ALU.add), r=krt + krt2, w=[wkey])
            A(lambda e: e.activation(out=out_ap[:, :, 16:64], in_=qn_[:, :, 16:64], func=AF.Copy), r=kqn, w=[wkey])

        def merge_streams(fa, fb):
            outer = sch.capture
            sch.capture = []
            fa()
            la = sch.capture
            sch.capture = []
            fb()
            lb = sch.capture
            sch.capture = outer
            na, nb = len(la), len(lb)
            ia = ib = 0
            while ia < na or ib < nb:
                if ib >= nb or (ia < na and ia * nb <= ib * na):
                    it = la[ia]
                    ia += 1
                else:
                    it = lb[ib]
                    ib += 1
                if outer is not None:
                    outer.append(it)
                else:
                    sch.replay(it)

        def silu_from_psum(pj, n, out_ap, c0, wkey):
            src = PJ[pj][:, 0:n]
            ek = ("ezs", c0)
            A(lambda e: e.activation(out=ezs[:, c0:c0 + n], in_=src, func=AF.Exp, scale=-1.0), r=[("PJ", pj)], w=[ek])
            A(lambda e: e.activation(out=ezs[:, c0:c0 + n], in_=ezs[:, c0:c0 + n], func=AF.Ln, bias=onef[:, 0:1]), r=[ek, "onef"], w=[ek])
            A(lambda e: e.activation(out=ezs[:, c0:c0 + n], in_=ezs[:, c0:c0 + n], func=AF.Exp, scale=-1.0), r=[ek], w=[ek])
            V(lambda e: e.tensor_tensor(out=out_ap, in0=src, in1=ezs[:, c0:c0 + n], op=ALU.mult), r=[("PJ", pj), ek], w=[wkey])

        def transpose_heads(src_tile, ncol, dst_tile, rkeys, wkey, halves=(0, 1)):
            for half in halves:
                pt = nxt("PT")
                for hh in range(8):
                    h = half * 8 + hh
                    P(lambda e, h=h, hh=hh, pt=pt: e.transpose(PT[pt][0:ncol, hh * 128:(hh + 1) * 128], src_tile[:, h, 0:ncol], ident[:]),
                      r=rkeys + ["ident"], w=[("PT", pt)])
                if half == 0:
                    V(lambda e, pt=pt: e.tensor_copy(out=dst_tile[0:ncol, 0:8, :].rearrange("p a b -> p (a b)"), in_=PT[pt][0:ncol, :]),
                      r=[("PT", pt)], w=[wkey])
                else:
                    V(lambda e, pt=pt: e.tensor_copy(out=dst_tile[0:ncol, 8:16, :].rearrange("p a b -> p (a b)"), in_=PT[pt][0:ncol, :]),
                      r=[("PT", pt)], w=[wkey])

        def k_to_KT(dst_ap, wkey, ncopy=128):
            pt = nxt("PT")
            for g in range(NG):
                P(lambda e, g=g, pt=pt: e.transpose(PT[pt][0:64, g * 128:(g + 1) * 128], kb[:, g, :], ident[:]),
                  r=["kb", "ident"], w=[("PT", pt)])
            V(lambda e, pt=pt: e.tensor_copy(out=dst_ap, in_=PT[pt][0:64, 0:512].rearrange("p (a b) -> p a b", a=NG)[:, :, 0:ncopy]),
              r=[("PT", pt)], w=[wkey])

        class Attn:
            def __init__(self):
                self.pend = None

            def unit(self, lhsT_k, rhs_q, bias_ap, v_ap, vw, po, first, rk, on_done=None):
                ps = nxt("PS")
                P(lambda e: e.matmul(PS[ps][:], lhsT_k, rhs_q, start=True, stop=(bias_ap is None)),
                  r=rk, w=[("PS", ps)])
                if bias_ap is not None:
                    P(lambda e: e.matmul(PS[ps][:], ident[:], bias_ap, start=False, stop=True),
                      r=["ident", "cb", "cb2", "cmask"], w=[("PS", ps)])
                sl = nxt("pT", 3)
                pt_t = T["pT%d" % sl]
                A(lambda e: e.activation(out=pt_t[:], in_=PS[ps][:], func=AF.Exp, scale=0.125),
                  r=[("PS", ps)], w=[("pT", sl)])
                self.flush()
                self.pend = (pt_t, sl, v_ap, vw, po, first, rk, on_done)

            def flush(self):
                if self.pend is None:
                    return
                pt_t, sl, v_ap, vw, po, first, rk, on_done = self.pend
                self.pend = None
                for r in range(4):
                    P(lambda e, r=r: e.matmul(PO[po][:, r * vw:(r + 1) * vw], pt_t[:, r * 128:(r + 1) * 128], v_ap,
                                              start=(first and r == 0), stop=True, skip_group_check=True),
                      r=[("pT", sl)] + rk, w=[("PO", po)])
                if on_done is not None:
                    on_done()

        def branch_post(po, vw, g, gate_idx, first_branch, clamp=False, gates=None, sl=0):
            rs, cc, otmp, oacc = T["rs"], T["cc"], T["otmp"], T["oacc"][sl]
            pov = PO[po][:, 0:4 * vw].rearrange("p (a b) -> p a b", a=4)
            if clamp:
                V(lambda e: e.tensor_scalar(out=rs[:], in0=pov[:, :, 64:65].rearrange("p a b -> p (a b)"), scalar1=1e-30, scalar2=None, op0=ALU.max),
                  r=[("PO", po)], w=["rs"])
                V(lambda e: e.reciprocal(rs[:], rs[:]), r=["rs"], w=["rs"])
            else:
                V(lambda e: e.reciprocal(rs[:], pov[:, :, 64:65].rearrange("p a b -> p (a b)")), r=[("PO", po)], w=["rs"])
            if gate_idx is not None:
                V(lambda e: e.tensor_tensor(out=cc[:], in0=rs[:], in1=gates[:, 4 * g:4 * g + 4, gate_idx:gate_idx + 1].rearrange("p a b -> p (a b)"),
                                            op=ALU.mult), r=["rs", ("gates", 0), ("gates", 1)], w=["cc"])
                cap = cc
                ck = "cc"
            else:
                cap = rs
                ck = "rs"
            if first_branch:
                V(lambda e: e.tensor_tensor(out=oacc[:, 4 * g:4 * g + 4, :], in0=pov[:, :, 0:64],
                                            in1=cap[:].unsqueeze(2).to_broadcast([128, 4, 64]), op=ALU.mult),
                  r=[("PO", po), ck], w=[("oacc", sl, g)])
            else:
                V(lambda e: e.tensor_tensor(out=otmp[:], in0=pov[:, :, 0:64],
                                            in1=cap[:].unsqueeze(2).to_broadcast([128, 4, 64]), op=ALU.mult),
                  r=[("PO", po), ck], w=["otmp"])
                G(lambda e: e.tensor_tensor(out=oacc[:, 4 * g:4 * g + 4, :], in0=oacc[:, 4 * g:4 * g + 4, :], in1=otmp[:], op=ALU.add),
                  r=["otmp", ("oacc", sl, g)], w=[("oacc", sl, g)])

        def out_proj_store(b, i, xs, sl):
            oacc, og, ogT, ytmp, szb = T["oacc"][sl], T["og"], T["ogT"], T["ytmp"], T["szb"][sl]
            V(lambda e: e.tensor_tensor(out=og[:], in0=oacc[:].rearrange("p a b -> p (a b)"), in1=szb[:], op=ALU.mult),
              r=[("oacc", sl, g) for g in range(NG)] + [("szb", sl)], w=["og"])
            yield
            pt = nxt("PT")
            for k in range(8):
                P(lambda e, k=k, pt=pt: e.transpose(PT[pt][:, k * 128:(k + 1) * 128], og[:, k * 128:(k + 1) * 128], ident[:]),
                  r=["og", "ident"], w=[("PT", pt)])
            V(lambda e: e.tensor_copy(out=ogT[:].rearrange("p a b -> p (a b)"), in_=PT[pt][:]),
              r=[("PT", pt)], w=["ogT"])
            yield
            for half in range(2):
                pj = rr["PO"]
                for k in range(8):
                    P(lambda e, k=k, pj=pj, half=half: e.matmul(PO[pj][:], ogT[:, k, :], WO[:, k, half * 512:(half + 1) * 512],
                                                               start=(k == 0), stop=(k == 7)),
                      r=["ogT", ("WO", half)], w=[("PO", pj)])
                V(lambda e, pj=pj, half=half: e.tensor_tensor(out=ytmp[:], in0=PO[pj][:], in1=modbc[:, 2, half * 512:(half + 1) * 512], op=ALU.mult),
                  r=[("PO", pj), ("modbc", 2)], w=["ytmp"])
                G(lambda e, half=half: e.tensor_tensor(out=xt[xs][:, half * 512:(half + 1) * 512], in0=ytmp[:],
                                                       in1=xt[xs][:, half * 512:(half + 1) * 512], op=ALU.add),
                  r=["ytmp", ("xt", xs)], w=[("xt", xs)])
                if half == 0:
                    yield
            sch.dma("sp", y_d[b, i * 128:(i + 1) * 128, :], xt[xs][:], reads=[("xt", xs)], writes=[("res", b, i)])
            yield

        def load_w_cols(dst_col0, src_ap_fn, ncols, wstride, key="W1"):
            c = 0
            w1v = W1[:, 0:8 * wstride].rearrange("p (k n) -> p k n", k=8)
            while c < ncols:
                n = min(512 - (dst_col0 + c) % 512, ncols - c)
                sch.dma("pool", w1v[:, :, dst_col0 + c:dst_col0 + c + n], src_ap_fn(c, n), writes=[(key, (dst_col0 + c) // 512)])
                c += n

        def load_wo(wo_src):
            for hlf in range(2):
                sch.dma("pool", WO[:, :, hlf * 512:(hlf + 1) * 512],
                        wo_src[:, hlf * 512:(hlf + 1) * 512].rearrange("(k p) n -> p k n", p=128), writes=[("WO", hlf)])

        def bload(dst_ap, src_row_ap, n, key):
            sch.dma("sp", dst_ap, src_row_ap.to_broadcast([128, n]), writes=[key])

        at_box = [None]

        def at_flush():
            at_box[0].flush()

        def run_pipelined(front, back_units, tail, n_tiles, prologue=None):
            def capture_front(i):
                fixed_bank[0] = 0
                sch.capture = []
                crit = None
                for m in front(i):
                    if m == "crit":
                        crit = len(sch.capture)
                ops = sch.capture
                sch.capture = None
                fixed_bank[0] = 1
                return ops, (len(ops) if crit is None else crit)

            ops, crit_pos = capture_front(0)
            pos = 0
            while pos < crit_pos:
                sch.replay(ops[pos])
                pos += 1
            leftover = ops[pos:]
            if prologue is not None:
                prologue()
            tail_gen = iter(())
            DONE = object()
            for i in range(n_tiles):
                if i + 1 < n_tiles:
                    new_ops, new_crit = capture_front(i + 1)
                else:
                    new_ops, new_crit = [], 0
                n_left = len(leftover)
                ops = leftover + new_ops
                crit_pos = n_left + new_crit
                pos = 0
                units = back_units(i)
                nf = units.index("need_front") if "need_front" in units else len(units)
                nu_front = sum(1 for u in units[:nf] if u != "need_front")
                ui = 0
                tail_done = False
                for u in units:
                    if u == "need_front":
                        for _ in tail_gen:
                            pass
                        tail_done = True
                        while pos < crit_pos:
                            sch.replay(ops[pos])
                            pos += 1
                        continue
                    u()
                    if ui >= 1 and not tail_done:
                        if next(tail_gen, DONE) is DONE:
                            tail_done = True
                    lim = len(ops) if tail_done else n_left
                    if pos < lim:
                        if pos < crit_pos:
                            left = max(1, nu_front - ui - 1)
                            k = -(-(crit_pos - pos) // left)
                        else:
                            k = 3
                        for _ in range(k):
                            if pos < lim:
                                sch.replay(ops[pos])
                                pos += 1
                    ui += 1
                for _ in tail_gen:
                    pass
                while pos < crit_pos:
                    sch.replay(ops[pos])
                    pos += 1
                leftover = ops[pos:]
                tail_gen = tail(i)
            for it in leftover:
                sch.replay(it)
            at_flush()
            for _ in tail_gen:
                pass
            fixed_bank[0] = None

        def moba_layer(b, j, src_d):
            WS = MOBA_IN
            load_w_cols(0, lambda c, n: dr["moba_w_in"][j, :, c:c + n].rearrange("(k p) n -> p k n", p=128), MOBA_IN, WS)
            bload(gq[:], dr["moba_q_norm"][j:j + 1, :], 64, "gains")
            bload(gk[:, 0, :], dr["moba_k_norm"][j:j + 1, :], 64, "gains")
            for g in range(NG):
                sch.dma("pool", KA[64:72, g, :], dr["c_ind_moba"], writes=["KAind"])
            load_wo(dr["moba_w_out"][j])
            with ExitStack() as esc:
                alloc_attn_tiles(esc)
                sg, m8 = T["sg"], T["m8"]
                ctx = {}

                def front(i):
                    sl = i % 2
                    qaug, qT1, qaT, szb = T["qaug"][sl], T["qT1"][sl], T["qaT"][sl], T["szb"][sl]
                    xs, hs = load_norm_h(b, i, src_d[b, i * 128:(i + 1) * 128, :])
                    ctx[i] = xs
                    yield
                    cs, sn, nsn = cosT[:, i, :], sinT[:, i, :], nsinT[:, i, :]
                    pq0 = project(hs, 0, 512, WS)
                    pq1 = project(hs, 512, 512, WS)
                    head_norm_rope(pq0, 0, 8, gq[:], cs, sn, nsn, qaug[:, 0:8, 0:64], ("qaug", sl, 0))
                    pkv = project(hs, 1024, 512, WS)
                    transpose_heads(qaug, 64, qT1, [("qaug", sl, 0)], ("qT1", sl), halves=(0,))
                    merge_streams(lambda: head_norm_rope(pq1, 0, 8, gq[:], cs, sn, nsn, qaug[:, 8:16, 0:64], ("qaug", sl, 1)),
                                  lambda: head_norm_rope(pkv, 0, 4, gk[:, 0, :], cs, sn, nsn, kb[:], "kb", alt=True))
                    transpose_heads(qaug, 64, qT1, [("qaug", sl, 1)], ("qT1", sl), halves=(1,))
                    A(lambda e, pj=pkv, i=i: e.activation(out=VA[:, i, :, 0:64], in_=PJ[pj][:, 256:512].rearrange("p (a b) -> p a b", a=NG), func=AF.Copy),
                      r=[("PJ", pkv)], w=[("VA", i)])
                    k_to_KT(KA[0:64, :, i * 128:(i + 1) * 128], ("KA", i))
                    pj3 = nxt("PJ")
                    for h in range(NH):
                        P(lambda e, h=h, pj3=pj3: e.matmul(PJ[pj3][:, h * 8:(h + 1) * 8], qT1[0:64, h, :], kmT[0:64, h // 4, :], start=True, stop=True),
                          r=[("qT1", sl), "kmT"], w=[("PJ", pj3)])
                    V(lambda e, pj3=pj3, i=i: e.tensor_tensor(out=sg[:], in0=PJ[pj3][:, 0:128].rearrange("p (a b) -> p a b", a=NH),
                                                             in1=fbm[:, i, :].unsqueeze(1).to_broadcast([128, NH, 8]), op=ALU.add),
                      r=[("PJ", pj3), "fbm"], w=["sg"])
                    for h in range(NH):
                        V(lambda e, h=h: e.max(out=m8[:, h, :], in_=sg[:, h, :]), r=["sg"], w=[("m8", h)])
                    V(lambda e: e.tensor_tensor(out=qaug[:, :, 64:72], in0=sg[:], in1=m8[:, :, 3:4].to_broadcast([128, NH, 8]), op=ALU.is_lt),
                      r=["sg"] + [("m8", h) for h in range(NH)], w=[("qaug", sl, 2)])
                    transpose_heads(qaug, 72, qaT, [("qaug", sl, 0), ("qaug", sl, 1), ("qaug", sl, 2)], ("qaT", sl))
                    yield "crit"
                    pj2 = nxt("PJ")
                    for g in range(NG):
                        P(lambda e, g=g, pj2=pj2: e.matmul(PJ[pj2][0:64, g:g + 1], kb[:, g, :], onesc[:, 0:1], start=True, stop=True),
                          r=["kb", "onesc"], w=[("PJ", pj2)])
                    V(lambda e, pj2=pj2, i=i: e.tensor_copy(out=ksum[:, i, :], in_=PJ[pj2][0:64, 0:NG]), r=[("PJ", pj2)], w=["ksum"])
                    if i % 2 == 1:
                        jb = i // 2
                        V(lambda e, i=i: e.tensor_tensor(out=kmtmp[:], in0=ksum[:, i - 1, :], in1=ksum[:, i, :], op=ALU.add), r=["ksum"], w=["kmtmp"])
                        V(lambda e, jb=jb: e.tensor_scalar(out=kmT[:, :, jb:jb + 1].rearrange("p a b -> p (a b)"), in0=kmtmp[:], scalar1=1.0 / 256.0,
                                                           scalar2=None, op0=ALU.mult), r=["kmtmp"], w=["kmT"])
                    pz0 = project(hs, 1536, 512, WS)
                    pz1 = project(hs, 2048, 512, WS)
                    silu_from_psum(pz0, 512, szb[:, 0:512], 0, ("szb", sl))
                    silu_from_psum(pz1, 512, szb[:, 512:1024], 512, ("szb", sl))
                    yield

                at = Attn()

                def back_units(i):
                    sl = i % 2
                    qaT = T["qaT"][sl]
                    units = []
                    for g in range(NG):
                        po_box = []
                        for jj in range(i + 1):
                            def u(g=g, jj=jj, po_box=po_box):
                                if jj == 0:
                                    po_box.append(nxt("PO"))
                                po = po_box[0]
                                bias = cb[:].unsqueeze(1).to_broadcast([128, 4, 128]) if jj == i else None
                                done = (lambda: branch_post(po, 65, g, None, True, sl=sl)) if jj == i else None
                                at.unit(KA[:, g, jj * 128:(jj + 1) * 128], qaT[:, 4 * g:4 * g + 4, :], bias, VA[:, jj, g, :], 65, po,
                                        jj == 0, [("KA", jj), "KAind", ("qaT", sl), ("VA", jj), "VA1"], done)
                            units.append(u)
                    return units

                def tail(i):
                    return out_proj_store(b, i, ctx[i], i % 2)

                at_box[0] = at
                run_pipelined(front, back_units, tail, NT)
                sch.barrier()

        def nsa_layer(b, j, src_d):
            w_in = dr["nsa_w_in"]
            load_w_cols(0, lambda c, n: w_in[j, :, 1024 + c:1024 + c + n].rearrange("(k p) n -> p k n", p=128), 1024, 1024)
            CW1 = W1[:, 8192:16384].rearrange("p (a m h) -> p a m h", a=2, m=16)
            for kv in range(2):
                for mh in range(2):
                    sch.dma("pool", CW1[:, kv, mh * 8:(mh + 1) * 8, :],
                            dr["nsa_cmp_w1"][j, kv, mh * 1024:(mh + 1) * 1024, :].rearrange("(m p) h -> p m h", p=128), writes=[("CW1", kv, mh)])
                sch.dma("pool", cw2[:, kv, :, :], dr["nsa_cmp_w2"][j, kv].rearrange("(h p) d -> p h d", p=128), writes=["cw2"])
                sch.dma("sp", peT2f[:, kv, :], dr["nsa_cmp_pe"][j, kv].rearrange("l d -> (l d)").rearrange("(m p) -> p m", p=128),
                        writes=["peT2f"], allow_slow_non_contiguous=True)
                sch.dma("sp", b1c[:, kv, :], dr["nsa_cmp_b1"][j, kv].rearrange("(h p) -> p h", p=128), writes=["b1c"],
                        allow_slow_non_contiguous=True)
            V(lambda e: e.tensor_copy(out=peT2[:], in_=peT2f[:]), r=["peT2f"], w=["peT2"])
            bload(gq[:], dr["nsa_q_norm"][j:j + 1, :], 64, "gains")
            for m in range(3):
                bload(gk[:, m, :], dr["nsa_k_norm"][j, m:m + 1, :], 64, "gains")
            bload(gateb[:], dr["nsa_gate_b"][j:j + 1, :], 48, "gateb")
            for g in range(NG):
                sch.dma("pool", KA[64:96, g, :], dr["c_ind_nsa"], writes=["KAind"])
            load_wo(dr["nsa_w_out"][j])
            if DBG_STAGE[0] < 2:
                return
            with ExitStack() as esa:
                alloc_cmp_tiles(esa)
                KC2, hidT, hu, hw, kvb = T["KC2"], T["hidT"], T["hu"], T["hw"], T["kvb"]

                def a_stage2(i, hs):
                    pjA = project(hs, 0, 512, 1024)
                    pjB = project(hs, 512, 512, 1024)
                    A(lambda e, pjA=pjA: e.activation(out=kvb[:], in_=PJ[pjA][:], func=AF.Copy), r=[("PJ", pjA)], w=["kvb"])
                    head_norm_rope(pjB, 0, 4, gk[:, 1, :], cosT[:, i, :], sinT[:, i, :], nsinT[:, i, :], kb[:], "kb")
                    A(lambda e, pjB=pjB, i=i: e.activation(out=VA[:, i, :, 0:64], in_=PJ[pjB][:, 256:512].rearrange("p (a b) -> p a b", a=NG), func=AF.Copy),
                      r=[("PJ", pjB)], w=[("VA", i)])
                    pjs = nxt("PJ")
                    for kv in range(2):
                        for g in range(NG):
                            for par in range(2):
                                c0 = (kv * 4 + g) * 64
                                P(lambda e, kv=kv, g=g, par=par, c0=c0, pjs=pjs: e.matmul(
                                    PJ[pjs][par * 64:(par + 1) * 64, c0:c0 + 64], kvb[:, kv * 256 + g * 64:kv * 256 + (g + 1) * 64],
                                    selm[:, par, :], start=True, stop=True), r=["kvb", "selm"], w=[("PJ", pjs)])
                    V(lambda e, pjs=pjs, i=i: e.tensor_copy(out=KC2[:, :, :, i * 64:(i + 1) * 64],
                                                           in_=PJ[pjs][:].rearrange("p (a g u) -> p a g u", a=2, g=NG)),
                      r=[("PJ", pjs)], w=[("KC2", i)])
                    k_to_KT(KA[0:64, :, i * 128:(i + 1) * 128], ("KA", i))

                def cap(fn, pt):
                    pin_pt[0] = pt
                    sch.capture = []
                    r = fn()
                    ops_ = sch.capture
                    sch.capture = None
                    pin_pt[0] = None
                    return ops_, r

                def merge(o1, o2):
                    n1, n2 = len(o1), len(o2)
                    i1 = i2 = 0
                    while i1 < n1 or i2 < n2:
                        if i2 >= n2 or (i1 < n1 and i1 * n2 <= i2 * n1):
                            sch.replay(o1[i1])
                            i1 += 1
                        else:
                            sch.replay(o2[i2])
                            i2 += 1

                o1, (xs0, hs_prev) = cap(lambda: load_norm_h(b, 0, src_d[b, 0:128, :]), 0)
                merge(o1, [])
                for i in range(NT):
                    if i + 1 < NT:
                        o1, (xs1, hs_next) = cap(lambda i=i: load_norm_h(b, i + 1, src_d[b, (i + 1) * 128:(i + 2) * 128, :]), 0)
                    else:
                        o1, hs_next = [], None
                    o2, _ = cap(lambda i=i, hs_prev=hs_prev: a_stage2(i, hs_prev), 1)
                    merge(o1, o2)
                    hs_prev = hs_next
                if DBG_STAGE[0] < 3:
                    sch.barrier()
                    return
                kc2keys = [("KC2", i) for i in range(NT)]
                pjb = nxt("PJ")
                for kv in range(2):
                    for half in range(2):
                        col = kv * 2 + half
                        for m in range(16):
                            P(lambda e, kv=kv, half=half, m=m, col=col, pjb=pjb: e.matmul(
                                PJ[pjb][:, col:col + 1], CW1[:, kv, m, half * 128:(half + 1) * 128], peT2[:, kv, m:m + 1],
                                start=(m == 0), stop=(m == 15)), r=[("CW1", kv, m // 8), "peT2"], w=[("PJ", pjb)])
                V(lambda e, pjb=pjb: e.tensor_tensor(out=b1p[:].rearrange("p a b -> p (a b)"), in0=PJ[pjb][:, 0:4],
                                                    in1=b1c[:].rearrange("p a b -> p (a b)"), op=ALU.add), r=[("PJ", pjb), "b1c"], w=["b1p"])
                for kv in range(2):
                    for half in range(2):
                        ps_ = nxt("PJ")
                        for m in range(16):
                            P(lambda e, kv=kv, half=half, m=m, ps_=ps_: e.matmul(
                                PJ[ps_][:, 0:508], CW1[:, kv, m, half * 128:(half + 1) * 128], KC2[:, kv, :, m:m + 1009:8],
                                start=(m == 0), stop=(m == 15)), r=[("CW1", kv, m // 8)] + kc2keys, w=[("PJ", ps_)])
                        V(lambda e, kv=kv, half=half, ps_=ps_: e.tensor_scalar(out=hu[:], in0=PJ[ps_][:, 0:508], scalar1=b1p[:, kv, half:half + 1],
                                                                              scalar2=None, op0=ALU.add), r=[("PJ", ps_), "b1p"], w=["hu"])
                        V(lambda e: e.tensor_tensor(out=hw[:], in0=hu[:], in1=hu[:], op=ALU.mult), r=["hu"], w=["hw"])
                        V(lambda e: e.tensor_scalar(out=hw[:], in0=hw[:], scalar1=0.044715, scalar2=1.0, op0=ALU.mult, op1=ALU.add), r=["hw"], w=["hw"])
                        V(lambda e: e.tensor_tensor(out=hw[:], in0=hw[:], in1=hu[:], op=ALU.mult), r=["hw", "hu"], w=["hw"])
                        A(lambda e: e.activation(out=hw[:], in_=hw[:], func=AF.Exp, scale=-1.5957691216057308), r=["hw"], w=["hw"])
                        V(lambda e: e.tensor_scalar_add(hw[:], hw[:], 1.0), r=["hw"], w=["hw"])
                        V(lambda e: e.reciprocal(hw[:], hw[:]), r=["hw"], w=["hw"])
                        V(lambda e, kv=kv, half=half: e.tensor_tensor(out=hidT[:, kv * 2 + half, :, :].rearrange("p a b -> p (a b)"), in0=hu[:], in1=hw[:],
                                                                      op=ALU.mult), r=["hu", "hw"], w=["hidT"])
                pj2 = nxt("PJ")
                for kv in range(2):
                    for g in range(NG):
                        c0 = (kv * 4 + g) * 64
                        for half in range(2):
                            P(lambda e, kv=kv, g=g, half=half, c0=c0, pj2=pj2: e.matmul(
                                PJ[pj2][0:127, c0:c0 + 64], hidT[:, kv * 2 + half, g, :], cw2[:, kv, half, :],
                                start=(half == 0), stop=(half == 1)), r=["hidT", "cw2"], w=[("PJ", pj2)])
                head_norm_rope(pj2, 0, 4, gk[:, 0, :], cosC[:], sinC[:], nsinC[:], kb[:], "kb")
                A(lambda e, pj2=pj2: e.activation(out=vcmp[0:127, :, 0:64], in_=PJ[pj2][0:127, 256:512].rearrange("p (a b) -> p a b", a=NG), func=AF.Copy),
                  r=[("PJ", pj2)], w=["vcmp"])
                k_to_KT(kcmpT[0:64, :, 0:127], "kcmpT", 127)
                sch.barrier()
            if DBG_STAGE[0] < 4:
                return
            WS = 2608
            load_w_cols(0, lambda c, n: w_in[j, :, c:c + n].rearrange("(k p) n -> p k n", p=128), 1024, WS)
            load_w_cols(1024, lambda c, n: w_in[j, :, 2048 + c:2048 + c + n].rearrange("(k p) n -> p k n", p=128), 1584, WS)
            with ExitStack() as esc:
                alloc_attn_tiles(esc)
                gtmp, imp, m8, rs = T["gtmp"], T["imp"], T["m8"], T["rs"]
                ctx = {}

                def front(i):
                    sl = i % 2
                    qaug, qT1, szb, gates = T["qaug"][sl], T["qT1"][sl], T["szb"][sl], T["gates"][sl]
                    xs, hs = load_norm_h(b, i, src_d[b, i * 128:(i + 1) * 128, :])
                    ctx[i] = xs
                    yield
                    cs, sn, nsn = cosT[:, i, :], sinT[:, i, :], nsinT[:, i, :]
                    slot = i % 4
                    pq0 = project(hs, 0, 512, WS)
                    pq1 = project(hs, 512, 512, WS)
                    head_norm_rope(pq0, 0, 8, gq[:], cs, sn, nsn, qaug[:, 0:8, 0:64], ("qaug", sl, 0))
                    pkw = project(hs, 1024, 512, WS)
                    transpose_heads(qaug, 64, qT1, [("qaug", sl, 0)], ("qT1", sl), halves=(0,))
                    merge_streams(lambda: head_norm_rope(pq1, 0, 8, gq[:], cs, sn, nsn, qaug[:, 8:16, 0:64], ("qaug", sl, 1)),
                                  lambda: head_norm_rope(pkw, 0, 4, gk[:, 2, :], cs, sn, nsn, kb[:], "kb", alt=True))
                    transpose_heads(qaug, 64, qT1, [("qaug", sl, 1)], ("qT1", sl), halves=(1,))
                    pgl = project(hs, 1536, 48, WS)
                    A(lambda e, pj=pkw, slot=slot: e.activation(out=VW[:, slot, :, 0:64], in_=PJ[pj][:, 256:512].rearrange("p (a b) -> p a b", a=NG), func=AF.Copy),
                      r=[("PJ", pkw)], w=[("VW", slot)])
                    k_to_KT(KW[0:64, :, slot, :], ("KW", slot))
                    V(lambda e, pj=pgl: e.tensor_tensor(out=gtmp[:], in0=PJ[pj][:, 0:48], in1=gateb[:], op=ALU.add), r=[("PJ", pgl), "gateb"], w=["gtmp"])
                    A(lambda e: e.activation(out=gtmp[:], in_=gtmp[:], func=AF.Exp, scale=-1.0), r=["gtmp"], w=["gtmp"])
                    A(lambda e: e.activation(out=gtmp[:], in_=gtmp[:], func=AF.Ln, bias=onef[:, 0:1]), r=["gtmp", "onef"], w=["gtmp"])
                    A(lambda e: e.activation(out=gates[:].rearrange("p a b -> p (a b)"), in_=gtmp[:], func=AF.Exp, scale=-1.0), r=["gtmp"], w=[("gates", sl)])
                    yield "crit"
                    pz0 = project(hs, 1584, 512, WS)
                    pz1 = project(hs, 2096, 512, WS)
                    silu_from_psum(pz0, 512, szb[:, 0:512], 0, ("szb", sl))
                    silu_from_psum(pz1, 512, szb[:, 512:1024], 512, ("szb", sl))
                    yield

                at = Attn()

                def cmp_post(po, g, i):
                    sl = i % 2
                    qaug = T["qaug"][sl]
                    branch_post(po, 97, g, 0, True, clamp=True, gates=T["gates"][sl], sl=sl)
                    pov = PO[po][:, 0:388].rearrange("p (a b) -> p a b", a=4)
                    V(lambda e: e.tensor_scalar(out=imp[:, g, :], in0=pov[:, 0, 65:97], scalar1=rs[:, 0:1], scalar2=None, op0=ALU.mult),
                      r=[("PO", po), "rs"], w=[("imp", g)])
                    for r_ in range(1, 4):
                        V(lambda e, r_=r_: e.scalar_tensor_tensor(out=imp[:, g, :], in0=pov[:, r_, 65:97], scalar=rs[:, r_:r_ + 1], in1=imp[:, g, :],
                                                                  op0=ALU.mult, op1=ALU.add), r=[("PO", po), "rs", ("imp", g)], w=[("imp", g)])
                    V(lambda e: e.tensor_tensor(out=imp[:, g, :], in0=imp[:, g, :], in1=fb[:, i, :], op=ALU.add), r=[("imp", g), "fb"], w=[("imp", g)])
                    V(lambda e: e.max(out=m8[:, g, :], in_=imp[:, g, :]), r=[("imp", g)], w=[("m8", g)])
                    V(lambda e: e.tensor_tensor(out=qaug[:, 4 * g:4 * g + 4, 64:96], in0=imp[:, g, :].unsqueeze(1).to_broadcast([128, 4, 32]),
                                                in1=m8[:, g, 7:8].unsqueeze(1).to_broadcast([128, 4, 32]), op=ALU.is_lt),
                      r=[("imp", g), ("m8", g)], w=[("qaug", sl, 2)])

                def cmp_unit(i, g):
                    sl = i % 2
                    qT1 = T["qT1"][sl]

                    def u():
                        po = nxt("PO")
                        at.unit(kcmpT[:, g, :], qT1[:, 4 * g:4 * g + 4, :], cmask[:, i * 128:(i + 1) * 128].unsqueeze(1).to_broadcast([128, 4, 128]),
                                vcmp[:, g, :], 97, po, True, ["kcmpT", ("qT1", sl), "vcmp"], (lambda: cmp_post(po, g, i)))
                    return u

                def back_units(i):
                    sl = i % 2
                    qaug, qT1, qaT, gates = T["qaug"][sl], T["qT1"][sl], T["qaT"][sl], T["gates"][sl]
                    units = []
                    tiles = [jj for jj in (i - 2, i - 1, i) if jj >= 0]
                    for g in range(NG):
                        po_box = []
                        for idx, jj in enumerate(tiles):
                            def u(g=g, jj=jj, idx=idx, po_box=po_box):
                                if idx == 0:
                                    po_box.append(nxt("PO"))
                                po = po_box[0]
                                if jj == i:
                                    bias = cb[:].unsqueeze(1).to_broadcast([128, 4, 128])
                                elif jj == i - 2:
                                    bias = cb2[:].unsqueeze(1).to_broadcast([128, 4, 128])
                                else:
                                    bias = None
                                done = (lambda: branch_post(po, 65, g, 2, False, gates=gates, sl=sl)) if jj == i else None
                                at.unit(KW[:, g, jj % 4, :], qT1[:, 4 * g:4 * g + 4, :], bias, VW[:, jj % 4, g, :], 65, po, idx == 0,
                                        [("KW", jj % 4), ("qT1", sl), ("VW", jj % 4), "VW1"], done)
                            units.append(u)

                    def tr():
                        at.flush()
                        transpose_heads(qaug, 96, qaT, [("qaug", sl, 0), ("qaug", sl, 1), ("qaug", sl, 2)], ("qaT", sl))
                    units.append(tr)
                    for g in range(NG):
                        po_box = []
                        for jj in range(i + 1):
                            def u(g=g, jj=jj, po_box=po_box):
                                if jj == 0:
                                    po_box.append(nxt("PO"))
                                po = po_box[0]
                                bias = cb[:].unsqueeze(1).to_broadcast([128, 4, 128]) if jj == i else None
                                done = (lambda: branch_post(po, 65, g, 1, False, gates=gates, sl=sl)) if jj == i else None
                                at.unit(KA[:, g, jj * 128:(jj + 1) * 128], qaT[:, 4 * g:4 * g + 4, :], bias, VA[:, jj, g, :], 65, po,
                                        jj == 0, [("KA", jj), "KAind", ("qaT", sl), ("VA", jj), "VA1"], done)
                            units.append(u)
                        if i + 1 < NT:
                            if g == 0:
                                units.append("need_front")
                            units.append(cmp_unit(i + 1, g))
                    return units

                def tail(i):
                    return out_proj_store(b, i, ctx[i], i % 2)

                def prologue():
                    for g in range(NG):
                        cmp_unit(0, g)()

                at_box[0] = at
                run_pipelined(front, back_units, tail, NT, prologue)
                sch.barrier()

        for b in range(n_seq):
            with ExitStack() as es3:
                E3 = es3.enter_context
                posi = E3(nc.sbuf_tensor("posi%d" % b, [128, NT + 1], I32))
                posf = E3(nc.sbuf_tensor("posf%d" % b, [128, NT + 1], F32))
                ang = E3(nc.sbuf_tensor("ang%d" % b, [128, NT + 1, 8], F32))
                kq = E3(nc.sbuf_tensor("kq%d" % b, [128, NT + 1, 8], F32))
                ki = E3(nc.sbuf_tensor("ki%d" % b, [128, NT + 1, 8], I32))
                red = E3(nc.sbuf_tensor("red%d" % b, [128, NT + 1, 8], F32))
                V(lambda e: e.memset(posi[:], 0), w=["posi"])
                sch.dma("sp", posi[:, 0:NT], dr["positions"][b].rearrange("(i p) -> p i", p=128), writes=["posi"],
                        allow_slow_non_contiguous=True)
                sch.dma("sp", posi[0:127, NT:NT + 1], dr["positions"][b, 16:16 + 16 * 127].rearrange("(n s) -> n s", s=16)[:, 15:16],
                        writes=["posi"], allow_slow_non_contiguous=True)
                V(lambda e: e.tensor_copy(out=posf[:], in_=posi[:]), r=["posi"], w=["posf"])
                V(lambda e: e.tensor_tensor(out=ang[:], in0=posf[:].unsqueeze(2).to_broadcast([128, NT + 1, 8]),
                                            in1=invf[:].unsqueeze(1).to_broadcast([128, NT + 1, 8]), op=ALU.mult),
                  r=["posf", "invf"], w=["ang"])
                C1 = 6.28125
                C2 = float(2.0 * np.pi - 6.28125)
                PI_LO = 3.1415925
                for which, off in (("sin", 0.0), ("cos", 0.25)):
                    V(lambda e, off=off: e.tensor_scalar(out=kq[:], in0=ang[:], scalar1=float(1.0 / (2.0 * np.pi)), scalar2=off,
                                                         op0=ALU.mult, op1=ALU.add), r=["ang"], w=["kq"])
                    V(lambda e: e.tensor_copy(out=ki[:], in_=kq[:]), r=["kq"], w=["ki"])
                    V(lambda e: e.tensor_copy(out=kq[:], in_=ki[:]), r=["ki"], w=["kq"])
                    V(lambda e: e.scalar_tensor_tensor(out=red[:], in0=kq[:], scalar=-C1, in1=ang[:], op0=ALU.mult, op1=ALU.add),
                      r=["kq", "ang"], w=["red"])
                    V(lambda e: e.scalar_tensor_tensor(out=red[:], in0=kq[:], scalar=-C2, in1=red[:], op0=ALU.mult, op1=ALU.add),
                      r=["kq", "red"], w=["red"])
                    if which == "cos":
                        V(lambda e: e.tensor_scalar(out=red[:], in0=red[:], scalar1=float(np.pi / 2), scalar2=PI_LO,
                                                    op0=ALU.add, op1=ALU.min), r=["red"], w=["red"])
                        V(lambda e: e.tensor_scalar(out=red[:], in0=red[:], scalar1=-PI_LO, scalar2=None, op0=ALU.max), r=["red"], w=["red"])
                        A(lambda e: e.activation(out=cosT[:], in_=red[:, 0:NT, :], func=AF.Sin), r=["red"], w=["rope"])
                        A(lambda e: e.activation(out=cosC[:], in_=red[:, NT, :], func=AF.Sin), r=["red"], w=["ropeC"])
                    else:
                        V(lambda e: e.tensor_scalar(out=red[:], in0=red[:], scalar1=PI_LO, scalar2=-PI_LO,
                                                    op0=ALU.min, op1=ALU.max), r=["red"], w=["red"])
                        A(lambda e: e.activation(out=sinT[:], in_=red[:, 0:NT, :], func=AF.Sin), r=["red"], w=["rope"])
                        A(lambda e: e.activation(out=sinC[:], in_=red[:, NT, :], func=AF.Sin), r=["red"], w=["ropeC"])
                V(lambda e: e.tensor_scalar(out=nsinT[:], in0=sinT[:], scalar1=-1.0, scalar2=None, op0=ALU.mult), r=["rope"], w=["rope"])
                V(lambda e: e.tensor_scalar(out=nsinC[:], in0=sinC[:], scalar1=-1.0, scalar2=None, op0=ALU.mult), r=["ropeC"], w=["ropeC"])
                sch.barrier()

            for li, L in enumerate(layers):
                src_d = dr["x"] if (li == 0 and first_from_x) else y_d
                is_nsa = (L % 2 == 0)
                j = L // 2
                for m in range(3):
                    sch.dma("sp", modbc[:, m, :], mod_d[b, L:L + 1, m * D:(m + 1) * D].to_broadcast([128, D]),
                            reads=[("mod", b, L)], writes=[("modbc", m)])
                if is_nsa:
                    nsa_layer(b, j, src_d)
                else:
                    moba_layer(b, j, src_d)
        sch.finish()
    return nc


_CONSTS = None


def _consts():
    global _CONSTS
    if _CONSTS is None:
        _CONSTS = host_consts()
    return _CONSTS


def run_layers(inputs, layers, n_seq, n_cores, first_from_x=True, trace=False):
    nc = build(layers, n_seq, first_from_x)
    cst = _consts()
    in_maps = []
    for c in range(n_cores):
        m = {}
        m["x"] = np.ascontiguousarray(inputs["x"][c * n_seq:(c + 1) * n_seq], dtype=np.float32)
        m["c"] = np.ascontiguousarray(inputs["c"][c * n_seq:(c + 1) * n_seq], dtype=np.float32)
        m["positions"] = np.ascontiguousarray(inputs["positions"][c * n_seq:(c + 1) * n_seq], dtype=np.int32)
        for k in WEIGHT_SHAPES:
            m[k] = np.ascontiguousarray(inputs[k], dtype=np.float32)
        for k in CONST_SHAPES:
            m[k] = cst[k]
        in_maps.append(m)
    res = run_bass_kernel_spmd(nc, in_maps, core_ids=list(range(n_cores)), trace=trace)
    out = np.concatenate([np.asarray(r["y"]) for r in res.results], axis=0)
    return out, res


def kernel(**inputs):
    out, _ = run_layers(inputs, [0, 1, 2, 3], 2, N_CORES)
    return out.astype(np.float32)
```

```python
import numpy as np
from contextlib import ExitStack
import concourse.bass as bass
import concourse.mybir as mybir
from concourse.bass_utils import run_bass_kernel_spmd

F32 = mybir.dt.float32
BF16 = mybir.dt.bfloat16
I32 = mybir.dt.int32
AF = mybir.ActivationFunctionType
ALU = mybir.AluOpType
AX = mybir.AxisListType

S = 2048
D = 1024
NT = 16
HD = 64
NH = 16
NG = 4
NEG = -30000.0
EPS = 1e-6
N_CORES = 8
NSA_IN = 3632
MOBA_IN = 2560
NDMA = 24
FRESH_POOL_SEMS = [False]
STRICT_SAME_ENGINE = [False]
POOL_DESC_LIMIT = 6144


class Sched:
    def __init__(self, nc, es):
        self.nc = nc
        self.es = es
        self.nfresh = 0
        self.capture = None
        self.pool_fifo = []
        self.pool_desc = 0
        self.eng = {"pe": nc.tensor, "act": nc.scalar, "dve": nc.vector, "pool": nc.gpsimd, "sp": nc.sync}
        self.semh = {}
        for k in self.eng:
            self.semh[("e", k)] = es.enter_context(nc.semaphore("sem_" + k))
        for i in range(NDMA):
            self.semh[("d", i)] = es.enter_context(nc.semaphore("dsem%d" % i))
        self.cnt = {k: 0 for k in self.eng}
        self.dcnt = [0] * NDMA
        self.drr = 0
        self.seen = {k: {} for k in self.eng}
        self.lastw = {}
        self.readers = {}

    def _deps(self, eng, reads, writes, strict=False):
        deps = {}

        def add(t, same_ok):
            sk, v = t
            if same_ok and not strict and sk == ("e", eng) and (eng == "pe" or not STRICT_SAME_ENGINE[0]):
                return
            if deps.get(sk, 0) < v:
                deps[sk] = v

        for k in reads:
            if k in self.lastw:
                add(self.lastw[k], False)
        for k in writes:
            if k in self.lastw:
                add(self.lastw[k], True)
            for sk, v in self.readers.get(k, {}).items():
                add((sk, v), True)
        return deps

    def _wait(self, eng, deps):
        for sk, v in deps.items():
            if eng == "pe" and sk == ("e", "pe"):
                continue
            if self.seen[eng].get(sk, 0) < v:
                self.eng[eng].wait_ge(self.semh[sk], v)
                self.seen[eng][sk] = v

    def _record(self, tok, reads, writes):
        for k in writes:
            self.lastw[k] = tok
            self.readers[k] = {}
        for k in reads:
            d = self.readers.setdefault(k, {})
            if d.get(tok[0], 0) < tok[1]:
                d[tok[0]] = tok[1]

    def op(self, eng, fn, reads=(), writes=()):
        if self.capture is not None:
            self.capture.append(("op", eng, fn, tuple(reads), tuple(writes), None))
            return
        self._wait(eng, self._deps(eng, reads, writes))
        inst = fn(self.eng[eng])
        self.cnt[eng] += 1
        inst.then_inc(self.semh[("e", eng)], 1)
        self._record((("e", eng), self.cnt[eng]), reads, writes)

    def replay(self, item):
        kind, a, b, reads, writes, kw = item
        if kind == "op":
            self.op(a, b, reads, writes)
        else:
            self.dma(a, b[0], b[1], reads, writes, **kw)

    def dma(self, queue, out, in_, reads=(), writes=(), **kw):
        if self.capture is not None:
            self.capture.append(("dma", queue, (out, in_), tuple(reads), tuple(writes), dict(kw)))
            return
        if FRESH_POOL_SEMS[0] and queue == "pool":
            kw.pop("ndesc", None)
            idx = NDMA + self.nfresh
            self.nfresh += 1
            self.semh[("d", idx)] = self.es.enter_context(self.nc.semaphore("fsem%d" % idx))
            self.dcnt.append(0)
            self._wait(queue, self._deps(queue, reads, writes, strict=True))
            inst = self.eng[queue].dma_start(out=out, in_=in_, **kw)
            self.dcnt[idx] += 16
            inst.then_inc(self.semh[("d", idx)], 16)
            self._record((("d", idx), self.dcnt[idx]), reads, writes)
            return
        ndesc = kw.pop("ndesc", 1024)
        idx = self.drr
        self.drr = (idx + 1) % NDMA
        if queue == "pool":
            while self.pool_fifo and self.pool_desc + ndesc > POOL_DESC_LIMIT:
                sk, v, nd = self.pool_fifo.pop(0)
                self.pool_desc -= nd
                if self.seen["pool"].get(sk, 0) < v:
                    self.eng["pool"].wait_ge(self.semh[sk], v)
                    self.seen["pool"][sk] = v
        deps = self._deps(queue, reads, writes, strict=True)
        if self.dcnt[idx]:
            deps[("d", idx)] = max(deps.get(("d", idx), 0), self.dcnt[idx])
        self._wait(queue, deps)
        inst = self.eng[queue].dma_start(out=out, in_=in_, **kw)
        self.dcnt[idx] += 16
        inst.then_inc(self.semh[("d", idx)], 16)
        self._record((("d", idx), self.dcnt[idx]), reads, writes)
        if queue == "pool":
            self.pool_fifo.append((("d", idx), self.dcnt[idx], ndesc))
            self.pool_desc += ndesc

    def barrier(self):
        for k in self.eng:
            for k2 in self.eng:
                if k2 != k and self.cnt[k2] > self.seen[k].get(("e", k2), 0):
                    self.eng[k].wait_ge(self.semh[("e", k2)], self.cnt[k2])
                    self.seen[k][("e", k2)] = self.cnt[k2]
            for i in range(len(self.dcnt)):
                if self.dcnt[i] > self.seen[k].get(("d", i), 0):
                    self.eng[k].wait_ge(self.semh[("d", i)], self.dcnt[i])
                    self.seen[k][("d", i)] = self.dcnt[i]

    def finish(self):
        e = self.eng["sp"]
        for i in range(len(self.dcnt)):
            if self.dcnt[i] and self.seen["sp"].get(("d", i), 0) < self.dcnt[i]:
                e.wait_ge(self.semh[("d", i)], self.dcnt[i])
        for k in self.eng:
            if k != "sp" and self.cnt[k] and self.seen["sp"].get(("e", k), 0) < self.cnt[k]:
                e.wait_ge(self.semh[("e", k)], self.cnt[k])


def host_consts():
    p = np.arange(128)[:, None]
    f = np.arange(128)[None, :]
    c = {}
    c["c_ident"] = (p == f).astype(np.float32)
    c["c_cb"] = np.where(p > f, NEG, 0.0).astype(np.float32)
    c["c_cb2"] = np.where(p <= f, NEG, 0.0).astype(np.float32)
    t = np.arange(S)[None, :]
    cm = np.where((16 * p + 31 > t) | (p == 127), NEG, 0.0)
    c["c_cmask"] = cm.astype(np.float32)
    fb = np.zeros((128, NT, 32), np.float32)
    for i in range(NT):
        cur = (128 * i + np.arange(128)) // 64
        for j in range(32):
            fb[:, i, j] = np.where((j == 0) | (j == cur) | (j == cur - 1), 100.0, 0.0)
    c["c_fb"] = fb.reshape(128, NT * 32)
    fbm = np.zeros((128, NT, 8), np.float32)
    for i in range(NT):
        own = i // 2
        for j in range(8):
            fbm[:, i, j] = 0.0 if j < own else (1e4 if j == own else -1e4)
    c["c_fbm"] = fbm.reshape(128, NT * 8)
    key = np.arange(S)[None, :]
    c["c_ind_nsa"] = np.where(key // 64 == np.arange(32)[:, None], NEG, 0.0).astype(np.float32)
    c["c_ind_moba"] = np.where(key // 256 == np.arange(8)[:, None], NEG, 0.0).astype(np.float32)
    cs = np.arange(127)[:, None] * 16
    ss = np.arange(32)[None, :] * 64
    shared = np.clip(np.minimum(cs + 32, ss + 64) - np.maximum(cs, ss), 0, None)
    sw = np.zeros((128, 32), np.float32)
    sw[:127] = shared / 32.0
    c["c_selw"] = sw
    sel = np.zeros((128, 2, 64), np.float32)
    for par in range(2):
        for u in range(64):
            sel[2 * u + par, par, u] = 1.0
    c["c_sel"] = sel.reshape(128, 128)
    inv = (np.float32(500000.0) ** (-np.arange(0, 16, 2, dtype=np.float32) / np.float32(16))).astype(np.float32)
    c["c_invf"] = np.tile(inv[None, :], (128, 1)).astype(np.float32)
    return c


CONST_SHAPES = {
    "c_ident": [128, 128], "c_cb": [128, 128], "c_cb2": [128, 128], "c_cmask": [128, S],
    "c_fb": [128, NT * 32], "c_fbm": [128, NT * 8], "c_ind_nsa": [32, S], "c_ind_moba": [8, S],
    "c_selw": [128, 32], "c_sel": [128, 128], "c_invf": [128, 8],
}

WEIGHT_SHAPES = {
    "norm_g": [4, D], "ada_w": [4, D, 3 * D], "ada_b": [4, 3 * D],
    "nsa_w_in": [2, D, NSA_IN], "nsa_w_out": [2, D, D], "nsa_q_norm": [2, 64], "nsa_k_norm": [2, 3, 64],
    "nsa_cmp_pe": [2, 2, 32, 64], "nsa_cmp_w1": [2, 2, 2048, 256], "nsa_cmp_b1": [2, 2, 256],
    "nsa_cmp_w2": [2, 2, 256, 64], "nsa_gate_b": [2, 48],
    "moba_w_in": [2, D, MOBA_IN], "moba_w_out": [2, D, D], "moba_q_norm": [2, 64], "moba_k_norm": [2, 64],
}


DBG_STAGE = [9]


def build(layers, n_seq, first_from_x=True):
    nc = bass.Bass("TRN2", target_bir_lowering=False, dynamic_dma_scratch_size=8192)
    dr = {}
    dr["x"] = nc.dram_tensor("x", [n_seq, S, D], F32, kind="ExternalInput").ap()
    dr["c"] = nc.dram_tensor("c", [n_seq, D], F32, kind="ExternalInput").ap()
    dr["positions"] = nc.dram_tensor("positions", [n_seq, S], I32, kind="ExternalInput").ap()
    for k, shp in WEIGHT_SHAPES.items():
        dr[k] = nc.dram_tensor(k, shp, F32, kind="ExternalInput").ap()
    for k, shp in CONST_SHAPES.items():
        dr[k] = nc.dram_tensor(k, shp, F32, kind="ExternalInput").ap()
    y_d = nc.dram_tensor("y", [n_seq, S, D], F32, kind="ExternalOutput").ap()
    mod_d = nc.dram_tensor("mod_scr", [n_seq, 4, 3 * D], F32, kind="Internal").ap()

    with ExitStack() as es:
        E = es.enter_context
        sch = Sched(nc, es)

        def V(fn, r=(), w=()):
            sch.op("dve", fn, r, w)

        def A(fn, r=(), w=()):
            sch.op("act", fn, r, w)

        def G(fn, r=(), w=()):
            sch.op("pool", fn, r, w)

        def P(fn, r=(), w=()):
            sch.op("pe", fn, r, w)

        def sb(name, shape, dt):
            return E(nc.sbuf_tensor(name, shape, dt))

        T = {}

        def sb(name, shape, dt, stack=None):
            t = (stack or es).enter_context(nc.sbuf_tensor(name, shape, dt))
            T[name] = t
            return t

        ident = sb("ident", [128, 128], BF16)
        cb = sb("cb", [128, 128], BF16)
        cb2 = sb("cb2", [128, 128], BF16)
        cmask = sb("cmask", [128, S], BF16)
        fb = sb("fb", [128, NT, 32], F32)
        fbm = sb("fbm", [128, NT, 8], F32)
        selm = sb("selm", [128, 2, 64], BF16)
        invf = sb("invf", [128, 8], F32)
        onesc = sb("onesc", [128, 1], BF16)
        W1 = sb("W1", [128, 20864], BF16)
        WO = sb("WO", [128, 8, D], BF16)
        KA = sb("KA", [128, NG, S], BF16)
        VA = sb("VA", [128, NT, NG, 65], BF16)
        KW = sb("KW", [128, NG, 4, 128], BF16)
        VW = sb("VW", [128, 4, NG, 65], BF16)
        onef = sb("onef", [128, 1], F32)
        epsf = sb("epsf", [128, 1], F32)
        kcmpT = sb("kcmpT", [128, NG, 128], BF16)
        vcmp = sb("vcmp", [128, NG, 97], BF16)
        modbc = sb("modbc", [128, 3, D], F32)
        cosT = sb("cosT", [128, NT, 8], F32)
        sinT = sb("sinT", [128, NT, 8], F32)
        nsinT = sb("nsinT", [128, NT, 8], F32)
        cosC = sb("cosC", [128, 8], F32)
        sinC = sb("sinC", [128, 8], F32)
        nsinC = sb("nsinC", [128, 8], F32)
        gq = sb("gq", [128, 64], F32)
        gk = sb("gk", [128, 3, 64], F32)
        gateb = sb("gateb", [128, 48], F32)
        kmT = sb("kmT", [64, NG, 8], BF16)
        kmtmp = sb("kmtmp", [64, NG], F32)
        ksum = sb("ksum", [64, NT, NG], F32)
        cw2 = sb("cw2", [128, 2, 2, 64], BF16)
        peT2 = sb("peT2", [128, 2, 16], BF16)
        peT2f = sb("peT2f", [128, 2, 16], F32)
        b1c = sb("b1c", [128, 2, 2], F32)
        b1p = sb("b1p", [128, 2, 2], F32)
        xt = [sb("xt%d" % i, [128, D], F32) for i in range(2)]
        st = sb("st", [128, 8], F32)
        hb = sb("hb", [128, D], BF16)
        hT = [sb("hT%d" % i, [128, 8, 128], BF16) for i in range(2)]
        sq = sb("sq", [128, 512], F32)
        ssq = sb("ssq", [128, 16], F32)
        rq = sb("rq", [128, 16], F32)
        qn = sb("qn", [128, 8, 64], F32)
        rtmp = sb("rtmp", [128, 8, 16], F32)
        rtmp2 = sb("rtmp2", [128, 8, 16], F32)
        kb = sb("kb", [128, NG, 64], BF16)
        ezs = sb("ezs", [128, D], F32)
        PJ = [E(nc.psum_tensor("PJ%d" % i, [128, 512], F32)) for i in range(2)]
        PS = [E(nc.psum_tensor("PS%d" % i, [128, 512], F32)) for i in range(2)]
        PO = [E(nc.psum_tensor("PO%d" % i, [128, 512], F32)) for i in range(2)]
        PT = [E(nc.psum_tensor("PT%d" % i, [128, 1024], BF16)) for i in range(2)]
        rr = {"PJ": 0, "PS": 0, "PO": 0, "PT": 0, "pT": 0, "xt": 0, "hT": 0}
        uid = [0]

        fixed_bank = [None]
        pin_pt = [None]

        def nxt(k, n=2):
            if pin_pt[0] is not None and k == "PT":
                return pin_pt[0]
            if fixed_bank[0] is not None and k == "PT":
                return fixed_bank[0]
            if fixed_bank[0] == 1 and k == "PJ":
                raise RuntimeError("atomic-mode code must not allocate PJ banks (owned by the deferred front stage)")
            v = rr[k]
            rr[k] = (v + 1) % n
            return v

        def alloc_attn_tiles(stack):
            uid[0] += 1
            u = "_%d" % uid[0]
            for nm, shp, dt in [("qaug", [128, NH, 96], BF16), ("qT1", [128, NH, 128], BF16), ("qaT", [128, NH, 128], BF16),
                                ("szb", [128, D], BF16), ("gates", [128, NH, 3], F32), ("oacc", [128, NH, 64], F32)]:
                T[nm] = [stack.enter_context(nc.sbuf_tensor("%s%s_%d" % (nm, u, k), shp, dt)) for k in range(2)]
                if nm in ("qT1", "qaT"):
                    for k in range(2):
                        G(lambda e, t=T[nm][k]: e.memset(t[:], 0.0), w=[(nm, k)])
            for nm, shp, dt in [("gtmp", [128, 48], F32),
                                ("sg", [128, NH, 8], F32), ("m8", [128, NH, 8], F32), ("imp", [128, NG, 32], F32),
                                ("rs", [128, 4], F32), ("cc", [128, 4], F32), ("otmp", [128, 4, 64], F32),
                                ("og", [128, D], BF16), ("ogT", [128, 8, 128], BF16),
                                ("ytmp", [128, 512], F32), ("pT0", [128, 512], BF16), ("pT1", [128, 512], BF16),
                                ("pT2", [128, 512], BF16)]:
                T[nm] = stack.enter_context(nc.sbuf_tensor(nm + u, shp, dt))

        def alloc_cmp_tiles(stack):
            uid[0] += 1
            u = "_%d" % uid[0]
            for nm, shp, dt in [("KC2", [128, 2, NG, 1024], BF16), ("hidT", [128, 4, NG, 127], BF16),
                                ("hu", [128, 508], F32), ("hw", [128, 508], F32), ("kvb", [128, 512], BF16)]:
                T[nm] = stack.enter_context(nc.sbuf_tensor(nm + u, shp, dt))

        def cload(tile_ap, src, key, queue="pool"):
            sch.dma(queue, tile_ap, src, writes=[key])

        cload(ident[:], dr["c_ident"], "ident")
        cload(cb[:], dr["c_cb"], "cb")
        cload(cb2[:], dr["c_cb2"], "cb2")
        for h in range(2):
            cload(cmask[:, h * 1024:(h + 1) * 1024], dr["c_cmask"][:, h * 1024:(h + 1) * 1024], "cmask")
        cload(fb[:].rearrange("p a b -> p (a b)"), dr["c_fb"], "fb", "sp")
        cload(fbm[:].rearrange("p a b -> p (a b)"), dr["c_fbm"], "fbm", "sp")
        cload(selm[:].rearrange("p a b -> p (a b)"), dr["c_sel"], "selm")
        cload(invf[:], dr["c_invf"], "invf", "sp")
        V(lambda e: e.memset(onesc[:], 1.0), w=["onesc"])
        V(lambda e: e.memset(onef[:], 1.0), w=["onef"])
        V(lambda e: e.memset(epsf[:], EPS), w=["epsf"])
        V(lambda e: e.memset(VA[:, :, :, 64:65], 1.0), w=["VA1"])
        V(lambda e: e.memset(VW[:, :, :, 64:65], 1.0), w=["VW1"])
        V(lambda e: e.memset(kmT[:], 0.0), w=["kmT"])
        V(lambda e: e.memset(ksum[:], 0.0), w=["ksum"])
        V(lambda e: e.memset(kcmpT[:], 0.0), w=["kcmpT"])
        G(lambda e: e.memset(KA[64:128, :, :], 0.0), w=["KAind"])
        G(lambda e: e.memset(KW[:], 0.0), w=[("KW", 0), ("KW", 1), ("KW", 2), ("KW", 3)])
        V(lambda e: e.memset(vcmp[:], 0.0), w=["vcmp"])
        V(lambda e: e.memset(vcmp[:, :, 64:65], 1.0), w=["vcmp"])
        for g in range(NG):
            cload(vcmp[:, g, 65:97], dr["c_selw"], "vcmp")

        with ExitStack() as es2:
            E2 = es2.enter_context
            cT = E2(nc.sbuf_tensor("cT", [128, 8, 2], F32))
            cTe = E2(nc.sbuf_tensor("cTe", [128, 8, 2], F32))
            cTb = E2(nc.sbuf_tensor("cTb", [128, 8, 2], BF16))
            adaw = [E2(nc.sbuf_tensor("adaw%d" % i, [128, 8, 512], BF16)) for i in range(3)]
            adab = E2(nc.sbuf_tensor("adab", [2, 512], F32))
            grow = E2(nc.sbuf_tensor("grow", [2, D], F32))
            modr = [E2(nc.sbuf_tensor("modr%d" % i, [2, 512], F32)) for i in range(2)]
            V(lambda e: e.memset(cT[:], 0.0), w=["cT"])
            for b in range(n_seq):
                sch.dma("sp", cT[:, :, b:b + 1], dr["c"][b].rearrange("(k p o) -> p k o", p=128, o=1),
                        writes=["cT"], allow_slow_non_contiguous=True)
            A(lambda e: e.activation(out=cTe[:], in_=cT[:], func=AF.Exp, scale=-1.0), r=["cT"], w=["cTe"])
            V(lambda e: e.tensor_scalar_add(cTe[:], cTe[:], 1.0), r=["cTe"], w=["cTe"])
            V(lambda e: e.reciprocal(cTe[:], cTe[:]), r=["cTe"], w=["cTe"])
            V(lambda e: e.tensor_tensor(out=cTb[:], in0=cT[:], in1=cTe[:], op=ALU.mult), r=["cT", "cTe"], w=["cTb"])
            ci = 0
            for L in layers:
                for b in range(2):
                    sch.dma("sp", grow[b:b + 1, :], dr["norm_g"][L:L + 1, :], writes=["grow"])
                for ch in range(6):
                    slot = ci % 3
                    ms = ci % 2
                    ci += 1
                    sch.dma("pool", adaw[slot][:], dr["ada_w"][L, :, ch * 512:(ch + 1) * 512].rearrange("(k p) n -> p k n", p=128),
                            writes=[("adaw", slot)])
                    for b in range(2):
                        sch.dma("sp", adab[b:b + 1, :], dr["ada_b"][L:L + 1, ch * 512:(ch + 1) * 512], writes=["adab"])
                    pj = nxt("PJ")
                    for k in range(8):
                        P(lambda e, k=k, pj=pj, slot=slot: e.matmul(PJ[pj][0:2, :], cTb[:, k, :], adaw[slot][:, k, :],
                                                                     start=(k == 0), stop=(k == 7)),
                          r=["cTb", ("adaw", slot)], w=[("PJ", pj)])
                    V(lambda e, pj=pj, ms=ms: e.tensor_tensor(out=modr[ms][:], in0=PJ[pj][0:2, :], in1=adab[:], op=ALU.add),
                      r=[("PJ", pj), "adab"], w=[("modr", ms)])
                    if ch in (2, 3):
                        g0 = (ch - 2) * 512
                        V(lambda e, ms=ms, g0=g0: e.scalar_tensor_tensor(out=modr[ms][:], in0=modr[ms][:], scalar=1.0,
                                                                         in1=grow[:, g0:g0 + 512], op0=ALU.add, op1=ALU.mult),
                          r=[("modr", ms), "grow"], w=[("modr", ms)])
                        dst0 = g0
                    elif ch in (0, 1):
                        dst0 = D + ch * 512
                    else:
                        dst0 = 2 * D + (ch - 4) * 512
                    for b in range(n_seq):
                        sch.dma("sp", mod_d[b, L:L + 1, dst0:dst0 + 512], modr[ms][b:b + 1, :],
                                reads=[("modr", ms)], writes=[("mod", b, L)])
            sch.barrier()

        def rstd_from_ss(ss_ap, out_ap, n, keys_r, key_w):
            A(lambda e: e.activation(out=out_ap, in_=ss_ap, func=AF.Ln, scale=1.0 / n, bias=epsf[:, 0:1]), r=list(keys_r) + ["epsf"], w=[key_w])
            A(lambda e: e.activation(out=out_ap, in_=out_ap, func=AF.Exp, scale=-0.5), r=[key_w], w=[key_w])

        def load_norm_h(b, i, src_ap, xs=None):
            if xs is None:
                xs = nxt("xt")
            sch.dma("sp", xt[xs][:], src_ap, reads=[("res", b, i)], writes=[("xt", xs)])
            A(lambda e: e.activation(out=hb[:], in_=xt[xs][:], func=AF.Square, accum_out=st[:, 0:1]),
              r=[("xt", xs)], w=["hb", "st0"])
            rstd_from_ss(st[:, 0:1], st[:, 1:2], D, ["st0"], "st1")
            V(lambda e: e.scalar_tensor_tensor(out=ezs[:], in0=xt[xs][:], scalar=st[:, 1:2], in1=modbc[:, 0, :],
                                               op0=ALU.mult, op1=ALU.mult), r=[("xt", xs), "st1", ("modbc", 0)], w=[("ezs", 0), ("ezs", 512)])
            V(lambda e: e.tensor_tensor(out=hb[:], in0=ezs[:], in1=modbc[:, 1, :], op=ALU.add), r=[("ezs", 0), ("ezs", 512), ("modbc", 1)], w=["hb"])
            pt = nxt("PT")
            for k in range(8):
                P(lambda e, k=k, pt=pt: e.transpose(PT[pt][:, k * 128:(k + 1) * 128], hb[:, k * 128:(k + 1) * 128], ident[:]),
                  r=["hb", "ident"], w=[("PT", pt)])
            hs = nxt("hT")
            V(lambda e: e.tensor_copy(out=hT[hs][:].rearrange("p a b -> p (a b)"), in_=PT[pt][:]),
              r=[("PT", pt)], w=[("hT", hs)])
            return xs, hs

        def project(hs, wcol0, ncols, wstride):
            pj = nxt("PJ")
            for k in range(8):
                P(lambda e, k=k, pj=pj: e.matmul(PJ[pj][:, 0:ncols], hT[hs][:, k, :],
                                                 W1[:, k * wstride + wcol0:k * wstride + wcol0 + ncols],
                                                 start=(k == 0), stop=(k == 7)),
                  r=[("hT", hs)] + [("W1", c) for c in range(wcol0 // 512, (wcol0 + ncols - 1) // 512 + 1)], w=[("PJ", pj)])
            return pj

        def head_norm_rope(pj, c0, nh, g_ap, cos_ap, sin_ap, nsin_ap, out_ap, wkey, alt=False):
            src = PJ[pj][:, c0:c0 + nh * 64]
            src3 = src.rearrange("p (a b) -> p a b", a=nh)
            if alt:
                sq_ = ezs[:, 0:nh * 64]
                qn_ = ezs[:, 256:256 + nh * 64].rearrange("p (a b) -> p a b", a=nh)
                rt_ = ezs[:, 512:512 + nh * 16].rearrange("p (a b) -> p a b", a=nh)
                rt2_ = ezs[:, 576:576 + nh * 16].rearrange("p (a b) -> p a b", a=nh)
                ssq_ = ezs[:, 640:640 + nh]
                rq_ = ezs[:, 648:648 + nh]
                ka = [("ezs", 0), ("ezs", 512)]
                k_sq = k_ssq = k_rq = k_qn = k_rt = k_rt2 = None
                ksq, kssq, krq, kqn, krt, krt2 = ka, ka, ka, ka, ka, ka
            else:
                sq_ = sq[:, 0:nh * 64]
                qn_ = qn[:, 0:nh, :]
                rt_ = rtmp[:, 0:nh, :]
                rt2_ = rtmp2[:, 0:nh, :]
                ssq_ = ssq[:, 0:nh]
                rq_ = rq[:, 0:nh]
                ksq, kssq, krq, kqn, krt, krt2 = ["sq"], ["ssq"], ["rq"], ["qn"], ["rtmp"], ["rtmp2"]
            A(lambda e: e.activation(out=sq_, in_=src, func=AF.Square), r=[("PJ", pj)], w=ksq)
            V(lambda e: e.tensor_reduce(out=ssq_, in_=sq_.rearrange("p (a b) -> p a b", a=nh), axis=AX.X, op=ALU.add), r=ksq, w=kssq)
            A(lambda e: e.activation(out=rq_, in_=ssq_, func=AF.Ln, scale=1.0 / HD, bias=epsf[:, 0:1]), r=kssq + ["epsf"], w=krq)
            A(lambda e: e.activation(out=rq_, in_=rq_, func=AF.Exp, scale=-0.5), r=krq, w=krq)
            V(lambda e: e.tensor_tensor(out=qn_, in0=src3, in1=rq_.unsqueeze(2).to_broadcast([128, nh, 64]), op=ALU.mult),
              r=[("PJ", pj)] + krq, w=kqn)
            V(lambda e: e.tensor_tensor(out=qn_, in0=qn_, in1=g_ap.unsqueeze(1).to_broadcast([128, nh, 64]), op=ALU.mult),
              r=kqn + ["gains"], w=kqn)
            cosb = cos_ap.unsqueeze(1).to_broadcast([128, nh, 8])
            sinb = sin_ap.unsqueeze(1).to_broadcast([128, nh, 8])
            nsinb = nsin_ap.unsqueeze(1).to_broadcast([128, nh, 8])
            V(lambda e: e.tensor_tensor(out=rt_[:, :, 0:8], in0=qn_[:, :, 0:8], in1=cosb, op=ALU.mult), r=kqn + ["rope", "ropeC"], w=krt)
            V(lambda e: e.tensor_tensor(out=rt_[:, :, 8:16], in0=qn_[:, :, 8:16], in1=cosb, op=ALU.mult), r=kqn + ["rope", "ropeC"], w=krt)
            V(lambda e: e.tensor_tensor(out=rt2_[:, :, 0:8], in0=qn_[:, :, 8:16], in1=nsinb, op=ALU.mult), r=kqn + ["rope", "ropeC"], w=krt2)
            V(lambda e: e.tensor_tensor(out=rt2_[:, :, 8:16], in0=qn_[:, :, 0:8], in1=sinb, op=ALU.mult), r=kqn + ["rope", "ropeC"], w=krt2)
            V(lambda e: e.tensor_tensor(out=out_ap[:, :, 0:16], in0=rt_, in1=rt2_, op=ALU.add), r=krt + krt2, w=[wkey])
            A(lambda e: e.activation(out=out_ap[:, :, 16:64], in_=qn_[:, :, 16:64], func=AF.Copy), r=kqn, w=[wkey])

        def merge_streams(fa, fb):
            outer = sch.capture
            sch.capture = []
            fa()
            la = sch.capture
            sch.capture = []
            fb()
            lb = sch.capture
            sch.capture = outer
            na, nb = len(la), len(lb)
            ia = ib = 0
            while ia < na or ib < nb:
                if ib >= nb or (ia < na and ia * nb <= ib * na):
                    it = la[ia]
                    ia += 1
                else:
                    it = lb[ib]
                    ib += 1
                if outer is not None:
                    outer.append(it)
                else:
                    sch.replay(it)

        def silu_from_psum(pj, n, out_ap, c0, wkey):
            src = PJ[pj][:, 0:n]
            ek = ("ezs", c0)
            A(lambda e: e.activation(out=ezs[:, c0:c0 + n], in_=src, func=AF.Exp, scale=-1.0), r=[("PJ", pj)], w=[ek])
            A(lambda e: e.activation(out=ezs[:, c0:c0 + n], in_=ezs[:, c0:c0 + n], func=AF.Ln, bias=onef[:, 0:1]), r=[ek, "onef"], w=[ek])
            A(lambda e: e.activation(out=ezs[:, c0:c0 + n], in_=ezs[:, c0:c0 + n], func=AF.Exp, scale=-1.0), r=[ek], w=[ek])
            V(lambda e: e.tensor_tensor(out=out_ap, in0=src, in1=ezs[:, c0:c0 + n], op=ALU.mult), r=[("PJ", pj), ek], w=[wkey])

        def transpose_heads(src_tile, ncol, dst_tile, rkeys, wkey, halves=(0, 1)):
            for half in halves:
                pt = nxt("PT")
                for hh in range(8):
                    h = half * 8 + hh
                    P(lambda e, h=h, hh=hh, pt=pt: e.transpose(PT[pt][0:ncol, hh * 128:(hh + 1) * 128], src_tile[:, h, 0:ncol], ident[:]),
                      r=rkeys + ["ident"], w=[("PT", pt)])
                if half == 0:
                    V(lambda e, pt=pt: e.tensor_copy(out=dst_tile[0:ncol, 0:8, :].rearrange("p a b -> p (a b)"), in_=PT[pt][0:ncol, :]),
                      r=[("PT", pt)], w=[wkey])
                else:
                    V(lambda e, pt=pt: e.tensor_copy(out=dst_tile[0:ncol, 8:16, :].rearrange("p a b -> p (a b)"), in_=PT[pt][0:ncol, :]),
                      r=[("PT", pt)], w=[wkey])

        def k_to_KT(dst_ap, wkey, ncopy=128):
            pt = nxt("PT")
            for g in range(NG):
                P(lambda e, g=g, pt=pt: e.transpose(PT[pt][0:64, g * 128:(g + 1) * 128], kb[:, g, :], ident[:]),
                  r=["kb", "ident"], w=[("PT", pt)])
            V(lambda e, pt=pt: e.tensor_copy(out=dst_ap, in_=PT[pt][0:64, 0:512].rearrange("p (a b) -> p a b", a=NG)[:, :, 0:ncopy]),
              r=[("PT", pt)], w=[wkey])

        class Attn:
            def __init__(self):
                self.pend = None

            def unit(self, lhsT_k, rhs_q, bias_ap, v_ap, vw, po, first, rk, on_done=None):
                ps = nxt("PS")
                P(lambda e: e.matmul(PS[ps][:], lhsT_k, rhs_q, start=True, stop=(bias_ap is None)),
                  r=rk, w=[("PS", ps)])
                if bias_ap is not None:
                    P(lambda e: e.matmul(PS[ps][:], ident[:], bias_ap, start=False, stop=True),
                      r=["ident", "cb", "cb2", "cmask"], w=[("PS", ps)])
                sl = nxt("pT", 3)
                pt_t = T["pT%d" % sl]
                A(lambda e: e.activation(out=pt_t[:], in_=PS[ps][:], func=AF.Exp, scale=0.125),
                  r=[("PS", ps)], w=[("pT", sl)])
                self.flush()
                self.pend = (pt_t, sl, v_ap, vw, po, first, rk, on_done)

            def flush(self):
                if self.pend is None:
                    return
                pt_t, sl, v_ap, vw, po, first, rk, on_done = self.pend
                self.pend = None
                for r in range(4):
                    P(lambda e, r=r: e.matmul(PO[po][:, r * vw:(r + 1) * vw], pt_t[:, r * 128:(r + 1) * 128], v_ap,
                                              start=(first and r == 0), stop=True, skip_group_check=True),
                      r=[("pT", sl)] + rk, w=[("PO", po)])
                if on_done is not None:
                    on_done()

        def branch_post(po, vw, g, gate_idx, first_branch, clamp=False, gates=None, sl=0):
            rs, cc, otmp, oacc = T["rs"], T["cc"], T["otmp"], T["oacc"][sl]
            pov = PO[po][:, 0:4 * vw].rearrange("p (a b) -> p a b", a=4)
            if clamp:
                V(lambda e: e.tensor_scalar(out=rs[:], in0=pov[:, :, 64:65].rearrange("p a b -> p (a b)"), scalar1=1e-30, scalar2=None, op0=ALU.max),
                  r=[("PO", po)], w=["rs"])
                V(lambda e: e.reciprocal(rs[:], rs[:]), r=["rs"], w=["rs"])
            else:
                V(lambda e: e.reciprocal(rs[:], pov[:, :, 64:65].rearrange("p a b -> p (a b)")), r=[("PO", po)], w=["rs"])
            if gate_idx is not None:
                V(lambda e: e.tensor_tensor(out=cc[:], in0=rs[:], in1=gates[:, 4 * g:4 * g + 4, gate_idx:gate_idx + 1].rearrange("p a b -> p (a b)"),
                                            op=ALU.mult), r=["rs", ("gates", 0), ("gates", 1)], w=["cc"])
                cap = cc
                ck = "cc"
            else:
                cap = rs
                ck = "rs"
            if first_branch:
                V(lambda e: e.tensor_tensor(out=oacc[:, 4 * g:4 * g + 4, :], in0=pov[:, :, 0:64],
                                            in1=cap[:].unsqueeze(2).to_broadcast([128, 4, 64]), op=ALU.mult),
                  r=[("PO", po), ck], w=[("oacc", sl, g)])
            else:
                V(lambda e: e.tensor_tensor(out=otmp[:], in0=pov[:, :, 0:64],
                                            in1=cap[:].unsqueeze(2).to_broadcast([128, 4, 64]), op=ALU.mult),
                  r=[("PO", po), ck], w=["otmp"])
                G(lambda e: e.tensor_tensor(out=oacc[:, 4 * g:4 * g + 4, :], in0=oacc[:, 4 * g:4 * g + 4, :], in1=otmp[:], op=ALU.add),
                  r=["otmp", ("oacc", sl, g)], w=[("oacc", sl, g)])

        def out_proj_store(b, i, src_d_, sl):
            oacc, og, ogT, ytmp, szb = T["oacc"][sl], T["og"], T["ogT"], T["ytmp"], T["szb"][sl]
            xs = 1
            sch.dma("sp", xt[xs][:], src_d_[b, i * 128:(i + 1) * 128, :], reads=[("res", b, i)], writes=[("xt", xs)])
            V(lambda e: e.tensor_tensor(out=og[:], in0=oacc[:].rearrange("p a b -> p (a b)"), in1=szb[:], op=ALU.mult),
              r=[("oacc", sl, g) for g in range(NG)] + [("szb", sl)], w=["og"])
            yield
            pt = nxt("PT")
            for k in range(8):
                P(lambda e, k=k, pt=pt: e.transpose(PT[pt][:, k * 128:(k + 1) * 128], og[:, k * 128:(k + 1) * 128], ident[:]),
                  r=["og", "ident"], w=[("PT", pt)])
            V(lambda e: e.tensor_copy(out=ogT[:].rearrange("p a b -> p (a b)"), in_=PT[pt][:]),
              r=[("PT", pt)], w=["ogT"])
            yield
            for half in range(2):
                pj = rr["PO"]
                for k in range(8):
                    P(lambda e, k=k, pj=pj, half=half: e.matmul(PO[pj][:], ogT[:, k, :], WO[:, k, half * 512:(half + 1) * 512],
                                                               start=(k == 0), stop=(k == 7)),
                      r=["ogT", ("WO", half)], w=[("PO", pj)])
                V(lambda e, pj=pj, half=half: e.tensor_tensor(out=ytmp[:], in0=PO[pj][:], in1=modbc[:, 2, half * 512:(half + 1) * 512], op=ALU.mult),
                  r=[("PO", pj), ("modbc", 2)], w=["ytmp"])
                G(lambda e, half=half: e.tensor_tensor(out=xt[xs][:, half * 512:(half + 1) * 512], in0=ytmp[:],
                                                       in1=xt[xs][:, half * 512:(half + 1) * 512], op=ALU.add),
                  r=["ytmp", ("xt", xs)], w=[("xt", xs)])
                if half == 0:
                    yield
            sch.dma("sp", y_d[b, i * 128:(i + 1) * 128, :], xt[xs][:], reads=[("xt", xs)], writes=[("res", b, i)])
            yield

        def load_w_cols(dst_col0, src_ap_fn, ncols, wstride, key="W1"):
            c = 0
            w1v = W1[:, 0:8 * wstride].rearrange("p (k n) -> p k n", k=8)
            while c < ncols:
                n = min(512 - (dst_col0 + c) % 512, ncols - c)
                sch.dma("pool", w1v[:, :, dst_col0 + c:dst_col0 + c + n], src_ap_fn(c, n), writes=[(key, (dst_col0 + c) // 512)])
                c += n

        def load_wo(wo_src):
            for hlf in range(2):
                sch.dma("pool", WO[:, :, hlf * 512:(hlf + 1) * 512],
                        wo_src[:, hlf * 512:(hlf + 1) * 512].rearrange("(k p) n -> p k n", p=128), writes=[("WO", hlf)])

        def bload(dst_ap, src_row_ap, n, key):
            sch.dma("sp", dst_ap, src_row_ap.to_broadcast([128, n]), writes=[key])

        at_box = [None]

        def at_flush():
            at_box[0].flush()

        def run_pipelined(front, back_units, tail, n_tiles, prologue=None):
            def capture_front(i):
                fixed_bank[0] = 0
                sch.capture = []
                crit = None
                for m in front(i):
                    if m == "crit":
                        crit = len(sch.capture)
                ops = sch.capture
                sch.capture = None
                fixed_bank[0] = 1
                return ops, (len(ops) if crit is None else crit)

            ops, crit_pos = capture_front(0)
            pos = 0
            while pos < crit_pos:
                sch.replay(ops[pos])
                pos += 1
            leftover = ops[pos:]
            if prologue is not None:
                prologue()
            tail_gen = iter(())
            DONE = object()
            for i in range(n_tiles):
                if i + 1 < n_tiles:
                    new_ops, new_crit = capture_front(i + 1)
                else:
                    new_ops, new_crit = [], 0
                n_left = len(leftover)
                ops = leftover + new_ops
                crit_pos = n_left + new_crit
                pos = 0
                units = back_units(i)
                nf = units.index("need_front") if "need_front" in units else len(units)
                nu_front = sum(1 for u in units[:nf] if u != "need_front")
                ui = 0
                tail_done = False
                for u in units:
                    if u == "need_front":
                        for _ in tail_gen:
                            pass
                        tail_done = True
                        while pos < crit_pos:
                            sch.replay(ops[pos])
                            pos += 1
                        continue
                    u()
                    if ui >= 1 and not tail_done:
                        if next(tail_gen, DONE) is DONE:
                            tail_done = True
                    lim = len(ops)
                    if pos < lim:
                        if pos < crit_pos:
                            left = max(1, nu_front - ui - 1)
                            k = -(-(crit_pos - pos) // left)
                        else:
                            k = 3
                        for _ in range(k):
                            if pos < lim:
                                sch.replay(ops[pos])
                                pos += 1
                    ui += 1
                for _ in tail_gen:
                    pass
                while pos < crit_pos:
                    sch.replay(ops[pos])
                    pos += 1
                leftover = ops[pos:]
                tail_gen = tail(i)
            for it in leftover:
                sch.replay(it)
            at_flush()
            for _ in tail_gen:
                pass
            fixed_bank[0] = None

        def moba_layer(b, j, src_d):
            WS = MOBA_IN
            load_w_cols(0, lambda c, n: dr["moba_w_in"][j, :, c:c + n].rearrange("(k p) n -> p k n", p=128), MOBA_IN, WS)
            bload(gq[:], dr["moba_q_norm"][j:j + 1, :], 64, "gains")
            bload(gk[:, 0, :], dr["moba_k_norm"][j:j + 1, :], 64, "gains")
            for g in range(NG):
                sch.dma("pool", KA[64:72, g, :], dr["c_ind_moba"], writes=["KAind"])
            load_wo(dr["moba_w_out"][j])
            with ExitStack() as esc:
                alloc_attn_tiles(esc)
                sg, m8 = T["sg"], T["m8"]
                ctx = {}

                def front(i):
                    sl = i % 2
                    qaug, qT1, qaT, szb = T["qaug"][sl], T["qT1"][sl], T["qaT"][sl], T["szb"][sl]
                    xs, hs = load_norm_h(b, i, src_d[b, i * 128:(i + 1) * 128, :], xs=0)
                    ctx[i] = src_d
                    yield
                    cs, sn, nsn = cosT[:, i, :], sinT[:, i, :], nsinT[:, i, :]
                    pq0 = project(hs, 0, 512, WS)
                    pq1 = project(hs, 512, 512, WS)
                    head_norm_rope(pq0, 0, 8, gq[:], cs, sn, nsn, qaug[:, 0:8, 0:64], ("qaug", sl, 0))
                    pkv = project(hs, 1024, 512, WS)
                    transpose_heads(qaug, 64, qT1, [("qaug", sl, 0)], ("qT1", sl), halves=(0,))
                    merge_streams(lambda: head_norm_rope(pq1, 0, 8, gq[:], cs, sn, nsn, qaug[:, 8:16, 0:64], ("qaug", sl, 1)),
                                  lambda: head_norm_rope(pkv, 0, 4, gk[:, 0, :], cs, sn, nsn, kb[:], "kb", alt=True))
                    transpose_heads(qaug, 64, qT1, [("qaug", sl, 1)], ("qT1", sl), halves=(1,))
                    A(lambda e, pj=pkv, i=i: e.activation(out=VA[:, i, :, 0:64], in_=PJ[pj][:, 256:512].rearrange("p (a b) -> p a b", a=NG), func=AF.Copy),
                      r=[("PJ", pkv)], w=[("VA", i)])
                    k_to_KT(KA[0:64, :, i * 128:(i + 1) * 128], ("KA", i))
                    pj3 = nxt("PJ")
                    for h in range(NH):
                        P(lambda e, h=h, pj3=pj3: e.matmul(PJ[pj3][:, h * 8:(h + 1) * 8], qT1[0:64, h, :], kmT[0:64, h // 4, :], start=True, stop=True),
                          r=[("qT1", sl), "kmT"], w=[("PJ", pj3)])
                    V(lambda e, pj3=pj3, i=i: e.tensor_tensor(out=sg[:], in0=PJ[pj3][:, 0:128].rearrange("p (a b) -> p a b", a=NH),
                                                             in1=fbm[:, i, :].unsqueeze(1).to_broadcast([128, NH, 8]), op=ALU.add),
                      r=[("PJ", pj3), "fbm"], w=["sg"])
                    for h in range(NH):
                        V(lambda e, h=h: e.max(out=m8[:, h, :], in_=sg[:, h, :]), r=["sg"], w=[("m8", h)])
                    V(lambda e: e.tensor_tensor(out=qaug[:, :, 64:72], in0=sg[:], in1=m8[:, :, 3:4].to_broadcast([128, NH, 8]), op=ALU.is_lt),
                      r=["sg"] + [("m8", h) for h in range(NH)], w=[("qaug", sl, 2)])
                    transpose_heads(qaug, 72, qaT, [("qaug", sl, 0), ("qaug", sl, 1), ("qaug", sl, 2)], ("qaT", sl))
                    yield "crit"
                    pj2 = nxt("PJ")
                    for g in range(NG):
                        P(lambda e, g=g, pj2=pj2: e.matmul(PJ[pj2][0:64, g:g + 1], kb[:, g, :], onesc[:, 0:1], start=True, stop=True),
                          r=["kb", "onesc"], w=[("PJ", pj2)])
                    V(lambda e, pj2=pj2, i=i: e.tensor_copy(out=ksum[:, i, :], in_=PJ[pj2][0:64, 0:NG]), r=[("PJ", pj2)], w=["ksum"])
                    if i % 2 == 1:
                        jb = i // 2
                        V(lambda e, i=i: e.tensor_tensor(out=kmtmp[:], in0=ksum[:, i - 1, :], in1=ksum[:, i, :], op=ALU.add), r=["ksum"], w=["kmtmp"])
                        V(lambda e, jb=jb: e.tensor_scalar(out=kmT[:, :, jb:jb + 1].rearrange("p a b -> p (a b)"), in0=kmtmp[:], scalar1=1.0 / 256.0,
                                                           scalar2=None, op0=ALU.mult), r=["kmtmp"], w=["kmT"])
                    pz0 = project(hs, 1536, 512, WS)
                    pz1 = project(hs, 2048, 512, WS)
                    silu_from_psum(pz0, 512, szb[:, 0:512], 0, ("szb", sl))
                    silu_from_psum(pz1, 512, szb[:, 512:1024], 512, ("szb", sl))
                    yield

                at = Attn()

                def back_units(i):
                    sl = i % 2
                    qaT = T["qaT"][sl]
                    units = []
                    for g in range(NG):
                        po_box = []
                        for jj in range(i + 1):
                            def u(g=g, jj=jj, po_box=po_box):
                                if jj == 0:
                                    po_box.append(nxt("PO"))
                                po = po_box[0]
                                bias = cb[:].unsqueeze(1).to_broadcast([128, 4, 128]) if jj == i else None
                                done = (lambda: branch_post(po, 65, g, None, True, sl=sl)) if jj == i else None
                                at.unit(KA[:, g, jj * 128:(jj + 1) * 128], qaT[:, 4 * g:4 * g + 4, :], bias, VA[:, jj, g, :], 65, po,
                                        jj == 0, [("KA", jj), "KAind", ("qaT", sl), ("VA", jj), "VA1"], done)
                            units.append(u)
                    return units

                def tail(i):
                    return out_proj_store(b, i, ctx[i], i % 2)

                at_box[0] = at
                run_pipelined(front, back_units, tail, NT)
                sch.barrier()

        def nsa_layer(b, j, src_d):
            w_in = dr["nsa_w_in"]
            load_w_cols(0, lambda c, n: w_in[j, :, 1024 + c:1024 + c + n].rearrange("(k p) n -> p k n", p=128), 1024, 1024)
            CW1 = W1[:, 8192:16384].rearrange("p (a m h) -> p a m h", a=2, m=16)
            for kv in range(2):
                for mh in range(2):
                    sch.dma("pool", CW1[:, kv, mh * 8:(mh + 1) * 8, :],
                            dr["nsa_cmp_w1"][j, kv, mh * 1024:(mh + 1) * 1024, :].rearrange("(m p) h -> p m h", p=128), writes=[("CW1", kv, mh)])
                sch.dma("pool", cw2[:, kv, :, :], dr["nsa_cmp_w2"][j, kv].rearrange("(h p) d -> p h d", p=128), writes=["cw2"])
                sch.dma("sp", peT2f[:, kv, :], dr["nsa_cmp_pe"][j, kv].rearrange("l d -> (l d)").rearrange("(m p) -> p m", p=128),
                        writes=["peT2f"], allow_slow_non_contiguous=True)
                sch.dma("sp", b1c[:, kv, :], dr["nsa_cmp_b1"][j, kv].rearrange("(h p) -> p h", p=128), writes=["b1c"],
                        allow_slow_non_contiguous=True)
            V(lambda e: e.tensor_copy(out=peT2[:], in_=peT2f[:]), r=["peT2f"], w=["peT2"])
            bload(gq[:], dr["nsa_q_norm"][j:j + 1, :], 64, "gains")
            for m in range(3):
                bload(gk[:, m, :], dr["nsa_k_norm"][j, m:m + 1, :], 64, "gains")
            bload(gateb[:], dr["nsa_gate_b"][j:j + 1, :], 48, "gateb")
            for g in range(NG):
                sch.dma("pool", KA[64:96, g, :], dr["c_ind_nsa"], writes=["KAind"])
            load_wo(dr["nsa_w_out"][j])
            if DBG_STAGE[0] < 2:
                return
            with ExitStack() as esa:
                alloc_cmp_tiles(esa)
                KC2, hidT, hu, hw, kvb = T["KC2"], T["hidT"], T["hu"], T["hw"], T["kvb"]

                def a_stage2(i, hs):
                    pjA = project(hs, 0, 512, 1024)
                    pjB = project(hs, 512, 512, 1024)
                    A(lambda e, pjA=pjA: e.activation(out=kvb[:], in_=PJ[pjA][:], func=AF.Copy), r=[("PJ", pjA)], w=["kvb"])
                    head_norm_rope(pjB, 0, 4, gk[:, 1, :], cosT[:, i, :], sinT[:, i, :], nsinT[:, i, :], kb[:], "kb")
                    A(lambda e, pjB=pjB, i=i: e.activation(out=VA[:, i, :, 0:64], in_=PJ[pjB][:, 256:512].rearrange("p (a b) -> p a b", a=NG), func=AF.Copy),
                      r=[("PJ", pjB)], w=[("VA", i)])
                    pjs = nxt("PJ")
                    for kv in range(2):
                        for g in range(NG):
                            for par in range(2):
                                c0 = (kv * 4 + g) * 64
                                P(lambda e, kv=kv, g=g, par=par, c0=c0, pjs=pjs: e.matmul(
                                    PJ[pjs][par * 64:(par + 1) * 64, c0:c0 + 64], kvb[:, kv * 256 + g * 64:kv * 256 + (g + 1) * 64],
                                    selm[:, par, :], start=True, stop=True), r=["kvb", "selm"], w=[("PJ", pjs)])
                    V(lambda e, pjs=pjs, i=i: e.tensor_copy(out=KC2[:, :, :, i * 64:(i + 1) * 64],
                                                           in_=PJ[pjs][:].rearrange("p (a g u) -> p a g u", a=2, g=NG)),
                      r=[("PJ", pjs)], w=[("KC2", i)])
                    k_to_KT(KA[0:64, :, i * 128:(i + 1) * 128], ("KA", i))

                def cap(fn, pt):
                    pin_pt[0] = pt
                    sch.capture = []
                    r = fn()
                    ops_ = sch.capture
                    sch.capture = None
                    pin_pt[0] = None
                    return ops_, r

                def merge(o1, o2):
                    n1, n2 = len(o1), len(o2)
                    i1 = i2 = 0
                    while i1 < n1 or i2 < n2:
                        if i2 >= n2 or (i1 < n1 and i1 * n2 <= i2 * n1):
                            sch.replay(o1[i1])
                            i1 += 1
                        else:
                            sch.replay(o2[i2])
                            i2 += 1

                o1, (xs0, hs_prev) = cap(lambda: load_norm_h(b, 0, src_d[b, 0:128, :]), 0)
                merge(o1, [])
                for i in range(NT):
                    if i + 1 < NT:
                        o1, (xs1, hs_next) = cap(lambda i=i: load_norm_h(b, i + 1, src_d[b, (i + 1) * 128:(i + 2) * 128, :]), 0)
                    else:
                        o1, hs_next = [], None
                    o2, _ = cap(lambda i=i, hs_prev=hs_prev: a_stage2(i, hs_prev), 1)
                    merge(o1, o2)
                    hs_prev = hs_next
                if DBG_STAGE[0] < 3:
                    sch.barrier()
                    return
                kc2keys = [("KC2", i) for i in range(NT)]
                pjb = nxt("PJ")
                for kv in range(2):
                    for half in range(2):
                        col = kv * 2 + half
                        for m in range(16):
                            P(lambda e, kv=kv, half=half, m=m, col=col, pjb=pjb: e.matmul(
                                PJ[pjb][:, col:col + 1], CW1[:, kv, m, half * 128:(half + 1) * 128], peT2[:, kv, m:m + 1],
                                start=(m == 0), stop=(m == 15)), r=[("CW1", kv, m // 8), "peT2"], w=[("PJ", pjb)])
                V(lambda e, pjb=pjb: e.tensor_tensor(out=b1p[:].rearrange("p a b -> p (a b)"), in0=PJ[pjb][:, 0:4],
                                                    in1=b1c[:].rearrange("p a b -> p (a b)"), op=ALU.add), r=[("PJ", pjb), "b1c"], w=["b1p"])
                for kv in range(2):
                    for half in range(2):
                        ps_ = nxt("PJ")
                        for m in range(16):
                            P(lambda e, kv=kv, half=half, m=m, ps_=ps_: e.matmul(
                                PJ[ps_][:, 0:508], CW1[:, kv, m, half * 128:(half + 1) * 128], KC2[:, kv, :, m:m + 1009:8],
                                start=(m == 0), stop=(m == 15)), r=[("CW1", kv, m // 8)] + kc2keys, w=[("PJ", ps_)])
                        V(lambda e, kv=kv, half=half, ps_=ps_: e.tensor_scalar(out=hu[:], in0=PJ[ps_][:, 0:508], scalar1=b1p[:, kv, half:half + 1],
                                                                              scalar2=None, op0=ALU.add), r=[("PJ", ps_), "b1p"], w=["hu"])
                        V(lambda e: e.tensor_tensor(out=hw[:], in0=hu[:], in1=hu[:], op=ALU.mult), r=["hu"], w=["hw"])
                        V(lambda e: e.tensor_scalar(out=hw[:], in0=hw[:], scalar1=0.044715, scalar2=1.0, op0=ALU.mult, op1=ALU.add), r=["hw"], w=["hw"])
                        V(lambda e: e.tensor_tensor(out=hw[:], in0=hw[:], in1=hu[:], op=ALU.mult), r=["hw", "hu"], w=["hw"])
                        A(lambda e: e.activation(out=hw[:], in_=hw[:], func=AF.Exp, scale=-1.5957691216057308), r=["hw"], w=["hw"])
                        V(lambda e: e.tensor_scalar_add(hw[:], hw[:], 1.0), r=["hw"], w=["hw"])
                        V(lambda e: e.reciprocal(hw[:], hw[:]), r=["hw"], w=["hw"])
                        V(lambda e, kv=kv, half=half: e.tensor_tensor(out=hidT[:, kv * 2 + half, :, :].rearrange("p a b -> p (a b)"), in0=hu[:], in1=hw[:],
                                                                      op=ALU.mult), r=["hu", "hw"], w=["hidT"])
                pj2 = nxt("PJ")
                for kv in range(2):
                    for g in range(NG):
                        c0 = (kv * 4 + g) * 64
                        for half in range(2):
                            P(lambda e, kv=kv, g=g, half=half, c0=c0, pj2=pj2: e.matmul(
                                PJ[pj2][0:127, c0:c0 + 64], hidT[:, kv * 2 + half, g, :], cw2[:, kv, half, :],
                                start=(half == 0), stop=(half == 1)), r=["hidT", "cw2"], w=[("PJ", pj2)])
                head_norm_rope(pj2, 0, 4, gk[:, 0, :], cosC[:], sinC[:], nsinC[:], kb[:], "kb")
                A(lambda e, pj2=pj2: e.activation(out=vcmp[0:127, :, 0:64], in_=PJ[pj2][0:127, 256:512].rearrange("p (a b) -> p a b", a=NG), func=AF.Copy),
                  r=[("PJ", pj2)], w=["vcmp"])
                k_to_KT(kcmpT[0:64, :, 0:127], "kcmpT", 127)
                sch.barrier()
            if DBG_STAGE[0] < 4:
                return
            WS = 2608
            load_w_cols(0, lambda c, n: w_in[j, :, c:c + n].rearrange("(k p) n -> p k n", p=128), 1024, WS)
            load_w_cols(1024, lambda c, n: w_in[j, :, 2048 + c:2048 + c + n].rearrange("(k p) n -> p k n", p=128), 1584, WS)
            with ExitStack() as esc:
                alloc_attn_tiles(esc)
                gtmp, imp, m8, rs = T["gtmp"], T["imp"], T["m8"], T["rs"]
                ctx = {}

                def front(i):
                    sl = i % 2
                    qaug, qT1, szb, gates = T["qaug"][sl], T["qT1"][sl], T["szb"][sl], T["gates"][sl]
                    xs, hs = load_norm_h(b, i, src_d[b, i * 128:(i + 1) * 128, :], xs=0)
                    ctx[i] = src_d
                    yield
                    cs, sn, nsn = cosT[:, i, :], sinT[:, i, :], nsinT[:, i, :]
                    slot = i % 4
                    pq0 = project(hs, 0, 512, WS)
                    pq1 = project(hs, 512, 512, WS)
                    head_norm_rope(pq0, 0, 8, gq[:], cs, sn, nsn, qaug[:, 0:8, 0:64], ("qaug", sl, 0))
                    pkw = project(hs, 1024, 512, WS)
                    transpose_heads(qaug, 64, qT1, [("qaug", sl, 0)], ("qT1", sl), halves=(0,))
                    merge_streams(lambda: head_norm_rope(pq1, 0, 8, gq[:], cs, sn, nsn, qaug[:, 8:16, 0:64], ("qaug", sl, 1)),
                                  lambda: head_norm_rope(pkw, 0, 4, gk[:, 2, :], cs, sn, nsn, kb[:], "kb", alt=True))
                    transpose_heads(qaug, 64, qT1, [("qaug", sl, 1)], ("qT1", sl), halves=(1,))
                    pgl = project(hs, 1536, 48, WS)
                    A(lambda e, pj=pkw, slot=slot: e.activation(out=VW[:, slot, :, 0:64], in_=PJ[pj][:, 256:512].rearrange("p (a b) -> p a b", a=NG), func=AF.Copy),
                      r=[("PJ", pkw)], w=[("VW", slot)])
                    k_to_KT(KW[0:64, :, slot, :], ("KW", slot))
                    V(lambda e, pj=pgl: e.tensor_tensor(out=gtmp[:], in0=PJ[pj][:, 0:48], in1=gateb[:], op=ALU.add), r=[("PJ", pgl), "gateb"], w=["gtmp"])
                    A(lambda e: e.activation(out=gtmp[:], in_=gtmp[:], func=AF.Exp, scale=-1.0), r=["gtmp"], w=["gtmp"])
                    A(lambda e: e.activation(out=gtmp[:], in_=gtmp[:], func=AF.Ln, bias=onef[:, 0:1]), r=["gtmp", "onef"], w=["gtmp"])
                    A(lambda e: e.activation(out=gates[:].rearrange("p a b -> p (a b)"), in_=gtmp[:], func=AF.Exp, scale=-1.0), r=["gtmp"], w=[("gates", sl)])
                    yield "crit"
                    pz0 = project(hs, 1584, 512, WS)
                    pz1 = project(hs, 2096, 512, WS)
                    silu_from_psum(pz0, 512, szb[:, 0:512], 0, ("szb", sl))
                    silu_from_psum(pz1, 512, szb[:, 512:1024], 512, ("szb", sl))
                    yield

                at = Attn()

                def cmp_post(po, g, i):
                    sl = i % 2
                    qaug = T["qaug"][sl]
                    branch_post(po, 97, g, 0, True, clamp=True, gates=T["gates"][sl], sl=sl)
                    pov = PO[po][:, 0:388].rearrange("p (a b) -> p a b", a=4)
                    V(lambda e: e.tensor_scalar(out=imp[:, g, :], in0=pov[:, 0, 65:97], scalar1=rs[:, 0:1], scalar2=None, op0=ALU.mult),
                      r=[("PO", po), "rs"], w=[("imp", g)])
                    for r_ in range(1, 4):
                        V(lambda e, r_=r_: e.scalar_tensor_tensor(out=imp[:, g, :], in0=pov[:, r_, 65:97], scalar=rs[:, r_:r_ + 1], in1=imp[:, g, :],
                                                                  op0=ALU.mult, op1=ALU.add), r=[("PO", po), "rs", ("imp", g)], w=[("imp", g)])
                    V(lambda e: e.tensor_tensor(out=imp[:, g, :], in0=imp[:, g, :], in1=fb[:, i, :], op=ALU.add), r=[("imp", g), "fb"], w=[("imp", g)])
                    V(lambda e: e.max(out=m8[:, g, :], in_=imp[:, g, :]), r=[("imp", g)], w=[("m8", g)])
                    V(lambda e: e.tensor_tensor(out=qaug[:, 4 * g:4 * g + 4, 64:96], in0=imp[:, g, :].unsqueeze(1).to_broadcast([128, 4, 32]),
                                                in1=m8[:, g, 7:8].unsqueeze(1).to_broadcast([128, 4, 32]), op=ALU.is_lt),
                      r=[("imp", g), ("m8", g)], w=[("qaug", sl, 2)])

                def cmp_unit(i, g):
                    sl = i % 2
                    qT1 = T["qT1"][sl]

                    def u():
                        po = nxt("PO")
                        at.unit(kcmpT[:, g, :], qT1[:, 4 * g:4 * g + 4, :], cmask[:, i * 128:(i + 1) * 128].unsqueeze(1).to_broadcast([128, 4, 128]),
                                vcmp[:, g, :], 97, po, True, ["kcmpT", ("qT1", sl), "vcmp"], (lambda: cmp_post(po, g, i)))
                    return u

                def back_units(i):
                    sl = i % 2
                    qaug, qT1, qaT, gates = T["qaug"][sl], T["qT1"][sl], T["qaT"][sl], T["gates"][sl]
                    units = []
                    tiles = [jj for jj in (i - 2, i - 1, i) if jj >= 0]
                    for g in range(NG):
                        po_box = []
                        for idx, jj in enumerate(tiles):
                            def u(g=g, jj=jj, idx=idx, po_box=po_box):
                                if idx == 0:
                                    po_box.append(nxt("PO"))
                                po = po_box[0]
                                if jj == i:
                                    bias = cb[:].unsqueeze(1).to_broadcast([128, 4, 128])
                                elif jj == i - 2:
                                    bias = cb2[:].unsqueeze(1).to_broadcast([128, 4, 128])
                                else:
                                    bias = None
                                done = (lambda: branch_post(po, 65, g, 2, False, gates=gates, sl=sl)) if jj == i else None
                                at.unit(KW[:, g, jj % 4, :], qT1[:, 4 * g:4 * g + 4, :], bias, VW[:, jj % 4, g, :], 65, po, idx == 0,
                                        [("KW", jj % 4), ("qT1", sl), ("VW", jj % 4), "VW1"], done)
                            units.append(u)

                    def tr():
                        at.flush()
                        transpose_heads(qaug, 96, qaT, [("qaug", sl, 0), ("qaug", sl, 1), ("qaug", sl, 2)], ("qaT", sl))
                    units.append(tr)
                    for g in range(NG):
                        po_box = []
                        for jj in range(i + 1):
                            def u(g=g, jj=jj, po_box=po_box):
                                if jj == 0:
                                    po_box.append(nxt("PO"))
                                po = po_box[0]
                                bias = cb[:].unsqueeze(1).to_broadcast([128, 4, 128]) if jj == i else None
                                done = (lambda: branch_post(po, 65, g, 1, False, gates=gates, sl=sl)) if jj == i else None
                                at.unit(KA[:, g, jj * 128:(jj + 1) * 128], qaT[:, 4 * g:4 * g + 4, :], bias, VA[:, jj, g, :], 65, po,
                                        jj == 0, [("KA", jj), "KAind", ("qaT", sl), ("VA", jj), "VA1"], done)
                            units.append(u)
                        if i + 1 < NT:
                            if g == 0:
                                units.append("need_front")
                            units.append(cmp_unit(i + 1, g))
                    return units

                def tail(i):
                    return out_proj_store(b, i, ctx[i], i % 2)

                def prologue():
                    for g in range(NG):
                        cmp_unit(0, g)()

                at_box[0] = at
                run_pipelined(front, back_units, tail, NT, prologue)
                sch.barrier()

        for b in range(n_seq):
            with ExitStack() as es3:
                E3 = es3.enter_context
                posi = E3(nc.sbuf_tensor("posi%d" % b, [128, NT + 1], I32))
                posf = E3(nc.sbuf_tensor("posf%d" % b, [128, NT + 1], F32))
                ang = E3(nc.sbuf_tensor("ang%d" % b, [128, NT + 1, 8], F32))
                kq = E3(nc.sbuf_tensor("kq%d" % b, [128, NT + 1, 8], F32))
                ki = E3(nc.sbuf_tensor("ki%d" % b, [128, NT + 1, 8], I32))
                red = E3(nc.sbuf_tensor("red%d" % b, [128, NT + 1, 8], F32))
                V(lambda e: e.memset(posi[:], 0), w=["posi"])
                sch.dma("sp", posi[:, 0:NT], dr["positions"][b].rearrange("(i p) -> p i", p=128), writes=["posi"],
                        allow_slow_non_contiguous=True)
                sch.dma("sp", posi[0:127, NT:NT + 1], dr["positions"][b, 16:16 + 16 * 127].rearrange("(n s) -> n s", s=16)[:, 15:16],
                        writes=["posi"], allow_slow_non_contiguous=True)
                V(lambda e: e.tensor_copy(out=posf[:], in_=posi[:]), r=["posi"], w=["posf"])
                V(lambda e: e.tensor_tensor(out=ang[:], in0=posf[:].unsqueeze(2).to_broadcast([128, NT + 1, 8]),
                                            in1=invf[:].unsqueeze(1).to_broadcast([128, NT + 1, 8]), op=ALU.mult),
                  r=["posf", "invf"], w=["ang"])
                C1 = 6.28125
                C2 = float(2.0 * np.pi - 6.28125)
                PI_LO = 3.1415925
                for which, off in (("sin", 0.0), ("cos", 0.25)):
                    V(lambda e, off=off: e.tensor_scalar(out=kq[:], in0=ang[:], scalar1=float(1.0 / (2.0 * np.pi)), scalar2=off,
                                                         op0=ALU.mult, op1=ALU.add), r=["ang"], w=["kq"])
                    V(lambda e: e.tensor_copy(out=ki[:], in_=kq[:]), r=["kq"], w=["ki"])
                    V(lambda e: e.tensor_copy(out=kq[:], in_=ki[:]), r=["ki"], w=["kq"])
                    V(lambda e: e.scalar_tensor_tensor(out=red[:], in0=kq[:], scalar=-C1, in1=ang[:], op0=ALU.mult, op1=ALU.add),
                      r=["kq", "ang"], w=["red"])
                    V(lambda e: e.scalar_tensor_tensor(out=red[:], in0=kq[:], scalar=-C2, in1=red[:], op0=ALU.mult, op1=ALU.add),
                      r=["kq", "red"], w=["red"])
                    if which == "cos":
                        V(lambda e: e.tensor_scalar(out=red[:], in0=red[:], scalar1=float(np.pi / 2), scalar2=PI_LO,
                                                    op0=ALU.add, op1=ALU.min), r=["red"], w=["red"])
                        V(lambda e: e.tensor_scalar(out=red[:], in0=red[:], scalar1=-PI_LO, scalar2=None, op0=ALU.max), r=["red"], w=["red"])
                        A(lambda e: e.activation(out=cosT[:], in_=red[:, 0:NT, :], func=AF.Sin), r=["red"], w=["rope"])
                        A(lambda e: e.activation(out=cosC[:], in_=red[:, NT, :], func=AF.Sin), r=["red"], w=["ropeC"])
                    else:
                        V(lambda e: e.tensor_scalar(out=red[:], in0=red[:], scalar1=PI_LO, scalar2=-PI_LO,
                                                    op0=ALU.min, op1=ALU.max), r=["red"], w=["red"])
                        A(lambda e: e.activation(out=sinT[:], in_=red[:, 0:NT, :], func=AF.Sin), r=["red"], w=["rope"])
                        A(lambda e: e.activation(out=sinC[:], in_=red[:, NT, :], func=AF.Sin), r=["red"], w=["ropeC"])
                V(lambda e: e.tensor_scalar(out=nsinT[:], in0=sinT[:], scalar1=-1.0, scalar2=None, op0=ALU.mult), r=["rope"], w=["rope"])
                V(lambda e: e.tensor_scalar(out=nsinC[:], in0=sinC[:], scalar1=-1.0, scalar2=None, op0=ALU.mult), r=["ropeC"], w=["ropeC"])
                sch.barrier()

            for li, L in enumerate(layers):
                src_d = dr["x"] if (li == 0 and first_from_x) else y_d
                is_nsa = (L % 2 == 0)
                j = L // 2
                for m in range(3):
                    sch.dma("sp", modbc[:, m, :], mod_d[b, L:L + 1, m * D:(m + 1) * D].to_broadcast([128, D]),
                            reads=[("mod", b, L)], writes=[("modbc", m)])
                if is_nsa:
                    nsa_layer(b, j, src_d)
                else:
                    moba_layer(b, j, src_d)
        sch.finish()
    return nc


_CONSTS = None


def _consts():
    global _CONSTS
    if _CONSTS is None:
        _CONSTS = host_consts()
    return _CONSTS


def run_layers(inputs, layers, n_seq, n_cores, first_from_x=True, trace=False):
    nc = build(layers, n_seq, first_from_x)
    cst = _consts()
    in_maps = []
    for c in range(n_cores):
        m = {}
        m["x"] = np.ascontiguousarray(inputs["x"][c * n_seq:(c + 1) * n_seq], dtype=np.float32)
        m["c"] = np.ascontiguousarray(inputs["c"][c * n_seq:(c + 1) * n_seq], dtype=np.float32)
        m["positions"] = np.ascontiguousarray(inputs["positions"][c * n_seq:(c + 1) * n_seq], dtype=np.int32)
        for k in WEIGHT_SHAPES:
            m[k] = np.ascontiguousarray(inputs[k], dtype=np.float32)
        for k in CONST_SHAPES:
            m[k] = cst[k]
        in_maps.append(m)
    res = run_bass_kernel_spmd(nc, in_maps, core_ids=list(range(n_cores)), trace=trace)
    out = np.concatenate([np.asarray(r["y"]) for r in res.results], axis=0)
    return out, res


def kernel(**inputs):
    out, _ = run_layers(inputs, [0, 1, 2, 3], 2, N_CORES)
    return out.astype(np.float32)
```

```python
import numpy as np
from contextlib import ExitStack
import concourse.bass as bass
import concourse.mybir as mybir
from concourse.bass_utils import run_bass_kernel_spmd

F32 = mybir.dt.float32
BF16 = mybir.dt.bfloat16
I32 = mybir.dt.int32
AF = mybir.ActivationFunctionType
ALU = mybir.AluOpType
AX = mybir.AxisListType

S = 2048
D = 1024
NT = 16
HD = 64
NH = 16
NG = 4
NEG = -30000.0
EPS = 1e-6
N_CORES = 8
NSA_IN = 3632
MOBA_IN = 2560
NDMA = 24
FRESH_POOL_SEMS = [False]
STRICT_SAME_ENGINE = [False]
POOL_DESC_LIMIT = 4608


class Sched:
    def __init__(self, nc, es):
        self.nc = nc
        self.es = es
        self.nfresh = 0
        self.capture = None
        self.pool_fifo = []
        self.pool_desc = 0
        self.eng = {"pe": nc.tensor, "act": nc.scalar, "dve": nc.vector, "pool": nc.gpsimd, "sp": nc.sync}
        self.semh = {}
        for k in self.eng:
            self.semh[("e", k)] = es.enter_context(nc.semaphore("sem_" + k))
        for i in range(NDMA):
            self.semh[("d", i)] = es.enter_context(nc.semaphore("dsem%d" % i))
        self.cnt = {k: 0 for k in self.eng}
        self.dcnt = [0] * NDMA
        self.drr = 0
        self.seen = {k: {} for k in self.eng}
        self.lastw = {}
        self.readers = {}

    def _deps(self, eng, reads, writes, strict=False):
        deps = {}

        def add(t, same_ok):
            sk, v = t
            if same_ok and not strict and sk == ("e", eng) and (eng == "pe" or not STRICT_SAME_ENGINE[0]):
                return
            if deps.get(sk, 0) < v:
                deps[sk] = v

        for k in reads:
            if k in self.lastw:
                add(self.lastw[k], False)
        for k in writes:
            if k in self.lastw:
                add(self.lastw[k], True)
            for sk, v in self.readers.get(k, {}).items():
                add((sk, v), True)
        return deps

    def _wait(self, eng, deps):
        for sk, v in deps.items():
            if eng == "pe" and sk == ("e", "pe"):
                continue
            if self.seen[eng].get(sk, 0) < v:
                self.eng[eng].wait_ge(self.semh[sk], v)
                self.seen[eng][sk] = v

    def _record(self, tok, reads, writes):
        for k in writes:
            self.lastw[k] = tok
            self.readers[k] = {}
        for k in reads:
            d = self.readers.setdefault(k, {})
            if d.get(tok[0], 0) < tok[1]:
                d[tok[0]] = tok[1]

    def op(self, eng, fn, reads=(), writes=()):
        if self.capture is not None:
            self.capture.append(("op", eng, fn, tuple(reads), tuple(writes), None))
            return
        self._wait(eng, self._deps(eng, reads, writes))
        inst = fn(self.eng[eng])
        self.cnt[eng] += 1
        inst.then_inc(self.semh[("e", eng)], 1)
        self._record((("e", eng), self.cnt[eng]), reads, writes)

    def replay(self, item):
        kind, a, b, reads, writes, kw = item
        if kind == "op":
            self.op(a, b, reads, writes)
        else:
            self.dma(a, b[0], b[1], reads, writes, **kw)

    def dma(self, queue, out, in_, reads=(), writes=(), **kw):
        if self.capture is not None:
            self.capture.append(("dma", queue, (out, in_), tuple(reads), tuple(writes), dict(kw)))
            return
        if FRESH_POOL_SEMS[0] and queue == "pool":
            kw.pop("ndesc", None)
            idx = NDMA + self.nfresh
            self.nfresh += 1
            self.semh[("d", idx)] = self.es.enter_context(self.nc.semaphore("fsem%d" % idx))
            self.dcnt.append(0)
            self._wait(queue, self._deps(queue, reads, writes, strict=True))
            inst = self.eng[queue].dma_start(out=out, in_=in_, **kw)
            self.dcnt[idx] += 16
            inst.then_inc(self.semh[("d", idx)], 16)
            self._record((("d", idx), self.dcnt[idx]), reads, writes)
            return
        ndesc = kw.pop("ndesc", 1024)
        idx = self.drr
        self.drr = (idx + 1) % NDMA
        if queue == "pool":
            while self.pool_fifo and self.pool_desc + ndesc > POOL_DESC_LIMIT:
                sk, v, nd = self.pool_fifo.pop(0)
                self.pool_desc -= nd
                if self.seen["pool"].get(sk, 0) < v:
                    self.eng["pool"].wait_ge(self.semh[sk], v)
                    self.seen["pool"][sk] = v
        deps = self._deps(queue, reads, writes, strict=True)
        if self.dcnt[idx]:
            deps[("d", idx)] = max(deps.get(("d", idx), 0), self.dcnt[idx])
        self._wait(queue, deps)
        inst = self.eng[queue].dma_start(out=out, in_=in_, **kw)
        self.dcnt[idx] += 16
        inst.then_inc(self.semh[("d", idx)], 16)
        self._record((("d", idx), self.dcnt[idx]), reads, writes)
        if queue == "pool":
            self.pool_fifo.append((("d", idx), self.dcnt[idx], ndesc))
            self.pool_desc += ndesc

    def barrier(self):
        for k in self.eng:
            for k2 in self.eng:
                if k2 != k and self.cnt[k2] > self.seen[k].get(("e", k2), 0):
                    self.eng[k].wait_ge(self.semh[("e", k2)], self.cnt[k2])
                    self.seen[k][("e", k2)] = self.cnt[k2]
            for i in range(len(self.dcnt)):
                if self.dcnt[i] > self.seen[k].get(("d", i), 0):
                    self.eng[k].wait_ge(self.semh[("d", i)], self.dcnt[i])
                    self.seen[k][("d", i)] = self.dcnt[i]

    def finish(self):
        e = self.eng["sp"]
        for i in range(len(self.dcnt)):
            if self.dcnt[i] and self.seen["sp"].get(("d", i), 0) < self.dcnt[i]:
                e.wait_ge(self.semh[("d", i)], self.dcnt[i])
        for k in self.eng:
            if k != "sp" and self.cnt[k] and self.seen["sp"].get(("e", k), 0) < self.cnt[k]:
                e.wait_ge(self.semh[("e", k)], self.cnt[k])


def host_consts():
    p = np.arange(128)[:, None]
    f = np.arange(128)[None, :]
    c = {}
    c["c_ident"] = (p == f).astype(np.float32)
    c["c_cb"] = np.where(p > f, NEG, 0.0).astype(np.float32)
    c["c_cb2"] = np.where(p <= f, NEG, 0.0).astype(np.float32)
    t = np.arange(S)[None, :]
    cm = np.where((16 * p + 31 > t) | (p == 127), NEG, 0.0)
    c["c_cmask"] = cm.astype(np.float32)
    fb = np.zeros((128, NT, 32), np.float32)
    for i in range(NT):
        cur = (128 * i + np.arange(128)) // 64
        for j in range(32):
            fb[:, i, j] = np.where((j == 0) | (j == cur) | (j == cur - 1), 100.0, 0.0)
    c["c_fb"] = fb.reshape(128, NT * 32)
    fbm = np.zeros((128, NT, 8), np.float32)
    for i in range(NT):
        own = i // 2
        for j in range(8):
            fbm[:, i, j] = 0.0 if j < own else (1e4 if j == own else -1e4)
    c["c_fbm"] = fbm.reshape(128, NT * 8)
    key = np.arange(S)[None, :]
    c["c_ind_nsa"] = np.where(key // 64 == np.arange(32)[:, None], NEG, 0.0).astype(np.float32)
    c["c_ind_moba"] = np.where(key // 256 == np.arange(8)[:, None], NEG, 0.0).astype(np.float32)
    cs = np.arange(127)[:, None] * 16
    ss = np.arange(32)[None, :] * 64
    shared = np.clip(np.minimum(cs + 32, ss + 64) - np.maximum(cs, ss), 0, None)
    sw = np.zeros((128, 32), np.float32)
    sw[:127] = shared / 32.0
    c["c_selw"] = sw
    sel = np.zeros((128, 2, 64), np.float32)
    for par in range(2):
        for u in range(64):
            sel[2 * u + par, par, u] = 1.0
    c["c_sel"] = sel.reshape(128, 128)
    inv = (np.float32(500000.0) ** (-np.arange(0, 16, 2, dtype=np.float32) / np.float32(16))).astype(np.float32)
    c["c_invf"] = np.tile(inv[None, :], (128, 1)).astype(np.float32)
    return c


CONST_SHAPES = {
    "c_ident": [128, 128], "c_cb": [128, 128], "c_cb2": [128, 128], "c_cmask": [128, S],
    "c_fb": [128, NT * 32], "c_fbm": [128, NT * 8], "c_ind_nsa": [32, S], "c_ind_moba": [8, S],
    "c_selw": [128, 32], "c_sel": [128, 128], "c_invf": [128, 8],
}

WEIGHT_SHAPES = {
    "norm_g": [4, D], "ada_w": [4, D, 3 * D], "ada_b": [4, 3 * D],
    "nsa_w_in": [2, D, NSA_IN], "nsa_w_out": [2, D, D], "nsa_q_norm": [2, 64], "nsa_k_norm": [2, 3, 64],
    "nsa_cmp_pe": [2, 2, 32, 64], "nsa_cmp_w1": [2, 2, 2048, 256], "nsa_cmp_b1": [2, 2, 256],
    "nsa_cmp_w2": [2, 2, 256, 64], "nsa_gate_b": [2, 48],
    "moba_w_in": [2, D, MOBA_IN], "moba_w_out": [2, D, D], "moba_q_norm": [2, 64], "moba_k_norm": [2, 64],
}


DBG_STAGE = [9]


def build(layers, n_seq, first_from_x=True):
    nc = bass.Bass("TRN2", target_bir_lowering=False, dynamic_dma_scratch_size=8192)
    dr = {}
    dr["x"] = nc.dram_tensor("x", [n_seq, S, D], F32, kind="ExternalInput").ap()
    dr["c"] = nc.dram_tensor("c", [n_seq, D], F32, kind="ExternalInput").ap()
    dr["positions"] = nc.dram_tensor("positions", [n_seq, S], I32, kind="ExternalInput").ap()
    for k, shp in WEIGHT_SHAPES.items():
        dr[k] = nc.dram_tensor(k, shp, F32, kind="ExternalInput").ap()
    for k, shp in CONST_SHAPES.items():
        dr[k] = nc.dram_tensor(k, shp, F32, kind="ExternalInput").ap()
    y_d = nc.dram_tensor("y", [n_seq, S, D], F32, kind="ExternalOutput").ap()
    mod_d = nc.dram_tensor("mod_scr", [n_seq, 4, 3 * D], F32, kind="Internal").ap()

    with ExitStack() as es:
        E = es.enter_context
        sch = Sched(nc, es)

        def V(fn, r=(), w=()):
            sch.op("dve", fn, r, w)

        def A(fn, r=(), w=()):
            sch.op("act", fn, r, w)

        def G(fn, r=(), w=()):
            sch.op("pool", fn, r, w)

        def P(fn, r=(), w=()):
            sch.op("pe", fn, r, w)

        def sb(name, shape, dt):
            return E(nc.sbuf_tensor(name, shape, dt))

        T = {}

        def sb(name, shape, dt, stack=None):
            t = (stack or es).enter_context(nc.sbuf_tensor(name, shape, dt))
            T[name] = t
            return t

        ident = sb("ident", [128, 128], BF16)
        cb = sb("cb", [128, 128], BF16)
        cb2 = sb("cb2", [128, 128], BF16)
        cmask = sb("cmask", [128, S], BF16)
        fb = sb("fb", [128, NT, 32], F32)
        fbm = sb("fbm", [128, NT, 8], F32)
        selm = sb("selm", [128, 2, 64], BF16)
        invf = sb("invf", [128, 8], F32)
        onesc = sb("onesc", [128, 1], BF16)
        W1 = sb("W1", [128, 20864], BF16)
        WO = sb("WO", [128, 8, D], BF16)
        KA = sb("KA", [128, NG, S], BF16)
        VA = sb("VA", [128, NT, NG, 65], BF16)
        KW = sb("KW", [128, NG, 4, 128], BF16)
        VW = sb("VW", [128, 4, NG, 65], BF16)
        onef = sb("onef", [128, 1], F32)
        epsf = sb("epsf", [128, 1], F32)
        kcmpT = sb("kcmpT", [128, NG, 128], BF16)
        vcmp = sb("vcmp", [128, NG, 97], BF16)
        modbc = sb("modbc", [128, 3, D], F32)
        cosT = sb("cosT", [128, NT, 8], F32)
        sinT = sb("sinT", [128, NT, 8], F32)
        nsinT = sb("nsinT", [128, NT, 8], F32)
        cosC = sb("cosC", [128, 8], F32)
        sinC = sb("sinC", [128, 8], F32)
        nsinC = sb("nsinC", [128, 8], F32)
        gq = sb("gq", [128, 64], F32)
        gk = sb("gk", [128, 3, 64], F32)
        gateb = sb("gateb", [128, 48], F32)
        kmT = sb("kmT", [64, NG, 8], BF16)
        kmtmp = sb("kmtmp", [64, NG], F32)
        ksum = sb("ksum", [64, NT, NG], F32)
        cw2 = sb("cw2", [128, 2, 2, 64], BF16)
        peT2 = sb("peT2", [128, 2, 16], BF16)
        peT2f = sb("peT2f", [128, 2, 16], F32)
        b1c = sb("b1c", [128, 2, 2], F32)
        b1p = sb("b1p", [128, 2, 2], F32)
        xt = [sb("xt%d" % i, [128, D], F32) for i in range(2)]
        st = sb("st", [128, 8], F32)
        hb = sb("hb", [128, D], BF16)
        hT = [sb("hT%d" % i, [128, 8, 128], BF16) for i in range(2)]
        sq = sb("sq", [128, 512], F32)
        ssq = sb("ssq", [128, 16], F32)
        rq = sb("rq", [128, 16], F32)
        qn = sb("qn", [128, 8, 64], F32)
        rtmp = sb("rtmp", [128, 8, 16], F32)
        rtmp2 = sb("rtmp2", [128, 8, 16], F32)
        kb = sb("kb", [128, NG, 64], BF16)
        ezs = sb("ezs", [128, D], F32)
        PJ = [E(nc.psum_tensor("PJ%d" % i, [128, 512], F32)) for i in range(2)]
        PS = [E(nc.psum_tensor("PS%d" % i, [128, 512], F32)) for i in range(2)]
        PO = [E(nc.psum_tensor("PO%d" % i, [128, 512], F32)) for i in range(2)]
        PT = [E(nc.psum_tensor("PT%d" % i, [128, 1024], BF16)) for i in range(2)]
        rr = {"PJ": 0, "PS": 0, "PO": 0, "PT": 0, "pT": 0, "xt": 0, "hT": 0}
        uid = [0]

        fixed_bank = [None]
        pin_pt = [None]

        def nxt(k, n=2):
            if pin_pt[0] is not None and k == "PT":
                return pin_pt[0]
            if fixed_bank[0] is not None and k == "PT":
                return fixed_bank[0]
            if fixed_bank[0] == 1 and k == "PJ":
                raise RuntimeError("atomic-mode code must not allocate PJ banks (owned by the deferred front stage)")
            v = rr[k]
            rr[k] = (v + 1) % n
            return v

        def alloc_attn_tiles(stack):
            uid[0] += 1
            u = "_%d" % uid[0]
            for nm, shp, dt in [("qaug", [128, NH, 96], BF16), ("qT1", [128, NH, 128], BF16), ("qaT", [128, NH, 128], BF16),
                                ("szb", [128, D], BF16), ("gates", [128, NH, 3], F32), ("oacc", [128, NH, 64], F32)]:
                T[nm] = [stack.enter_context(nc.sbuf_tensor("%s%s_%d" % (nm, u, k), shp, dt)) for k in range(2)]
                if nm in ("qT1", "qaT"):
                    for k in range(2):
                        G(lambda e, t=T[nm][k]: e.memset(t[:], 0.0), w=[(nm, k)])
            for nm, shp, dt in [("gtmp", [128, 48], F32),
                                ("sg", [128, NH, 8], F32), ("m8", [128, NH, 8], F32), ("imp", [128, NG, 32], F32),
                                ("rs", [128, 4], F32), ("cc", [128, 4], F32), ("otmp", [128, 4, 64], F32),
                                ("og", [128, D], BF16), ("ogT", [128, 8, 128], BF16),
                                ("ytmp", [128, 512], F32), ("pT0", [128, 512], BF16), ("pT1", [128, 512], BF16),
                                ("pT2", [128, 512], BF16)]:
                T[nm] = stack.enter_context(nc.sbuf_tensor(nm + u, shp, dt))

        def alloc_cmp_tiles(stack):
            uid[0] += 1
            u = "_%d" % uid[0]
            for nm, shp, dt in [("KC2", [128, 2, NG, 1024], BF16), ("hidT", [128, 4, NG, 127], BF16),
                                ("hu", [128, 508], F32), ("hw", [128, 508], F32), ("kvb", [128, 512], BF16)]:
                T[nm] = stack.enter_context(nc.sbuf_tensor(nm + u, shp, dt))

        def cload(tile_ap, src, key, queue="pool"):
            sch.dma(queue, tile_ap, src, writes=[key])

        cload(ident[:], dr["c_ident"], "ident")
        cload(cb[:], dr["c_cb"], "cb")
        cload(cb2[:], dr["c_cb2"], "cb2")
        for h in range(2):
            cload(cmask[:, h * 1024:(h + 1) * 1024], dr["c_cmask"][:, h * 1024:(h + 1) * 1024], "cmask")
        cload(fb[:].rearrange("p a b -> p (a b)"), dr["c_fb"], "fb", "sp")
        cload(fbm[:].rearrange("p a b -> p (a b)"), dr["c_fbm"], "fbm", "sp")
        cload(selm[:].rearrange("p a b -> p (a b)"), dr["c_sel"], "selm")
        cload(invf[:], dr["c_invf"], "invf", "sp")
        V(lambda e: e.memset(onesc[:], 1.0), w=["onesc"])
        V(lambda e: e.memset(onef[:], 1.0), w=["onef"])
        V(lambda e: e.memset(epsf[:], EPS), w=["epsf"])
        V(lambda e: e.memset(VA[:, :, :, 64:65], 1.0), w=["VA1"])
        V(lambda e: e.memset(VW[:, :, :, 64:65], 1.0), w=["VW1"])
        V(lambda e: e.memset(kmT[:], 0.0), w=["kmT"])
        V(lambda e: e.memset(ksum[:], 0.0), w=["ksum"])
        V(lambda e: e.memset(kcmpT[:], 0.0), w=["kcmpT"])
        G(lambda e: e.memset(KA[64:128, :, :], 0.0), w=["KAind"])
        G(lambda e: e.memset(KW[:], 0.0), w=[("KW", 0), ("KW", 1), ("KW", 2), ("KW", 3)])
        V(lambda e: e.memset(vcmp[:], 0.0), w=["vcmp"])
        V(lambda e: e.memset(vcmp[:, :, 64:65], 1.0), w=["vcmp"])
        for g in range(NG):
            cload(vcmp[:, g, 65:97], dr["c_selw"], "vcmp")

        with ExitStack() as es2:
            E2 = es2.enter_context
            cT = E2(nc.sbuf_tensor("cT", [128, 8, 2], F32))
            cTe = E2(nc.sbuf_tensor("cTe", [128, 8, 2], F32))
            cTb = E2(nc.sbuf_tensor("cTb", [128, 8, 2], BF16))
            adaw = [E2(nc.sbuf_tensor("adaw%d" % i, [128, 8, 512], BF16)) for i in range(3)]
            adab = E2(nc.sbuf_tensor("adab", [2, 512], F32))
            grow = E2(nc.sbuf_tensor("grow", [2, D], F32))
            modr = [E2(nc.sbuf_tensor("modr%d" % i, [2, 512], F32)) for i in range(2)]
            V(lambda e: e.memset(cT[:], 0.0), w=["cT"])
            for b in range(n_seq):
                sch.dma("sp", cT[:, :, b:b + 1], dr["c"][b].rearrange("(k p o) -> p k o", p=128, o=1),
                        writes=["cT"], allow_slow_non_contiguous=True)
            A(lambda e: e.activation(out=cTe[:], in_=cT[:], func=AF.Exp, scale=-1.0), r=["cT"], w=["cTe"])
            V(lambda e: e.tensor_scalar_add(cTe[:], cTe[:], 1.0), r=["cTe"], w=["cTe"])
            V(lambda e: e.reciprocal(cTe[:], cTe[:]), r=["cTe"], w=["cTe"])
            V(lambda e: e.tensor_tensor(out=cTb[:], in0=cT[:], in1=cTe[:], op=ALU.mult), r=["cT", "cTe"], w=["cTb"])
            ci = 0
            for L in layers:
                for b in range(2):
                    sch.dma("sp", grow[b:b + 1, :], dr["norm_g"][L:L + 1, :], writes=["grow"])
                for ch in range(6):
                    slot = ci % 3
                    ms = ci % 2
                    ci += 1
                    sch.dma("pool", adaw[slot][:], dr["ada_w"][L, :, ch * 512:(ch + 1) * 512].rearrange("(k p) n -> p k n", p=128),
                            writes=[("adaw", slot)])
                    for b in range(2):
                        sch.dma("sp", adab[b:b + 1, :], dr["ada_b"][L:L + 1, ch * 512:(ch + 1) * 512], writes=["adab"])
                    pj = nxt("PJ")
                    for k in range(8):
                        P(lambda e, k=k, pj=pj, slot=slot: e.matmul(PJ[pj][0:2, :], cTb[:, k, :], adaw[slot][:, k, :],
                                                                     start=(k == 0), stop=(k == 7)),
                          r=["cTb", ("adaw", slot)], w=[("PJ", pj)])
                    V(lambda e, pj=pj, ms=ms: e.tensor_tensor(out=modr[ms][:], in0=PJ[pj][0:2, :], in1=adab[:], op=ALU.add),
                      r=[("PJ", pj), "adab"], w=[("modr", ms)])
                    if ch in (2, 3):
                        g0 = (ch - 2) * 512
                        V(lambda e, ms=ms, g0=g0: e.scalar_tensor_tensor(out=modr[ms][:], in0=modr[ms][:], scalar=1.0,
                                                                         in1=grow[:, g0:g0 + 512], op0=ALU.add, op1=ALU.mult),
                          r=[("modr", ms), "grow"], w=[("modr", ms)])
                        dst0 = g0
                    elif ch in (0, 1):
                        dst0 = D + ch * 512
                    else:
                        dst0 = 2 * D + (ch - 4) * 512
                    for b in range(n_seq):
                        sch.dma("sp", mod_d[b, L:L + 1, dst0:dst0 + 512], modr[ms][b:b + 1, :],
                                reads=[("modr", ms)], writes=[("mod", b, L)])
            sch.barrier()

        def rstd_from_ss(ss_ap, out_ap, n, keys_r, key_w):
            A(lambda e: e.activation(out=out_ap, in_=ss_ap, func=AF.Ln, scale=1.0 / n, bias=epsf[:, 0:1]), r=list(keys_r) + ["epsf"], w=[key_w])
            A(lambda e: e.activation(out=out_ap, in_=out_ap, func=AF.Exp, scale=-0.5), r=[key_w], w=[key_w])

        def load_norm_h(b, i, src_ap, xs=None):
            if xs is None:
                xs = nxt("xt")
            sch.dma("sp", xt[xs][:], src_ap, reads=[("res", b, i)], writes=[("xt", xs)])
            A(lambda e: e.activation(out=hb[:], in_=xt[xs][:], func=AF.Square, accum_out=st[:, 0:1]),
              r=[("xt", xs)], w=["hb", "st0"])
            rstd_from_ss(st[:, 0:1], st[:, 1:2], D, ["st0"], "st1")
            V(lambda e: e.scalar_tensor_tensor(out=ezs[:], in0=xt[xs][:], scalar=st[:, 1:2], in1=modbc[:, 0, :],
                                               op0=ALU.mult, op1=ALU.mult), r=[("xt", xs), "st1", ("modbc", 0)], w=[("ezs", 0), ("ezs", 512)])
            V(lambda e: e.tensor_tensor(out=hb[:], in0=ezs[:], in1=modbc[:, 1, :], op=ALU.add), r=[("ezs", 0), ("ezs", 512), ("modbc", 1)], w=["hb"])
            pt = nxt("PT")
            for k in range(8):
                P(lambda e, k=k, pt=pt: e.transpose(PT[pt][:, k * 128:(k + 1) * 128], hb[:, k * 128:(k + 1) * 128], ident[:]),
                  r=["hb", "ident"], w=[("PT", pt)])
            hs = nxt("hT")
            V(lambda e: e.tensor_copy(out=hT[hs][:].rearrange("p a b -> p (a b)"), in_=PT[pt][:]),
              r=[("PT", pt)], w=[("hT", hs)])
            return xs, hs

        def project(hs, wcol0, ncols, wstride):
            pj = nxt("PJ")
            for k in range(8):
                P(lambda e, k=k, pj=pj: e.matmul(PJ[pj][:, 0:ncols], hT[hs][:, k, :],
                                                 W1[:, k * wstride + wcol0:k * wstride + wcol0 + ncols],
                                                 start=(k == 0), stop=(k == 7)),
                  r=[("hT", hs)] + [("W1", c) for c in range(wcol0 // 512, (wcol0 + ncols - 1) // 512 + 1)], w=[("PJ", pj)])
            return pj

        def head_norm_rope(pj, c0, nh, g_ap, cos_ap, sin_ap, nsin_ap, out_ap, wkey, alt=False):
            src = PJ[pj][:, c0:c0 + nh * 64]
            src3 = src.rearrange("p (a b) -> p a b", a=nh)
            if alt:
                sq_ = ezs[:, 0:nh * 64]
                qn_ = ezs[:, 256:256 + nh * 64].rearrange("p (a b) -> p a b", a=nh)
                rt_ = ezs[:, 512:512 + nh * 16].rearrange("p (a b) -> p a b", a=nh)
                rt2_ = ezs[:, 576:576 + nh * 16].rearrange("p (a b) -> p a b", a=nh)
                ssq_ = ezs[:, 640:640 + nh]
                rq_ = ezs[:, 648:648 + nh]
                ka = [("ezs", 0), ("ezs", 512)]
                k_sq = k_ssq = k_rq = k_qn = k_rt = k_rt2 = None
                ksq, kssq, krq, kqn, krt, krt2 = ka, ka, ka, ka, ka, ka
            else:
                sq_ = sq[:, 0:nh * 64]
                qn_ = qn[:, 0:nh, :]
                rt_ = rtmp[:, 0:nh, :]
                rt2_ = rtmp2[:, 0:nh, :]
                ssq_ = ssq[:, 0:nh]
                rq_ = rq[:, 0:nh]
                ksq, kssq, krq, kqn, krt, krt2 = ["sq"], ["ssq"], ["rq"], ["qn"], ["rtmp"], ["rtmp2"]
            A(lambda e: e.activation(out=sq_, in_=src, func=AF.Square), r=[("PJ", pj)], w=ksq)
            V(lambda e: e.tensor_reduce(out=ssq_, in_=sq_.rearrange("p (a b) -> p a b", a=nh), axis=AX.X, op=ALU.add), r=ksq, w=kssq)
            A(lambda e: e.activation(out=rq_, in_=ssq_, func=AF.Ln, scale=1.0 / HD, bias=epsf[:, 0:1]), r=kssq + ["epsf"], w=krq)
            A(lambda e: e.activation(out=rq_, in_=rq_, func=AF.Exp, scale=-0.5), r=krq, w=krq)
            V(lambda e: e.tensor_tensor(out=qn_, in0=src3, in1=rq_.unsqueeze(2).to_broadcast([128, nh, 64]), op=ALU.mult),
              r=[("PJ", pj)] + krq, w=kqn)
            V(lambda e: e.tensor_tensor(out=qn_, in0=qn_, in1=g_ap.unsqueeze(1).to_broadcast([128, nh, 64]), op=ALU.mult),
              r=kqn + ["gains"], w=kqn)
            cosb = cos_ap.unsqueeze(1).to_broadcast([128, nh, 8])
            sinb = sin_ap.unsqueeze(1).to_broadcast([128, nh, 8])
            nsinb = nsin_ap.unsqueeze(1).to_broadcast([128, nh, 8])
            V(lambda e: e.tensor_tensor(out=rt_[:, :, 0:8], in0=qn_[:, :, 0:8], in1=cosb, op=ALU.mult), r=kqn + ["rope", "ropeC"], w=krt)
            V(lambda e: e.tensor_tensor(out=rt_[:, :, 8:16], in0=qn_[:, :, 8:16], in1=cosb, op=ALU.mult), r=kqn + ["rope", "ropeC"], w=krt)
            V(lambda e: e.tensor_tensor(out=rt2_[:, :, 0:8], in0=qn_[:, :, 8:16], in1=nsinb, op=ALU.mult), r=kqn + ["rope", "ropeC"], w=krt2)
            V(lambda e: e.tensor_tensor(out=rt2_[:, :, 8:16], in0=qn_[:, :, 0:8], in1=sinb, op=ALU.mult), r=kqn + ["rope", "ropeC"], w=krt2)
            V(lambda e: e.tensor_tensor(out=out_ap[:, :, 0:16], in0=rt_, in1=rt2_, op=ALU.add), r=krt + krt2, w=[wkey])
            A(lambda e: e.activation(out=out_ap[:, :, 16:64], in_=qn_[:, :, 16:64], func=AF.Copy), r=kqn, w=[wkey])

        def merge_streams(fa, fb):
            outer = sch.capture
            sch.capture = []
            fa()
            la = sch.capture
            sch.capture = []
            fb()
            lb = sch.capture
            sch.capture = outer
            na, nb = len(la), len(lb)
            ia = ib = 0
            while ia < na or ib < nb:
                if ib >= nb or (ia < na and ia * nb <= ib * na):
                    it = la[ia]
                    ia += 1
                else:
                    it = lb[ib]
                    ib += 1
                if outer is not None:
                    outer.append(it)
                else:
                    sch.replay(it)

        def silu_from_psum(pj, n, out_ap, c0, wkey):
            src = PJ[pj][:, 0:n]
            ek = ("ezs", c0)
            A(lambda e: e.activation(out=ezs[:, c0:c0 + n], in_=src, func=AF.Exp, scale=-1.0), r=[("PJ", pj)], w=[ek])
            A(lambda e: e.activation(out=ezs[:, c0:c0 + n], in_=ezs[:, c0:c0 + n], func=AF.Ln, bias=onef[:, 0:1]), r=[ek, "onef"], w=[ek])
            A(lambda e: e.activation(out=ezs[:, c0:c0 + n], in_=ezs[:, c0:c0 + n], func=AF.Exp, scale=-1.0), r=[ek], w=[ek])
            V(lambda e: e.tensor_tensor(out=out_ap, in0=src, in1=ezs[:, c0:c0 + n], op=ALU.mult), r=[("PJ", pj), ek], w=[wkey])

        def transpose_heads(src_tile, ncol, dst_tile, rkeys, wkey, halves=(0, 1)):
            for half in halves:
                pt = nxt("PT")
                for hh in range(8):
                    h = half * 8 + hh
                    P(lambda e, h=h, hh=hh, pt=pt: e.transpose(PT[pt][0:ncol, hh * 128:(hh + 1) * 128], src_tile[:, h, 0:ncol], ident[:]),
                      r=rkeys + ["ident"], w=[("PT", pt)])
                if half == 0:
                    V(lambda e, pt=pt: e.tensor_copy(out=dst_tile[0:ncol, 0:8, :].rearrange("p a b -> p (a b)"), in_=PT[pt][0:ncol, :]),
                      r=[("PT", pt)], w=[wkey])
                else:
                    V(lambda e, pt=pt: e.tensor_copy(out=dst_tile[0:ncol, 8:16, :].rearrange("p a b -> p (a b)"), in_=PT[pt][0:ncol, :]),
                      r=[("PT", pt)], w=[wkey])

        def k_to_KT(dst_ap, wkey, ncopy=128):
            pt = nxt("PT")
            for g in range(NG):
                P(lambda e, g=g, pt=pt: e.transpose(PT[pt][0:64, g * 128:(g + 1) * 128], kb[:, g, :], ident[:]),
                  r=["kb", "ident"], w=[("PT", pt)])
            V(lambda e, pt=pt: e.tensor_copy(out=dst_ap, in_=PT[pt][0:64, 0:512].rearrange("p (a b) -> p a b", a=NG)[:, :, 0:ncopy]),
              r=[("PT", pt)], w=[wkey])

        class Attn:
            def __init__(self):
                self.pend = None

            def unit(self, lhsT_k, rhs_q, bias_ap, v_ap, vw, po, first, rk, on_done=None):
                ps = nxt("PS")
                P(lambda e: e.matmul(PS[ps][:], lhsT_k, rhs_q, start=True, stop=(bias_ap is None)),
                  r=rk, w=[("PS", ps)])
                if bias_ap is not None:
                    P(lambda e: e.matmul(PS[ps][:], ident[:], bias_ap, start=False, stop=True),
                      r=["ident", "cb", "cb2", "cmask"], w=[("PS", ps)])
                sl = nxt("pT", 3)
                pt_t = T["pT%d" % sl]
                A(lambda e: e.activation(out=pt_t[:], in_=PS[ps][:], func=AF.Exp, scale=0.125),
                  r=[("PS", ps)], w=[("pT", sl)])
                self.flush()
                self.pend = (pt_t, sl, v_ap, vw, po, first, rk, on_done)

            def flush(self):
                if self.pend is None:
                    return
                pt_t, sl, v_ap, vw, po, first, rk, on_done = self.pend
                self.pend = None
                for r in range(4):
                    P(lambda e, r=r: e.matmul(PO[po][:, r * vw:(r + 1) * vw], pt_t[:, r * 128:(r + 1) * 128], v_ap,
                                              start=(first and r == 0), stop=True, skip_group_check=True),
                      r=[("pT", sl)] + rk, w=[("PO", po)])
                if on_done is not None:
                    on_done()

        def branch_post(po, vw, g, gate_idx, first_branch, clamp=False, gates=None, sl=0):
            rs, cc, otmp, oacc = T["rs"], T["cc"], T["otmp"], T["oacc"][sl]
            pov = PO[po][:, 0:4 * vw].rearrange("p (a b) -> p a b", a=4)
            if clamp:
                V(lambda e: e.tensor_scalar(out=rs[:], in0=pov[:, :, 64:65].rearrange("p a b -> p (a b)"), scalar1=1e-30, scalar2=None, op0=ALU.max),
                  r=[("PO", po)], w=["rs"])
                V(lambda e: e.reciprocal(rs[:], rs[:]), r=["rs"], w=["rs"])
            else:
                V(lambda e: e.reciprocal(rs[:], pov[:, :, 64:65].rearrange("p a b -> p (a b)")), r=[("PO", po)], w=["rs"])
            if gate_idx is not None:
                V(lambda e: e.tensor_tensor(out=cc[:], in0=rs[:], in1=gates[:, 4 * g:4 * g + 4, gate_idx:gate_idx + 1].rearrange("p a b -> p (a b)"),
                                            op=ALU.mult), r=["rs", ("gates", 0), ("gates", 1)], w=["cc"])
                cap = cc
                ck = "cc"
            else:
                cap = rs
                ck = "rs"
            if first_branch:
                V(lambda e: e.tensor_tensor(out=oacc[:, 4 * g:4 * g + 4, :], in0=pov[:, :, 0:64],
                                            in1=cap[:].unsqueeze(2).to_broadcast([128, 4, 64]), op=ALU.mult),
                  r=[("PO", po), ck], w=[("oacc", sl, g)])
            else:
                V(lambda e: e.tensor_tensor(out=otmp[:], in0=pov[:, :, 0:64],
                                            in1=cap[:].unsqueeze(2).to_broadcast([128, 4, 64]), op=ALU.mult),
                  r=[("PO", po), ck], w=["otmp"])
                G(lambda e: e.tensor_tensor(out=oacc[:, 4 * g:4 * g + 4, :], in0=oacc[:, 4 * g:4 * g + 4, :], in1=otmp[:], op=ALU.add),
                  r=["otmp", ("oacc", sl, g)], w=[("oacc", sl, g)])

        def out_proj_store(b, i, src_d_, sl):
            oacc, og, ogT, ytmp, szb = T["oacc"][sl], T["og"], T["ogT"], T["ytmp"], T["szb"][sl]
            xs = 1
            sch.dma("sp", xt[xs][:], src_d_[b, i * 128:(i + 1) * 128, :], reads=[("res", b, i)], writes=[("xt", xs)])
            V(lambda e: e.tensor_tensor(out=og[:], in0=oacc[:].rearrange("p a b -> p (a b)"), in1=szb[:], op=ALU.mult),
              r=[("oacc", sl, g) for g in range(NG)] + [("szb", sl)], w=["og"])
            yield
            pt = nxt("PT")
            for k in range(8):
                P(lambda e, k=k, pt=pt: e.transpose(PT[pt][:, k * 128:(k + 1) * 128], og[:, k * 128:(k + 1) * 128], ident[:]),
                  r=["og", "ident"], w=[("PT", pt)])
            V(lambda e: e.tensor_copy(out=ogT[:].rearrange("p a b -> p (a b)"), in_=PT[pt][:]),
              r=[("PT", pt)], w=["ogT"])
            yield
            for half in range(2):
                pj = rr["PO"]
                for k in range(8):
                    P(lambda e, k=k, pj=pj, half=half: e.matmul(PO[pj][:], ogT[:, k, :], WO[:, k, half * 512:(half + 1) * 512],
                                                               start=(k == 0), stop=(k == 7)),
                      r=["ogT", ("WO", half)], w=[("PO", pj)])
                V(lambda e, pj=pj, half=half: e.tensor_tensor(out=ytmp[:], in0=PO[pj][:], in1=modbc[:, 2, half * 512:(half + 1) * 512], op=ALU.mult),
                  r=[("PO", pj), ("modbc", 2)], w=["ytmp"])
                G(lambda e, half=half: e.tensor_tensor(out=xt[xs][:, half * 512:(half + 1) * 512], in0=ytmp[:],
                                                       in1=xt[xs][:, half * 512:(half + 1) * 512], op=ALU.add),
                  r=["ytmp", ("xt", xs)], w=[("xt", xs)])
                if half == 0:
                    yield
            sch.dma("sp", y_d[b, i * 128:(i + 1) * 128, :], xt[xs][:], reads=[("xt", xs)], writes=[("res", b, i)])
            yield

        def load_w_cols(dst_col0, src_ap_fn, ncols, wstride, key="W1"):
            c = 0
            w1v = W1[:, 0:8 * wstride].rearrange("p (k n) -> p k n", k=8)
            while c < ncols:
                n = min(512 - (dst_col0 + c) % 512, ncols - c)
                sch.dma("pool", w1v[:, :, dst_col0 + c:dst_col0 + c + n], src_ap_fn(c, n), writes=[(key, (dst_col0 + c) // 512)])
                c += n

        def load_wo(wo_src):
            for hlf in range(2):
                sch.dma("pool", WO[:, :, hlf * 512:(hlf + 1) * 512],
                        wo_src[:, hlf * 512:(hlf + 1) * 512].rearrange("(k p) n -> p k n", p=128), writes=[("WO", hlf)])

        def bload(dst_ap, src_row_ap, n, key):
            sch.dma("sp", dst_ap, src_row_ap.to_broadcast([128, n]), writes=[key])

        at_box = [None]

        def at_flush():
            at_box[0].flush()

        def run_pipelined(front, back_units, tail, n_tiles, prologue=None):
            def capture_front(i):
                fixed_bank[0] = 0
                sch.capture = []
                crit = None
                for m in front(i):
                    if m == "crit":
                        crit = len(sch.capture)
                ops = sch.capture
                sch.capture = None
                fixed_bank[0] = 1
                return ops, (len(ops) if crit is None else crit)

            ops, crit_pos = capture_front(0)
            pos = 0
            while pos < crit_pos:
                sch.replay(ops[pos])
                pos += 1
            leftover = ops[pos:]
            if prologue is not None:
                prologue()
            tail_gen = iter(())
            DONE = object()
            for i in range(n_tiles):
                if i + 1 < n_tiles:
                    new_ops, new_crit = capture_front(i + 1)
                else:
                    new_ops, new_crit = [], 0
                n_left = len(leftover)
                ops = leftover + new_ops
                crit_pos = n_left + new_crit
                pos = 0
                units = back_units(i)
                nf = units.index("need_front") if "need_front" in units else len(units)
                nu_front = sum(1 for u in units[:nf] if u != "need_front")
                ui = 0
                tail_done = False
                for u in units:
                    if u == "need_front":
                        for _ in tail_gen:
                            pass
                        tail_done = True
                        while pos < crit_pos:
                            sch.replay(ops[pos])
                            pos += 1
                        continue
                    u()
                    if ui >= 1 and not tail_done:
                        if next(tail_gen, DONE) is DONE:
                            tail_done = True
                    lim = len(ops)
                    if pos < lim:
                        if pos < crit_pos:
                            left = max(1, nu_front - ui - 1)
                            k = -(-(crit_pos - pos) // left)
                        else:
                            k = 3
                        for _ in range(k):
                            if pos < lim:
                                sch.replay(ops[pos])
                                pos += 1
                    ui += 1
                for _ in tail_gen:
                    pass
                while pos < crit_pos:
                    sch.replay(ops[pos])
                    pos += 1
                leftover = ops[pos:]
                tail_gen = tail(i)
            for it in leftover:
                sch.replay(it)
            at_flush()
            for _ in tail_gen:
                pass
            fixed_bank[0] = None

        def moba_layer(b, j, src_d):
            WS = MOBA_IN
            load_w_cols(0, lambda c, n: dr["moba_w_in"][j, :, c:c + n].rearrange("(k p) n -> p k n", p=128), MOBA_IN, WS)
            bload(gq[:], dr["moba_q_norm"][j:j + 1, :], 64, "gains")
            bload(gk[:, 0, :], dr["moba_k_norm"][j:j + 1, :], 64, "gains")
            for g in range(NG):
                sch.dma("pool", KA[64:72, g, :], dr["c_ind_moba"], writes=["KAind"])
            load_wo(dr["moba_w_out"][j])
            with ExitStack() as esc:
                alloc_attn_tiles(esc)
                sg, m8 = T["sg"], T["m8"]
                ctx = {}

                def front(i):
                    sl = i % 2
                    qaug, qT1, qaT, szb = T["qaug"][sl], T["qT1"][sl], T["qaT"][sl], T["szb"][sl]
                    xs, hs = load_norm_h(b, i, src_d[b, i * 128:(i + 1) * 128, :], xs=0)
                    ctx[i] = src_d
                    yield
                    cs, sn, nsn = cosT[:, i, :], sinT[:, i, :], nsinT[:, i, :]
                    pq0 = project(hs, 0, 512, WS)
                    pq1 = project(hs, 512, 512, WS)
                    head_norm_rope(pq0, 0, 8, gq[:], cs, sn, nsn, qaug[:, 0:8, 0:64], ("qaug", sl, 0))
                    pkv = project(hs, 1024, 512, WS)
                    transpose_heads(qaug, 64, qT1, [("qaug", sl, 0)], ("qT1", sl), halves=(0,))
                    merge_streams(lambda: head_norm_rope(pq1, 0, 8, gq[:], cs, sn, nsn, qaug[:, 8:16, 0:64], ("qaug", sl, 1)),
                                  lambda: head_norm_rope(pkv, 0, 4, gk[:, 0, :], cs, sn, nsn, kb[:], "kb", alt=True))
                    transpose_heads(qaug, 64, qT1, [("qaug", sl, 1)], ("qT1", sl), halves=(1,))
                    A(lambda e, pj=pkv, i=i: e.activation(out=VA[:, i, :, 0:64], in_=PJ[pj][:, 256:512].rearrange("p (a b) -> p a b", a=NG), func=AF.Copy),
                      r=[("PJ", pkv)], w=[("VA", i)])
                    k_to_KT(KA[0:64, :, i * 128:(i + 1) * 128], ("KA", i))
                    pj3 = nxt("PJ")
                    for h in range(NH):
                        P(lambda e, h=h, pj3=pj3: e.matmul(PJ[pj3][:, h * 8:(h + 1) * 8], qT1[0:64, h, :], kmT[0:64, h // 4, :], start=True, stop=True),
                          r=[("qT1", sl), "kmT"], w=[("PJ", pj3)])
                    V(lambda e, pj3=pj3, i=i: e.tensor_tensor(out=sg[:], in0=PJ[pj3][:, 0:128].rearrange("p (a b) -> p a b", a=NH),
                                                             in1=fbm[:, i, :].unsqueeze(1).to_broadcast([128, NH, 8]), op=ALU.add),
                      r=[("PJ", pj3), "fbm"], w=["sg"])
                    for h in range(NH):
                        V(lambda e, h=h: e.max(out=m8[:, h, :], in_=sg[:, h, :]), r=["sg"], w=[("m8", h)])
                    V(lambda e: e.tensor_tensor(out=qaug[:, :, 64:72], in0=sg[:], in1=m8[:, :, 3:4].to_broadcast([128, NH, 8]), op=ALU.is_lt),
                      r=["sg"] + [("m8", h) for h in range(NH)], w=[("qaug", sl, 2)])
                    transpose_heads(qaug, 72, qaT, [("qaug", sl, 0), ("qaug", sl, 1), ("qaug", sl, 2)], ("qaT", sl))
                    yield "crit"
                    pj2 = nxt("PJ")
                    for g in range(NG):
                        P(lambda e, g=g, pj2=pj2: e.matmul(PJ[pj2][0:64, g:g + 1], kb[:, g, :], onesc[:, 0:1], start=True, stop=True),
                          r=["kb", "onesc"], w=[("PJ", pj2)])
                    V(lambda e, pj2=pj2, i=i: e.tensor_copy(out=ksum[:, i, :], in_=PJ[pj2][0:64, 0:NG]), r=[("PJ", pj2)], w=["ksum"])
                    if i % 2 == 1:
                        jb = i // 2
                        V(lambda e, i=i: e.tensor_tensor(out=kmtmp[:], in0=ksum[:, i - 1, :], in1=ksum[:, i, :], op=ALU.add), r=["ksum"], w=["kmtmp"])
                        V(lambda e, jb=jb: e.tensor_scalar(out=kmT[:, :, jb:jb + 1].rearrange("p a b -> p (a b)"), in0=kmtmp[:], scalar1=1.0 / 256.0,
                                                           scalar2=None, op0=ALU.mult), r=["kmtmp"], w=["kmT"])
                    pz0 = project(hs, 1536, 512, WS)
                    pz1 = project(hs, 2048, 512, WS)
                    silu_from_psum(pz0, 512, szb[:, 0:512], 0, ("szb", sl))
                    silu_from_psum(pz1, 512, szb[:, 512:1024], 512, ("szb", sl))
                    yield

                at = Attn()

                def back_units(i):
                    sl = i % 2
                    qaT = T["qaT"][sl]
                    units = []
                    for g in range(NG):
                        po_box = []
                        for jj in range(i + 1):
                            def u(g=g, jj=jj, po_box=po_box):
                                if jj == 0:
                                    po_box.append(nxt("PO"))
                                po = po_box[0]
                                bias = cb[:].unsqueeze(1).to_broadcast([128, 4, 128]) if jj == i else None
                                done = (lambda: branch_post(po, 65, g, None, True, sl=sl)) if jj == i else None
                                at.unit(KA[:, g, jj * 128:(jj + 1) * 128], qaT[:, 4 * g:4 * g + 4, :], bias, VA[:, jj, g, :], 65, po,
                                        jj == 0, [("KA", jj), "KAind", ("qaT", sl), ("VA", jj), "VA1"], done)
                            units.append(u)
                    return units

                def tail(i):
                    return out_proj_store(b, i, ctx[i], i % 2)

                at_box[0] = at
                run_pipelined(front, back_units, tail, NT)
                sch.barrier()

        def nsa_layer(b, j, src_d):
            w_in = dr["nsa_w_in"]
            load_w_cols(0, lambda c, n: w_in[j, :, 1024 + c:1024 + c + n].rearrange("(k p) n -> p k n", p=128), 1024, 1024)
            CW1 = W1[:, 8192:16384].rearrange("p (a m h) -> p a m h", a=2, m=16)
            for kv in range(2):
                for mh in range(2):
                    sch.dma("pool", CW1[:, kv, mh * 8:(mh + 1) * 8, :],
                            dr["nsa_cmp_w1"][j, kv, mh * 1024:(mh + 1) * 1024, :].rearrange("(m p) h -> p m h", p=128), writes=[("CW1", kv, mh)])
                sch.dma("pool", cw2[:, kv, :, :], dr["nsa_cmp_w2"][j, kv].rearrange("(h p) d -> p h d", p=128), writes=["cw2"])
                sch.dma("sp", peT2f[:, kv, :], dr["nsa_cmp_pe"][j, kv].rearrange("l d -> (l d)").rearrange("(m p) -> p m", p=128),
                        writes=["peT2f"], allow_slow_non_contiguous=True)
                sch.dma("sp", b1c[:, kv, :], dr["nsa_cmp_b1"][j, kv].rearrange("(h p) -> p h", p=128), writes=["b1c"],
                        allow_slow_non_contiguous=True)
            V(lambda e: e.tensor_copy(out=peT2[:], in_=peT2f[:]), r=["peT2f"], w=["peT2"])
            bload(gq[:], dr["nsa_q_norm"][j:j + 1, :], 64, "gains")
            for m in range(3):
                bload(gk[:, m, :], dr["nsa_k_norm"][j, m:m + 1, :], 64, "gains")
            bload(gateb[:], dr["nsa_gate_b"][j:j + 1, :], 48, "gateb")
            for g in range(NG):
                sch.dma("pool", KA[64:96, g, :], dr["c_ind_nsa"], writes=["KAind"])
            load_wo(dr["nsa_w_out"][j])
            if DBG_STAGE[0] < 2:
                return
            with ExitStack() as esa:
                alloc_cmp_tiles(esa)
                KC2, hidT, hu, hw, kvb = T["KC2"], T["hidT"], T["hu"], T["hw"], T["kvb"]

                def a_stage2(i, hs):
                    pjA = project(hs, 0, 512, 1024)
                    pjB = project(hs, 512, 512, 1024)
                    A(lambda e, pjA=pjA: e.activation(out=kvb[:], in_=PJ[pjA][:], func=AF.Copy), r=[("PJ", pjA)], w=["kvb"])
                    head_norm_rope(pjB, 0, 4, gk[:, 1, :], cosT[:, i, :], sinT[:, i, :], nsinT[:, i, :], kb[:], "kb")
                    A(lambda e, pjB=pjB, i=i: e.activation(out=VA[:, i, :, 0:64], in_=PJ[pjB][:, 256:512].rearrange("p (a b) -> p a b", a=NG), func=AF.Copy),
                      r=[("PJ", pjB)], w=[("VA", i)])
                    pjs = nxt("PJ")
                    for kv in range(2):
                        for g in range(NG):
                            for par in range(2):
                                c0 = (kv * 4 + g) * 64
                                P(lambda e, kv=kv, g=g, par=par, c0=c0, pjs=pjs: e.matmul(
                                    PJ[pjs][par * 64:(par + 1) * 64, c0:c0 + 64], kvb[:, kv * 256 + g * 64:kv * 256 + (g + 1) * 64],
                                    selm[:, par, :], start=True, stop=True), r=["kvb", "selm"], w=[("PJ", pjs)])
                    V(lambda e, pjs=pjs, i=i: e.tensor_copy(out=KC2[:, :, :, i * 64:(i + 1) * 64],
                                                           in_=PJ[pjs][:].rearrange("p (a g u) -> p a g u", a=2, g=NG)),
                      r=[("PJ", pjs)], w=[("KC2", i)])
                    k_to_KT(KA[0:64, :, i * 128:(i + 1) * 128], ("KA", i))

                def cap(fn, pt):
                    pin_pt[0] = pt
                    sch.capture = []
                    r = fn()
                    ops_ = sch.capture
                    sch.capture = None
                    pin_pt[0] = None
                    return ops_, r

                def merge(o1, o2):
                    n1, n2 = len(o1), len(o2)
                    i1 = i2 = 0
                    while i1 < n1 or i2 < n2:
                        if i2 >= n2 or (i1 < n1 and i1 * n2 <= i2 * n1):
                            sch.replay(o1[i1])
                            i1 += 1
                        else:
                            sch.replay(o2[i2])
                            i2 += 1

                o1, (xs0, hs_prev) = cap(lambda: load_norm_h(b, 0, src_d[b, 0:128, :]), 0)
                merge(o1, [])
                for i in range(NT):
                    if i + 1 < NT:
                        o1, (xs1, hs_next) = cap(lambda i=i: load_norm_h(b, i + 1, src_d[b, (i + 1) * 128:(i + 2) * 128, :]), 0)
                    else:
                        o1, hs_next = [], None
                    o2, _ = cap(lambda i=i, hs_prev=hs_prev: a_stage2(i, hs_prev), 1)
                    merge(o1, o2)
                    hs_prev = hs_next
                if DBG_STAGE[0] < 3:
                    sch.barrier()
                    return
                kc2keys = [("KC2", i) for i in range(NT)]
                pjb = nxt("PJ")
                for kv in range(2):
                    for half in range(2):
                        col = kv * 2 + half
                        for m in range(16):
                            P(lambda e, kv=kv, half=half, m=m, col=col, pjb=pjb: e.matmul(
                                PJ[pjb][:, col:col + 1], CW1[:, kv, m, half * 128:(half + 1) * 128], peT2[:, kv, m:m + 1],
                                start=(m == 0), stop=(m == 15)), r=[("CW1", kv, m // 8), "peT2"], w=[("PJ", pjb)])
                V(lambda e, pjb=pjb: e.tensor_tensor(out=b1p[:].rearrange("p a b -> p (a b)"), in0=PJ[pjb][:, 0:4],
                                                    in1=b1c[:].rearrange("p a b -> p (a b)"), op=ALU.add), r=[("PJ", pjb), "b1c"], w=["b1p"])
                for kv in range(2):
                    for half in range(2):
                        ps_ = nxt("PJ")
                        for m in range(16):
                            P(lambda e, kv=kv, half=half, m=m, ps_=ps_: e.matmul(
                                PJ[ps_][:, 0:508], CW1[:, kv, m, half * 128:(half + 1) * 128], KC2[:, kv, :, m:m + 1009:8],
                                start=(m == 0), stop=(m == 15)), r=[("CW1", kv, m // 8)] + kc2keys, w=[("PJ", ps_)])
                        V(lambda e, kv=kv, half=half, ps_=ps_: e.tensor_scalar(out=hu[:], in0=PJ[ps_][:, 0:508], scalar1=b1p[:, kv, half:half + 1],
                                                                              scalar2=None, op0=ALU.add), r=[("PJ", ps_), "b1p"], w=["hu"])
                        V(lambda e: e.tensor_tensor(out=hw[:], in0=hu[:], in1=hu[:], op=ALU.mult), r=["hu"], w=["hw"])
                        V(lambda e: e.tensor_scalar(out=hw[:], in0=hw[:], scalar1=0.044715, scalar2=1.0, op0=ALU.mult, op1=ALU.add), r=["hw"], w=["hw"])
                        V(lambda e: e.tensor_tensor(out=hw[:], in0=hw[:], in1=hu[:], op=ALU.mult), r=["hw", "hu"], w=["hw"])
                        A(lambda e: e.activation(out=hw[:], in_=hw[:], func=AF.Exp, scale=-1.5957691216057308), r=["hw"], w=["hw"])
                        V(lambda e: e.tensor_scalar_add(hw[:], hw[:], 1.0), r=["hw"], w=["hw"])
                        V(lambda e: e.reciprocal(hw[:], hw[:]), r=["hw"], w=["hw"])
                        V(lambda e, kv=kv, half=half: e.tensor_tensor(out=hidT[:, kv * 2 + half, :, :].rearrange("p a b -> p (a b)"), in0=hu[:], in1=hw[:],
                                                                      op=ALU.mult), r=["hu", "hw"], w=["hidT"])
                pj2 = nxt("PJ")
                for kv in range(2):
                    for g in range(NG):
                        c0 = (kv * 4 + g) * 64
                        for half in range(2):
                            P(lambda e, kv=kv, g=g, half=half, c0=c0, pj2=pj2: e.matmul(
                                PJ[pj2][0:127, c0:c0 + 64], hidT[:, kv * 2 + half, g, :], cw2[:, kv, half, :],
                                start=(half == 0), stop=(half == 1)), r=["hidT", "cw2"], w=[("PJ", pj2)])
                head_norm_rope(pj2, 0, 4, gk[:, 0, :], cosC[:], sinC[:], nsinC[:], kb[:], "kb")
                A(lambda e, pj2=pj2: e.activation(out=vcmp[0:127, :, 0:64], in_=PJ[pj2][0:127, 256:512].rearrange("p (a b) -> p a b", a=NG), func=AF.Copy),
                  r=[("PJ", pj2)], w=["vcmp"])
                k_to_KT(kcmpT[0:64, :, 0:127], "kcmpT", 127)
                sch.barrier()
            if DBG_STAGE[0] < 4:
                return
            WS = 2608
            load_w_cols(0, lambda c, n: w_in[j, :, c:c + n].rearrange("(k p) n -> p k n", p=128), 1024, WS)
            load_w_cols(1024, lambda c, n: w_in[j, :, 2048 + c:2048 + c + n].rearrange("(k p) n -> p k n", p=128), 1584, WS)
            with ExitStack() as esc:
                alloc_attn_tiles(esc)
                gtmp, imp, m8, rs = T["gtmp"], T["imp"], T["m8"], T["rs"]
                ctx = {}

                def front(i):
                    sl = i % 2
                    qaug, qT1, szb, gates = T["qaug"][sl], T["qT1"][sl], T["szb"][sl], T["gates"][sl]
                    xs, hs = load_norm_h(b, i, src_d[b, i * 128:(i + 1) * 128, :], xs=0)
                    ctx[i] = src_d
                    yield
                    cs, sn, nsn = cosT[:, i, :], sinT[:, i, :], nsinT[:, i, :]
                    slot = i % 4
                    pq0 = project(hs, 0, 512, WS)
                    pq1 = project(hs, 512, 512, WS)
                    head_norm_rope(pq0, 0, 8, gq[:], cs, sn, nsn, qaug[:, 0:8, 0:64], ("qaug", sl, 0))
                    pkw = project(hs, 1024, 512, WS)
                    transpose_heads(qaug, 64, qT1, [("qaug", sl, 0)], ("qT1", sl), halves=(0,))
                    merge_streams(lambda: head_norm_rope(pq1, 0, 8, gq[:], cs, sn, nsn, qaug[:, 8:16, 0:64], ("qaug", sl, 1)),
                                  lambda: head_norm_rope(pkw, 0, 4, gk[:, 2, :], cs, sn, nsn, kb[:], "kb", alt=True))
                    transpose_heads(qaug, 64, qT1, [("qaug", sl, 1)], ("qT1", sl), halves=(1,))
                    pgl = project(hs, 1536, 48, WS)
                    A(lambda e, pj=pkw, slot=slot: e.activation(out=VW[:, slot, :, 0:64], in_=PJ[pj][:, 256:512].rearrange("p (a b) -> p a b", a=NG), func=AF.Copy),
                      r=[("PJ", pkw)], w=[("VW", slot)])
                    k_to_KT(KW[0:64, :, slot, :], ("KW", slot))
                    V(lambda e, pj=pgl: e.tensor_tensor(out=gtmp[:], in0=PJ[pj][:, 0:48], in1=gateb[:], op=ALU.add), r=[("PJ", pgl), "gateb"], w=["gtmp"])
                    A(lambda e: e.activation(out=gtmp[:], in_=gtmp[:], func=AF.Exp, scale=-1.0), r=["gtmp"], w=["gtmp"])
                    A(lambda e: e.activation(out=gtmp[:], in_=gtmp[:], func=AF.Ln, bias=onef[:, 0:1]), r=["gtmp", "onef"], w=["gtmp"])
                    A(lambda e: e.activation(out=gates[:].rearrange("p a b -> p (a b)"), in_=gtmp[:], func=AF.Exp, scale=-1.0), r=["gtmp"], w=[("gates", sl)])
                    yield "crit"
                    pz0 = project(hs, 1584, 512, WS)
                    pz1 = project(hs, 2096, 512, WS)
                    silu_from_psum(pz0, 512, szb[:, 0:512], 0, ("szb", sl))
                    silu_from_psum(pz1, 512, szb[:, 512:1024], 512, ("szb", sl))
                    yield

                at = Attn()

                def cmp_post(po, g, i):
                    sl = i % 2
                    qaug = T["qaug"][sl]
                    branch_post(po, 97, g, 0, True, clamp=True, gates=T["gates"][sl], sl=sl)
                    pov = PO[po][:, 0:388].rearrange("p (a b) -> p a b", a=4)
                    V(lambda e: e.tensor_scalar(out=imp[:, g, :], in0=pov[:, 0, 65:97], scalar1=rs[:, 0:1], scalar2=None, op0=ALU.mult),
                      r=[("PO", po), "rs"], w=[("imp", g)])
                    for r_ in range(1, 4):
                        V(lambda e, r_=r_: e.scalar_tensor_tensor(out=imp[:, g, :], in0=pov[:, r_, 65:97], scalar=rs[:, r_:r_ + 1], in1=imp[:, g, :],
                                                                  op0=ALU.mult, op1=ALU.add), r=[("PO", po), "rs", ("imp", g)], w=[("imp", g)])
                    V(lambda e: e.tensor_tensor(out=imp[:, g, :], in0=imp[:, g, :], in1=fb[:, i, :], op=ALU.add), r=[("imp", g), "fb"], w=[("imp", g)])
                    V(lambda e: e.max(out=m8[:, g, :], in_=imp[:, g, :]), r=[("imp", g)], w=[("m8", g)])
                    V(lambda e: e.tensor_tensor(out=qaug[:, 4 * g:4 * g + 4, 64:96], in0=imp[:, g, :].unsqueeze(1).to_broadcast([128, 4, 32]),
                                                in1=m8[:, g, 7:8].unsqueeze(1).to_broadcast([128, 4, 32]), op=ALU.is_lt),
                      r=[("imp", g), ("m8", g)], w=[("qaug", sl, 2)])

                def cmp_unit(i, g):
                    sl = i % 2
                    qT1 = T["qT1"][sl]

                    def u():
                        po = nxt("PO")
                        at.unit(kcmpT[:, g, :], qT1[:, 4 * g:4 * g + 4, :], cmask[:, i * 128:(i + 1) * 128].unsqueeze(1).to_broadcast([128, 4, 128]),
                                vcmp[:, g, :], 97, po, True, ["kcmpT", ("qT1", sl), "vcmp"], (lambda: cmp_post(po, g, i)))
                    return u

                def back_units(i):
                    sl = i % 2
                    qaug, qT1, qaT, gates = T["qaug"][sl], T["qT1"][sl], T["qaT"][sl], T["gates"][sl]
                    units = []
                    tiles = [jj for jj in (i - 2, i - 1, i) if jj >= 0]
                    for g in range(NG):
                        po_box = []
                        for idx, jj in enumerate(tiles):
                            def u(g=g, jj=jj, idx=idx, po_box=po_box):
                                if idx == 0:
                                    po_box.append(nxt("PO"))
                                po = po_box[0]
                                if jj == i:
                                    bias = cb[:].unsqueeze(1).to_broadcast([128, 4, 128])
                                elif jj == i - 2:
                                    bias = cb2[:].unsqueeze(1).to_broadcast([128, 4, 128])
                                else:
                                    bias = None
                                done = (lambda: branch_post(po, 65, g, 2, False, gates=gates, sl=sl)) if jj == i else None
                                at.unit(KW[:, g, jj % 4, :], qT1[:, 4 * g:4 * g + 4, :], bias, VW[:, jj % 4, g, :], 65, po, idx == 0,
                                        [("KW", jj % 4), ("qT1", sl), ("VW", jj % 4), "VW1"], done)
                            units.append(u)

                    def tr():
                        at.flush()
                        transpose_heads(qaug, 96, qaT, [("qaug", sl, 0), ("qaug", sl, 1), ("qaug", sl, 2)], ("qaT", sl))
                    units.append(tr)
                    for g in range(NG):
                        po_box = []
                        for jj in range(i + 1):
                            def u(g=g, jj=jj, po_box=po_box):
                                if jj == 0:
                                    po_box.append(nxt("PO"))
                                po = po_box[0]
                                bias = cb[:].unsqueeze(1).to_broadcast([128, 4, 128]) if jj == i else None
                                done = (lambda: branch_post(po, 65, g, 1, False, gates=gates, sl=sl)) if jj == i else None
                                at.unit(KA[:, g, jj * 128:(jj + 1) * 128], qaT[:, 4 * g:4 * g + 4, :], bias, VA[:, jj, g, :], 65, po,
                                        jj == 0, [("KA", jj), "KAind", ("qaT", sl), ("VA", jj), "VA1"], done)
                            units.append(u)
                        if i + 1 < NT:
                            if g == 0:
                                units.append("need_front")
                            units.append(cmp_unit(i + 1, g))
                    return units

                def tail(i):
                    return out_proj_store(b, i, ctx[i], i % 2)

                def prologue():
                    for g in range(NG):
                        cmp_unit(0, g)()

                at_box[0] = at
                run_pipelined(front, back_units, tail, NT, prologue)
                sch.barrier()

        for b in range(n_seq):
            with ExitStack() as es3:
                E3 = es3.enter_context
                posi = E3(nc.sbuf_tensor("posi%d" % b, [128, NT + 1], I32))
                posf = E3(nc.sbuf_tensor("posf%d" % b, [128, NT + 1], F32))
                ang = E3(nc.sbuf_tensor("ang%d" % b, [128, NT + 1, 8], F32))
                kq = E3(nc.sbuf_tensor("kq%d" % b, [128, NT + 1, 8], F32))
                ki = E3(nc.sbuf_tensor("ki%d" % b, [128, NT + 1, 8], I32))
                red = E3(nc.sbuf_tensor("red%d" % b, [128, NT + 1, 8], F32))
                V(lambda e: e.memset(posi[:], 0), w=["posi"])
                sch.dma("sp", posi[:, 0:NT], dr["positions"][b].rearrange("(i p) -> p i", p=128), writes=["posi"],
                        allow_slow_non_contiguous=True)
                sch.dma("sp", posi[0:127, NT:NT + 1], dr["positions"][b, 16:16 + 16 * 127].rearrange("(n s) -> n s", s=16)[:, 15:16],
                        writes=["posi"], allow_slow_non_contiguous=True)
                V(lambda e: e.tensor_copy(out=posf[:], in_=posi[:]), r=["posi"], w=["posf"])
                V(lambda e: e.tensor_tensor(out=ang[:], in0=posf[:].unsqueeze(2).to_broadcast([128, NT + 1, 8]),
                                            in1=invf[:].unsqueeze(1).to_broadcast([128, NT + 1, 8]), op=ALU.mult),
                  r=["posf", "invf"], w=["ang"])
                C1 = 6.28125
                C2 = float(2.0 * np.pi - 6.28125)
                PI_LO = 3.1415925
                for which, off in (("sin", 0.0), ("cos", 0.25)):
                    V(lambda e, off=off: e.tensor_scalar(out=kq[:], in0=ang[:], scalar1=float(1.0 / (2.0 * np.pi)), scalar2=off,
                                                         op0=ALU.mult, op1=ALU.add), r=["ang"], w=["kq"])
                    V(lambda e: e.tensor_copy(out=ki[:], in_=kq[:]), r=["kq"], w=["ki"])
                    V(lambda e: e.tensor_copy(out=kq[:], in_=ki[:]), r=["ki"], w=["kq"])
                    V(lambda e: e.scalar_tensor_tensor(out=red[:], in0=kq[:], scalar=-C1, in1=ang[:], op0=ALU.mult, op1=ALU.add),
                      r=["kq", "ang"], w=["red"])
                    V(lambda e: e.scalar_tensor_tensor(out=red[:], in0=kq[:], scalar=-C2, in1=red[:], op0=ALU.mult, op1=ALU.add),
                      r=["kq", "red"], w=["red"])
                    if which == "cos":
                        V(lambda e: e.tensor_scalar(out=red[:], in0=red[:], scalar1=float(np.pi / 2), scalar2=PI_LO,
                                                    op0=ALU.add, op1=ALU.min), r=["red"], w=["red"])
                        V(lambda e: e.tensor_scalar(out=red[:], in0=red[:], scalar1=-PI_LO, scalar2=None, op0=ALU.max), r=["red"], w=["red"])
                        A(lambda e: e.activation(out=cosT[:], in_=red[:, 0:NT, :], func=AF.Sin), r=["red"], w=["rope"])
                        A(lambda e: e.activation(out=cosC[:], in_=red[:, NT, :], func=AF.Sin), r=["red"], w=["ropeC"])
                    else:
                        V(lambda e: e.tensor_scalar(out=red[:], in0=red[:], scalar1=PI_LO, scalar2=-PI_LO,
                                                    op0=ALU.min, op1=ALU.max), r=["red"], w=["red"])
                        A(lambda e: e.activation(out=sinT[:], in_=red[:, 0:NT, :], func=AF.Sin), r=["red"], w=["rope"])
                        A(lambda e: e.activation(out=sinC[:], in_=red[:, NT, :], func=AF.Sin), r=["red"], w=["ropeC"])
                V(lambda e: e.tensor_scalar(out=nsinT[:], in0=sinT[:], scalar1=-1.0, scalar2=None, op0=ALU.mult), r=["rope"], w=["rope"])
                V(lambda e: e.tensor_scalar(out=nsinC[:], in0=sinC[:], scalar1=-1.0, scalar2=None, op0=ALU.mult), r=["ropeC"], w=["ropeC"])
                sch.barrier()

            for li, L in enumerate(layers):
                src_d = dr["x"] if (li == 0 and first_from_x) else y_d
                is_nsa = (L % 2 == 0)
                j = L // 2
                for m in range(3):
                    sch.dma("sp", modbc[:, m, :], mod_d[b, L:L + 1, m * D:(m + 1) * D].to_broadcast([128, D]),
                            reads=[("mod", b, L)], writes=[("modbc", m)])
                if is_nsa:
                    nsa_layer(b, j, src_d)
                else:
                    moba_layer(b, j, src_d)
        sch.finish()
    return nc


_CONSTS = None


def _consts():
    global _CONSTS
    if _CONSTS is None:
        _CONSTS = host_consts()
    return _CONSTS


def run_layers(inputs, layers, n_seq, n_cores, first_from_x=True, trace=False):
    nc = build(layers, n_seq, first_from_x)
    cst = _consts()
    in_maps = []
    for c in range(n_cores):
        m = {}
        m["x"] = np.ascontiguousarray(inputs["x"][c * n_seq:(c + 1) * n_seq], dtype=np.float32)
        m["c"] = np.ascontiguousarray(inputs["c"][c * n_seq:(c + 1) * n_seq], dtype=np.float32)
        m["positions"] = np.ascontiguousarray(inputs["positions"][c * n_seq:(c + 1) * n_seq], dtype=np.int32)
        for k in WEIGHT_SHAPES:
            m[k] = np.ascontiguousarray(inputs[k], dtype=np.float32)
        for k in CONST_SHAPES:
            m[k] = cst[k]
        in_maps.append(m)
    res = run_bass_kernel_spmd(nc, in_maps, core_ids=list(range(n_cores)), trace=trace)
    out = np.concatenate([np.asarray(r["y"]) for r in res.results], axis=0)
    return out, res


def kernel(**inputs):
    out, _ = run_layers(inputs, [0, 1, 2, 3], 2, N_CORES)
    return out.astype(np.float32)
```
